# Optimizing a Trainium2 kernel written in Bass

```python
import math
import jax
import jax.numpy as jnp
from jax import lax
import numpy as np

D_MODEL = 1024
BATCH = 8
SEQ = 4096
DEPTH = 2

F32 = jnp.float32
GRID_W = 64
CTX_LEN = 256
EPS = 1e-6
CONV_K = 7
ROPE_THETA = 10000.0
N_BRANCH = 4

MLSTM_HEAD_DIM = 64
MLSTM_WIDTH = D_MODEL // 4
MLSTM_HEADS = MLSTM_WIDTH // MLSTM_HEAD_DIM
MLSTM_CHUNK = 64
S5_WIDTH = D_MODEL // 4
S5_GROUP = 16
S5_GROUPS = S5_WIDTH // S5_GROUP
S5_STATE = 64
NA_HEAD_DIM = 64
NA_WIDTH = D_MODEL // 4
NA_HEADS = NA_WIDTH // NA_HEAD_DIM
NA_KH = 8
NA_KW = 16
SSD_HEAD_DIM = 64
SSD_WIDTH = D_MODEL // 2
SSD_HEADS = SSD_WIDTH // SSD_HEAD_DIM
SSD_GROUPS = 2
SSD_STATE = 128
SSD_CHUNK = 128
FFN_HIDDEN = -(-8 * D_MODEL // (3 * 256)) * 256

IN_LAYOUT = (
    ('mq', MLSTM_WIDTH), ('mk', MLSTM_WIDTH), ('mv', MLSTM_WIDTH), ('mo', MLSTM_WIDTH),
    ('mi', 2 * MLSTM_HEADS), ('mf', 2 * MLSTM_HEADS),
    ('su', S5_WIDTH),
    ('nqkv', 3 * NA_WIDTH),
    ('dz', SSD_WIDTH), ('dx', SSD_WIDTH), ('dB', SSD_GROUPS * SSD_STATE), ('dC', SSD_GROUPS * SSD_STATE),
    ('ddt', 2 * SSD_HEADS),
    ('gate', N_BRANCH * D_MODEL),
)
IN_TOTAL = sum(n for _, n in IN_LAYOUT)

kernel_name = 'hybrid_mlstm_s5_natten_ssd_ctxprefix'


def rmsnorm(x, w):
    x32 = x.astype(F32)
    y = x32 * lax.rsqrt(jnp.mean(x32 * x32, axis=-1, keepdims=True) + EPS)
    return (y * w.astype(F32)).astype(x.dtype)


def dwconv(x, w, b):
    k = w.shape[0]
    y = lax.conv_general_dilated(x, w[:, None, :].astype(x.dtype), (1,), [(k // 2, k // 2)],
                                 dimension_numbers=('NWC', 'WIO', 'NWC'), feature_group_count=x.shape[-1])
    return y + b.astype(x.dtype)


def split_heads(x, n_heads):
    b, t, _ = x.shape
    return x.reshape(b, t, n_heads, -1).transpose(0, 2, 1, 3)


def merge_heads(x):
    b, h, t, d = x.shape
    return x.transpose(0, 2, 1, 3).reshape(b, t, h * d)


def split_in(proj):
    out = {}
    off = 0
    for name, n in IN_LAYOUT:
        out[name] = proj[..., off:off + n]
        off += n
    return out


def axial_rope(x, rows, cols):
    d = x.shape[-1]
    half = d // 2
    nf = half // 2
    inv = ROPE_THETA ** (-jnp.arange(nf, dtype=F32) / nf)

    def rot(xa, pos):
        ang = pos.astype(F32)[:, None] * inv
        cos, sin = jnp.cos(ang), jnp.sin(ang)
        x1 = xa[..., :nf].astype(F32)
        x2 = xa[..., nf:].astype(F32)
        return jnp.concatenate([x1 * cos - x2 * sin, x1 * sin + x2 * cos], axis=-1)

    return jnp.concatenate([rot(x[..., :half], rows), rot(x[..., half:], cols)], axis=-1).astype(x.dtype)


def mlstm_scan(q, k, v, li, lf, state):
    b, h, t, d = q.shape
    L = min(MLSTM_CHUNK, t)
    nc = t // L

    def chunks(a):
        return jnp.moveaxis(a.reshape(a.shape[:2] + (nc, L) + a.shape[3:]), 2, 0)

    lower = jnp.tril(jnp.ones((L, L), dtype=bool))

    def step(carry, inp):
        c_mat, n_vec, m = carry
        qc, kc, vc, lic, lfc = inp
        bcum = jnp.cumsum(lfc, axis=-1)
        log_w = jnp.where(lower, bcum[..., :, None] - bcum[..., None, :] + lic[..., None, :], -jnp.inf)
        inter = bcum + m[..., None]
        m_t = jnp.maximum(inter, jnp.max(log_w, axis=-1))
        s = jnp.einsum('bhtd,bhsd->bhts', qc, kc) * jnp.exp(log_w - m_t[..., None])
        g = jnp.exp(inter - m_t)
        num = jnp.einsum('bhts,bhsd->bhtd', s, vc) + g[..., None] * jnp.einsum('bhtd,bhde->bhte', qc, c_mat)
        den = jnp.sum(s, axis=-1) + g * jnp.einsum('bhtd,bhd->bht', qc, n_vec)
        h_out = num / jnp.maximum(jnp.abs(den), jnp.exp(-m_t))[..., None]
        b_last = bcum[..., -1]
        log_k = b_last[..., None] - bcum + lic
        m_new = jnp.maximum(b_last + m, jnp.max(log_k, axis=-1))
        w_k = jnp.exp(log_k - m_new[..., None])
        decay = jnp.exp(b_last + m - m_new)
        c_new = decay[..., None, None] * c_mat + jnp.einsum('bhs,bhsd,bhse->bhde', w_k, kc, vc)
        n_new = decay[..., None] * n_vec + jnp.einsum('bhs,bhsd->bhd', w_k, kc)
        return (c_new, n_new, m_new), h_out

    state, hs = lax.scan(step, state, (chunks(q), chunks(k), chunks(v), chunks(li), chunks(lf)))
    return jnp.moveaxis(hs, 0, 2).reshape(b, h, t, d), state


def mlstm_zero_state(b):
    h, d = MLSTM_HEADS, MLSTM_HEAD_DIM
    return (jnp.zeros((b, h, d, d), F32), jnp.zeros((b, h, d), F32), jnp.zeros((b, h), F32))


def mlstm_qkv(pp, conv_w, conv_b, pos):
    qk = jax.nn.silu(dwconv(jnp.concatenate([pp['mq'], pp['mk']], axis=-1), conv_w, conv_b))
    q = split_heads(qk[..., :MLSTM_WIDTH], MLSTM_HEADS).astype(F32)
    k = split_heads(qk[..., MLSTM_WIDTH:], MLSTM_HEADS).astype(F32)
    v = split_heads(pp['mv'], MLSTM_HEADS).astype(F32)
    if pos is not None:
        q = axial_rope(q, pos[0], pos[1])
        k = axial_rope(k, pos[0], pos[1])
    return q * MLSTM_HEAD_DIM ** -0.5, k, v


def mlstm_gates(pp, ib, fb, direction):
    sl = slice(direction * MLSTM_HEADS, (direction + 1) * MLSTM_HEADS)
    li = pp['mi'][..., sl].astype(F32) + ib[direction].astype(F32)
    lf = jax.nn.log_sigmoid(pp['mf'][..., sl].astype(F32) + fb[direction].astype(F32))
    return jnp.swapaxes(li, 1, 2), jnp.swapaxes(lf, 1, 2)


def mlstm_readout(h, o_pre, norm_w):
    b, hh, t, d = h.shape
    h = h.transpose(0, 2, 1, 3) * jax.nn.sigmoid(o_pre.astype(F32)).reshape(b, t, hh, d)
    h = h * lax.rsqrt(jnp.mean(h * h, axis=-1, keepdims=True) + EPS)
    return (h.reshape(b, t, hh * d) * norm_w.astype(F32)).astype(o_pre.dtype)


def mlstm_branch(px, pc, conv_w, conv_b, ib, fb, norm_w, pos, with_ctx):
    qx, kx, vx = mlstm_qkv(px, conv_w, conv_b, pos)
    qc, kc, vc = mlstm_qkv(pc, conv_w, conv_b, None)
    hx_dirs, hc_dirs = [], []
    for direction in range(2):
        lix, lfx = mlstm_gates(px, ib, fb, direction)
        lic, lfc = mlstm_gates(pc, ib, fb, direction)
        seq_x = (qx, kx, vx, lix, lfx)
        seq_c = (qc, kc, vc, lic, lfc)
        if direction == 1:
            seq_x = tuple(jnp.flip(a, axis=2) for a in seq_x)
            seq_c = tuple(jnp.flip(a, axis=2) for a in seq_c)
        h_c, state = mlstm_scan(*seq_c, mlstm_zero_state(qc.shape[0]))
        h_x, _ = mlstm_scan(*seq_x, state)
        if direction == 1:
            h_x = jnp.flip(h_x, axis=2)
            h_c = jnp.flip(h_c, axis=2)
        hx_dirs.append(h_x)
        hc_dirs.append(h_c)
    y_x = mlstm_readout(hx_dirs[0] + hx_dirs[1], px['mo'], norm_w)
    y_c = mlstm_readout(hc_dirs[0] + hc_dirs[1], pc['mo'], norm_w) if with_ctx else None
    return y_x, y_c


def s5_discretize(lam_re, lam_im, log_dt, b_re, b_im):
    lam_re = lam_re.astype(F32)
    lam_im = lam_im.astype(F32)
    dt = jnp.exp(log_dt.astype(F32))[:, None]
    mag = jnp.exp(lam_re * dt)
    a_re = mag * jnp.cos(lam_im * dt)
    a_im = mag * jnp.sin(lam_im * dt)
    den = lam_re * lam_re + lam_im * lam_im
    nr = a_re - 1.0
    coef_re = (nr * lam_re + a_im * lam_im) / den
    coef_im = (a_im * lam_re - nr * lam_im) / den
    b_re = b_re.astype(F32)
    b_im = b_im.astype(F32)
    bb_re = coef_re[..., None] * b_re - coef_im[..., None] * b_im
    bb_im = coef_re[..., None] * b_im + coef_im[..., None] * b_re
    return a_re, a_im, bb_re, bb_im


def s5_scan(u, a_re, a_im, bb_re, bb_im, h0_re, h0_im):
    t = u.shape[1]
    bu_re = jnp.einsum('gpc,btgc->tbgp', bb_re, u)
    bu_im = jnp.einsum('gpc,btgc->tbgp', bb_im, u)
    bu_re = bu_re.at[0].add(a_re * h0_re - a_im * h0_im)
    bu_im = bu_im.at[0].add(a_re * h0_im + a_im * h0_re)
    shape = (t, 1) + a_re.shape
    elems = (jnp.broadcast_to(a_re, shape), jnp.broadcast_to(a_im, shape), bu_re, bu_im)

    def combine(e1, e2):
        a1r, a1i, b1r, b1i = e1
        a2r, a2i, b2r, b2i = e2
        return (a2r * a1r - a2i * a1i, a2r * a1i + a2i * a1r,
                a2r * b1r - a2i * b1i + b2r, a2r * b1i + a2i * b1r + b2i)

    _, _, h_re, h_im = lax.associative_scan(combine, elems, axis=0)
    return h_re, h_im


def s5_readout(h_re, h_im, c_re, c_im):
    return (jnp.einsum('gcp,tbgp->btgc', c_re.astype(F32), h_re)
            - jnp.einsum('gcp,tbgp->btgc', c_im.astype(F32), h_im))


def s5_output(y, u, d_skip, glu_w, dtype):
    b, t = y.shape[:2]
    y = (y + d_skip.astype(F32).reshape(S5_GROUPS, S5_GROUP) * u).reshape(b, t, S5_WIDTH)
    y = jax.nn.gelu(y).astype(dtype)
    ab = y @ glu_w
    return ab[..., :S5_WIDTH] * jax.nn.sigmoid(ab[..., S5_WIDTH:])


def s5_branch(px, pc, lam_re, lam_im, log_dt, b_re, b_im, c_re, c_im, d_skip, glu_w, with_ctx):
    def groups(u):
        b, t, _ = u.shape
        return u.astype(F32).reshape(b, t, S5_GROUPS, S5_GROUP)

    ux, uc = groups(px['su']), groups(pc['su'])
    zeros = jnp.zeros((ux.shape[0], S5_GROUPS, S5_STATE), F32)
    yx, yc = [], []
    for direction in range(2):
        a_re, a_im, bb_re, bb_im = s5_discretize(lam_re[direction], lam_im[direction], log_dt[direction], b_re, b_im)
        uxd = ux if direction == 0 else jnp.flip(ux, axis=1)
        ucd = uc if direction == 0 else jnp.flip(uc, axis=1)
        hc_re, hc_im = s5_scan(ucd, a_re, a_im, bb_re, bb_im, zeros, zeros)
        hx_re, hx_im = s5_scan(uxd, a_re, a_im, bb_re, bb_im, hc_re[-1], hc_im[-1])
        y_x = s5_readout(hx_re, hx_im, c_re, c_im)
        yx.append(y_x if direction == 0 else jnp.flip(y_x, axis=1))
        if with_ctx:
            y_c = s5_readout(hc_re, hc_im, c_re, c_im)
            yc.append(y_c if direction == 0 else jnp.flip(y_c, axis=1))
    out_x = s5_output(yx[0] + yx[1], ux, d_skip, glu_w, px['su'].dtype)
    out_c = s5_output(yc[0] + yc[1], uc, d_skip, glu_w, pc['su'].dtype) if with_ctx else None
    return out_x, out_c


def na_branch(px, pc, rpb, with_ctx):
    qx, kx, vx = [split_heads(a, NA_HEADS) for a in jnp.split(px['nqkv'], 3, axis=-1)]
    qc, kc, vc = [split_heads(a, NA_HEADS) for a in jnp.split(pc['nqkv'], 3, axis=-1)]
    b, h, t, d = qx.shape
    rows = t // GRID_W
    kh = min(NA_KH, rows)
    scale = d ** -0.5
    qg = (qx * scale).reshape(b, h, rows, GRID_W, d)
    kg = kx.reshape(b, h, rows, GRID_W, d)
    vg = vx.reshape(b, h, rows, GRID_W, d)
    r = jnp.arange(rows)
    row_idx = jnp.clip(r - kh // 2, 0, rows - kh)[:, None] + jnp.arange(kh)[None, :]
    kb = kg[:, :, row_idx]
    vb = vg[:, :, row_idx]
    col = jnp.arange(GRID_W)
    col0 = jnp.clip(col - NA_KW // 2, 0, GRID_W - NA_KW)
    in_win = (col[None, :] >= col0[:, None]) & (col[None, :] < col0[:, None] + NA_KW)
    dr = row_idx - r[:, None] + (NA_KH - 1)
    dc = jnp.clip(col[None, :] - col[:, None], -(NA_KW - 1), NA_KW - 1) + (NA_KW - 1)
    bias = rpb.astype(F32)[:, dr][..., dc].transpose(0, 1, 3, 2, 4)
    s_lat = jnp.einsum('bhrqd,bhrikd->bhrqik', qg, kb).astype(F32) + bias
    s_lat = jnp.where(in_win[:, None, :], s_lat, -jnp.inf)
    s_ctx = jnp.einsum('bhrqd,bhjd->bhrqj', qg, kc).astype(F32)
    n_lat = kh * GRID_W
    p = jax.nn.softmax(jnp.concatenate([s_lat.reshape(b, h, rows, GRID_W, n_lat), s_ctx], axis=-1), axis=-1)
    p = p.astype(vx.dtype)
    o = (jnp.einsum('bhrqik,bhrikd->bhrqd', p[..., :n_lat].reshape(b, h, rows, GRID_W, kh, GRID_W), vb)
         + jnp.einsum('bhrqj,bhjd->bhrqd', p[..., n_lat:], vc))
    y_x = merge_heads(o.reshape(b, h, t, d))
    y_c = None
    if with_ctx:
        s = jnp.einsum('bhid,bhjd->bhij', qc * scale, kc).astype(F32)
        y_c = merge_heads(jnp.einsum('bhij,bhjd->bhid', jax.nn.softmax(s, axis=-1).astype(vc.dtype), vc))
    return y_x, y_c


def segsum(a):
    t = a.shape[-1]
    x = jnp.broadcast_to(a[..., None], a.shape + (t,))
    x = jnp.where(jnp.tril(jnp.ones((t, t), dtype=bool), -1), x, 0.0)
    s = jnp.cumsum(x, axis=-2)
    return jnp.where(jnp.tril(jnp.ones((t, t), dtype=bool)), s, -jnp.inf)


def ssd_scan(xdt, a, bh, ch, h0):
    bsz, t, h, p = xdt.shape
    n = bh.shape[-1]
    L = min(SSD_CHUNK, t)
    nc = t // L
    xdt = xdt.reshape(bsz, nc, L, h, p)
    bh = bh.reshape(bsz, nc, L, h, n)
    ch = ch.reshape(bsz, nc, L, h, n)
    a = a.reshape(bsz, nc, L, h).transpose(0, 3, 1, 2)
    a_cs = jnp.cumsum(a, axis=-1)
    scores = jnp.einsum('bclhn,bcshn->bhcls', ch, bh) * jnp.exp(segsum(a))
    y_diag = jnp.einsum('bhcls,bcshp->bclhp', scores, xdt)
    decay_states = jnp.exp(a_cs[..., -1:] - a_cs)
    states = jnp.einsum('bclhn,bhcl,bclhp->bchpn', bh, decay_states, xdt)
    states = jnp.concatenate([h0[:, None], states], axis=1)
    decay_chunk = jnp.exp(segsum(jnp.pad(a_cs[..., -1], ((0, 0), (0, 0), (1, 0)))))
    new_states = jnp.einsum('bhzc,bchpn->bzhpn', decay_chunk, states)
    y_off = jnp.einsum('bclhn,bchpn,bhcl->bclhp', ch, new_states[:, :-1], jnp.exp(a_cs))
    return (y_diag + y_off).reshape(bsz, t, h, p), new_states[:, -1]


def ssd_inputs(pp, conv_w, conv_b):
    xbc = jax.nn.silu(dwconv(jnp.concatenate([pp['dx'], pp['dB'], pp['dC']], axis=-1), conv_w, conv_b)).astype(F32)
    b, t, _ = xbc.shape
    gn = SSD_GROUPS * SSD_STATE
    rep = SSD_HEADS // SSD_GROUPS
    xs = xbc[..., :SSD_WIDTH].reshape(b, t, SSD_HEADS, SSD_HEAD_DIM)
    bm = jnp.repeat(xbc[..., SSD_WIDTH:SSD_WIDTH + gn].reshape(b, t, SSD_GROUPS, SSD_STATE), rep, axis=2)
    cm = jnp.repeat(xbc[..., SSD_WIDTH + gn:].reshape(b, t, SSD_GROUPS, SSD_STATE), rep, axis=2)
    return xs, bm, cm


def ssd_branch(px, pc, conv_w, conv_b, a_log, dt_bias, d_skip, norm_w, with_ctx):
    xs_x, b_x, c_x = ssd_inputs(px, conv_w, conv_b)
    xs_c, b_c, c_c = ssd_inputs(pc, conv_w, conv_b)
    h0 = jnp.zeros((xs_x.shape[0], SSD_HEADS, SSD_HEAD_DIM, SSD_STATE), F32)
    yx, yc = [], []
    for direction in range(2):
        sl = slice(direction * SSD_HEADS, (direction + 1) * SSD_HEADS)
        a = -jnp.exp(a_log[direction].astype(F32))
        dt_x = jax.nn.softplus(px['ddt'][..., sl].astype(F32) + dt_bias[direction].astype(F32))
        dt_c = jax.nn.softplus(pc['ddt'][..., sl].astype(F32) + dt_bias[direction].astype(F32))
        seq_x = (xs_x * dt_x[..., None], dt_x * a, b_x, c_x)
        seq_c = (xs_c * dt_c[..., None], dt_c * a, b_c, c_c)
        if direction == 1:
            seq_x = tuple(jnp.flip(s, axis=1) for s in seq_x)
            seq_c = tuple(jnp.flip(s, axis=1) for s in seq_c)
        y_c, state = ssd_scan(*seq_c, h0)
        y_x, _ = ssd_scan(*seq_x, state)
        if direction == 1:
            y_x = jnp.flip(y_x, axis=1)
            y_c = jnp.flip(y_c, axis=1)
        yx.append(y_x)
        yc.append(y_c)

    def finish(y, xs, z):
        b, t = y.shape[:2]
        y = (y + d_skip.astype(F32)[:, None] * xs).reshape(b, t, SSD_WIDTH)
        return rmsnorm(y * jax.nn.silu(z.astype(F32)), norm_w).astype(z.dtype)

    out_x = finish(yx[0] + yx[1], xs_x, px['dz'])
    out_c = finish(yc[0] + yc[1], xs_c, pc['dz']) if with_ctx else None
    return out_x, out_c


def gated_merge(gate_pre, ys, w_br, w_out):
    g = jax.nn.sigmoid(gate_pre.astype(F32)).astype(gate_pre.dtype)
    m = g[..., :D_MODEL] * (ys[0] @ w_br[0])
    for i in range(1, N_BRANCH):
        m = m + g[..., i * D_MODEL:(i + 1) * D_MODEL] * (ys[i] @ w_br[i])
    return m @ w_out


def mixer(hx, hc, p, pos, with_ctx):
    px = split_in(hx @ p['w_in'])
    pc = split_in(hc @ p['w_in'])
    ya_x, ya_c = mlstm_branch(px, pc, p['mlstm_conv_w'], p['mlstm_conv_b'], p['mlstm_ib'], p['mlstm_fb'],
                              p['mlstm_norm_w'], pos, with_ctx)
    yb_x, yb_c = s5_branch(px, pc, p['s5_lam_re'], p['s5_lam_im'], p['s5_log_dt'], p['s5_b_re'], p['s5_b_im'],
                           p['s5_c_re'], p['s5_c_im'], p['s5_d'], p['s5_glu_w'], with_ctx)
    yc_x, yc_c = na_branch(px, pc, p['na_rpb'], with_ctx)
    yd_x, yd_c = ssd_branch(px, pc, p['ssd_conv_w'], p['ssd_conv_b'], p['ssd_a_log'], p['ssd_dt_bias'],
                            p['ssd_d'], p['ssd_norm_w'], with_ctx)
    w_br = (p['w_branch_a'], p['w_branch_b'], p['w_branch_c'], p['w_branch_d'])
    out_x = gated_merge(px['gate'], (ya_x, yb_x, yc_x, yd_x), w_br, p['w_out'])
    out_c = gated_merge(pc['gate'], (ya_c, yb_c, yc_c, yd_c), w_br, p['w_out']) if with_ctx else None
    return out_x, out_c


def adaln(cvec, w, b):
    m = jax.nn.silu(cvec) @ w + b
    return jnp.split(m, 6, axis=-1)


def swiglu(h, w_in, w_out):
    ab = h @ w_in
    return (jax.nn.silu(ab[..., :FFN_HIDDEN]) * ab[..., FFN_HIDDEN:]) @ w_out


def setup_inputs(seed: int = 0) -> dict:
    key = jax.random.key(seed)
    ks = iter(jax.random.split(key, 48))
    L, D = DEPTH, D_MODEL
    G, P = S5_GROUPS, S5_STATE

    def nrm(shape, scale=1.0):
        return scale * jax.random.normal(next(ks), shape, F32)

    def gain(shape):
        return 1.0 + 0.01 * jax.random.normal(next(ks), shape, F32)

    def unif(shape, lo, hi):
        return jax.random.uniform(next(ks), shape, F32, lo, hi)

    x = nrm((BATCH, SEQ, D))
    c = nrm((BATCH, D))
    ctx = nrm((BATCH, CTX_LEN, D))
    c_ctx = nrm((D,))
    ada_w = nrm((L, D, 6 * D), 0.5 * D ** -0.5)
    ada_b = nrm((L, 6 * D), 0.02)
    norm1_w = gain((L, D))
    norm2_w = gain((L, D))
    w_in = nrm((L, D, IN_TOTAL), D ** -0.5)
    mlstm_conv_w = nrm((L, CONV_K, 2 * MLSTM_WIDTH), CONV_K ** -0.5)
    mlstm_conv_b = nrm((L, 2 * MLSTM_WIDTH), 0.02)
    mlstm_ib = nrm((L, 2, MLSTM_HEADS), 0.1)
    mlstm_fb = jnp.linspace(3.0, 6.0, MLSTM_HEADS, dtype=F32) + nrm((L, 2, MLSTM_HEADS), 0.1)
    mlstm_norm_w = gain((L, MLSTM_WIDTH))
    s5_lam_re = -0.5 + nrm((L, 2, G, P), 0.01)
    s5_lam_im = math.pi * jnp.arange(P, dtype=F32) + nrm((L, 2, G, P), 0.01)
    s5_log_dt = unif((L, 2, G), math.log(1e-3), math.log(1e-1))
    s5_b_re = nrm((L, G, P, S5_GROUP), (2 * S5_GROUP) ** -0.5)
    s5_b_im = nrm((L, G, P, S5_GROUP), (2 * S5_GROUP) ** -0.5)
    s5_c_re = nrm((L, G, S5_GROUP, P), P ** -0.5)
    s5_c_im = nrm((L, G, S5_GROUP, P), P ** -0.5)
    s5_d = nrm((L, S5_WIDTH))
    s5_glu_w = nrm((L, S5_WIDTH, 2 * S5_WIDTH), S5_WIDTH ** -0.5)
    na_rpb = nrm((L, NA_HEADS, 2 * NA_KH - 1, 2 * NA_KW - 1), 0.1)
    conv_ch = SSD_WIDTH + 2 * SSD_GROUPS * SSD_STATE
    ssd_conv_w = nrm((L, CONV_K, conv_ch), CONV_K ** -0.5)
    ssd_conv_b = nrm((L, conv_ch), 0.02)
    ssd_a_log = jnp.log(unif((L, 2, SSD_HEADS), 1.0, 16.0))
    dt0 = jnp.exp(unif((L, 2, SSD_HEADS), math.log(1e-3), math.log(1e-1)))
    ssd_dt_bias = dt0 + jnp.log(-jnp.expm1(-dt0))
    ssd_d = 1.0 + nrm((L, SSD_HEADS), 0.1)
    ssd_norm_w = gain((L, SSD_WIDTH))
    w_branch_a = nrm((L, MLSTM_WIDTH, D), MLSTM_WIDTH ** -0.5)
    w_branch_b = nrm((L, S5_WIDTH, D), S5_WIDTH ** -0.5)
    w_branch_c = nrm((L, NA_WIDTH, D), NA_WIDTH ** -0.5)
    w_branch_d = nrm((L, SSD_WIDTH, D), SSD_WIDTH ** -0.5)
    w_out = nrm((L, D, D), D ** -0.5)
    ffn_w_in = nrm((L, D, 2 * FFN_HIDDEN), D ** -0.5)
    ffn_w_out = nrm((L, FFN_HIDDEN, D), FFN_HIDDEN ** -0.5)
    final_norm_w = gain((D,))
    return {
        'x': x, 'c': c, 'ctx': ctx, 'c_ctx': c_ctx, 'ada_w': ada_w, 'ada_b': ada_b,
        'norm1_w': norm1_w, 'norm2_w': norm2_w, 'w_in': w_in,
        'mlstm_conv_w': mlstm_conv_w, 'mlstm_conv_b': mlstm_conv_b, 'mlstm_ib': mlstm_ib, 'mlstm_fb': mlstm_fb,
        'mlstm_norm_w': mlstm_norm_w,
        's5_lam_re': s5_lam_re, 's5_lam_im': s5_lam_im, 's5_log_dt': s5_log_dt, 's5_b_re': s5_b_re,
        's5_b_im': s5_b_im, 's5_c_re': s5_c_re, 's5_c_im': s5_c_im, 's5_d': s5_d, 's5_glu_w': s5_glu_w,
        'na_rpb': na_rpb,
        'ssd_conv_w': ssd_conv_w, 'ssd_conv_b': ssd_conv_b, 'ssd_a_log': ssd_a_log, 'ssd_dt_bias': ssd_dt_bias,
        'ssd_d': ssd_d, 'ssd_norm_w': ssd_norm_w,
        'w_branch_a': w_branch_a, 'w_branch_b': w_branch_b, 'w_branch_c': w_branch_c, 'w_branch_d': w_branch_d,
        'w_out': w_out, 'ffn_w_in': ffn_w_in, 'ffn_w_out': ffn_w_out, 'final_norm_w': final_norm_w,
    }


def reference(x, c, ctx, c_ctx, ada_w, ada_b, norm1_w, norm2_w, w_in,
              mlstm_conv_w, mlstm_conv_b, mlstm_ib, mlstm_fb, mlstm_norm_w,
              s5_lam_re, s5_lam_im, s5_log_dt, s5_b_re, s5_b_im, s5_c_re, s5_c_im, s5_d, s5_glu_w,
              na_rpb,
              ssd_conv_w, ssd_conv_b, ssd_a_log, ssd_dt_bias, ssd_d, ssd_norm_w,
              w_branch_a, w_branch_b, w_branch_c, w_branch_d, w_out, ffn_w_in, ffn_w_out, final_norm_w):
    t = x.shape[1]
    tok = jnp.arange(t)
    pos = (tok // GRID_W, tok % GRID_W)
    for l in range(DEPTH):
        with_ctx = l < DEPTH - 1
        p = {
            'w_in': w_in[l], 'mlstm_conv_w': mlstm_conv_w[l], 'mlstm_conv_b': mlstm_conv_b[l],
            'mlstm_ib': mlstm_ib[l], 'mlstm_fb': mlstm_fb[l], 'mlstm_norm_w': mlstm_norm_w[l],
            's5_lam_re': s5_lam_re[l], 's5_lam_im': s5_lam_im[l], 's5_log_dt': s5_log_dt[l],
            's5_b_re': s5_b_re[l], 's5_b_im': s5_b_im[l], 's5_c_re': s5_c_re[l], 's5_c_im': s5_c_im[l],
            's5_d': s5_d[l], 's5_glu_w': s5_glu_w[l], 'na_rpb': na_rpb[l],
            'ssd_conv_w': ssd_conv_w[l], 'ssd_conv_b': ssd_conv_b[l], 'ssd_a_log': ssd_a_log[l],
            'ssd_dt_bias': ssd_dt_bias[l], 'ssd_d': ssd_d[l], 'ssd_norm_w': ssd_norm_w[l],
            'w_branch_a': w_branch_a[l], 'w_branch_b': w_branch_b[l], 'w_branch_c': w_branch_c[l],
            'w_branch_d': w_branch_d[l], 'w_out': w_out[l],
        }
        sh1, sc1, g1, sh2, sc2, g2 = [m[:, None, :] for m in adaln(c, ada_w[l], ada_b[l])]
        csh1, csc1, cg1, csh2, csc2, cg2 = adaln(c_ctx, ada_w[l], ada_b[l])
        hx = rmsnorm(x, norm1_w[l]) * (1 + sc1) + sh1
        hc = rmsnorm(ctx, norm1_w[l]) * (1 + csc1) + csh1
        mx, mc = mixer(hx, hc, p, pos, with_ctx)
        x = x + g1 * mx
        hx = rmsnorm(x, norm2_w[l]) * (1 + sc2) + sh2
        x = x + g2 * swiglu(hx, ffn_w_in[l], ffn_w_out[l])
        if with_ctx:
            ctx = ctx + cg1 * mc
            hc = rmsnorm(ctx, norm2_w[l]) * (1 + csc2) + csh2
            ctx = ctx + cg2 * swiglu(hc, ffn_w_in[l], ffn_w_out[l])
    return rmsnorm(x, final_norm_w)
```

```python
import contextlib
import math
import numpy as np
import concourse.bass as bass
import concourse.mybir as mybir
from concourse.bass_utils import run_bass_kernel_spmd

F32 = mybir.dt.float32
BF16 = mybir.dt.bfloat16
ALU = mybir.AluOpType
AF = mybir.ActivationFunctionType
AX = mybir.AxisListType

ENGS = ('pe', 'act', 'dve', 'pool', 'sp')
NDMASEM = 12
SAME_ENGINE_SYNC = True

D = 1024
SEQ = 4096
CTX = 256
S = SEQ + CTX
DEPTH = 2
GRID_W = 64
EPS = 1e-6
IN_TOTAL = 7712
FFN_H = 2816
O_MQ, O_MK, O_MV, O_MO, O_MI, O_MF, O_SU = 0, 256, 512, 768, 1024, 1032, 1040
O_NQ, O_NK, O_NV = 1296, 1552, 1808
O_DZ, O_DX, O_DB, O_DC, O_DDT, O_GATE = 2064, 2576, 3088, 3344, 3600, 3616
NT128 = S // 128
TT = [(0, 256)] + [(256 + 512 * i, 512) for i in range(8)]


class Res:
    __slots__ = ('name', 'lw', 'rd')

    def __init__(self, name=''):
        self.name = name
        self.lw = None
        self.rd = []


class Inst:
    __slots__ = ('eng', 'fn', 'dma', 'seq', 'deps', 'signal', 'idx', 'clock', 'dsem', 'dval', 'dmaid', 'emitted')

    def __init__(self, eng, fn, dma):
        self.eng = eng
        self.fn = fn
        self.dma = dma
        self.deps = []
        self.signal = False
        self.idx = None
        self.clock = None
        self.emitted = False


class Prog:
    def __init__(self, nc, es):
        self.nc = nc
        self.ins = {e: [] for e in ENGS}
        self.known = {e: {x: -1 for x in ENGS} for e in ENGS}
        self.known_dma = {e: set() for e in ENGS}
        self.ndma = {e: 0 for e in ENGS}
        self.dma_list = {e: [] for e in ENGS}
        self.all_dma = []
        self.sigcount = {e: 0 for e in ENGS}
        self.sem = {e: es.enter_context(nc.semaphore('s_' + e)) for e in ENGS}
        self.dsem = {}
        for e in ('sp', 'pool', 'act'):
            for k in range(NDMASEM):
                self.dsem[(e, k)] = es.enter_context(nc.semaphore('d_%s_%d' % (e, k)))
        self.pos = {e: 0 for e in ENGS}
        self.ninst = 0

    def op(self, eng, fn, reads=(), writes=(), dma=False):
        ins = Inst(eng, fn, dma)
        lst = self.ins[eng]
        ins.seq = len(lst)
        self.ninst += 1
        need = []
        for r in reads:
            r = getattr(r, 'r', r)
            if r.lw is not None:
                need.append(r.lw)
        for w in writes:
            w = getattr(w, 'r', w)
            if w.lw is not None:
                need.append(w.lw)
            need.extend(w.rd)
        if dma:
            j = self.ndma[eng]
            self.ndma[eng] += 1
            ins.dmaid = len(self.all_dma)
            self.all_dma.append(ins)
            ins.dsem = (eng, j % NDMASEM)
            ins.dval = 16 * (j // NDMASEM + 1)
            if j >= NDMASEM:
                need.append(self.dma_list[eng][j - NDMASEM])
            self.dma_list[eng].append(ins)
            ins.signal = True
        kn = self.known[eng]
        kd = self.known_dma[eng]
        deps = []
        for d in need:
            if d.dma:
                if d.dmaid in kd:
                    continue
                kd.add(d.dmaid)
                deps.append(d)
                for x, s in d.clock.items():
                    if s > kn[x]:
                        kn[x] = s
            else:
                if d.eng == eng and (eng == 'pe' or not SAME_ENGINE_SYNC):
                    continue
                if d.seq <= kn[d.eng]:
                    continue
                assert not d.emitted or d.signal, "dependency on already-emitted unsignalled inst"
                deps.append(d)
                d.signal = True
                kn[d.eng] = d.seq
                for x, s in d.clock.items():
                    if s > kn[x]:
                        kn[x] = s
        best = {}
        out = []
        for d in deps:
            if d.dma:
                out.append(d)
            elif d.eng not in best or best[d.eng].seq < d.seq:
                best[d.eng] = d
        out.extend(best.values())
        ins.deps = out
        ins.clock = dict(kn)
        lst.append(ins)
        for r in reads:
            r = getattr(r, 'r', r)
            r.rd.append(ins)
        for w in writes:
            w = getattr(w, 'r', w)
            w.lw = ins
            w.rd = []
        return ins

    def pe(self, fn, reads=(), writes=()):
        return self.op('pe', fn, reads, writes)

    def act(self, fn, reads=(), writes=()):
        return self.op('act', fn, reads, writes)

    def dve(self, fn, reads=(), writes=()):
        return self.op('dve', fn, reads, writes)

    def pool(self, fn, reads=(), writes=()):
        return self.op('pool', fn, reads, writes)

    def dmaq(self, q, out, in_, reads=(), writes=(), **kw):
        return self.op(q, lambda e: e.dma_start(out=out, in_=in_, **kw), reads, writes, dma=True)

    def dma(self, out, in_, reads=(), writes=(), **kw):
        return self.dmaq('sp', out, in_, reads, writes, **kw)

    def barrier(self):
        lasts = []
        for e in ENGS:
            for i in reversed(self.ins[e]):
                if not i.dma and i.fn is not None:
                    lasts.append(i)
                    break
        pend = []
        for e in ENGS:
            pend.extend(self.dma_list[e][-NDMASEM:])
        for e in ENGS:
            ins = Inst(e, None, False)
            ins.seq = len(self.ins[e])
            kn = self.known[e]
            kd = self.known_dma[e]
            for d in lasts:
                if d.seq <= kn[d.eng]:
                    continue
                assert not d.emitted or d.signal
                ins.deps.append(d)
                d.signal = True
                kn[d.eng] = d.seq
            for d in pend:
                if d.dmaid in kd:
                    continue
                kd.add(d.dmaid)
                ins.deps.append(d)
            ins.clock = dict(kn)
            self.ins[e].append(ins)

    def flush(self, final_wait=()):
        self.barrier()
        nc = self.nc
        for e in ENGS:
            for i in self.ins[e][self.pos[e]:]:
                if i.signal and not i.dma:
                    self.sigcount[e] += 1
                    i.idx = self.sigcount[e]
        sem, dsem = self.sem, self.dsem

        def run(e, eng):
            for i in self.ins[e][self.pos[e]:]:
                for d in i.deps:
                    if d.dma:
                        eng.wait_ge(dsem[d.dsem], d.dval)
                    else:
                        eng.wait_ge(sem[d.eng], d.idx)
                i.emitted = True
                if i.fn is None:
                    continue
                bi = i.fn(eng)
                if i.dma:
                    bi.then_inc(dsem[i.dsem], 16)
                elif i.signal:
                    bi.then_inc(sem[e], 1)
            if e == 'sp':
                for d in final_wait:
                    eng.wait_ge(dsem[d.dsem], d.dval)
            self.pos[e] = len(self.ins[e])

        with nc.Block() as block:
            @block.tensor
            def _(eng):
                run('pe', eng)

            @block.scalar
            def _(eng):
                run('act', eng)

            @block.vector
            def _(eng):
                run('dve', eng)

            @block.gpsimd
            def _(eng):
                run('pool', eng)

            @block.sync
            def _(eng):
                run('sp', eng)


class Buf:
    __slots__ = ('t', 'r')

    def __init__(self, t, name=''):
        self.t = t
        self.r = Res(name)

    def __getitem__(self, k):
        return self.t[k]


class Ctx:
    pass


def build(debug_outs=(), stop_after=None):
    nc = bass.Bass("TRN2", target_bir_lowering=False)
    top = contextlib.ExitStack()
    G = Ctx()
    G.nc = nc
    G.dbg = set(debug_outs)
    with top:
        P = Prog(nc, top)
        G.P = P

        def din(name, shape):
            return nc.dram_tensor(name, list(shape), F32, kind="ExternalInput").ap()

        I = {}
        I['x'] = din('x', [SEQ, D]); I['ctx'] = din('ctx', [CTX, D])
        I['c'] = din('c', [1, D]); I['c_ctx'] = din('c_ctx', [1, D])
        for nm, shp in WEIGHT_SHAPES:
            I[nm] = din(nm, shp)
        G.I = I
        G.out = nc.dram_tensor('out', [SEQ, D], F32, kind="ExternalOutput").ap()
        G.scr = {}

        def scratch(name, shape, dt=F32):
            kind = "ExternalOutput" if name in G.dbg else "Internal"
            b = Buf(nc.dram_tensor(name, list(shape), dt, kind=kind).ap(), name)
            G.scr[name] = b
            return b
        G.scratch = scratch

        G.uid = 0

        def sb(es, name, shape, dt=F32):
            G.uid += 1
            return Buf(es.enter_context(nc.sbuf_tensor('%s_%d' % (name, G.uid), list(shape), dt)), name)
        G.sb = sb
        G.ps = [Buf(top.enter_context(nc.psum_tensor('ps%d' % i, [128, 512], F32)), 'ps%d' % i) for i in range(8)]
        G.psi = 0

        def nextps():
            b = G.ps[G.psi % 8]
            G.psi += 1
            return b
        G.nextps = nextps

        stages = [('consts', stage_consts), ('adaln', stage_adaln), ('load', stage_load)]
        for l in range(DEPTH):
            stages += [('proj%d' % l, lambda G, l=l: stage_norm_proj(G, l))]
            stages += STAGES_AFTER_PROJ(l)
        for nm, st in stages:
            st(G)
            P.flush()
            if stop_after is not None and nm == stop_after:
                break
        P.flush(final_wait=P.all_dma[-3 * NDMASEM:])
        G.keep.close()
    return nc


WEIGHT_SHAPES = [
    ('ada_w', [2, 1024, 6144]), ('ada_b', [2, 6144]), ('norm1_w', [2, 1024]), ('norm2_w', [2, 1024]),
    ('w_in', [2, 1024, 7712]), ('mlstm_conv_w', [2, 7, 512]), ('mlstm_conv_b', [2, 512]),
    ('mlstm_ib', [2, 2, 4]), ('mlstm_fb', [2, 2, 4]), ('mlstm_norm_w', [2, 256]),
    ('s5_lam_re', [2, 2, 16, 64]), ('s5_lam_im', [2, 2, 16, 64]), ('s5_log_dt', [2, 2, 16]),
    ('s5_b_re', [2, 16, 64, 16]), ('s5_b_im', [2, 16, 64, 16]), ('s5_c_re', [2, 16, 16, 64]),
    ('s5_c_im', [2, 16, 16, 64]), ('s5_d', [2, 256]), ('s5_glu_w', [2, 256, 512]), ('na_rpb', [2, 4, 15, 31]),
    ('ssd_conv_w', [2, 7, 1024]), ('ssd_conv_b', [2, 1024]), ('ssd_a_log', [2, 2, 8]), ('ssd_dt_bias', [2, 2, 8]),
    ('ssd_d', [2, 8]), ('ssd_norm_w', [2, 512]), ('w_branch_a', [2, 256, 1024]), ('w_branch_b', [2, 256, 1024]),
    ('w_branch_c', [2, 256, 1024]), ('w_branch_d', [2, 512, 1024]), ('w_out', [2, 1024, 1024]),
    ('ffn_w_in', [2, 1024, 5632]), ('ffn_w_out', [2, 2816, 1024]), ('final_norm_w', [1024]),
]


def stage_consts(G):
    nc, P = G.nc, G.P
    top = contextlib.ExitStack()
    G.keep = top
    sb = G.sb
    G.ones = sb(top, 'ones', [128, 128]); G.ident = sb(top, 'ident', [128, 128]); G.identb = sb(top, 'identb', [128, 128], BF16)
    G.triU = sb(top, 'triU', [128, 128]); G.triL = sb(top, 'triL', [128, 128])
    G.onesb = sb(top, 'onesb', [128, 128], BF16)
    P.pool(lambda e: e.memset(G.ones[:], 1.0), writes=[G.ones])
    P.pool(lambda e: e.affine_select(out=G.ident[:], in_=G.ones[:], pattern=[[-1, 128]], compare_op=ALU.is_equal,
                                     fill=0.0, base=0, channel_multiplier=1), reads=[G.ones], writes=[G.ident])
    P.pool(lambda e: e.affine_select(out=G.triU[:], in_=G.ones[:], pattern=[[1, 128]], compare_op=ALU.is_ge,
                                     fill=0.0, base=0, channel_multiplier=-1), reads=[G.ones], writes=[G.triU])
    P.pool(lambda e: e.affine_select(out=G.triL[:], in_=G.ones[:], pattern=[[-1, 128]], compare_op=ALU.is_ge,
                                     fill=0.0, base=0, channel_multiplier=1), reads=[G.ones], writes=[G.triL])
    P.dve(lambda e: e.tensor_copy(out=G.identb[:], in_=G.ident[:]), reads=[G.ident], writes=[G.identb])
    P.dve(lambda e: e.tensor_copy(out=G.onesb[:], in_=G.ones[:]), reads=[G.ones], writes=[G.onesb])
    G.cm = sb(top, 'cm', [128, DEPTH, 6, 8, 2])
    G.epsb = sb(top, 'epsb', [128, 1])
    P.pool(lambda e: e.memset(G.epsb[:], EPS), writes=[G.epsb])
    G.scratch('xsT', [8, 128, S])
    G.scratch('hxT', [8, 128, S], BF16)


def stage_adaln(G):
    nc, P, I = G.nc, G.P, G.I
    with contextlib.ExitStack() as es:
        sb = G.sb
        c2 = sb(es, 'c2', [128, 8, 2]); sc2 = sb(es, 'sc2', [128, 8, 2])
        P.dma(c2[:, :, 0], I['c'][0, :].rearrange("(k p) -> p k", p=128), writes=[c2], allow_slow_non_contiguous=True)
        P.dma(c2[:, :, 1], I['c_ctx'][0, :].rearrange("(k p) -> p k", p=128), writes=[c2], allow_slow_non_contiguous=True)
        P.act(lambda e: e.activation(out=sc2[:], in_=c2[:], func=AF.Silu), reads=[c2], writes=[sc2])
        wt = [sb(es, 'adaw%d' % i, [128, 8, 512]) for i in range(3)]
        bias = sb(es, 'adab', [128, DEPTH, 48]); nw = sb(es, 'nw', [128, DEPTH, 2, 8])
        modtm = sb(es, 'modtm', [2, 6144]); cmraw = sb(es, 'cmraw', [128, 48, 2])
        modT = G.scratch('modT', [DEPTH, 2, 6144])
        for l in range(DEPTH):
            P.dma(bias[:, l, :], I['ada_b'][l, :].rearrange("(j p) -> p j", p=128), writes=[bias], allow_slow_non_contiguous=True)
            P.dma(nw[:, l, 0, :], I['norm1_w'][l, :].rearrange("(k p) -> p k", p=128), writes=[nw], allow_slow_non_contiguous=True)
            P.dma(nw[:, l, 1, :], I['norm2_w'][l, :].rearrange("(k p) -> p k", p=128), writes=[nw], allow_slow_non_contiguous=True)
        n = 0
        for l in range(DEPTH):
            for c in range(12):
                w = wt[n % 3]; n += 1
                P.dma(w[:], I['ada_w'][l, :, c * 512:(c + 1) * 512].rearrange("(k p) n -> p k n", p=128), writes=[w])
                ps = G.nextps()
                for k in range(8):
                    P.pe(lambda e, w=w, k=k, ps=ps: e.matmul(ps[0:2, :], lhsT=sc2[:, k, :], rhs=w[:, k, :], start=(k == 0), stop=(k == 7)), reads=[w, sc2], writes=[ps])
                P.act(lambda e, ps=ps, c=c: e.activation(out=modtm[:, c * 512:(c + 1) * 512], in_=ps[0:2, :], func=AF.Copy), reads=[ps], writes=[modtm])
            P.dma(modT[l], modtm[:], reads=[modtm], writes=[modT])
            for s in range(2):
                P.dma(cmraw[:, :, s], modT[l, s, :].rearrange("(j p) -> p j", p=128), reads=[modT], writes=[cmraw], allow_slow_non_contiguous=True)
            for s in range(2):
                P.dve(lambda e, l=l, s=s: e.tensor_tensor(
                    out=G.cm[:, l, :, :, s], in0=cmraw[:, :, s].rearrange("p (w k) -> p w k", w=6),
                    in1=bias[:, l, :].rearrange("p (w k) -> p w k", w=6), op=ALU.add), reads=[cmraw, bias], writes=[G.cm])
            for (wi, ni) in ((1, 0), (4, 1)):
                for s in range(2):
                    P.dve(lambda e, l=l, s=s, wi=wi, ni=ni: e.scalar_tensor_tensor(
                        out=G.cm[:, l, wi, :, s], in0=G.cm[:, l, wi, :, s], scalar=1.0, in1=nw[:, l, ni, :],
                        op0=ALU.add, op1=ALU.mult), reads=[G.cm, nw], writes=[G.cm])
        if 'cm_dbg' in G.dbg:
            d = G.scratch('cm_dbg', [128, DEPTH * 6 * 8 * 2])
            P.dma(d[:], G.cm[:].rearrange("p l w k s -> p (l w k s)"), reads=[G.cm], writes=[d])
        P.flush()


def stage_load(G):
    nc, P, I = G.nc, G.P, G.I
    xsT = G.scr['xsT']
    with contextlib.ExitStack() as es:
        sb = G.sb
        xin = [sb(es, 'xin%d' % i, [128, D]) for i in range(3)]
        xo = [sb(es, 'xo%d' % i, [128, 8, 512]) for i in range(2)]
        ti = 0
        for gi, (t0, n) in enumerate(TT):
            o = xo[gi % 2]
            for q in range(n // 128):
                xi = xin[ti % 3]; ti += 1
                tok = t0 + q * 128
                src = I['ctx'][tok:tok + 128, :] if tok < CTX else I['x'][tok - CTX:tok - CTX + 128, :]
                P.dma(xi[:], src, writes=[xi])
                for half in range(2):
                    ps = G.nextps()
                    for kk in range(4):
                        k = half * 4 + kk
                        P.pe(lambda e, ps=ps, kk=kk, k=k, xi=xi: e.transpose(out=ps[:, kk * 128:(kk + 1) * 128], in_=xi[:, k * 128:(k + 1) * 128],
                                                                          identity=G.ident[:]), reads=[xi, G.ident], writes=[ps])
                    eng = P.act if half == 0 else P.dve
                    if half == 0:
                        P.act(lambda e, ps=ps, o=o, q=q: e.activation(out=o[:, 0:4, q * 128:(q + 1) * 128], in_=ps[:].rearrange("p (k t) -> p k t", k=4), func=AF.Copy),
                              reads=[ps], writes=[o])
                    else:
                        P.dve(lambda e, ps=ps, o=o, q=q: e.tensor_copy(out=o[:, 4:8, q * 128:(q + 1) * 128], in_=ps[:].rearrange("p (k t) -> p k t", k=4)),
                              reads=[ps], writes=[o])
            P.dma(xsT[:, :, t0:t0 + n].rearrange("k p t -> p k t"), o[:, :, 0:n], reads=[o], writes=[xsT])
        P.flush()


def kernel(**inputs):
    n = 8
    nc = build()
    in_maps = []
    w = {nm: np.ascontiguousarray(inputs[nm], dtype=np.float32) for nm, _ in WEIGHT_SHAPES}
    for b in range(n):
        m = dict(w)
        m['x'] = np.ascontiguousarray(inputs['x'][b], dtype=np.float32)
        m['ctx'] = np.ascontiguousarray(inputs['ctx'][b], dtype=np.float32)
        m['c'] = np.ascontiguousarray(inputs['c'][b:b + 1], dtype=np.float32)
        m['c_ctx'] = np.ascontiguousarray(inputs['c_ctx'][None, :], dtype=np.float32)
        in_maps.append(m)
    res = run_bass_kernel_spmd(nc, in_maps, core_ids=list(range(n)))
    return np.stack([r['out'] for r in res.results], axis=0)


def STAGES_AFTER_PROJ(l):
    return [('ssd%d' % l, lambda G, l=l: stage_mlstm_ssd(G, l)), ('na%d' % l, lambda G, l=l: stage_na(G, l)), ('s5%d' % l, lambda G, l=l: stage_s5(G, l)), ('tail%d' % l, lambda G, l=l: stage_tail(G, l))]


def bc_ap(ap, pattern):
    return bass.AP(tensor=ap.tensor, offset=ap.offset, ap=[list(ap.ap[0])] + [list(p) for p in pattern])


def emit_norm(G, xt, n, l, which_gam, s, out_fn, out_res, tmp_bufs):
    P = G.P
    sq, rstd, tmp = tmp_bufs
    ps = G.nextps()
    P.act(lambda e: e.activation(out=sq[:, :, 0:n], in_=xt[:, :, 0:n], func=AF.Square), reads=[xt], writes=[sq])
    for k in range(8):
        P.pe(lambda e, k=k: e.matmul(ps[:, 0:n], lhsT=G.onesb[:], rhs=sq[:, k, 0:n], start=(k == 0), stop=(k == 7)),
             reads=[sq, G.onesb], writes=[ps])
    P.act(lambda e: e.activation(out=rstd[:, 0:n], in_=ps[:, 0:n], func=AF.Sqrt, scale=1.0 / D, bias=G.epsb[:, 0:1]),
          reads=[ps, G.epsb], writes=[rstd])
    P.dve(lambda e: e.reciprocal(out=rstd[:, 0:n], in_=rstd[:, 0:n]), reads=[rstd], writes=[rstd])
    for k in range(8):
        t = tmp[k % len(tmp)]
        P.dve(lambda e, k=k, t=t: e.tensor_tensor(out=t[:, 0:n], in0=xt[:, k, 0:n], in1=rstd[:, 0:n], op=ALU.mult),
              reads=[xt, rstd], writes=[t])
        P.act(lambda e, k=k, t=t: e.activation(out=out_fn(k), in_=t[:, 0:n], func=AF.Identity,
                                               scale=G.cm[:, l, which_gam, k, s:s + 1], bias=G.cm[:, l, which_gam - 1, k, s:s + 1]),
              reads=[t, G.cm], writes=[out_res])


def stage_norm_proj(G, l):
    nc, P, I = G.nc, G.P, G.I
    sb = G.sb
    xsT = G.scr['xsT']
    hxT_d = G.scr['hxT']
    sc = G.scr
    if l == 0:
        G.scratch('mqT', [2, 128, S]); G.scratch('mkT', [2, 128, S]); G.scratch('mkTM', [S, 256])
        G.scratch('TM1', [S, 784]); G.scratch('nvTM', [S, 256], BF16); G.scratch('dzTM', [S, 512]); G.scratch('ddtTM', [S, 16])
        G.scratch('nqT', [2, 128, S], BF16); G.scratch('nkT', [2, 128, S], BF16)
        G.scratch('xTM', [S, 512]); G.scratch('BT', [2, 128, S]); G.scratch('BTM', [S, 256]); G.scratch('CT', [2, 128, S])
    with contextlib.ExitStack() as es:
        hxT = sb(es, 'hxT_sb', [128, 8, S], BF16)
        hres = [Res('hx%d' % i) for i in range(len(TT))]
        with contextlib.ExitStack() as es2:
            xt = [sb(es2, 'nxt%d' % i, [128, 8, 512]) for i in range(3)]
            sq = [sb(es2, 'nsq%d' % i, [128, 8, 512], BF16) for i in range(2)]; rstd = [sb(es2, 'nrstd%d' % i, [128, 512]) for i in range(2)]
            tmp = [sb(es2, 'ntmp%d' % i, [128, 512]) for i in range(4)]
            for ti, (t0, n) in enumerate(TT):
                x = xt[ti % 3]
                P.dma(x[:, :, 0:n], xsT[:, :, t0:t0 + n].rearrange("k p t -> p k t"), reads=[xsT], writes=[x])
                emit_norm(G, x, n, l, 1, 1 if ti == 0 else 0, lambda k, t0=t0, n=n: hxT[:, k, t0:t0 + n], hres[ti], (sq[ti % 2], rstd[ti % 2], tmp))
                P.dma(hxT_d[:, :, t0:t0 + n].rearrange("k p t -> p k t"), hxT[:, :, t0:t0 + n], reads=[hres[ti]], writes=[hxT_d])
            P.flush()
        tmgroups = [(O_MV, 512), (O_MI, 272), (O_NV, 256), (O_DZ, 512), (O_DDT, 16)]
        with contextlib.ExitStack() as es2:
            wtm = sb(es2, 'wtm', [128, 8, 1568], BF16)
            off = 0
            offs = []
            for (c0, n) in tmgroups:
                P.dmaq('pool', wtm[:, :, off:off + n], I['w_in'][l, :, c0:c0 + n].rearrange("(k p) n -> p k n", p=128), writes=[wtm])
                offs.append(off)
                off += n
            st = [sb(es2, 'tmst%d' % i, [128, 1312]) for i in range(2)]
            stb = [sb(es2, 'tmstb%d' % i, [128, 256], BF16) for i in range(2)]
            for q in range(NT128):
                tok = q * 128
                ti = 0 if tok < CTX else 1 + (tok - CTX) // 512
                s_, sb_ = st[q % 2], stb[q % 2]
                for gi, (c0, n) in enumerate(tmgroups):
                    ps = G.nextps()
                    for k in range(8):
                        P.pe(lambda e, ps=ps, k=k, n=n, o=offs[gi], tok=tok: e.matmul(ps[:, 0:n], lhsT=hxT[:, k, tok:tok + 128], rhs=wtm[:, k, o:o + n],
                                                                                     start=(k == 0), stop=(k == 7)), reads=[hres[ti], wtm], writes=[ps])
                    if gi == 2:
                        P.act(lambda e, ps=ps, sb_=sb_: e.activation(out=sb_[:], in_=ps[:, 0:256], func=AF.Copy), reads=[ps], writes=[sb_])
                    else:
                        so = {0: 0, 1: 512, 3: 784, 4: 1296}[gi]
                        if gi % 2 == 0:
                            P.dve(lambda e, ps=ps, s_=s_, so=so, n=n: e.tensor_copy(out=s_[:, so:so + n], in_=ps[:, 0:n]), reads=[ps], writes=[s_])
                        else:
                            P.act(lambda e, ps=ps, s_=s_, so=so, n=n: e.activation(out=s_[:, so:so + n], in_=ps[:, 0:n], func=AF.Copy), reads=[ps], writes=[s_])
                P.dma(sc['TM1'][tok:tok + 128, :], s_[:, 0:784], reads=[s_], writes=[sc['TM1']])
                P.dma(sc['dzTM'][tok:tok + 128, :], s_[:, 784:1296], reads=[s_], writes=[sc['dzTM']])
                P.dma(sc['ddtTM'][tok:tok + 128, :], s_[:, 1296:1312], reads=[s_], writes=[sc['ddtTM']])
                P.dma(sc['nvTM'][tok:tok + 128, :], sb_[:], reads=[sb_], writes=[sc['nvTM']])
            P.flush()
        with contextlib.ExitStack() as es2:
            PADL = S + 12
            XOFF = 265
            rowbuf = [sb(es2, 'rowbuf%d' % i, [128, PADL], BF16) for i in range(2)]
            dgt = [sb(es2, 'dgt%d' % i, [128, 7, 128], BF16) for i in range(2)]
            cacc = sb(es2, 'cacc', [128, S])
            cout = [sb(es2, 'cout%d' % i, [128, S]) for i in range(1)]
            wfm = [sb(es2, 'wfm%d' % i, [128, 8, 128], BF16) for i in range(2)]
            cw_m = sb(es2, 'cw_m', [128, 4, 7]); cb_m = sb(es2, 'cb_m', [128, 4])
            cw_d = sb(es2, 'cw_d', [128, 8, 7]); cb_d = sb(es2, 'cb_d', [128, 8])
            for c in range(4):
                P.dma(cw_m[:, c, :], I['mlstm_conv_w'][l, :, c * 128:(c + 1) * 128].rearrange("j p -> p j"), writes=[cw_m], allow_slow_non_contiguous=True)
            P.dma(cb_m[:], I['mlstm_conv_b'][l].rearrange("(c p) -> p c", p=128), writes=[cb_m], allow_slow_non_contiguous=True)
            for c in range(8):
                P.dma(cw_d[:, c, :], I['ssd_conv_w'][l, :, c * 128:(c + 1) * 128].rearrange("j p -> p j"), writes=[cw_d], allow_slow_non_contiguous=True)
            P.dma(cb_d[:], I['ssd_conv_b'][l].rearrange("(c p) -> p c", p=128), writes=[cb_d], allow_slow_non_contiguous=True)
            for rb in rowbuf:
                P.pool(lambda e, rb=rb: e.memset(rb[:], 0.0), writes=[rb])
            cosT = sb(es2, 'cosT', [128, SEQ]); sinT = sb(es2, 'sinT', [128, SEQ]); perm = sb(es2, 'perm', [128, 128])
            build_rope_tables(G, es2, cosT, sinT, perm)
            trst = [sb(es2, 'trst%d' % i, [128, 4, 128]) for i in range(2)]
            cbf = [sb(es2, 'cbf%d' % i, [128, S], BF16) for i in range(1)]
            chunks = [(O_MQ + 128 * i, 'mq', i) for i in range(2)] + [(O_MK + 128 * i, 'mk', i) for i in range(2)]
            chunks += [(O_NQ + 128 * i, 'nq', i) for i in range(2)] + [(O_NK + 128 * i, 'nk', i) for i in range(2)]
            chunks += [(O_DX + 128 * i, 'dx', i) for i in range(4)] + [(O_DB + 128 * i, 'dB', i) for i in range(2)]
            chunks += [(O_DC + 128 * i, 'dC', i) for i in range(2)]
            trn = 0
            for ci, (c0, kind, idx) in enumerate(chunks):
                w = wfm[ci % 2]; rb = rowbuf[ci % 2]; co = cout[0]; dg = dgt[ci % 2]
                P.dmaq('pool', w[:], I['w_in'][l, :, c0:c0 + 128].rearrange("(k p) n -> p k n", p=128), writes=[w])
                for ti, (t0, n) in enumerate(TT):
                    ps = G.nextps()
                    for k in range(8):
                        P.pe(lambda e, ps=ps, k=k, n=n, t0=t0, w=w: e.matmul(ps[:, 0:n], lhsT=w[:, k, :], rhs=hxT[:, k, t0:t0 + n], start=(k == 0), stop=(k == 7)),
                             reads=[hres[ti], w], writes=[ps])
                    if kind in ('nq', 'nk'):
                        dst = cbf[0]
                        scale = 0.125 if kind == 'nq' else 1.0
                        P.act(lambda e, ps=ps, n=n, t0=t0, dst=dst, scale=scale: e.activation(out=dst[:, t0:t0 + n], in_=ps[:, 0:n], func=AF.Copy, scale=scale),
                              reads=[ps], writes=[dst])
                    else:
                        o0 = 3 if ti == 0 else XOFF + (t0 - CTX)
                        if ti % 2 == 0:
                            P.act(lambda e, ps=ps, n=n, o0=o0, rb=rb: e.activation(out=rb[:, o0:o0 + n], in_=ps[:, 0:n], func=AF.Copy), reads=[ps], writes=[rb])
                        else:
                            P.dve(lambda e, ps=ps, n=n, o0=o0, rb=rb: e.tensor_copy(out=rb[:, o0:o0 + n], in_=ps[:, 0:n]), reads=[ps], writes=[rb])
                if kind in ('nq', 'nk'):
                    dst = cbf[0]
                    P.dma(sc['nqT' if kind == 'nq' else 'nkT'][idx], dst[:], reads=[dst], writes=[sc['nqT' if kind == 'nq' else 'nkT']])
                    continue
                if kind in ('mq', 'mk'):
                    cw, cb, cidx = cw_m, cb_m, (idx if kind == 'mq' else 2 + idx)
                else:
                    cw, cb, cidx = cw_d, cb_d, {'dx': idx, 'dB': 4 + idx, 'dC': 6 + idx}[kind]
                for j in range(7):
                    P.dve(lambda e, j=j, dg=dg, cw=cw, cidx=cidx: e.tensor_scalar(out=dg[:, j, :], in0=G.ident[:], scalar1=cw[:, cidx, j:j + 1], scalar2=None, op0=ALU.mult),
                          reads=[G.ident, cw], writes=[dg])
                segs = [(3, CTX, 0)] + [(XOFF + 512 * t, 512, CTX + 512 * t) for t in range(8)]
                for si, (o0, n, d0) in enumerate(segs):
                    ps = G.nextps()
                    for j in range(7):
                        P.pe(lambda e, ps=ps, j=j, o0=o0, n=n, dg=dg, rb=rb: e.matmul(ps[:, 0:n], lhsT=dg[:, j, :], rhs=rb[:, o0 - 3 + j:o0 - 3 + j + n], start=(j == 0), stop=(j == 6)),
                             reads=[dg, rb], writes=[ps])
                    P.act(lambda e, ps=ps, co=co, cb=cb, cidx=cidx, n=n, d0=d0: e.activation(out=co[:, d0:d0 + n], in_=ps[:, 0:n], func=AF.Silu, bias=cb[:, cidx:cidx + 1]),
                          reads=[ps, cb], writes=[co])
                if kind in ('mq', 'mk'):
                    for t in range(8):
                        t0 = CTX + 512 * t
                        ps = G.nextps()
                        P.pe(lambda e, ps=ps, t0=t0, co=co: e.matmul(ps[:, :], lhsT=perm[:], rhs=co[:, t0:t0 + 512], start=True, stop=True),
                             reads=[perm, co], writes=[ps])
                        P.dve(lambda e, ps=ps, t0=t0: e.tensor_tensor(out=cacc[:, t0:t0 + 512], in0=ps[:, :], in1=sinT[:, t0 - CTX:t0 - CTX + 512], op=ALU.mult),
                              reads=[ps, sinT], writes=[cacc])
                    P.pool(lambda e, co=co: e.tensor_tensor(out=co[:, CTX:S], in0=co[:, CTX:S], in1=cosT[:], op=ALU.mult), reads=[co, cosT], writes=[co])
                    P.dve(lambda e, co=co: e.tensor_tensor(out=co[:, CTX:S], in0=co[:, CTX:S], in1=cacc[:, CTX:S], op=ALU.add), reads=[co, cacc], writes=[co])
                    if kind == 'mq':
                        P.act(lambda e, co=co: e.activation(out=co[:], in_=co[:], func=AF.Copy, scale=0.125), reads=[co], writes=[co])
                name = {'mq': 'mqT', 'mk': 'mkT', 'dB': 'BT', 'dC': 'CT'}.get(kind)
                if name is not None:
                    P.dma(sc[name][idx], co[:], reads=[co], writes=[sc[name]])
                tmname = {'mk': 'mkTM', 'dx': 'xTM', 'dB': 'BTM'}.get(kind)
                if tmname is not None:
                    for g4 in range(0, NT128, 4):
                        nq = min(4, NT128 - g4)
                        ps = G.nextps()
                        tb = trst[trn % 2]; trn += 1
                        for qq in range(nq):
                            tok = (g4 + qq) * 128
                            P.pe(lambda e, ps=ps, qq=qq, tok=tok, co=co: e.transpose(out=ps[:, qq * 128:(qq + 1) * 128], in_=co[:, tok:tok + 128], identity=G.ident[:]),
                                 reads=[co, G.ident], writes=[ps])
                        P.act(lambda e, ps=ps, tb=tb, nq=nq: e.activation(out=tb[:, 0:nq, :], in_=ps[:, 0:nq * 128].rearrange("p (q c) -> p q c", q=nq), func=AF.Copy),
                              reads=[ps], writes=[tb])
                        P.dma(sc[tmname][g4 * 128:(g4 + nq) * 128, idx * 128:(idx + 1) * 128].rearrange("(q p) c -> p q c", p=128), tb[:, 0:nq, :],
                              reads=[tb], writes=[sc[tmname]])
            P.flush()


def build_rope_tables(G, es, cosT, sinT, perm):
    nc, P = G.nc, G.P
    sb = G.sb
    I32 = mybir.dt.int32
    pi_i = sb(es, 'rp_pi', [128, 1], I32); pf = sb(es, 'rp_pf', [128, 1]); inv = sb(es, 'rp_inv', [128, 1])
    mcol = sb(es, 'rp_mcol', [128, 1]); mrow = sb(es, 'rp_mrow', [128, 1]); t1 = sb(es, 'rp_t1', [128, 1])
    posf = sb(es, 'rp_pos', [128, 64]); ang = sb(es, 'rp_ang', [128, 64]); nn = sb(es, 'rp_n', [128, 64]); ni = sb(es, 'rp_ni', [128, 64], I32)
    cs = sb(es, 'rp_cs', [128, 64]); sn = sb(es, 'rp_sn', [128, 64]); fix = sb(es, 'rp_fix', [128, 64])
    csr = sb(es, 'rp_csr', [128, 64]); csc = sb(es, 'rp_csc', [128, 64]); snr = sb(es, 'rp_snr', [128, 64]); snc = sb(es, 'rp_snc', [128, 64])
    P.pool(lambda e: e.iota(pi_i[:], pattern=[[0, 1]], base=0, channel_multiplier=1), writes=[pi_i])
    P.dve(lambda e: e.tensor_copy(out=pf[:], in_=pi_i[:]), reads=[pi_i], writes=[pf])
    P.dve(lambda e: e.tensor_single_scalar(out=pi_i[:], in_=pi_i[:], scalar=15, op=ALU.bitwise_and), reads=[pi_i], writes=[pi_i])
    P.dve(lambda e: e.tensor_copy(out=inv[:], in_=pi_i[:]), reads=[pi_i], writes=[inv])
    P.act(lambda e: e.activation(out=inv[:], in_=inv[:], func=AF.Exp, scale=-math.log(10000.0) / 16.0), reads=[inv], writes=[inv])
    P.dve(lambda e: e.tensor_single_scalar(out=mcol[:], in_=pf[:], scalar=32.0, op=ALU.is_ge), reads=[pf], writes=[mcol])
    P.dve(lambda e: e.tensor_single_scalar(out=t1[:], in_=pf[:], scalar=64.0, op=ALU.is_ge), reads=[pf], writes=[t1])
    P.dve(lambda e: e.tensor_tensor(out=mcol[:], in0=mcol[:], in1=t1[:], op=ALU.subtract), reads=[mcol, t1], writes=[mcol])
    P.dve(lambda e: e.tensor_single_scalar(out=t1[:], in_=pf[:], scalar=96.0, op=ALU.is_ge), reads=[pf], writes=[t1])
    P.dve(lambda e: e.tensor_tensor(out=mcol[:], in0=mcol[:], in1=t1[:], op=ALU.add), reads=[mcol, t1], writes=[mcol])
    P.dve(lambda e: e.tensor_scalar(out=mrow[:], in0=mcol[:], scalar1=-1.0, scalar2=1.0, op0=ALU.mult, op1=ALU.add), reads=[mcol], writes=[mrow])
    P.pool(lambda e: e.iota(posf[:], pattern=[[1, 64]], base=0, channel_multiplier=0, allow_small_or_imprecise_dtypes=True), writes=[posf])
    P.dve(lambda e: e.tensor_scalar(out=ang[:], in0=posf[:], scalar1=inv[:, 0:1], scalar2=None, op0=ALU.mult), reads=[posf, inv], writes=[ang])

    def sincos(dst, shift):
        P.dve(lambda e: e.tensor_scalar(out=nn[:], in0=ang[:], scalar1=shift, scalar2=1.0 / (2 * math.pi), op0=ALU.add, op1=ALU.mult), reads=[ang], writes=[nn])
        P.dve(lambda e: e.tensor_copy(out=ni[:], in_=nn[:]), reads=[nn], writes=[ni])
        P.dve(lambda e: e.tensor_copy(out=nn[:], in_=ni[:]), reads=[ni], writes=[nn])
        P.dve(lambda e: e.scalar_tensor_tensor(out=nn[:], in0=nn[:], scalar=-2 * math.pi, in1=ang[:], op0=ALU.mult, op1=ALU.add), reads=[nn, ang], writes=[nn])
        P.dve(lambda e: e.tensor_scalar(out=nn[:], in0=nn[:], scalar1=shift, scalar2=None, op0=ALU.add), reads=[nn], writes=[nn])
        P.dve(lambda e: e.tensor_scalar(out=fix[:], in0=nn[:], scalar1=math.pi, scalar2=-2 * math.pi, op0=ALU.is_gt, op1=ALU.mult), reads=[nn], writes=[fix])
        P.dve(lambda e: e.tensor_tensor(out=nn[:], in0=nn[:], in1=fix[:], op=ALU.add), reads=[nn, fix], writes=[nn])
        P.dve(lambda e: e.tensor_scalar(out=fix[:], in0=nn[:], scalar1=-math.pi, scalar2=2 * math.pi, op0=ALU.is_lt, op1=ALU.mult), reads=[nn], writes=[fix])
        P.dve(lambda e: e.tensor_tensor(out=nn[:], in0=nn[:], in1=fix[:], op=ALU.add), reads=[nn, fix], writes=[nn])
        P.act(lambda e: e.activation(out=dst[:], in_=nn[:], func=AF.Sin), reads=[nn], writes=[dst])
    sincos(sn, 0.0)
    sincos(cs, math.pi / 2)
    for (src, a, b) in ((cs, csr, csc), (sn, snr, snc)):
        P.dve(lambda e, src=src, a=a: e.tensor_scalar(out=a[:], in0=src[:], scalar1=mrow[:, 0:1], scalar2=None, op0=ALU.mult), reads=[src, mrow], writes=[a])
        P.dve(lambda e, src=src, b=b: e.tensor_scalar(out=b[:], in0=src[:], scalar1=mcol[:, 0:1], scalar2=None, op0=ALU.mult), reads=[src, mcol], writes=[b])
    for (dst, a, b) in ((cosT, csr, csc), (sinT, snr, snc)):
        d3 = dst[:].rearrange("p (r c) -> p r c", r=64)
        P.dve(lambda e, d3=d3, a=a: e.tensor_copy(out=d3, in_=bc_ap(a[:], [[1, 64], [0, 64]])), reads=[a], writes=[dst])
        P.dve(lambda e, d3=d3, b=b: e.tensor_tensor(out=d3, in0=d3, in1=bc_ap(b[:], [[0, 64], [1, 64]]), op=ALU.add), reads=[b, dst], writes=[dst])
    for b in range(4):
        P.act(lambda e, b=b: e.activation(out=perm[:, 32 * b:32 * b + 16], in_=G.ident[:, 32 * b + 16:32 * b + 32], func=AF.Copy, scale=-1.0),
              reads=[G.ident], writes=[perm])
        P.act(lambda e, b=b: e.activation(out=perm[:, 32 * b + 16:32 * b + 32], in_=G.ident[:, 32 * b:32 * b + 16], func=AF.Copy),
              reads=[G.ident], writes=[perm])


def load_bcast(G, es, name, src_row_ap, n):
    t = G.sb(es, name, [128, n])
    G.P.dma(t[:], src_row_ap.partition_broadcast(128), writes=[t])
    return t


def mlstm_steps(G, l, es):
    nc, P, I = G.nc, G.P, G.I
    sb = G.sb
    sc = G.scr
    with_ctx = l < DEPTH - 1
    if 'yT' not in sc:
        G.scratch('yT', [10, 128, S], BF16)
    yT = sc['yT']
    if True:
        hacc = sb(es, 'm_hacc', [128, NT128, 256])
        gates = sb(es, 'm_gates', [128, NT128, 24])
        ib_b = load_bcast(G, es, 'm_ib', I['mlstm_ib'][l:l + 1].rearrange("o d h -> o (d h)"), 8)
        fb_b = load_bcast(G, es, 'm_fb', I['mlstm_fb'][l:l + 1].rearrange("o d h -> o (d h)"), 8)
        nw_b = load_bcast(G, es, 'm_nw', I['mlstm_norm_w'][l:l + 1, :], 256)
        NB = 2
        qT = [sb(es, 'm_qT%d' % i, [128, 2, 128]) for i in range(NB)]
        kT = [sb(es, 'm_kT%d' % i, [128, 2, 128]) for i in range(NB)]
        kTM = [sb(es, 'm_kTM%d' % i, [128, 256]) for i in range(NB)]
        Vp = [sb(es, 'm_Vp%d' % i, [128, 4, 65]) for i in range(NB)]
        gi_ = [sb(es, 'm_gi%d' % i, [128, 16]) for i in range(NB)]
        mo = [sb(es, 'm_mo%d' % i, [128, 256]) for i in range(NB)]
        for v in Vp:
            P.pool(lambda e, v=v: e.memset(v[:], 1.0), writes=[v])
        Cbd = [[sb(es, 'm_C%d%d' % (pr, d), [128, 130]) for d in range(2)] for pr in range(2)]
        for pr in range(2):
            for d in range(2):
                P.pool(lambda e, c=Cbd[pr][d]: e.memset(c[:], 0.0), writes=[Cbd[pr][d]])
        g1 = sb(es, 'm_g1', [128, 8]); g2 = sb(es, 'm_g2', [128, 8]); cum = sb(es, 'm_cum', [128, 16])
        pmt = [sb(es, 'm_pmt%d' % i, [128, 128]) for i in range(2)]
        pm = [sb(es, 'm_pm%d' % i, [128, 128]) for i in range(8)]
        uV = [sb(es, 'm_uV%d' % i, [128, 130]) for i in range(2)]
        ep = [sb(es, 'm_ep%d' % i, [128, 8]) for i in range(2)]
        stt = [sb(es, 'm_stt%d' % i, [128, 65]) for i in range(2)]
        ho = [sb(es, 'm_ho%d' % i, [128, 256]) for i in range(2)]
        sg = [sb(es, 'm_sg%d' % i, [128, 256]) for i in range(2)]
        ss = [sb(es, 'm_ss%d' % i, [128, 4]) for i in range(2)]
        junk = sb(es, 'm_junk', [128, 64])
        ytb = [sb(es, 'm_ytb%d' % i, [128, 2, 128], BF16) for i in range(2)]
        cnt = {'pm': 0, 'n': 0}

        def chunk_pass(q, d, it, first_pass):
            tok = q * 128
            b = it % NB
            P.dma(qT[b][:], sc['mqT'][:, :, tok:tok + 128].rearrange("c p t -> p c t"), reads=[sc['mqT']], writes=[qT[b]])
            P.dma(kT[b][:], sc['mkT'][:, :, tok:tok + 128].rearrange("c p t -> p c t"), reads=[sc['mkT']], writes=[kT[b]])
            P.dma(kTM[b][:], sc['mkTM'][tok:tok + 128, :], reads=[sc['mkTM']], writes=[kTM[b]])
            P.dma(Vp[b][:, :, 0:64], sc['TM1'][tok:tok + 128, 0:256].rearrange("p (h e) -> p h e", h=4), reads=[sc['TM1']], writes=[Vp[b]])
            if first_pass:
                g = gi_[b]
                P.dma(g[:], sc['TM1'][tok:tok + 128, 512:528], reads=[sc['TM1']], writes=[g])
                P.dve(lambda e: e.tensor_tensor(out=g1[:], in0=g[:, 0:8], in1=ib_b[:], op=ALU.add), reads=[g, ib_b], writes=[g1])
                P.dve(lambda e: e.tensor_tensor(out=g2[:], in0=g[:, 8:16], in1=fb_b[:], op=ALU.add), reads=[g, fb_b], writes=[g2])
                P.act(lambda e: e.activation(out=g2[:], in_=g2[:], func=AF.Exp, scale=-1.0), reads=[g2], writes=[g2])
                P.act(lambda e: e.activation(out=g2[:], in_=g2[:], func=AF.Ln, bias=1.0), reads=[g2], writes=[g2])
                P.dve(lambda e: e.tensor_scalar(out=g2[:], in0=g2[:], scalar1=-1.0, scalar2=None, op0=ALU.mult), reads=[g2], writes=[g2])
                ps = G.nextps()
                P.pe(lambda e, ps=ps: e.matmul(ps[:, 0:4], lhsT=G.triU[:], rhs=g2[:, 0:4], start=True, stop=True), reads=[G.triU, g2], writes=[ps])
                P.pe(lambda e, ps=ps: e.matmul(ps[:, 4:8], lhsT=G.triL[:], rhs=g2[:, 4:8], start=True, stop=True), reads=[G.triL, g2], writes=[ps])
                P.pe(lambda e, ps=ps: e.matmul(ps[:, 8:16], lhsT=G.ones[:], rhs=g2[:, 0:8], start=True, stop=True), reads=[G.ones, g2], writes=[ps])
                P.dve(lambda e, ps=ps: e.tensor_copy(out=cum[:], in_=ps[:, 0:16]), reads=[ps], writes=[cum])
                P.act(lambda e: e.activation(out=gates[:, q, 0:8], in_=cum[:, 0:8], func=AF.Exp), reads=[cum], writes=[gates])
                P.dve(lambda e: e.tensor_tensor(out=g1[:], in0=g1[:], in1=cum[:, 0:8], op=ALU.subtract), reads=[g1, cum], writes=[g1])
                P.act(lambda e: e.activation(out=gates[:, q, 8:16], in_=g1[:], func=AF.Exp), reads=[g1], writes=[gates])
                P.act(lambda e: e.activation(out=gates[:, q, 16:24], in_=cum[:, 8:16], func=AF.Exp), reads=[cum], writes=[gates])
            else:
                P.dma(mo[b][:], sc['TM1'][tok:tok + 128, 256:512], reads=[sc['TM1']], writes=[mo[b]])
            mask = G.triU if d == 0 else G.triL
            pms_all = []
            for pr in range(2):
                pms = []
                pms_all.append(pms)
                for hh in range(2):
                    h = 2 * pr + hh
                    j = 4 * d + h
                    ps = G.nextps()
                    P.pe(lambda e, ps=ps, hh=hh, pr=pr: e.matmul(ps[:, 0:128], lhsT=kT[b][64 * hh:64 * hh + 64, pr, :], rhs=qT[b][64 * hh:64 * hh + 64, pr, :],
                                                                  start=True, stop=True), reads=[kT[b], qT[b]], writes=[ps])
                    t_ = pmt[cnt['pm'] % 2]; p_ = pm[cnt['pm'] % 8]; cnt['pm'] += 1
                    P.act(lambda e, ps=ps, t_=t_, j=j: e.activation(out=t_[:], in_=ps[:, 0:128], func=AF.Copy, scale=gates[:, q, 8 + j:9 + j]),
                          reads=[ps, gates], writes=[t_])
                    P.pool(lambda e, t_=t_, p_=p_: e.tensor_tensor(out=p_[:], in0=t_[:], in1=mask[:], op=ALU.mult), reads=[t_, mask], writes=[p_])
                    pms.append(p_)
            yield
            for pr in range(2):
                pms = pms_all[pr]
                j0 = 4 * d + 2 * pr
                C = Cbd[pr][d]
                ps2 = G.nextps()
                P.pe(lambda e, ps2=ps2, pr=pr, C=C: e.matmul(ps2[:, 0:130], lhsT=qT[b][:, pr, :], rhs=C[:], start=True, stop=False), reads=[qT[b], C], writes=[ps2])
                for hh in range(2):
                    P.pe(lambda e, ps2=ps2, hh=hh, pr=pr, p_=pms[hh]: e.matmul(ps2[:, 65 * hh:65 * hh + 65], lhsT=p_[:], rhs=Vp[b][:, 2 * pr + hh, :],
                                                                              start=False, stop=(hh == 1)), reads=[pms[hh], Vp[b]], writes=[ps2])
                e_ = ep[cnt['n'] % 2]; cnt['n'] += 1
                p3 = ps2[:, 0:130].rearrange("p (h e) -> p h e", e=65)
                P.dve(lambda e, e_=e_, p3=p3, j0=j0: e.tensor_tensor(out=e_[:, 0:2], in0=p3[:, :, 64], in1=gates[:, q, j0:j0 + 2], op=ALU.mult), reads=[ps2, gates], writes=[e_])
                P.dve(lambda e, e_=e_: e.scalar_tensor_tensor(out=e_[:, 6:8], in0=e_[:, 0:2], scalar=-1.0, in1=e_[:, 0:2], op0=ALU.mult, op1=ALU.max), reads=[e_], writes=[e_])
                P.dve(lambda e, e_=e_: e.tensor_scalar(out=e_[:, 0:2], in0=e_[:, 6:8], scalar1=1.0, scalar2=None, op0=ALU.max), reads=[e_], writes=[e_])
                P.dve(lambda e, e_=e_: e.reciprocal(out=e_[:, 2:4], in_=e_[:, 0:2]), reads=[e_], writes=[e_])
                P.dve(lambda e, e_=e_, j0=j0: e.tensor_tensor(out=e_[:, 4:6], in0=e_[:, 2:4], in1=gates[:, q, j0:j0 + 2], op=ALU.mult), reads=[e_, gates], writes=[e_])
                for hh in range(2):
                    h = 2 * pr + hh
                    if first_pass:
                        P.act(lambda e, hh=hh, h=h, e_=e_, p3=p3: e.activation(out=hacc[:, q, 64 * h:64 * h + 64], in_=p3[:, hh, 0:64], func=AF.Copy, scale=e_[:, 4 + hh:5 + hh]),
                              reads=[ps2, e_], writes=[hacc])
                    else:
                        P.dve(lambda e, hh=hh, h=h, e_=e_, p3=p3: e.scalar_tensor_tensor(out=hacc[:, q, 64 * h:64 * h + 64], in0=p3[:, hh, 0:64], scalar=e_[:, 4 + hh:5 + hh],
                                                                                      in1=hacc[:, q, 64 * h:64 * h + 64], op0=ALU.mult, op1=ALU.add),
                              reads=[ps2, e_, hacc], writes=[hacc])
                uv = uV[cnt['n'] % 2]
                for hh in range(2):
                    h = 2 * pr + hh
                    j = 4 * d + h
                    P.dve(lambda e, uv=uv, hh=hh, h=h, j=j: e.tensor_scalar(out=uv[:, 65 * hh:65 * hh + 65], in0=Vp[b][:, h, :], scalar1=gates[:, q, 8 + j:9 + j], scalar2=None, op0=ALU.mult),
                          reads=[Vp[b], gates], writes=[uv])
                ps3 = G.nextps()
                P.pe(lambda e, ps3=ps3, uv=uv, pr=pr: e.matmul(ps3[:, 0:130], lhsT=kTM[b][:, 128 * pr:128 * pr + 128], rhs=uv[:], start=True, stop=True), reads=[kTM[b], uv], writes=[ps3])
                for hh in range(2):
                    j = 4 * d + 2 * pr + hh
                    rows = slice(64 * hh, 64 * hh + 64)
                    cols = slice(65 * hh, 65 * hh + 65)
                    P.dve(lambda e, ps3=ps3, rows=rows, cols=cols, C=C: e.tensor_tensor(out=C[rows, cols], in0=ps3[rows, cols], in1=C[rows, cols], op=ALU.add), reads=[ps3, C], writes=[C])
                    P.dve(lambda e, rows=rows, cols=cols, C=C, j=j: e.tensor_scalar(out=C[rows, cols], in0=C[rows, cols], scalar1=gates[rows, q, 16 + j:17 + j], scalar2=None, op0=ALU.mult),
                          reads=[C, gates], writes=[C])
            if not first_pass and (with_ctx or q >= 2):
                bb = it % 2
                P.act(lambda e: e.activation(out=sg[bb][:], in_=mo[b][:], func=AF.Sigmoid), reads=[mo[b]], writes=[sg[bb]])
                P.dve(lambda e: e.tensor_tensor(out=ho[bb][:], in0=hacc[:, q, :], in1=sg[bb][:], op=ALU.mult), reads=[hacc, sg[bb]], writes=[ho[bb]])
                for h in range(4):
                    P.act(lambda e, h=h: e.activation(out=junk[:], in_=ho[bb][:, 64 * h:64 * h + 64], func=AF.Square, accum_out=ss[bb][:, h:h + 1]), reads=[ho[bb]], writes=[junk, ss[bb]])
                P.act(lambda e: e.activation(out=ss[bb][:], in_=ss[bb][:], func=AF.Sqrt, scale=1.0 / 64, bias=G.epsb[:, 0:1]), reads=[ss[bb], G.epsb], writes=[ss[bb]])
                P.dve(lambda e: e.reciprocal(out=ss[bb][:], in_=ss[bb][:]), reads=[ss[bb]], writes=[ss[bb]])
                for h in range(4):
                    P.dve(lambda e, h=h: e.scalar_tensor_tensor(out=ho[bb][:, 64 * h:64 * h + 64], in0=ho[bb][:, 64 * h:64 * h + 64], scalar=ss[bb][:, h:h + 1],
                                                                 in1=nw_b[:, 64 * h:64 * h + 64], op0=ALU.mult, op1=ALU.mult), reads=[ho[bb], ss[bb], nw_b], writes=[ho[bb]])
                ps = G.nextps()
                for c in range(2):
                    P.pe(lambda e, ps=ps, c=c: e.transpose(out=ps[:, 128 * c:128 * c + 128], in_=ho[bb][:, 128 * c:128 * c + 128], identity=G.ident[:]), reads=[ho[bb], G.ident], writes=[ps])
                P.act(lambda e, ps=ps: e.activation(out=ytb[bb][:], in_=ps[:, 0:256].rearrange("p (c t) -> p c t", c=2), func=AF.Copy), reads=[ps], writes=[ytb[bb]])
                P.dma(yT[0:2, :, tok:tok + 128].rearrange("c p t -> p c t"), ytb[bb][:], reads=[ytb[bb]], writes=[yT])

        steps = []
        it = 0
        for q in range(NT128):
            steps.append(chunk_pass(q, 0, it, True)); it += 1
        order = [1, 0] + list(range(NT128 - 1, 1, -1))
        for q in order:
            steps.append(chunk_pass(q, 1, it, False)); it += 1
        return steps


def ssd_steps(G, l, es):
    nc, P, I = G.nc, G.P, G.I
    sb = G.sb
    sc = G.scr
    with_ctx = l < DEPTH - 1
    yT = sc['yT']
    if True:
        yacc = sb(es, 'd_yacc', [128, NT128, 512])
        alog_b = load_bcast(G, es, 'd_alog', I['ssd_a_log'][l:l + 1].rearrange("o d h -> o (d h)"), 16)
        dtb_b = load_bcast(G, es, 'd_dtb', I['ssd_dt_bias'][l:l + 1].rearrange("o d h -> o (d h)"), 16)
        D_b = load_bcast(G, es, 'd_D', I['ssd_d'][l:l + 1, :], 8)
        nw_b = load_bcast(G, es, 'd_nw', I['ssd_norm_w'][l:l + 1, :], 512)
        A_b = sb(es, 'd_A', [128, 16])
        P.act(lambda e: e.activation(out=A_b[:], in_=alog_b[:], func=AF.Exp), reads=[alog_b], writes=[A_b])
        P.dve(lambda e: e.tensor_scalar(out=A_b[:], in0=A_b[:], scalar1=-1.0, scalar2=None, op0=ALU.mult), reads=[A_b], writes=[A_b])
        NB = 2
        xt = [sb(es, 'd_xt%d' % i, [128, 512]) for i in range(NB)]
        Bt = [sb(es, 'd_Bt%d' % i, [128, 2, 128]) for i in range(NB)]
        Ct = [sb(es, 'd_Ct%d' % i, [128, 2, 128]) for i in range(NB)]
        Btm = [sb(es, 'd_Btm%d' % i, [128, 256]) for i in range(NB)]
        ddt = [sb(es, 'd_ddt%d' % i, [128, 16]) for i in range(NB)]
        dz = [sb(es, 'd_dz%d' % i, [128, 512]) for i in range(NB)]
        Hs = [sb(es, 'd_Hs%d' % d, [128, 8, 64]) for d in range(2)]
        for d in range(2):
            P.pool(lambda e, d=d: e.memset(Hs[d][:], 0.0), writes=[Hs[d]])
        dt = sb(es, 'd_dt', [128, 16]); a_ = sb(es, 'd_a', [128, 16]); cum = sb(es, 'd_cum', [128, 32])
        negcum = sb(es, 'd_negcum', [128, 16]); wgt = sb(es, 'd_w', [128, 16])
        rbig = sb(es, 'd_rbig', [128, 8, 128])
        scm = [sb(es, 'd_scm%d' % g, [128, 128]) for g in range(2)]
        arg = [sb(es, 'd_arg%d' % i, [128, 128]) for i in range(2)]
        ex = [sb(es, 'd_ex%d' % i, [128, 128]) for i in range(2)]
        pmb = [sb(es, 'd_pm%d' % i, [128, 128]) for i in range(16)]
        tmp = sb(es, 'd_tmp', [128, 8, 64]); wx2 = [sb(es, 'd_wx%d' % i, [128, 8, 64]) for i in range(2)]; htmp = sb(es, 'd_htmp', [128, 8, 64])
        acum2 = [sb(es, 'd_acum%d' % i, [128, 16]) for i in range(2)]; etot2 = [sb(es, 'd_etot%d' % i, [128, 16]) for i in range(2)]
        yz = sb(es, 'd_yz', [128, 512]); sz = sb(es, 'd_sz', [128, 512]); ssq = sb(es, 'd_ssq', [128, 1]); junk = sb(es, 'd_junk', [128, 512])
        ytb = [sb(es, 'd_ytb%d' % i, [128, 4, 128], BF16) for i in range(2)]
        cnt = {'n': 0}

        def chunk_pass(q, d, it, first_pass):
            tok = q * 128
            b = it % NB
            acum = acum2[it % 2]; etot = etot2[it % 2]; wx = wx2[it % 2]
            P.dma(xt[b][:], sc['xTM'][tok:tok + 128, :], reads=[sc['xTM']], writes=[xt[b]])
            P.dma(Bt[b][:], sc['BT'][:, :, tok:tok + 128].rearrange("g p t -> p g t"), reads=[sc['BT']], writes=[Bt[b]])
            P.dma(Ct[b][:], sc['CT'][:, :, tok:tok + 128].rearrange("g p t -> p g t"), reads=[sc['CT']], writes=[Ct[b]])
            P.dma(Btm[b][:], sc['BTM'][tok:tok + 128, :], reads=[sc['BTM']], writes=[Btm[b]])
            P.dma(ddt[b][:], sc['ddtTM'][tok:tok + 128, :], reads=[sc['ddtTM']], writes=[ddt[b]])
            if not first_pass:
                P.dma(dz[b][:], sc['dzTM'][tok:tok + 128, :], reads=[sc['dzTM']], writes=[dz[b]])
            P.dve(lambda e: e.tensor_tensor(out=dt[:], in0=ddt[b][:], in1=dtb_b[:], op=ALU.add), reads=[ddt[b], dtb_b], writes=[dt])
            P.act(lambda e: e.activation(out=dt[:], in_=dt[:], func=AF.Exp), reads=[dt], writes=[dt])
            P.act(lambda e: e.activation(out=dt[:], in_=dt[:], func=AF.Ln, bias=1.0), reads=[dt], writes=[dt])
            P.dve(lambda e: e.tensor_tensor(out=a_[:], in0=dt[:], in1=A_b[:], op=ALU.mult), reads=[dt, A_b], writes=[a_])
            ps = G.nextps()
            P.pe(lambda e, ps=ps: e.matmul(ps[:, 0:8], lhsT=G.triU[:], rhs=a_[:, 0:8], start=True, stop=True), reads=[G.triU, a_], writes=[ps])
            P.pe(lambda e, ps=ps: e.matmul(ps[:, 8:16], lhsT=G.triL[:], rhs=a_[:, 8:16], start=True, stop=True), reads=[G.triL, a_], writes=[ps])
            P.pe(lambda e, ps=ps: e.matmul(ps[:, 16:32], lhsT=G.ones[:], rhs=a_[:, 0:16], start=True, stop=True), reads=[G.ones, a_], writes=[ps])
            P.dve(lambda e, ps=ps: e.tensor_copy(out=cum[:], in_=ps[:, 0:32]), reads=[ps], writes=[cum])
            P.dve(lambda e: e.tensor_scalar(out=negcum[:], in0=cum[:, 0:16], scalar1=-1.0, scalar2=None, op0=ALU.mult), reads=[cum], writes=[negcum])
            P.act(lambda e: e.activation(out=acum[:], in_=cum[:, 0:16], func=AF.Exp), reads=[cum], writes=[acum])
            P.act(lambda e: e.activation(out=etot[:], in_=cum[:, 16:32], func=AF.Exp), reads=[cum], writes=[etot])
            P.dve(lambda e: e.tensor_tensor(out=wgt[:], in0=cum[:, 16:32], in1=cum[:, 0:16], op=ALU.subtract), reads=[cum], writes=[wgt])
            P.act(lambda e: e.activation(out=wgt[:], in_=wgt[:], func=AF.Exp), reads=[wgt], writes=[wgt])
            P.dve(lambda e: e.tensor_tensor(out=wgt[:], in0=wgt[:], in1=dt[:], op=ALU.mult), reads=[wgt, dt], writes=[wgt])
            mask = G.triU if d == 0 else G.triL
            P.dve(lambda e: e.tensor_tensor(out=rbig[:], in0=bc_ap(a_[:, 8 * d:8 * d + 8], [[1, 8], [0, 128]]), in1=bc_ap(mask[:], [[0, 8], [1, 128]]), op=ALU.mult),
                  reads=[a_, mask], writes=[rbig])
            cb = [G.nextps(), G.nextps()]
            for hf in range(2):
                P.pe(lambda e, hf=hf: e.matmul(cb[hf][:, :], lhsT=G.ones[:], rhs=rbig[:, 4 * hf:4 * hf + 4, :], start=True, stop=True), reads=[G.ones, rbig], writes=[cb[hf]])
            for g in range(2):
                ps = G.nextps()
                P.pe(lambda e, ps=ps, g=g: e.matmul(ps[:, 0:128], lhsT=Bt[b][:, g, :], rhs=Ct[b][:, g, :], start=True, stop=True), reads=[Bt[b], Ct[b]], writes=[ps])
                P.dve(lambda e, ps=ps, g=g: e.tensor_tensor(out=scm[g][:], in0=ps[:, 0:128], in1=mask[:], op=ALU.mult), reads=[ps, mask], writes=[scm[g]])
            pm_list = []
            for h in range(8):
                j = 8 * d + h
                g = h // 4
                n_ = cnt['n']; cnt['n'] += 1
                ar = arg[n_ % 2]; ex_ = ex[n_ % 2]; pm_ = pmb[n_ % 16]
                cbv = cb[h // 4][:, 128 * (h % 4):128 * (h % 4) + 128]
                P.dve(lambda e, ar=ar, cbv=cbv, j=j: e.tensor_scalar(out=ar[:], in0=cbv, scalar1=negcum[:, j:j + 1], scalar2=0.0, op0=ALU.add, op1=ALU.min),
                      reads=[cb[h // 4], negcum], writes=[ar])
                P.act(lambda e, ar=ar, ex_=ex_: e.activation(out=ex_[:], in_=ar[:], func=AF.Exp), reads=[ar], writes=[ex_])
                P.dve(lambda e, ex_=ex_, pm_=pm_, j=j, g=g: e.scalar_tensor_tensor(out=pm_[:], in0=ex_[:], scalar=dt[:, j:j + 1], in1=scm[g][:], op0=ALU.mult, op1=ALU.mult),
                      reads=[ex_, dt, scm[g]], writes=[pm_])
                pm_list.append(pm_)
            P.dve(lambda e: e.tensor_tensor(out=wx[:], in0=xt[b][:].rearrange("p (h e) -> p h e", h=8), in1=bc_ap(wgt[:, 8 * d:8 * d + 8], [[1, 8], [0, 64]]), op=ALU.mult),
                  reads=[xt[b], wgt], writes=[wx])
            yield
            psd = G.nextps()
            pso = G.nextps()
            for g in range(2):
                P.pe(lambda e, g=g: e.matmul(pso[:, 256 * g:256 * g + 256], lhsT=Ct[b][:, g, :], rhs=Hs[d][:, 4 * g:4 * g + 4, :], start=True, stop=True),
                     reads=[Ct[b], Hs[d]], writes=[pso])
            for h in range(8):
                pm_ = pm_list[h]
                P.pe(lambda e, pm_=pm_, h=h: e.matmul(psd[:, 64 * h:64 * h + 64], lhsT=pm_[:], rhs=xt[b][:, 64 * h:64 * h + 64], start=True, stop=True),
                     reads=[pm_, xt[b]], writes=[psd])
            P.dve(lambda e: e.tensor_tensor(out=tmp[:], in0=pso[:, :].rearrange("p (h e) -> p h e", h=8), in1=bc_ap(acum[:, 8 * d:8 * d + 8], [[1, 8], [0, 64]]), op=ALU.mult),
                  reads=[pso, acum], writes=[tmp])
            if first_pass:
                P.dve(lambda e: e.tensor_tensor(out=yacc[:, q, :], in0=psd[:, :], in1=tmp[:].rearrange("p h e -> p (h e)"), op=ALU.add), reads=[psd, tmp], writes=[yacc])
            else:
                P.dve(lambda e: e.tensor_tensor(out=tmp[:].rearrange("p h e -> p (h e)"), in0=psd[:, :], in1=tmp[:].rearrange("p h e -> p (h e)"), op=ALU.add), reads=[psd, tmp], writes=[tmp])
                P.pool(lambda e: e.tensor_tensor(out=yacc[:, q, :], in0=yacc[:, q, :], in1=tmp[:].rearrange("p h e -> p (h e)"), op=ALU.add), reads=[tmp, yacc], writes=[yacc])
            pst = G.nextps()
            for g in range(2):
                P.pe(lambda e, g=g: e.matmul(pst[:, 256 * g:256 * g + 256], lhsT=Btm[b][:, 128 * g:128 * g + 128], rhs=wx[:, 4 * g:4 * g + 4, :], start=True, stop=True),
                     reads=[Btm[b], wx], writes=[pst])
            P.dve(lambda e: e.tensor_tensor(out=htmp[:], in0=Hs[d][:], in1=bc_ap(etot[:, 8 * d:8 * d + 8], [[1, 8], [0, 64]]), op=ALU.mult), reads=[Hs[d], etot], writes=[htmp])
            P.dve(lambda e: e.tensor_tensor(out=Hs[d][:].rearrange("p h e -> p (h e)"), in0=pst[:, :], in1=htmp[:].rearrange("p h e -> p (h e)"), op=ALU.add),
                  reads=[pst, htmp], writes=[Hs[d]])
            if not first_pass and (with_ctx or q >= 2):
                bb = it % 2
                P.dve(lambda e: e.tensor_tensor(out=tmp[:], in0=xt[b][:].rearrange("p (h e) -> p h e", h=8), in1=bc_ap(D_b[:], [[1, 8], [0, 64]]), op=ALU.mult), reads=[xt[b], D_b], writes=[tmp])
                P.dve(lambda e: e.tensor_tensor(out=yz[:], in0=yacc[:, q, :], in1=tmp[:].rearrange("p h e -> p (h e)"), op=ALU.add), reads=[yacc, tmp], writes=[yz])
                P.act(lambda e: e.activation(out=sz[:], in_=dz[b][:], func=AF.Silu), reads=[dz[b]], writes=[sz])
                P.dve(lambda e: e.tensor_tensor(out=yz[:], in0=yz[:], in1=sz[:], op=ALU.mult), reads=[yz, sz], writes=[yz])
                P.act(lambda e: e.activation(out=junk[:], in_=yz[:], func=AF.Square, accum_out=ssq[:, 0:1]), reads=[yz], writes=[junk, ssq])
                P.act(lambda e: e.activation(out=ssq[:], in_=ssq[:], func=AF.Sqrt, scale=1.0 / 512, bias=G.epsb[:, 0:1]), reads=[ssq, G.epsb], writes=[ssq])
                P.dve(lambda e: e.reciprocal(out=ssq[:], in_=ssq[:]), reads=[ssq], writes=[ssq])
                P.dve(lambda e: e.scalar_tensor_tensor(out=yz[:], in0=yz[:], scalar=ssq[:, 0:1], in1=nw_b[:], op0=ALU.mult, op1=ALU.mult), reads=[yz, ssq, nw_b], writes=[yz])
                ps = G.nextps()
                for c in range(4):
                    P.pe(lambda e, ps=ps, c=c: e.transpose(out=ps[:, 128 * c:128 * c + 128], in_=yz[:, 128 * c:128 * c + 128], identity=G.ident[:]), reads=[yz, G.ident], writes=[ps])
                P.act(lambda e, ps=ps: e.activation(out=ytb[bb][:], in_=ps[:, :].rearrange("p (c t) -> p c t", c=4), func=AF.Copy), reads=[ps], writes=[ytb[bb]])
                P.dma(yT[6:10, :, tok:tok + 128].rearrange("c p t -> p c t"), ytb[bb][:], reads=[ytb[bb]], writes=[yT])

        steps = []
        it = 0
        for q in range(NT128):
            steps.append(chunk_pass(q, 0, it, True)); it += 1
        order = [1, 0] + list(range(NT128 - 1, 1, -1))
        for q in order:
            steps.append(chunk_pass(q, 1, it, False)); it += 1
        return steps


def stage_mlstm_ssd(G, l):
    with contextlib.ExitStack() as es:
        a = mlstm_steps2(G, l, es)
        b = ssd_steps2(G, l, es)
        n = len(a)
        assert len(b) == n
        next(a[0]); next(b[0])
        for i in range(n):
            if i + 1 < n:
                next(a[i + 1]); next(b[i + 1])
            for g in (a[i], b[i]):
                try:
                    next(g)
                except StopIteration:
                    pass
        G.P.flush()


def stage_na(G, l):
    nc, P, I = G.nc, G.P, G.I
    sb = G.sb
    sc = G.scr
    with_ctx = l < DEPTH - 1
    yT = sc['yT']
    if 'rpbpad' not in sc:
        G.scratch('rpbpad', [60, 192])
    rpbpad = sc['rpbpad']
    NEG = -30000.0
    with contextlib.ExitStack() as es:
        qT = sb(es, 'n_qT', [128, 2, S], BF16); kT = sb(es, 'n_kT', [128, 2, S], BF16)
        Vp = sb(es, 'n_Vp', [128, NT128, 4, 66], BF16)
        Vp2 = sb(es, 'n_Vp2', [128, NT128 - 1, 4, 66], BF16)
        BT = sb(es, 'n_BT', [128, 4, 14, 64])
        P.dma(qT[:], sc['nqT'][:].rearrange("c p t -> p c t"), reads=[sc['nqT']], writes=[qT])
        P.dma(kT[:], sc['nkT'][:].rearrange("c p t -> p c t"), reads=[sc['nkT']], writes=[kT])
        if 'na_q' in G.dbg:
            dq = G.scratch('na_q', [128, 2 * S], BF16)
            P.dma(dq[:], qT[:].rearrange('p c t -> p (c t)'), reads=[qT], writes=[dq])
            dk = G.scratch('na_k', [128, 2 * S], BF16)
            P.dma(dk[:], kT[:].rearrange('p c t -> p (c t)'), reads=[kT], writes=[dk])
        P.pool(lambda e: e.memset(Vp[:], 1.0), writes=[Vp])
        P.pool(lambda e: e.memset(Vp2[:], 1.0), writes=[Vp2])
        for q in range(NT128):
            P.dma(Vp[:, q, :, 0:64], sc['nvTM'][q * 128:(q + 1) * 128, :].rearrange("p (h e) -> p h e", h=4), reads=[sc['nvTM']], writes=[Vp])
        for q in range(NT128 - 1):
            P.dma(Vp2[:, q, :, 0:64], sc['nvTM'][q * 128 + 64:(q + 1) * 128 + 64, :].rearrange("p (h e) -> p h e", h=4), reads=[sc['nvTM']], writes=[Vp2])
        with contextlib.ExitStack() as es2:
            pad = sb(es2, 'n_pad', [60, 192])
            P.pool(lambda e: e.memset(pad[:], 0.0), writes=[pad])
            P.dma(pad[:, 80:111], I['na_rpb'][l].rearrange("h a b -> (h a) b"), writes=[pad])
            P.dma(rpbpad[:], pad[:], reads=[pad], writes=[rpbpad])
            L = sb(es2, 'n_L', [64, 4, 15, 64])
            for h in range(4):
                src = bass.AP(tensor=rpbpad.t.tensor, offset=rpbpad.t.offset + h * 15 * 192 + 32, ap=[[1, 64], [192, 15], [1, 64]])
                P.dma(L[:, h, :, :], src, reads=[rpbpad], writes=[L])
            antiI = sb(es2, 'n_antiI', [64, 64])
            P.pool(lambda e: e.affine_select(out=antiI[:], in_=G.ones[0:64, 0:64], pattern=[[1, 64]], compare_op=ALU.is_equal, fill=0.0, base=-63, channel_multiplier=1), reads=[G.ones], writes=[antiI])
            ckf = sb(es2, 'n_ckf', [128, 64]); c0 = sb(es2, 'n_c0', [128, 64]); m1 = sb(es2, 'n_m1', [128, 64]); m01 = sb(es2, 'n_m01', [128, 64]); negb = sb(es2, 'n_negb', [128, 64])
            P.pool(lambda e: e.iota(ckf[:], pattern=[[0, 64]], base=0, channel_multiplier=1, allow_small_or_imprecise_dtypes=True), writes=[ckf])
            P.dve(lambda e: e.tensor_scalar(out=ckf[64:128, :], in0=ckf[64:128, :], scalar1=-64.0, scalar2=None, op0=ALU.add), reads=[ckf], writes=[ckf])
            P.pool(lambda e: e.iota(c0[:], pattern=[[1, 64]], base=0, channel_multiplier=0, allow_small_or_imprecise_dtypes=True), writes=[c0])
            P.dve(lambda e: e.tensor_scalar(out=c0[:], in0=c0[:], scalar1=-8.0, scalar2=0.0, op0=ALU.add, op1=ALU.max), reads=[c0], writes=[c0])
            P.dve(lambda e: e.tensor_scalar(out=c0[:], in0=c0[:], scalar1=48.0, scalar2=None, op0=ALU.min), reads=[c0], writes=[c0])
            P.dve(lambda e: e.tensor_tensor(out=ckf[:], in0=ckf[:], in1=c0[:], op=ALU.subtract), reads=[ckf, c0], writes=[ckf])
            P.dve(lambda e: e.tensor_single_scalar(out=m1[:], in_=ckf[:], scalar=0.0, op=ALU.is_ge), reads=[ckf], writes=[m1])
            P.dve(lambda e: e.tensor_single_scalar(out=m01[:], in_=ckf[:], scalar=15.0, op=ALU.is_le), reads=[ckf], writes=[m01])
            P.dve(lambda e: e.tensor_tensor(out=m01[:], in0=m01[:], in1=m1[:], op=ALU.mult), reads=[m01, m1], writes=[m01])
            P.dve(lambda e: e.tensor_scalar(out=negb[:], in0=m01[:], scalar1=-1.0, scalar2=-NEG, op0=ALU.add, op1=ALU.mult), reads=[m01], writes=[negb])
            for h in range(4):
                for d0 in range(14):
                    ps = G.nextps()
                    P.pe(lambda e, ps=ps, h=h, d0=d0: e.matmul(ps[:, 0:64], lhsT=L[:, h, d0:d0 + 2, :].rearrange("p a b -> p (a b)"), rhs=antiI[:], start=True, stop=True),
                         reads=[L, antiI], writes=[ps])
                    P.dve(lambda e, ps=ps, h=h, d0=d0: e.tensor_tensor(out=BT[:, h, d0, :], in0=ps[:, 0:64], in1=m01[:], op=ALU.mult), reads=[ps, m01], writes=[BT])
                    P.pool(lambda e, h=h, d0=d0: e.tensor_tensor(out=BT[:, h, d0, :], in0=BT[:, h, d0, :], in1=negb[:], op=ALU.add), reads=[BT, negb], writes=[BT])
            P.flush()
        precast_tail_weights(G, l)
        ssb = [sb(es, 'n_ssb%d' % i, [128, 256]) for i in range(2)]
        pex = [sb(es, 'n_pex%d' % i, [128, 6, 64], BF16) for i in range(3)]
        rec = [sb(es, 'n_rec%d' % i, [128, 4]) for i in range(2)]
        O = [sb(es, 'n_O%d' % i, [128, 256]) for i in range(2)]
        ytb = [sb(es, 'n_ytb%d' % i, [128, 2, 128], BF16) for i in range(2)]
        pexc = [sb(es, 'n_pexc%d' % i, [128, 2, 128], BF16) for i in range(2)]
        n = 0
        if with_ctx:
            def ctx_tile(qt):
                nonlocal n
                o_ = O[qt % 2]; r_ = rec[qt % 2]
                for h in range(4):
                    pr, hh = h // 2, h % 2
                    ps = G.nextps()
                    for j in range(2):
                        P.pe(lambda e, ps=ps, j=j, pr=pr, hh=hh: e.matmul(ps[:, 128 * j:128 * j + 128], lhsT=kT[64 * hh:64 * hh + 64, pr, 128 * j:128 * j + 128],
                                                                         rhs=qT[64 * hh:64 * hh + 64, pr, 128 * qt:128 * qt + 128], start=True, stop=True), reads=[kT, qT], writes=[ps])
                    pc = pexc[n % 2]; n += 1
                    P.act(lambda e, ps=ps, pc=pc: e.activation(out=pc[:], in_=ps[:, 0:256].rearrange("p (j q) -> p j q", j=2), func=AF.Exp), reads=[ps], writes=[pc])
                    if 'na_dbg' in G.dbg and qt == 0 and h == 0:
                        dd = G.scratch('na_dbg', [128, 256], BF16)
                        P.dma(dd[:], pc[:].rearrange('p j q -> p (j q)'), reads=[pc], writes=[dd])
                    po = G.nextps()
                    for j in range(2):
                        P.pe(lambda e, po=po, j=j, pc=pc, h=h: e.matmul(po[:, 0:65], lhsT=pc[:, j, :], rhs=Vp[:, j, h, 0:65], start=(j == 0), stop=(j == 1)), reads=[pc, Vp], writes=[po])
                    P.dve(lambda e, po=po, r_=r_, h=h: e.reciprocal(out=r_[:, h:h + 1], in_=po[:, 64:65]), reads=[po], writes=[r_])
                    P.dve(lambda e, po=po, r_=r_, h=h, o_=o_: e.tensor_scalar(out=o_[:, 64 * h:64 * h + 64], in0=po[:, 0:64], scalar1=r_[:, h:h + 1], scalar2=None, op0=ALU.mult),
                          reads=[po, r_], writes=[o_])
                ps = G.nextps()
                yb_ = ytb[qt % 2]
                for c in range(2):
                    P.pe(lambda e, ps=ps, c=c, o_=o_: e.transpose(out=ps[:, 128 * c:128 * c + 128], in_=o_[:, 128 * c:128 * c + 128], identity=G.ident[:]), reads=[o_, G.ident], writes=[ps])
                P.act(lambda e, ps=ps, yb_=yb_: e.activation(out=yb_[:], in_=ps[:, 0:256].rearrange("p (c t) -> p c t", c=2), func=AF.Copy), reads=[ps], writes=[yb_])
                P.dma(yT[4:6, :, 128 * qt:128 * qt + 128].rearrange("c p t -> p c t"), yb_[:], reads=[yb_], writes=[yT])
            for qt in range(2):
                ctx_tile(qt)

        def lat_unit(rp, sub, h, o_, r_):
            nonlocal n
            if True:
                r = 2 * rp + sub
                r0 = min(max(r - 4, 0), 56)
                rows = slice(64 * sub, 64 * sub + 64)
                qtok = CTX + 64 * r
                if True:
                    pr, hh = h // 2, h % 2
                    ps = G.nextps()
                    ktoks = [CTX + 64 * (r0 + 2 * j) for j in range(4)] + [0, 128]
                    for j in range(6):
                        P.pe(lambda e, ps=ps, j=j, pr=pr, hh=hh, kt=ktoks[j]: e.matmul(ps[:, 64 * j:64 * j + 64], lhsT=kT[64 * hh:64 * hh + 64, pr, kt:kt + 128],
                                                                                      rhs=qT[64 * hh:64 * hh + 64, pr, qtok:qtok + 64], start=True, stop=True), reads=[kT, qT], writes=[ps])
                    s_ = ssb[n % 2]; pe_ = pex[n % 3]; n += 1
                    d0 = r0 - r + 7
                    P.dve(lambda e, ps=ps, s_=s_, h=h, d0=d0: e.tensor_tensor(out=s_[:].rearrange("p (j q) -> p j q", j=4), in0=ps[:, 0:256].rearrange("p (j q) -> p j q", j=4),
                                                                              in1=BT[:, h, d0:d0 + 7:2, :], op=ALU.add), reads=[ps, BT], writes=[s_])
                    P.act(lambda e, s_=s_, pe_=pe_: e.activation(out=pe_[:, 0:4, :], in_=s_[:].rearrange("p (j q) -> p j q", j=4), func=AF.Exp), reads=[s_], writes=[pe_])
                    P.act(lambda e, ps=ps, pe_=pe_: e.activation(out=pe_[:, 4:6, :], in_=ps[:, 256:384].rearrange("p (j q) -> p j q", j=2), func=AF.Exp), reads=[ps], writes=[pe_])
                    yield
                    po = G.nextps()
                    ktile = [(kt // 128) for kt in ktoks]
                    koff = [(kt % 128) for kt in ktoks]
                    for j in range(6):
                        vsrc = Vp if koff[j] == 0 else Vp2
                        P.pe(lambda e, po=po, j=j, pe_=pe_, h=h, kt=ktile[j], vsrc=vsrc: e.matmul(po[rows, 0:65], lhsT=pe_[:, j, :], rhs=vsrc[:, kt, h, 0:65], start=(j == 0), stop=(j == 5)),
                             reads=[pe_, vsrc], writes=[po])
                    P.dve(lambda e, po=po, r_=r_, h=h: e.reciprocal(out=r_[rows, h:h + 1], in_=po[rows, 64:65]), reads=[po], writes=[r_])
                    P.dve(lambda e, po=po, r_=r_, h=h, o_=o_: e.tensor_scalar(out=o_[rows, 64 * h:64 * h + 64], in0=po[rows, 0:64], scalar1=r_[rows, h:h + 1], scalar2=None, op0=ALU.mult),
                          reads=[po, r_], writes=[o_])
        def lat_finish(rp, o_):
            ps = G.nextps()
            yb_ = ytb[rp % 2]
            tok = CTX + 128 * rp
            for c in range(2):
                P.pe(lambda e, ps=ps, c=c, o_=o_: e.transpose(out=ps[:, 128 * c:128 * c + 128], in_=o_[:, 128 * c:128 * c + 128], identity=G.ident[:]), reads=[o_, G.ident], writes=[ps])
            P.act(lambda e, ps=ps, yb_=yb_: e.activation(out=yb_[:], in_=ps[:, 0:256].rearrange("p (c t) -> p c t", c=2), func=AF.Copy), reads=[ps], writes=[yb_])
            P.dma(yT[4:6, :, tok:tok + 128].rearrange("c p t -> p c t"), yb_[:], reads=[yb_], writes=[yT])

        def units():
            for rp in range(32):
                yield from lat_pair(rp)
        gen = units()
        pending = []
        import itertools

        def unit_iter():
            for rp in range(32):
                o_ = O[rp % 2]; r_ = rec[rp % 2]
                for sub in range(2):
                    for h in range(4):
                        yield (rp, sub, h, o_, r_)
        prev = None
        for (rp, sub, h, o_, r_) in unit_iter():
            g = lat_unit(rp, sub, h, o_, r_)
            next(g)
            if prev is not None:
                pg, prp, psub, ph, po_ = prev
                for _ in pg:
                    pass
                if psub == 1 and ph == 3:
                    lat_finish(prp, po_)
            prev = (g, rp, sub, h, o_)
        pg, prp, psub, ph, po_ = prev
        for _ in pg:
            pass
        lat_finish(prp, po_)
        P.flush()


def emit_sin(G, dst, ang, shift, nn, ni, fix):
    P = G.P
    P.dve(lambda e: e.tensor_scalar(out=nn[:], in0=ang[:], scalar1=shift, scalar2=1.0 / (2 * math.pi), op0=ALU.add, op1=ALU.mult), reads=[ang], writes=[nn])
    P.dve(lambda e: e.tensor_copy(out=ni[:], in_=nn[:]), reads=[nn], writes=[ni])
    P.dve(lambda e: e.tensor_copy(out=nn[:], in_=ni[:]), reads=[ni], writes=[nn])
    P.dve(lambda e: e.scalar_tensor_tensor(out=nn[:], in0=nn[:], scalar=-2 * math.pi, in1=ang[:], op0=ALU.mult, op1=ALU.add), reads=[nn, ang], writes=[nn])
    P.dve(lambda e: e.tensor_scalar(out=nn[:], in0=nn[:], scalar1=shift, scalar2=None, op0=ALU.add), reads=[nn], writes=[nn])
    P.dve(lambda e: e.tensor_scalar(out=fix[:], in0=nn[:], scalar1=math.pi, scalar2=-2 * math.pi, op0=ALU.is_gt, op1=ALU.mult), reads=[nn], writes=[fix])
    P.dve(lambda e: e.tensor_tensor(out=nn[:], in0=nn[:], in1=fix[:], op=ALU.add), reads=[nn, fix], writes=[nn])
    P.dve(lambda e: e.tensor_scalar(out=fix[:], in0=nn[:], scalar1=-math.pi, scalar2=2 * math.pi, op0=ALU.is_lt, op1=ALU.mult), reads=[nn], writes=[fix])
    P.dve(lambda e: e.tensor_tensor(out=nn[:], in0=nn[:], in1=fix[:], op=ALU.add), reads=[nn, fix], writes=[nn])
    P.dve(lambda e: e.tensor_scalar(out=nn[:], in0=nn[:], scalar1=3.1415925, scalar2=-3.1415925, op0=ALU.min, op1=ALU.max), reads=[nn], writes=[nn])
    P.act(lambda e: e.activation(out=dst[:], in_=nn[:], func=AF.Sin), reads=[nn], writes=[dst])


S5_BT = [(0, 32, 1)] + [(32 + 128 * i, 128, 35 + 128 * i) for i in range(4)]
ZW = 548


def s5_colF(k):
    return 3 + k


def s5_colB(k):
    return (k - 31) if k >= 32 else (513 + k)


def stage_s5(G, l):
    nc, P, I = G.nc, G.P, G.I
    sb = G.sb
    sc = G.scr
    with_ctx = l < DEPTH - 1
    yT = sc['yT']
    I32 = mybir.dt.int32
    with contextlib.ExitStack() as es:
        Toe = sb(es, 's_Toe', [128, 16, 128])
        PCrD = [sb(es, 's_PCrD%d' % d, [128, 16, 128]) for d in range(2)]; PCiND = [sb(es, 's_PCiND%d' % d, [128, 16, 128]) for d in range(2)]
        for t_ in PCrD + PCiND:
            P.pool(lambda e, t_=t_: e.memset(t_[:], 0.0), writes=[t_])
        AA = sb(es, 's_AA', [128, 16, 2]); AXm = sb(es, 's_AX', [128, 16, 2])
        A32A = sb(es, 's_A32A', [128, 16, 2]); A32X = sb(es, 's_A32X', [128, 16, 2])
        Pwr = sb(es, 's_Pwr', [128, 16, 32]); Pwi = sb(es, 's_Pwi', [128, 16, 32])
        PBT = sb(es, 's_PBT', [128, 2, 16, 128])
        with contextlib.ExitStack() as es2:
            with contextlib.ExitStack() as es3:
                lamr = sb(es3, 's_lamr', [128, 16]); lami = sb(es3, 's_lami', [128, 16]); dtl = sb(es3, 's_dt', [128, 16])
                Bre = sb(es3, 's_Bre', [128, 16, 16]); Bim = sb(es3, 's_Bim', [128, 16, 16])
                Cre = sb(es3, 's_Cre', [128, 16, 16]); Cim = sb(es3, 's_Cim', [128, 16, 16])
                for d in range(2):
                    hs = slice(64 * d, 64 * d + 64)
                    P.dma(lamr[hs, :], I['s5_lam_re'][l, d].rearrange("g p -> p g"), writes=[lamr], allow_slow_non_contiguous=True)
                    P.dma(lami[hs, :], I['s5_lam_im'][l, d].rearrange("g p -> p g"), writes=[lami], allow_slow_non_contiguous=True)
                    P.dma(dtl[hs, :], I['s5_log_dt'][l, d:d + 1, :].partition_broadcast(64), writes=[dtl])
                    P.dma(Bre[hs], I['s5_b_re'][l].rearrange("g p c -> p g c"), writes=[Bre])
                    P.dma(Bim[hs], I['s5_b_im'][l].rearrange("g p c -> p g c"), writes=[Bim])
                    for g in range(16):
                        P.dma(Cre[hs, g, :], I['s5_c_re'][l, g].rearrange("c p -> p c"), writes=[Cre], allow_slow_non_contiguous=True)
                        P.dma(Cim[hs, g, :], I['s5_c_im'][l, g].rearrange("c p -> p c"), writes=[Cim], allow_slow_non_contiguous=True)
                P.act(lambda e: e.activation(out=dtl[:], in_=dtl[:], func=AF.Exp), reads=[dtl], writes=[dtl])
                lrd = sb(es3, 's_lrd', [128, 16]); lid = sb(es3, 's_lid', [128, 16])
                P.dve(lambda e: e.tensor_tensor(out=lrd[:], in0=lamr[:], in1=dtl[:], op=ALU.mult), reads=[lamr, dtl], writes=[lrd])
                P.dve(lambda e: e.tensor_tensor(out=lid[:], in0=lami[:], in1=dtl[:], op=ALU.mult), reads=[lami, dtl], writes=[lid])
                NJ = 24
                jv = sb(es3, 's_jv', [128, NJ])
                P.pool(lambda e: e.iota(jv[:, 0:16], pattern=[[1, 16]], base=0, channel_multiplier=0, allow_small_or_imprecise_dtypes=True), writes=[jv])
                P.pool(lambda e: e.iota(jv[:, 16:24], pattern=[[-1, 8]], base=0, channel_multiplier=0, allow_small_or_imprecise_dtypes=True), writes=[jv])
                mag = sb(es3, 's_mag', [128, 16, NJ]); ang = sb(es3, 's_ang', [128, 16, NJ])
                P.dve(lambda e: e.tensor_tensor(out=mag[:], in0=bc_ap(lrd[:], [[1, 16], [0, NJ]]), in1=bc_ap(jv[:], [[0, 16], [1, NJ]]), op=ALU.mult), reads=[lrd, jv], writes=[mag])
                P.dve(lambda e: e.tensor_tensor(out=ang[:], in0=bc_ap(lid[:], [[1, 16], [0, NJ]]), in1=bc_ap(jv[:], [[0, 16], [1, NJ]]), op=ALU.mult), reads=[lid, jv], writes=[ang])
                P.act(lambda e: e.activation(out=mag[:], in_=mag[:], func=AF.Exp), reads=[mag], writes=[mag])
                nn = sb(es3, 's_nn', [128, 16, NJ]); ni = sb(es3, 's_ni', [128, 16, NJ], I32); fix = sb(es3, 's_fix', [128, 16, NJ])
                Pr = sb(es3, 's_Pr', [128, 16, NJ]); Pi = sb(es3, 's_Pi', [128, 16, NJ])
                emit_sin(G, Pi, ang, 0.0, nn, ni, fix)
                emit_sin(G, Pr, ang, math.pi / 2, nn, ni, fix)
                P.dve(lambda e: e.tensor_tensor(out=Pr[:], in0=Pr[:], in1=mag[:], op=ALU.mult), reads=[Pr, mag], writes=[Pr])
                P.dve(lambda e: e.tensor_tensor(out=Pi[:], in0=Pi[:], in1=mag[:], op=ALU.mult), reads=[Pi, mag], writes=[Pi])
                t1 = sb(es3, 's_t1', [128, 16]); t2 = sb(es3, 's_t2', [128, 16]); nr = sb(es3, 's_nr', [128, 16]); rden = sb(es3, 's_rden', [128, 16])
                cr = sb(es3, 's_cr', [128, 16]); ci = sb(es3, 's_ci', [128, 16])
                P.dve(lambda e: e.tensor_tensor(out=t1[:], in0=lamr[:], in1=lamr[:], op=ALU.mult), reads=[lamr], writes=[t1])
                P.dve(lambda e: e.tensor_tensor(out=t2[:], in0=lami[:], in1=lami[:], op=ALU.mult), reads=[lami], writes=[t2])
                P.dve(lambda e: e.tensor_tensor(out=t1[:], in0=t1[:], in1=t2[:], op=ALU.add), reads=[t1, t2], writes=[t1])
                P.dve(lambda e: e.reciprocal(out=rden[:], in_=t1[:]), reads=[t1], writes=[rden])
                P.dve(lambda e: e.tensor_scalar(out=nr[:], in0=Pr[:, :, 1], scalar1=-1.0, scalar2=None, op0=ALU.add), reads=[Pr], writes=[nr])
                P.dve(lambda e: e.tensor_tensor(out=t1[:], in0=nr[:], in1=lamr[:], op=ALU.mult), reads=[nr, lamr], writes=[t1])
                P.dve(lambda e: e.tensor_tensor(out=t2[:], in0=Pi[:, :, 1], in1=lami[:], op=ALU.mult), reads=[Pi, lami], writes=[t2])
                P.dve(lambda e: e.tensor_tensor(out=t1[:], in0=t1[:], in1=t2[:], op=ALU.add), reads=[t1, t2], writes=[t1])
                P.dve(lambda e: e.tensor_tensor(out=cr[:], in0=t1[:], in1=rden[:], op=ALU.mult), reads=[t1, rden], writes=[cr])
                P.dve(lambda e: e.tensor_tensor(out=t1[:], in0=Pi[:, :, 1], in1=lamr[:], op=ALU.mult), reads=[Pi, lamr], writes=[t1])
                P.dve(lambda e: e.tensor_tensor(out=t2[:], in0=nr[:], in1=lami[:], op=ALU.mult), reads=[nr, lami], writes=[t2])
                P.dve(lambda e: e.tensor_tensor(out=t1[:], in0=t1[:], in1=t2[:], op=ALU.subtract), reads=[t1, t2], writes=[t1])
                P.dve(lambda e: e.tensor_tensor(out=ci[:], in0=t1[:], in1=rden[:], op=ALU.mult), reads=[t1, rden], writes=[ci])
                Bbr = sb(es3, 's_Bbr', [128, 16, 16]); Bbi = sb(es3, 's_Bbi', [128, 16, 16]); tb = sb(es3, 's_tb', [128, 16, 16])
                crb = bc_ap(cr[:], [[1, 16], [0, 16]]); cib = bc_ap(ci[:], [[1, 16], [0, 16]])
                P.dve(lambda e: e.tensor_tensor(out=Bbr[:], in0=Bre[:], in1=crb, op=ALU.mult), reads=[Bre, cr], writes=[Bbr])
                P.dve(lambda e: e.tensor_tensor(out=tb[:], in0=Bim[:], in1=cib, op=ALU.mult), reads=[Bim, ci], writes=[tb])
                P.dve(lambda e: e.tensor_tensor(out=Bbr[:], in0=Bbr[:], in1=tb[:], op=ALU.subtract), reads=[Bbr, tb], writes=[Bbr])
                P.dve(lambda e: e.tensor_tensor(out=Bbi[:], in0=Bim[:], in1=crb, op=ALU.mult), reads=[Bim, cr], writes=[Bbi])
                P.dve(lambda e: e.tensor_tensor(out=tb[:], in0=Bre[:], in1=cib, op=ALU.mult), reads=[Bre, ci], writes=[tb])
                P.dve(lambda e: e.tensor_tensor(out=Bbi[:], in0=Bbi[:], in1=tb[:], op=ALU.add), reads=[Bbi, tb], writes=[Bbi])
                P.dve(lambda e: e.tensor_copy(out=AA[:, :, 0], in_=Pr[:, :, 8]), reads=[Pr], writes=[AA])
                P.dve(lambda e: e.tensor_copy(out=AA[:, :, 1], in_=Pr[:, :, 8]), reads=[Pr], writes=[AA])
                P.dve(lambda e: e.tensor_scalar(out=AXm[:, :, 0], in0=Pi[:, :, 8], scalar1=-1.0, scalar2=None, op0=ALU.mult), reads=[Pi], writes=[AXm])
                P.dve(lambda e: e.tensor_copy(out=AXm[:, :, 1], in_=Pi[:, :, 8]), reads=[Pi], writes=[AXm])
                jv2 = sb(es3, 's_jv2', [128, 32]); mag2 = sb(es3, 's_mag2', [128, 16, 32]); ang2 = sb(es3, 's_ang2', [128, 16, 32])
                nn2 = sb(es3, 's_nn2', [128, 16, 32]); ni2 = sb(es3, 's_ni2', [128, 16, 32], I32); fix2 = sb(es3, 's_fix2', [128, 16, 32])
                P.pool(lambda e: e.iota(jv2[:], pattern=[[8, 32]], base=8, channel_multiplier=0, allow_small_or_imprecise_dtypes=True), writes=[jv2])
                P.dve(lambda e: e.tensor_tensor(out=mag2[:], in0=bc_ap(lrd[:], [[1, 16], [0, 32]]), in1=bc_ap(jv2[:], [[0, 16], [1, 32]]), op=ALU.mult), reads=[lrd, jv2], writes=[mag2])
                P.dve(lambda e: e.tensor_tensor(out=ang2[:], in0=bc_ap(lid[:], [[1, 16], [0, 32]]), in1=bc_ap(jv2[:], [[0, 16], [1, 32]]), op=ALU.mult), reads=[lid, jv2], writes=[ang2])
                P.act(lambda e: e.activation(out=mag2[:], in_=mag2[:], func=AF.Exp), reads=[mag2], writes=[mag2])
                emit_sin(G, Pwi, ang2, 0.0, nn2, ni2, fix2)
                emit_sin(G, Pwr, ang2, math.pi / 2, nn2, ni2, fix2)
                P.dve(lambda e: e.tensor_tensor(out=Pwr[:], in0=Pwr[:], in1=mag2[:], op=ALU.mult), reads=[Pwr, mag2], writes=[Pwr])
                P.dve(lambda e: e.tensor_tensor(out=Pwi[:], in0=Pwi[:], in1=mag2[:], op=ALU.mult), reads=[Pwi, mag2], writes=[Pwi])
                P.dve(lambda e: e.tensor_copy(out=A32A[:, :, 0], in_=Pwr[:, :, 31]), reads=[Pwr], writes=[A32A])
                P.dve(lambda e: e.tensor_copy(out=A32A[:, :, 1], in_=Pwr[:, :, 31]), reads=[Pwr], writes=[A32A])
                P.dve(lambda e: e.tensor_scalar(out=A32X[:, :, 0], in0=Pwi[:, :, 31], scalar1=-1.0, scalar2=None, op0=ALU.mult), reads=[Pwi], writes=[A32X])
                P.dve(lambda e: e.tensor_copy(out=A32X[:, :, 1], in_=Pwi[:, :, 31]), reads=[Pwi], writes=[A32X])
                PBr = sb(es3, 's_PBr', [128, 16, 8, 16]); PBi = sb(es3, 's_PBi', [128, 16, 8, 16])
                PCr0 = sb(es3, 's_PCr0', [128, 16, 8, 16]); PCi0N = sb(es3, 's_PCi0N', [128, 16, 8, 16])
                tm1 = sb(es3, 's_tm1', [128, 16, 8, 16])

                def powslice(T_, d, start, step):
                    a_ = T_[64 * d:64 * d + 64, :, :]
                    return bass.AP(tensor=a_.tensor, offset=a_.offset + start, ap=[list(a_.ap[0]), [NJ, 16], [step, 8], [0, 16]])

                def vec16(T_, d):
                    a_ = T_[64 * d:64 * d + 64, :, :]
                    return bass.AP(tensor=a_.tensor, offset=a_.offset, ap=[list(a_.ap[0]), [16, 16], [0, 8], [1, 16]])

                def cmul(outr, outi, d, pstart, pstep, Vr, Vi, neg_im):
                    hs = slice(64 * d, 64 * d + 64)
                    pr_, pi_ = powslice(Pr, d, pstart, pstep), powslice(Pi, d, pstart, pstep)
                    vr_, vi_ = vec16(Vr, d), vec16(Vi, d)
                    P.dve(lambda e: e.tensor_tensor(out=outr[hs], in0=pr_, in1=vr_, op=ALU.mult), reads=[Pr, Vr], writes=[outr])
                    P.dve(lambda e: e.tensor_tensor(out=tm1[hs], in0=pi_, in1=vi_, op=ALU.mult), reads=[Pi, Vi], writes=[tm1])
                    P.dve(lambda e: e.tensor_tensor(out=outr[hs], in0=outr[hs], in1=tm1[hs], op=ALU.subtract), reads=[outr, tm1], writes=[outr])
                    P.dve(lambda e: e.tensor_tensor(out=outi[hs], in0=pr_, in1=vi_, op=ALU.mult), reads=[Pr, Vi], writes=[outi])
                    P.dve(lambda e: e.tensor_tensor(out=tm1[hs], in0=pi_, in1=vr_, op=ALU.mult), reads=[Pi, Vr], writes=[tm1])
                    if neg_im:
                        P.dve(lambda e: e.scalar_tensor_tensor(out=outi[hs], in0=outi[hs], scalar=-1.0, in1=tm1[hs], op0=ALU.mult, op1=ALU.subtract), reads=[outi, tm1], writes=[outi])
                    else:
                        P.dve(lambda e: e.tensor_tensor(out=outi[hs], in0=outi[hs], in1=tm1[hs], op=ALU.add), reads=[outi, tm1], writes=[outi])

                class V4:
                    def __init__(s_, ap, r):
                        s_.ap_ = ap; s_.r = r

                    def __getitem__(s_, k):
                        return s_.ap_[k]
                PCr_v = [V4(PCrD[d][:].rearrange("p g (t c) -> p g t c", t=8), PCrD[d].r) for d in range(2)]
                PCiN_v = [V4(PCiND[d][:].rearrange("p g (t c) -> p g t c", t=8), PCiND[d].r) for d in range(2)]
                cmul(PBr, PBi, 0, 16, 1, Bbr, Bbi, False)
                cmul(PBr, PBi, 1, 0, 1, Bbr, Bbi, False)
                cmul(PCr0, PCi0N, 0, 0, 1, Cre, Cim, True)
                cmul(PCr0, PCi0N, 1, 16, 1, Cre, Cim, True)
                cmul(PCr_v[0], PCiN_v[0], 0, 8, 1, Cre, Cim, True)
                cmul(PCr_v[1], PCiN_v[1], 1, 8, -1, Cre, Cim, True)
                ia = sb(es3, 's_ia', [128, 128], I32); ib = sb(es3, 's_ib', [128, 128], I32); fa = sb(es3, 's_fa', [128, 128]); fb = sb(es3, 's_fb', [128, 128])
                mF = sb(es3, 's_mF', [128, 128]); mB = sb(es3, 's_mB', [128, 128])
                P.pool(lambda e: e.iota(ia[:], pattern=[[0, 128]], base=0, channel_multiplier=1), writes=[ia])
                P.pool(lambda e: e.iota(ib[:], pattern=[[1, 128]], base=0, channel_multiplier=0), writes=[ib])
                P.dve(lambda e: e.tensor_single_scalar(out=ia[:], in_=ia[:], scalar=4, op=ALU.arith_shift_right), reads=[ia], writes=[ia])
                P.dve(lambda e: e.tensor_single_scalar(out=ib[:], in_=ib[:], scalar=4, op=ALU.arith_shift_right), reads=[ib], writes=[ib])
                P.dve(lambda e: e.tensor_copy(out=fa[:], in_=ia[:]), reads=[ia], writes=[fa])
                P.dve(lambda e: e.tensor_copy(out=fb[:], in_=ib[:]), reads=[ib], writes=[fb])
                P.dve(lambda e: e.tensor_tensor(out=mF[:], in0=fa[:], in1=fb[:], op=ALU.is_le), reads=[fa, fb], writes=[mF])
                P.dve(lambda e: e.tensor_tensor(out=mB[:], in0=fa[:], in1=fb[:], op=ALU.is_ge), reads=[fa, fb], writes=[mB])
                tt_ = [sb(es3, 's_tt%d' % i, [128, 128]) for i in range(2)]
                for g in range(16):
                    pss = []
                    for d in range(2):
                        hs = slice(64 * d, 64 * d + 64)
                        ps = G.nextps()
                        P.pe(lambda e, ps=ps, hs=hs, g=g: e.matmul(ps[:, 0:128], lhsT=PBr[hs, g].rearrange("p s c -> p (s c)"), rhs=PCr0[hs, g].rearrange("p s c -> p (s c)"), start=True, stop=False),
                             reads=[PBr, PCr0], writes=[ps])
                        P.pe(lambda e, ps=ps, hs=hs, g=g: e.matmul(ps[:, 0:128], lhsT=PBi[hs, g].rearrange("p s c -> p (s c)"), rhs=PCi0N[hs, g].rearrange("p s c -> p (s c)"), start=False, stop=True),
                             reads=[PBi, PCi0N], writes=[ps])
                        pss.append(ps)
                    P.dve(lambda e, g=g, ps=pss[0]: e.tensor_tensor(out=tt_[0][:], in0=ps[:, 0:128], in1=mF[:], op=ALU.mult), reads=[pss[0], mF], writes=[tt_[0]])
                    P.dve(lambda e, g=g, ps=pss[1]: e.tensor_tensor(out=tt_[1][:], in0=ps[:, 0:128], in1=mB[:], op=ALU.mult), reads=[pss[1], mB], writes=[tt_[1]])
                    P.pool(lambda e, g=g: e.tensor_tensor(out=Toe[:, g, :], in0=tt_[0][:], in1=tt_[1][:], op=ALU.add), reads=[tt_[0], tt_[1]], writes=[Toe])
                for ri, src in enumerate((PBr, PBi)):
                    for g4 in range(4):
                        ps = G.nextps()
                        for gg in range(4):
                            g = 4 * g4 + gg
                            P.pe(lambda e, ps=ps, gg=gg, g=g, src=src: e.transpose(out=ps[:, 128 * gg:128 * gg + 128], in_=src[:, g].rearrange("p s c -> p (s c)"), identity=G.ident[:]),
                                 reads=[src, G.ident], writes=[ps])
                        P.act(lambda e, ps=ps, ri=ri, g4=g4: e.activation(out=PBT[:, ri, 4 * g4:4 * g4 + 4, :], in_=ps[:, :].rearrange("p (g q) -> p g q", g=4), func=AF.Copy),
                              reads=[ps], writes=[PBT])
                P.flush()
            if 's5_p0' in G.dbg:
                return
            Z = sb(es, 's_Z', [128, 16, 2, ZW])
            X = sb(es, 's_X', [128, 16, 544])
            P.pool(lambda e: e.memset(Z[:], 0.0), writes=[Z])
            with contextlib.ExitStack() as es3:
                U = [sb(es3, 's_U%d' % i, [128, 8, 256]) for i in range(2)]
                Uc = [sb(es3, 's_Uc%d' % i, [128, 16, 128]) for i in range(2)]
                for ti, (k0, nb, zc) in enumerate(S5_BT):
                    u = U[ti % 2]; uc = Uc[ti % 2]
                    src = bass.AP(tensor=sc['TM1'].t.tensor, offset=sc['TM1'].t.offset + k0 * 8 * 784 + 528, ap=[[8 * 784, nb], [784, 8], [1, 256]])
                    P.dma(u[0:nb], src, reads=[sc['TM1']], writes=[u])
                    P.pool(lambda e, u=u, uc=uc, nb=nb: e.tensor_copy(out=uc[0:nb].rearrange("p g (s c) -> p g s c", s=8), in_=u[0:nb].rearrange("p s (g c) -> p g s c", g=16)),
                           reads=[u], writes=[uc])
                    for g4 in range(4):
                        ps = G.nextps()
                        for gg in range(4):
                            P.pe(lambda e, ps=ps, gg=gg, g=4 * g4 + gg, uc=uc, nb=nb: e.transpose(out=ps[:, 128 * gg:128 * gg + nb], in_=uc[0:nb, g, :], identity=G.ident[0:nb, 0:nb]),
                                 reads=[uc, G.ident], writes=[ps])
                        xdst = X[:, 4 * g4:4 * g4 + 4, k0:k0 + nb]
                        P.act(lambda e, ps=ps, xdst=xdst, nb=nb: e.activation(out=xdst, in_=ps[:, :].rearrange("p (g q) -> p g q", g=4)[:, :, 0:nb], func=AF.Copy), reads=[ps], writes=[X])
                for g in range(16):
                    for ri in range(2):
                        for (k0, nb) in ((0, 32), (32, 256), (288, 256)):
                            ps = G.nextps()
                            P.pe(lambda e, ps=ps, g=g, ri=ri, k0=k0, nb=nb: e.matmul(ps[:, 0:nb], lhsT=PBT[:, ri, g, :], rhs=X[:, g, k0:k0 + nb], start=True, stop=True), reads=[PBT, X], writes=[ps])
                            zf, zb = s5_colF(k0), s5_colB(k0)
                            P.act(lambda e, ps=ps, g=g, ri=ri, zf=zf, nb=nb: e.activation(out=Z[0:64, g, ri, zf:zf + nb], in_=ps[0:64, 0:nb], func=AF.Copy), reads=[ps], writes=[Z])
                            P.dve(lambda e, ps=ps, g=g, ri=ri, zb=zb, nb=nb: e.tensor_copy(out=Z[64:128, g, ri, zb:zb + nb], in_=ps[64:128, 0:nb]), reads=[ps], writes=[Z])
                P.flush()
        if 's5_p2' in G.dbg:
            return
        with contextlib.ExitStack() as es2:
            m1 = [sb(es2, 's_m1%d' % d, [128, 16, 2, 17]) for d in range(2)]
            m2 = [sb(es2, 's_m2%d' % d, [128, 16, 2, 17]) for d in range(2)]
            Sb = sb(es2, 's_Sb', [128, 16, 2, 17])
            t1 = [sb(es2, 's_c1%d' % d, [128, 2, 16, 32]) for d in range(2)]
            t2 = [sb(es2, 's_c2%d' % d, [128, 2, 16, 32]) for d in range(2)]
            Zres = [Res('Zf'), Res('Zb')]

            def zview(d, col, swap=False, nseg=17):
                zp = Z[64 * d:64 * d + 64, :, :, col]
                if swap:
                    return bass.AP(tensor=zp.tensor, offset=zp.offset + ZW, ap=[list(zp.ap[0]), [2 * ZW, 16], [-ZW, 2], [32, nseg]])
                return bass.AP(tensor=zp.tensor, offset=zp.offset, ap=[list(zp.ap[0]), [2 * ZW, 16], [ZW, 2], [32, nseg]])

            def tbc(T_, d, n):
                a_ = T_[64 * d:64 * d + 64]
                return bass.AP(tensor=a_.tensor, offset=a_.offset, ap=[list(a_.ap[0]), [2, 16], [1, 2], [0, n]])

            def sbv(d, idx, swap=False):
                a_ = Sb[64 * d:64 * d + 64, :, :, idx]
                if swap:
                    return bass.AP(tensor=a_.tensor, offset=a_.offset + 17, ap=[list(a_.ap[0]), [34, 16], [-17, 2]])
                return a_
            def direction(d):
                hs = slice(64 * d, 64 * d + 64)
                emit = P.dve if d == 0 else P.pool
                base = 3 if d == 0 else 1
                js = range(1, 32) if d == 0 else range(30, -1, -1)
                for j in js:
                    cur = base + j
                    prev = cur - 1 if d == 0 else cur + 1
                    emit(lambda e, d=d, prev=prev: e.tensor_tensor(out=m1[d][hs], in0=zview(d, prev), in1=tbc(AA, d, 17), op=ALU.mult), reads=[Zres[d], AA], writes=[m1[d]])
                    emit(lambda e, d=d, prev=prev: e.tensor_tensor(out=m2[d][hs], in0=zview(d, prev, True), in1=tbc(AXm, d, 17), op=ALU.mult), reads=[Zres[d], AXm], writes=[m2[d]])
                    emit(lambda e, d=d: e.tensor_tensor(out=m1[d][hs], in0=m1[d][hs], in1=m2[d][hs], op=ALU.add), reads=[m1[d], m2[d]], writes=[m1[d]])
                    emit(lambda e, d=d, cur=cur: e.tensor_tensor(out=zview(d, cur), in0=zview(d, cur), in1=m1[d][hs], op=ALU.add), reads=[m1[d], Zres[d]], writes=[Zres[d]])
                sres = Res('Sb%d' % d)
                if d == 0:
                    order = list(range(0, 16)); endcol = lambda sg: 3 + 32 * sg + 31
                else:
                    order = list(range(16, 0, -1)); endcol = lambda sg: 1 + 32 * sg
                for n_, sg in enumerate(order):
                    if n_ == 0:
                        emit(lambda e, d=d, sg=sg: e.tensor_copy(out=Sb[hs, :, :, sg], in_=Z[hs, :, :, endcol(sg)]), reads=[Zres[d]], writes=[sres])
                    else:
                        pv = order[n_ - 1]
                        emit(lambda e, d=d, pv=pv: e.tensor_tensor(out=m1[d][hs, :, :, 0], in0=sbv(d, pv), in1=A32A[hs], op=ALU.mult), reads=[sres, A32A], writes=[m1[d]])
                        emit(lambda e, d=d, pv=pv: e.tensor_tensor(out=m2[d][hs, :, :, 0], in0=sbv(d, pv, True), in1=A32X[hs], op=ALU.mult), reads=[sres, A32X], writes=[m2[d]])
                        emit(lambda e, d=d: e.tensor_tensor(out=m1[d][hs, :, :, 0], in0=m1[d][hs, :, :, 0], in1=m2[d][hs, :, :, 0], op=ALU.add), reads=[m1[d], m2[d]], writes=[m1[d]])
                        emit(lambda e, d=d, sg=sg: e.tensor_tensor(out=Sb[hs, :, :, sg], in0=Z[hs, :, :, endcol(sg)], in1=m1[d][hs, :, :, 0], op=ALU.add), reads=[m1[d], Zres[d]], writes=[sres])
                def corr(gq):
                    gs = slice(2 * gq, 2 * gq + 2)

                    def pwv(T_, rev):
                        a_ = T_[hs, gs, :]
                        if rev:
                            return bass.AP(tensor=a_.tensor, offset=a_.offset + 31, ap=[list(a_.ap[0]), [32, 2], [0, 16], [-1, 32]])
                        return bass.AP(tensor=a_.tensor, offset=a_.offset, ap=[list(a_.ap[0]), [32, 2], [0, 16], [1, 32]])

                    def sbb(ri, start):
                        a_ = Sb[hs, gs, ri, start:start + 16]
                        return bass.AP(tensor=a_.tensor, offset=a_.offset, ap=[list(a_.ap[0]), [34, 2], [1, 16], [0, 32]])

                    def zt(ri, col0):
                        a_ = Z[hs, gs, ri, col0]
                        return bass.AP(tensor=a_.tensor, offset=a_.offset, ap=[list(a_.ap[0]), [2 * ZW, 2], [32, 16], [1, 32]])
                    rev = (d == 1)
                    s0 = 0 if d == 0 else 1
                    col0 = 35 if d == 0 else 1
                    a1, a2 = t1[d][hs], t2[d][hs]
                    for (ri, x_, y_, op_) in ((0, 0, 1, ALU.subtract), (1, 1, 0, ALU.add)):
                        emit(lambda e, x_=x_: e.tensor_tensor(out=a1, in0=pwv(Pwr, rev), in1=sbb(x_, s0), op=ALU.mult), reads=[Pwr, sres], writes=[t1[d]])
                        emit(lambda e, y_=y_: e.tensor_tensor(out=a2, in0=pwv(Pwi, rev), in1=sbb(y_, s0), op=ALU.mult), reads=[Pwi, sres], writes=[t2[d]])
                        emit(lambda e, op_=op_: e.tensor_tensor(out=a1, in0=a1, in1=a2, op=op_), reads=[t1[d], t2[d]], writes=[t1[d]])
                        emit(lambda e, ri=ri: e.tensor_tensor(out=zt(ri, col0), in0=zt(ri, col0), in1=a1, op=ALU.add), reads=[t1[d], Zres[d]], writes=[Zres[d]])
                for gq in range(8):
                    corr(gq)
            for d in range(2):
                direction(d)
            P.flush()
        if 's5_p3' in G.dbg:
            return
        with contextlib.ExitStack() as es2:
            U = [sb(es2, 's_U2%d' % i, [128, 8, 256]) for i in range(1)]
            Y = [sb(es2, 's_Y%d' % i, [128, 8, 256]) for i in range(1)]
            t3 = sb(es2, 's_t3', [128, 8, 256])
            gT = [sb(es2, 's_gT%d' % i, [128, 2, 1024], BF16) for i in range(1)]
            Dsk = load_bcast(G, es2, 's_D', I['s5_d'][l:l + 1, :], 256)
            wg = sb(es2, 's_wg', [128, 2, 512], BF16)
            P.dmaq('pool', wg[:], I['s5_glu_w'][l].rearrange("(k p) n -> p k n", p=128), writes=[wg])
            sg = [sb(es2, 's_sg%d' % i, [128, 512]) for i in range(2)]
            yb = [sb(es2, 's_yb%d' % i, [128, 512], BF16) for i in range(2)]
            cnt = {'n': 0}

            def out_tile(ti, k0, nb, zc):
                u = U[0]; y = Y[0]; g_ = gT[0]
                src = bass.AP(tensor=sc['TM1'].t.tensor, offset=sc['TM1'].t.offset + k0 * 8 * 784 + 528, ap=[[8 * 784, nb], [784, 8], [1, 256]])
                P.dma(u[0:nb], src, reads=[sc['TM1']], writes=[u])
                for g4 in range(4):
                    ps = G.nextps()
                    for gg in range(4):
                        g = 4 * g4 + gg
                        o = ps[0:nb, 128 * gg:128 * gg + 128]
                        P.pe(lambda e, o=o, g=g: e.matmul(o, lhsT=X[:, g, k0:k0 + nb], rhs=Toe[:, g, :], start=True, stop=False), reads=[X, Toe], writes=[ps])
                        zf = s5_colF(k0) - 1
                        zb = s5_colB(k0) + 1
                        P.pe(lambda e, o=o, g=g, zf=zf: e.matmul(o, lhsT=Z[:, g, 0, zf:zf + nb], rhs=PCrD[0][:, g, :], start=False, stop=False), reads=[Z, PCrD[0]], writes=[ps])
                        P.pe(lambda e, o=o, g=g, zf=zf: e.matmul(o, lhsT=Z[:, g, 1, zf:zf + nb], rhs=PCiND[0][:, g, :], start=False, stop=False), reads=[Z, PCiND[0]], writes=[ps])
                        P.pe(lambda e, o=o, g=g, zb=zb: e.matmul(o, lhsT=Z[:, g, 0, zb:zb + nb], rhs=PCrD[1][:, g, :], start=False, stop=False), reads=[Z, PCrD[1]], writes=[ps])
                        P.pe(lambda e, o=o, g=g, zb=zb: e.matmul(o, lhsT=Z[:, g, 1, zb:zb + nb], rhs=PCiND[1][:, g, :], start=False, stop=True), reads=[Z, PCiND[1]], writes=[ps])
                    ydst = y[0:nb].rearrange("p t (g c) -> p g t c", g=16)[:, 4 * g4:4 * g4 + 4]
                    if g4 % 2 == 0:
                        P.act(lambda e, ps=ps, ydst=ydst: e.activation(out=ydst, in_=ps[0:nb, :].rearrange("p (g t c) -> p g t c", g=4, t=8), func=AF.Copy), reads=[ps], writes=[y])
                    else:
                        P.dve(lambda e, ps=ps, ydst=ydst: e.tensor_copy(out=ydst, in_=ps[0:nb, :].rearrange("p (g t c) -> p g t c", g=4, t=8)), reads=[ps], writes=[y])
                if 's5_p4a' in G.dbg:
                    return
                P.pool(lambda e: e.tensor_tensor(out=t3[0:nb], in0=u[0:nb], in1=bc_ap(Dsk[0:nb, :], [[0, 8], [1, 256]]), op=ALU.mult), reads=[u, Dsk], writes=[t3])
                P.dve(lambda e: e.tensor_tensor(out=y[0:nb], in0=y[0:nb], in1=t3[0:nb], op=ALU.add), reads=[y, t3], writes=[y])
                P.pool(lambda e: e.tensor_tensor(out=t3[0:nb], in0=y[0:nb], in1=y[0:nb], op=ALU.mult), reads=[y], writes=[t3])
                P.dve(lambda e: e.tensor_scalar(out=t3[0:nb], in0=t3[0:nb], scalar1=0.044715, scalar2=1.0, op0=ALU.mult, op1=ALU.add), reads=[t3], writes=[t3])
                P.pool(lambda e: e.tensor_tensor(out=t3[0:nb], in0=t3[0:nb], in1=y[0:nb], op=ALU.mult), reads=[t3, y], writes=[t3])
                P.act(lambda e: e.activation(out=t3[0:nb], in_=t3[0:nb], func=AF.Sigmoid, scale=2.0 * math.sqrt(2.0 / math.pi)), reads=[t3], writes=[t3])
                P.dve(lambda e: e.tensor_tensor(out=y[0:nb], in0=y[0:nb], in1=t3[0:nb], op=ALU.mult), reads=[y, t3], writes=[y])
                if 's5_p4b' in G.dbg:
                    return
                for c2 in range(2):
                    for t4 in range(2):
                        ps = G.nextps()
                        for tt in range(4):
                            t = 4 * t4 + tt
                            P.pe(lambda e, ps=ps, tt=tt, t=t, c2=c2: e.transpose(out=ps[:, 128 * tt:128 * tt + nb], in_=y[0:nb, t, 128 * c2:128 * c2 + 128], identity=G.ident[0:nb, 0:nb]),
                                 reads=[y, G.ident], writes=[ps])
                        gdst = g_[:, c2, 0:8 * nb].rearrange("p (k t) -> p t k", t=8)[:, 4 * t4:4 * t4 + 4, :]
                        P.act(lambda e, ps=ps, gdst=gdst: e.activation(out=gdst, in_=ps[:, :].rearrange("p (t q) -> p t q", t=4)[:, :, 0:nb], func=AF.Copy), reads=[ps], writes=[g_])
                if 's5_p4c' in G.dbg:
                    return
                ntok = 8 * nb
                tok0 = 8 * k0
                for n0 in range(0, ntok, 512):
                    nn_ = min(512, ntok - n0)
                    for c in range(2):
                        psa = G.nextps(); psb = G.nextps()
                        for kc in range(2):
                            P.pe(lambda e, psa=psa, kc=kc, c=c, n0=n0, nn_=nn_: e.matmul(psa[:, 0:nn_], lhsT=wg[:, kc, 128 * c:128 * c + 128], rhs=g_[:, kc, n0:n0 + nn_], start=(kc == 0), stop=(kc == 1)),
                                 reads=[wg, g_], writes=[psa])
                        for kc in range(2):
                            P.pe(lambda e, psb=psb, kc=kc, c=c, n0=n0, nn_=nn_: e.matmul(psb[:, 0:nn_], lhsT=wg[:, kc, 256 + 128 * c:256 + 128 * c + 128], rhs=g_[:, kc, n0:n0 + nn_], start=(kc == 0), stop=(kc == 1)),
                                 reads=[wg, g_], writes=[psb])
                        s_ = sg[cnt['n'] % 2]; o_ = yb[cnt['n'] % 2]; cnt['n'] += 1
                        P.act(lambda e, psb=psb, s_=s_, nn_=nn_: e.activation(out=s_[:, 0:nn_], in_=psb[:, 0:nn_], func=AF.Sigmoid), reads=[psb], writes=[s_])
                        P.dve(lambda e, psa=psa, s_=s_, o_=o_, nn_=nn_: e.tensor_tensor(out=o_[:, 0:nn_], in0=psa[:, 0:nn_], in1=s_[:, 0:nn_], op=ALU.mult), reads=[psa, s_], writes=[o_])
                        P.dma(yT[2 + c, :, tok0 + n0:tok0 + n0 + nn_], o_[:, 0:nn_], reads=[o_], writes=[yT])

            for ti, (k0, nb, zc) in enumerate(S5_BT):
                if ti == 0 and not with_ctx:
                    continue
                out_tile(ti, k0, nb, zc)
            P.flush()


def emit_norm2(G, xt, n, gam_fn, sh_fn, out_fn, out_res, tmp_bufs):
    P = G.P
    sq, rstd, tmp = tmp_bufs
    ps = G.nextps()
    P.act(lambda e: e.activation(out=sq[:, :, 0:n], in_=xt[:, :, 0:n], func=AF.Square), reads=[xt], writes=[sq])
    for k in range(8):
        P.pe(lambda e, k=k: e.matmul(ps[:, 0:n], lhsT=G.onesb[:], rhs=sq[:, k, 0:n], start=(k == 0), stop=(k == 7)), reads=[sq, G.onesb], writes=[ps])
    P.act(lambda e: e.activation(out=rstd[:, 0:n], in_=ps[:, 0:n], func=AF.Sqrt, scale=1.0 / D, bias=G.epsb[:, 0:1]), reads=[ps, G.epsb], writes=[rstd])
    P.dve(lambda e: e.reciprocal(out=rstd[:, 0:n], in_=rstd[:, 0:n]), reads=[rstd], writes=[rstd])
    for k in range(8):
        t = tmp[k % len(tmp)]
        P.dve(lambda e, k=k, t=t: e.tensor_tensor(out=t[:, 0:n], in0=xt[:, k, 0:n], in1=rstd[:, 0:n], op=ALU.mult), reads=[xt, rstd], writes=[t])
        if sh_fn is None:
            P.act(lambda e, k=k, t=t: e.activation(out=out_fn(k), in_=t[:, 0:n], func=AF.Copy, scale=gam_fn(k)), reads=[t, G.cm], writes=[out_res])
        else:
            P.act(lambda e, k=k, t=t: e.activation(out=out_fn(k), in_=t[:, 0:n], func=AF.Identity, scale=gam_fn(k), bias=sh_fn(k)), reads=[t, G.cm], writes=[out_res])


def precast_tail_weights(G, l):
    P, I, sc = G.P, G.I, G.scr
    if 'wt_gate' not in sc:
        G.scratch('wt_gate', [32, 128, 8 * 128], BF16); G.scratch('wt_br', [32, 128, 4 * 128], BF16)
        G.scratch('wt_out', [8, 128, 8 * 128], BF16); G.scratch('wt_ffa', [22, 128, 8 * 128], BF16)
        G.scratch('wt_ffb', [22, 128, 8 * 128], BF16); G.scratch('wt_ff2', [8, 128, 22 * 128], BF16)
    BRW = [('w_branch_a', 2), ('w_branch_b', 2), ('w_branch_c', 2), ('w_branch_d', 4)]
    for oc in range(8):
        for i, (wname, nk) in enumerate(BRW):
            c = oc * 4 + i
            c0 = O_GATE + i * 1024 + oc * 128
            P.dmaq('pool', sc['wt_gate'][c].rearrange("p (k n) -> p k n", k=8), I['w_in'][l, :, c0:c0 + 128].rearrange("(k p) n -> p k n", p=128), writes=[sc['wt_gate']])
            P.dmaq('pool', sc['wt_br'][c, :, 0:nk * 128].rearrange("p (k n) -> p k n", k=nk), I[wname][l, :, oc * 128:(oc + 1) * 128].rearrange("(k p) n -> p k n", p=128), writes=[sc['wt_br']])
        P.dmaq('pool', sc['wt_out'][oc].rearrange("p (k n) -> p k n", k=8), I['w_out'][l, :, oc * 128:(oc + 1) * 128].rearrange("(k p) n -> p k n", p=128), writes=[sc['wt_out']])
        P.dmaq('pool', sc['wt_ff2'][oc].rearrange("p (k n) -> p k n", k=22), I['ffn_w_out'][l, :, oc * 128:(oc + 1) * 128].rearrange("(k p) n -> p k n", p=128), writes=[sc['wt_ff2']])
    for hc in range(22):
        P.dmaq('pool', sc['wt_ffa'][hc].rearrange("p (k n) -> p k n", k=8), I['ffn_w_in'][l, :, hc * 128:(hc + 1) * 128].rearrange("(k p) n -> p k n", p=128), writes=[sc['wt_ffa']])
        P.dmaq('pool', sc['wt_ffb'][hc].rearrange("p (k n) -> p k n", k=8), I['ffn_w_in'][l, :, FFN_H + hc * 128:FFN_H + (hc + 1) * 128].rearrange("(k p) n -> p k n", p=128), writes=[sc['wt_ffb']])


def stage_tail(G, l):
    nc, P, I = G.nc, G.P, G.I
    sb = G.sb
    sc = G.scr
    with_ctx = l < DEPTH - 1
    last = l == DEPTH - 1
    xsT, hxT_d, yT = sc['xsT'], sc['hxT'], sc['yT']
    BR = [('w_branch_a', 0, 2), ('w_branch_b', 2, 2), ('w_branch_c', 4, 2), ('w_branch_d', 6, 4)]
    with contextlib.ExitStack() as es:
        hx = sb(es, 't_hx', [128, 8, 512], BF16); y = sb(es, 't_y', [128, 10, 512], BF16); x = sb(es, 't_x', [128, 8, 512])
        m = sb(es, 't_m', [128, 8, 512], BF16); h2 = sb(es, 't_h2', [128, 8, 512], BF16); u = sb(es, 't_u', [128, 22, 512], BF16)
        sq = sb(es, 't_sq', [128, 8, 512], BF16); rstd = sb(es, 't_rstd', [128, 512]); tmp = [sb(es, 't_tmp%d' % i, [128, 512]) for i in range(2)]
        wg = [sb(es, 't_wg%d' % i, [128, 8, 128], BF16) for i in range(6)]
        wbr = [sb(es, 't_wbr%d' % i, [128, 4, 128], BF16) for i in range(6)]
        wo = [sb(es, 't_wo%d' % i, [128, 8, 128], BF16) for i in range(3)]
        wab = [sb(es, 't_wab%d' % i, [128, 8, 128], BF16) for i in range(8)]
        w2 = [sb(es, 't_w2%d' % i, [128, 22, 128], BF16) for i in range(3)]
        gs = [sb(es, 't_gs%d' % i, [128, 512]) for i in range(2)]
        acc = sb(es, 't_acc', [128, 512]); tm = [sb(es, 't_tm%d' % i, [128, 512]) for i in range(2)]
        if last:
            fnw = sb(es, 't_fnw', [128, 8])
            P.dma(fnw[:], I['final_norm_w'].rearrange("(k p) -> p k", p=128), writes=[fnw], allow_slow_non_contiguous=True)
            xn = sb(es, 't_xn', [128, 8, 512]); ot = [sb(es, 't_ot%d' % i, [128, D]) for i in range(2)]
        cnt = {'wg': 0, 'wbr': 0, 'wo': 0, 'wab': 0, 'w2': 0, 'g': 0, 'ot': 0}

        def tile(ti, t0, n):
            s = 1 if ti == 0 else 0
            P.dma(hx[:, :, 0:n], hxT_d[:, :, t0:t0 + n].rearrange("k p t -> p k t"), reads=[hxT_d], writes=[hx])
            P.dma(y[:, :, 0:n], yT[:, :, t0:t0 + n].rearrange("k p t -> p k t"), reads=[yT], writes=[y])
            P.dma(x[:, :, 0:n], xsT[:, :, t0:t0 + n].rearrange("k p t -> p k t"), reads=[xsT], writes=[x])
            for oc in range(8):
                for i, (wname, yb0, nk) in enumerate(BR):
                    w = wg[cnt['wg'] % 6]; cnt['wg'] += 1
                    c0 = O_GATE + i * 1024 + oc * 128
                    P.dma(w[:].rearrange("p k n -> p (k n)"), sc['wt_gate'][oc * 4 + i], reads=[sc['wt_gate']], writes=[w])
                    wb = wbr[cnt['wbr'] % 6]; cnt['wbr'] += 1
                    P.dma(wb[:, 0:nk, :].rearrange("p k n -> p (k n)"), sc['wt_br'][oc * 4 + i, :, 0:nk * 128], reads=[sc['wt_br']], writes=[wb])
                    psg = G.nextps(); psb = G.nextps()
                    for k in range(8):
                        P.pe(lambda e, psg=psg, k=k, w=w: e.matmul(psg[:, 0:n], lhsT=w[:, k, :], rhs=hx[:, k, 0:n], start=(k == 0), stop=(k == 7)), reads=[w, hx], writes=[psg])
                    for k in range(nk):
                        P.pe(lambda e, psb=psb, k=k, wb=wb, yb0=yb0, nk=nk: e.matmul(psb[:, 0:n], lhsT=wb[:, k, :], rhs=y[:, yb0 + k, 0:n], start=(k == 0), stop=(k == nk - 1)),
                             reads=[wb, y], writes=[psb])
                    g_ = gs[cnt['g'] % 2]; cnt['g'] += 1
                    P.act(lambda e, psg=psg, g_=g_: e.activation(out=g_[:, 0:n], in_=psg[:, 0:n], func=AF.Sigmoid), reads=[psg], writes=[g_])
                    if i == 0:
                        P.dve(lambda e, psb=psb, g_=g_: e.tensor_tensor(out=acc[:, 0:n], in0=psb[:, 0:n], in1=g_[:, 0:n], op=ALU.mult), reads=[psb, g_], writes=[acc])
                    else:
                        t_ = tm[i % 2]
                        P.dve(lambda e, psb=psb, g_=g_, t_=t_: e.tensor_tensor(out=t_[:, 0:n], in0=psb[:, 0:n], in1=g_[:, 0:n], op=ALU.mult), reads=[psb, g_], writes=[t_])
                        if i < 3:
                            P.pool(lambda e, t_=t_: e.tensor_tensor(out=acc[:, 0:n], in0=acc[:, 0:n], in1=t_[:, 0:n], op=ALU.add), reads=[acc, t_], writes=[acc])
                        else:
                            P.pool(lambda e, t_=t_, oc=oc: e.tensor_tensor(out=m[:, oc, 0:n], in0=acc[:, 0:n], in1=t_[:, 0:n], op=ALU.add), reads=[acc, t_], writes=[m])
            for oc in range(8):
                w = wo[cnt['wo'] % 3]; cnt['wo'] += 1
                P.dma(w[:].rearrange("p k n -> p (k n)"), sc['wt_out'][oc], reads=[sc['wt_out']], writes=[w])
                ps = G.nextps()
                for k in range(8):
                    P.pe(lambda e, ps=ps, k=k, w=w: e.matmul(ps[:, 0:n], lhsT=w[:, k, :], rhs=m[:, k, 0:n], start=(k == 0), stop=(k == 7)), reads=[w, m], writes=[ps])
                P.dve(lambda e, ps=ps, oc=oc: e.scalar_tensor_tensor(out=x[:, oc, 0:n], in0=ps[:, 0:n], scalar=G.cm[:, l, 2, oc, s:s + 1], in1=x[:, oc, 0:n], op0=ALU.mult, op1=ALU.add),
                      reads=[ps, G.cm, x], writes=[x])
            emit_norm2(G, x, n, lambda k: G.cm[:, l, 4, k, s:s + 1], lambda k: G.cm[:, l, 3, k, s:s + 1], lambda k: h2[:, k, 0:n], h2, (sq, rstd, tmp))
            for hc in range(22):
                wa = wab[cnt['wab'] % 8]; cnt['wab'] += 1
                wb = wab[cnt['wab'] % 8]; cnt['wab'] += 1
                P.dma(wa[:].rearrange("p k n -> p (k n)"), sc['wt_ffa'][hc], reads=[sc['wt_ffa']], writes=[wa])
                P.dma(wb[:].rearrange("p k n -> p (k n)"), sc['wt_ffb'][hc], reads=[sc['wt_ffb']], writes=[wb])
                psa = G.nextps(); psb = G.nextps()
                for k in range(8):
                    P.pe(lambda e, psa=psa, k=k, wa=wa: e.matmul(psa[:, 0:n], lhsT=wa[:, k, :], rhs=h2[:, k, 0:n], start=(k == 0), stop=(k == 7)), reads=[wa, h2], writes=[psa])
                for k in range(8):
                    P.pe(lambda e, psb=psb, k=k, wb=wb: e.matmul(psb[:, 0:n], lhsT=wb[:, k, :], rhs=h2[:, k, 0:n], start=(k == 0), stop=(k == 7)), reads=[wb, h2], writes=[psb])
                g_ = gs[cnt['g'] % 2]; cnt['g'] += 1
                P.act(lambda e, psa=psa, g_=g_: e.activation(out=g_[:, 0:n], in_=psa[:, 0:n], func=AF.Silu), reads=[psa], writes=[g_])
                P.dve(lambda e, psb=psb, g_=g_, hc=hc: e.tensor_tensor(out=u[:, hc, 0:n], in0=psb[:, 0:n], in1=g_[:, 0:n], op=ALU.mult), reads=[psb, g_], writes=[u])
            for oc in range(8):
                w = w2[cnt['w2'] % 3]; cnt['w2'] += 1
                P.dma(w[:].rearrange("p k n -> p (k n)"), sc['wt_ff2'][oc], reads=[sc['wt_ff2']], writes=[w])
                ps = G.nextps()
                for k in range(22):
                    P.pe(lambda e, ps=ps, k=k, w=w: e.matmul(ps[:, 0:n], lhsT=w[:, k, :], rhs=u[:, k, 0:n], start=(k == 0), stop=(k == 21)), reads=[w, u], writes=[ps])
                P.dve(lambda e, ps=ps, oc=oc: e.scalar_tensor_tensor(out=x[:, oc, 0:n], in0=ps[:, 0:n], scalar=G.cm[:, l, 5, oc, s:s + 1], in1=x[:, oc, 0:n], op0=ALU.mult, op1=ALU.add),
                      reads=[ps, G.cm, x], writes=[x])
            if not last:
                P.dma(xsT[:, :, t0:t0 + n].rearrange("k p t -> p k t"), x[:, :, 0:n], reads=[x], writes=[xsT])
            else:
                emit_norm2(G, x, n, lambda k: fnw[:, k:k + 1], None, lambda k: xn[:, k, 0:n], xn, (sq, rstd, tmp))
                for q in range(n // 128):
                    o = ot[cnt['ot'] % 2]; cnt['ot'] += 1
                    for half in range(2):
                        ps = G.nextps()
                        for kk in range(4):
                            k = 4 * half + kk
                            P.pe(lambda e, ps=ps, kk=kk, k=k, q=q: e.transpose(out=ps[:, 128 * kk:128 * kk + 128], in_=xn[:, k, 128 * q:128 * q + 128], identity=G.ident[:]),
                                 reads=[xn, G.ident], writes=[ps])
                        if half == 0:
                            P.act(lambda e, ps=ps, o=o: e.activation(out=o[:, 0:512], in_=ps[:, :], func=AF.Copy), reads=[ps], writes=[o])
                        else:
                            P.dve(lambda e, ps=ps, o=o: e.tensor_copy(out=o[:, 512:1024], in_=ps[:, :]), reads=[ps], writes=[o])
                    tok = t0 - CTX + 128 * q
                    P.dma(G.out[tok:tok + 128, :], o[:], reads=[o])

        for ti, (t0, n) in enumerate(TT):
            if ti == 0 and not with_ctx:
                continue
            tile(ti, t0, n)
        P.flush()


def gate_cumsums(G, es, pfx, lf_all, ncol):
    P, sb = G.P, G.sb
    h = ncol // 2
    lfF = sb(es, pfx + 'lfF', [128, NT128, h]); lfB = sb(es, pfx + 'lfB', [128, NT128, h])
    cum_all = sb(es, pfx + 'cum', [128, NT128, ncol]); tot_all = sb(es, pfx + 'tot', [128, NT128, ncol])
    P.dve(lambda e: e.tensor_copy(out=lfF[:], in_=lf_all[:, :, 0:h]), reads=[lf_all], writes=[lfF])
    P.dve(lambda e: e.tensor_copy(out=lfB[:], in_=lf_all[:, :, h:ncol]), reads=[lf_all], writes=[lfB])
    n = NT128 * h
    psF = G.nextps(); psB = G.nextps()
    P.pe(lambda e: e.matmul(psF[:, 0:n], lhsT=G.triU[:], rhs=lfF[:].rearrange("p q c -> p (q c)"), start=True, stop=True), reads=[G.triU, lfF], writes=[psF])
    P.pe(lambda e: e.matmul(psB[:, 0:n], lhsT=G.triL[:], rhs=lfB[:].rearrange("p q c -> p (q c)"), start=True, stop=True), reads=[G.triL, lfB], writes=[psB])
    P.dve(lambda e: e.tensor_copy(out=cum_all[:, :, 0:h], in_=psF[:, 0:n].rearrange("p (q c) -> p q c", c=h)), reads=[psF], writes=[cum_all])
    P.dve(lambda e: e.tensor_copy(out=cum_all[:, :, h:ncol], in_=psB[:, 0:n].rearrange("p (q c) -> p q c", c=h)), reads=[psB], writes=[cum_all])
    nt = NT128 * ncol
    half = nt // 2
    for i in range(2):
        ps = G.nextps()
        P.pe(lambda e, ps=ps, i=i: e.matmul(ps[:, 0:half], lhsT=G.ones[:], rhs=lf_all[:].rearrange("p q c -> p (q c)")[:, i * half:(i + 1) * half], start=True, stop=True),
             reads=[G.ones, lf_all], writes=[ps])
        P.act(lambda e, ps=ps, i=i: e.activation(out=tot_all[:].rearrange("p q c -> p (q c)")[:, i * half:(i + 1) * half], in_=ps[:, 0:half], func=AF.Copy), reads=[ps], writes=[tot_all])
    return cum_all, tot_all


def mlstm_steps2(G, l, es):
    nc, P, I = G.nc, G.P, G.I
    sb = G.sb
    sc = G.scr
    with_ctx = l < DEPTH - 1
    if 'yT' not in sc:
        G.scratch('yT', [10, 128, S], BF16)
    yT = sc['yT']
    hacc = sb(es, 'm_hacc', [128, NT128, 256])
    gates = sb(es, 'm_gates', [128, NT128, 24])
    nw_b = load_bcast(G, es, 'm_nw', I['mlstm_norm_w'][l:l + 1, :], 256)
    with contextlib.ExitStack() as es0:
        ib_b = load_bcast(G, es0, 'm_ib', I['mlstm_ib'][l:l + 1].rearrange("o d h -> o (d h)"), 8)
        fb_b = load_bcast(G, es0, 'm_fb', I['mlstm_fb'][l:l + 1].rearrange("o d h -> o (d h)"), 8)
        gi = sb(es0, 'm_gi', [128, NT128, 16]); li = sb(es0, 'm_li', [128, NT128, 8]); lf = sb(es0, 'm_lf', [128, NT128, 8])
        P.dma(gi[:], sc['TM1'][:, 512:528].rearrange("(q p) c -> p q c", p=128), reads=[sc['TM1']], writes=[gi])
        P.dve(lambda e: e.tensor_tensor(out=li[:], in0=gi[:, :, 0:8], in1=bc_ap(ib_b[:], [[0, NT128], [1, 8]]), op=ALU.add), reads=[gi, ib_b], writes=[li])
        P.dve(lambda e: e.tensor_tensor(out=lf[:], in0=gi[:, :, 8:16], in1=bc_ap(fb_b[:], [[0, NT128], [1, 8]]), op=ALU.add), reads=[gi, fb_b], writes=[lf])
        P.act(lambda e: e.activation(out=lf[:], in_=lf[:], func=AF.Exp, scale=-1.0), reads=[lf], writes=[lf])
        P.act(lambda e: e.activation(out=lf[:], in_=lf[:], func=AF.Ln, bias=1.0), reads=[lf], writes=[lf])
        P.dve(lambda e: e.tensor_scalar(out=lf[:], in0=lf[:], scalar1=-1.0, scalar2=None, op0=ALU.mult), reads=[lf], writes=[lf])
        cum_all, tot_all = gate_cumsums(G, es0, 'm_', lf, 8)
        P.act(lambda e: e.activation(out=gates[:, :, 0:8], in_=cum_all[:], func=AF.Exp), reads=[cum_all], writes=[gates])
        P.dve(lambda e: e.tensor_tensor(out=li[:], in0=li[:], in1=cum_all[:], op=ALU.subtract), reads=[li, cum_all], writes=[li])
        P.act(lambda e: e.activation(out=gates[:, :, 8:16], in_=li[:], func=AF.Exp), reads=[li], writes=[gates])
        P.act(lambda e: e.activation(out=gates[:, :, 16:24], in_=tot_all[:], func=AF.Exp), reads=[tot_all], writes=[gates])
        P.flush()
    NB = 2
    qT = [sb(es, 'm_qT%d' % i, [128, 2, 128]) for i in range(NB)]
    kT = [sb(es, 'm_kT%d' % i, [128, 2, 128]) for i in range(NB)]
    kTM = [sb(es, 'm_kTM%d' % i, [128, 256]) for i in range(NB)]
    Vp = [sb(es, 'm_Vp%d' % i, [128, 4, 65]) for i in range(NB)]
    mo = [sb(es, 'm_mo%d' % i, [128, 256]) for i in range(NB)]
    for v in Vp:
        P.pool(lambda e, v=v: e.memset(v[:], 1.0), writes=[v])
    Cd = [sb(es, 'm_C%d' % d, [128, 2, 130]) for d in range(2)]
    bmask = sb(es, 'm_bmask', [128, 2, 130])
    for d in range(2):
        P.pool(lambda e, d=d: e.memset(Cd[d][:], 0.0), writes=[Cd[d]])
    P.pool(lambda e: e.memset(bmask[:], 0.0), writes=[bmask])
    P.pool(lambda e: e.memset(bmask[0:64, :, 0:65], 1.0), writes=[bmask])
    P.pool(lambda e: e.memset(bmask[64:128, :, 65:130], 1.0), writes=[bmask])
    pmt = [sb(es, 'm_pmt%d' % i, [128, 128]) for i in range(2)]
    pm = [sb(es, 'm_pm%d' % i, [128, 128]) for i in range(8)]
    uV = [sb(es, 'm_uV%d' % i, [128, 4, 65]) for i in range(2)]
    ep = [sb(es, 'm_ep%d' % i, [128, 20]) for i in range(2)]
    ct = [sb(es, 'm_ct%d' % i, [128, 260]) for i in range(2)]
    htmp = sb(es, 'm_htmp', [128, 4, 64])
    ho = [sb(es, 'm_ho%d' % i, [128, 256]) for i in range(2)]
    sg = [sb(es, 'm_sg%d' % i, [128, 256]) for i in range(2)]
    ss = [sb(es, 'm_ss%d' % i, [128, 4]) for i in range(2)]
    junk = sb(es, 'm_junk', [128, 64])
    ytb = [sb(es, 'm_ytb%d' % i, [128, 2, 128], BF16) for i in range(2)]
    cnt = {'pm': 0}

    def chunk_pass(q, d, it, first_pass):
        tok = q * 128
        b = it % NB
        P.dma(qT[b][:], sc['mqT'][:, :, tok:tok + 128].rearrange("c p t -> p c t"), reads=[sc['mqT']], writes=[qT[b]])
        P.dma(kT[b][:], sc['mkT'][:, :, tok:tok + 128].rearrange("c p t -> p c t"), reads=[sc['mkT']], writes=[kT[b]])
        P.dma(kTM[b][:], sc['mkTM'][tok:tok + 128, :], reads=[sc['mkTM']], writes=[kTM[b]])
        P.dma(Vp[b][:, :, 0:64], sc['TM1'][tok:tok + 128, 0:256].rearrange("p (h e) -> p h e", h=4), reads=[sc['TM1']], writes=[Vp[b]])
        if not first_pass:
            P.dma(mo[b][:], sc['TM1'][tok:tok + 128, 256:512], reads=[sc['TM1']], writes=[mo[b]])
        mask = G.triU if d == 0 else G.triL
        pms = []
        for h in range(4):
            pr, hh = h // 2, h % 2
            j = 4 * d + h
            ps = G.nextps()
            P.pe(lambda e, ps=ps, hh=hh, pr=pr: e.matmul(ps[:, 0:128], lhsT=kT[b][64 * hh:64 * hh + 64, pr, :], rhs=qT[b][64 * hh:64 * hh + 64, pr, :], start=True, stop=True),
                 reads=[kT[b], qT[b]], writes=[ps])
            t_ = pmt[cnt['pm'] % 2]; p_ = pm[cnt['pm'] % 8]; cnt['pm'] += 1
            P.act(lambda e, ps=ps, t_=t_, j=j: e.activation(out=t_[:], in_=ps[:, 0:128], func=AF.Copy, scale=gates[:, q, 8 + j:9 + j]), reads=[ps, gates], writes=[t_])
            P.pool(lambda e, t_=t_, p_=p_: e.tensor_tensor(out=p_[:], in0=t_[:], in1=mask[:], op=ALU.mult), reads=[t_, mask], writes=[p_])
            pms.append(p_)
        uv = uV[it % 2]
        P.dve(lambda e: e.tensor_tensor(out=uv[:], in0=Vp[b][:], in1=bc_ap(gates[:, q, 8 + 4 * d:12 + 4 * d], [[1, 4], [0, 65]]), op=ALU.mult), reads=[Vp[b], gates], writes=[uv])
        yield
        C = Cd[d]
        ps2 = G.nextps()
        for pr in range(2):
            P.pe(lambda e, pr=pr: e.matmul(ps2[:, 130 * pr:130 * pr + 130], lhsT=qT[b][:, pr, :], rhs=C[:, pr, :], start=(pr == 0), stop=False), reads=[qT[b], C], writes=[ps2])
        for h in range(4):
            P.pe(lambda e, h=h, p_=pms[h]: e.matmul(ps2[:, 65 * h:65 * h + 65], lhsT=p_[:], rhs=Vp[b][:, h, :], start=False, stop=(h == 3)), reads=[pms[h], Vp[b]], writes=[ps2])
        e_ = ep[it % 2]
        p3 = ps2[:, 0:260].rearrange("p (h e) -> p h e", e=65)
        aq = gates[:, q, 4 * d:4 * d + 4]
        P.dve(lambda e: e.tensor_tensor(out=e_[:, 0:4], in0=p3[:, :, 64], in1=aq, op=ALU.mult), reads=[ps2, gates], writes=[e_])
        P.dve(lambda e: e.scalar_tensor_tensor(out=e_[:, 4:8], in0=e_[:, 0:4], scalar=-1.0, in1=e_[:, 0:4], op0=ALU.mult, op1=ALU.max), reads=[e_], writes=[e_])
        P.dve(lambda e: e.tensor_scalar(out=e_[:, 8:12], in0=e_[:, 4:8], scalar1=1.0, scalar2=None, op0=ALU.max), reads=[e_], writes=[e_])
        P.dve(lambda e: e.reciprocal(out=e_[:, 12:16], in_=e_[:, 8:12]), reads=[e_], writes=[e_])
        P.dve(lambda e: e.tensor_tensor(out=e_[:, 16:20], in0=e_[:, 12:16], in1=aq, op=ALU.mult), reads=[e_, gates], writes=[e_])
        sclb = bc_ap(e_[:, 16:20], [[1, 4], [0, 64]])
        hq = hacc[:, q, :].rearrange("p (h e) -> p h e", h=4)
        if first_pass:
            P.dve(lambda e: e.tensor_tensor(out=hq, in0=p3[:, :, 0:64], in1=sclb, op=ALU.mult), reads=[ps2, e_], writes=[hacc])
        else:
            P.dve(lambda e: e.tensor_tensor(out=htmp[:], in0=p3[:, :, 0:64], in1=sclb, op=ALU.mult), reads=[ps2, e_], writes=[htmp])
            P.pool(lambda e: e.tensor_tensor(out=hq, in0=hq, in1=htmp[:], op=ALU.add), reads=[hacc, htmp], writes=[hacc])
        ps3 = G.nextps()
        for pr in range(2):
            P.pe(lambda e, pr=pr: e.matmul(ps3[:, 130 * pr:130 * pr + 130], lhsT=kTM[b][:, 128 * pr:128 * pr + 128], rhs=uv[:, 2 * pr:2 * pr + 2, :], start=True, stop=True),
                 reads=[kTM[b], uv], writes=[ps3])
        c_ = ct[it % 2]
        Cf = C[:].rearrange("p a b -> p (a b)")
        P.dve(lambda e: e.tensor_tensor(out=c_[:], in0=ps3[:, 0:260], in1=Cf, op=ALU.add), reads=[ps3, C], writes=[c_])
        P.dve(lambda e: e.tensor_tensor(out=c_[:].rearrange("p (h e) -> p h e", h=4), in0=c_[:].rearrange("p (h e) -> p h e", h=4),
                                        in1=bc_ap(gates[:, q, 16 + 4 * d:20 + 4 * d], [[1, 4], [0, 65]]), op=ALU.mult), reads=[c_, gates], writes=[c_])
        P.pool(lambda e: e.tensor_tensor(out=Cf, in0=c_[:], in1=bmask[:].rearrange("p a b -> p (a b)"), op=ALU.mult), reads=[c_, bmask], writes=[C])
        if not first_pass and (with_ctx or q >= 2):
            bb = it % 2
            P.act(lambda e: e.activation(out=sg[bb][:], in_=mo[b][:], func=AF.Sigmoid), reads=[mo[b]], writes=[sg[bb]])
            P.dve(lambda e: e.tensor_tensor(out=ho[bb][:], in0=hacc[:, q, :], in1=sg[bb][:], op=ALU.mult), reads=[hacc, sg[bb]], writes=[ho[bb]])
            for h in range(4):
                P.act(lambda e, h=h: e.activation(out=junk[:], in_=ho[bb][:, 64 * h:64 * h + 64], func=AF.Square, accum_out=ss[bb][:, h:h + 1]), reads=[ho[bb]], writes=[junk, ss[bb]])
            P.act(lambda e: e.activation(out=ss[bb][:], in_=ss[bb][:], func=AF.Sqrt, scale=1.0 / 64, bias=G.epsb[:, 0:1]), reads=[ss[bb], G.epsb], writes=[ss[bb]])
            P.dve(lambda e: e.reciprocal(out=ss[bb][:], in_=ss[bb][:]), reads=[ss[bb]], writes=[ss[bb]])
            P.dve(lambda e: e.tensor_tensor(out=ho[bb][:].rearrange("p (h e) -> p h e", h=4), in0=ho[bb][:].rearrange("p (h e) -> p h e", h=4),
                                            in1=bc_ap(ss[bb][:], [[1, 4], [0, 64]]), op=ALU.mult), reads=[ho[bb], ss[bb]], writes=[ho[bb]])
            P.pool(lambda e: e.tensor_tensor(out=ho[bb][:], in0=ho[bb][:], in1=nw_b[:], op=ALU.mult), reads=[ho[bb], nw_b], writes=[ho[bb]])
            ps = G.nextps()
            for c in range(2):
                P.pe(lambda e, ps=ps, c=c: e.transpose(out=ps[:, 128 * c:128 * c + 128], in_=ho[bb][:, 128 * c:128 * c + 128], identity=G.ident[:]), reads=[ho[bb], G.ident], writes=[ps])
            P.act(lambda e, ps=ps: e.activation(out=ytb[bb][:], in_=ps[:, 0:256].rearrange("p (c t) -> p c t", c=2), func=AF.Copy), reads=[ps], writes=[ytb[bb]])
            P.dma(yT[0:2, :, tok:tok + 128].rearrange("c p t -> p c t"), ytb[bb][:], reads=[ytb[bb]], writes=[yT])

    steps = []
    it = 0
    for q in range(NT128):
        steps.append(chunk_pass(q, 0, it, True)); it += 1
    for q in [1, 0] + list(range(NT128 - 1, 1, -1)):
        steps.append(chunk_pass(q, 1, it, False)); it += 1
    return steps


def ssd_steps2(G, l, es):
    nc, P, I = G.nc, G.P, G.I
    sb = G.sb
    sc = G.scr
    with_ctx = l < DEPTH - 1
    yT = sc['yT']
    NEG = -30000.0
    yacc = sb(es, 'd_yacc', [128, NT128, 512])
    D_b = load_bcast(G, es, 'd_D', I['ssd_d'][l:l + 1, :], 8)
    nw_b = load_bcast(G, es, 'd_nw', I['ssd_norm_w'][l:l + 1, :], 512)
    dt_all = sb(es, 'd_dtall', [128, NT128, 16]); a_all = sb(es, 'd_aall', [128, NT128, 16])
    acum_all = sb(es, 'd_acall', [128, NT128, 16]); etot_all = sb(es, 'd_etall', [128, NT128, 16]); wgt_all = sb(es, 'd_wgall', [128, NT128, 16])
    negones = sb(es, 'd_negones', [128, 128])
    nm = [sb(es, 'd_nm%d' % d, [128, 4, 128], BF16) for d in range(2)]
    P.pool(lambda e: e.memset(negones[:], -1.0), writes=[negones])
    with contextlib.ExitStack() as es0:
        alog_b = load_bcast(G, es0, 'd_alog', I['ssd_a_log'][l:l + 1].rearrange("o d h -> o (d h)"), 16)
        dtb_b = load_bcast(G, es0, 'd_dtb', I['ssd_dt_bias'][l:l + 1].rearrange("o d h -> o (d h)"), 16)
        A_b = sb(es0, 'd_A', [128, 16]); nmf = sb(es0, 'd_nmf', [128, 128])
        P.act(lambda e: e.activation(out=A_b[:], in_=alog_b[:], func=AF.Exp), reads=[alog_b], writes=[A_b])
        P.dve(lambda e: e.tensor_scalar(out=A_b[:], in0=A_b[:], scalar1=-1.0, scalar2=None, op0=ALU.mult), reads=[A_b], writes=[A_b])
        for d, tri in enumerate((G.triU, G.triL)):
            P.dve(lambda e, tri=tri: e.tensor_scalar(out=nmf[:], in0=tri[:], scalar1=-1.0, scalar2=-NEG, op0=ALU.add, op1=ALU.mult), reads=[tri], writes=[nmf])
            P.dve(lambda e, d=d: e.tensor_copy(out=nm[d][:], in_=bc_ap(nmf[:], [[0, 4], [1, 128]])), reads=[nmf], writes=[nm[d]])
        P.dma(dt_all[:], sc['ddtTM'][:, :].rearrange("(q p) c -> p q c", p=128), reads=[sc['ddtTM']], writes=[dt_all])
        P.dve(lambda e: e.tensor_tensor(out=dt_all[:], in0=dt_all[:], in1=bc_ap(dtb_b[:], [[0, NT128], [1, 16]]), op=ALU.add), reads=[dt_all, dtb_b], writes=[dt_all])
        P.act(lambda e: e.activation(out=dt_all[:], in_=dt_all[:], func=AF.Exp), reads=[dt_all], writes=[dt_all])
        P.act(lambda e: e.activation(out=dt_all[:], in_=dt_all[:], func=AF.Ln, bias=1.0), reads=[dt_all], writes=[dt_all])
        P.dve(lambda e: e.tensor_tensor(out=a_all[:], in0=dt_all[:], in1=bc_ap(A_b[:], [[0, NT128], [1, 16]]), op=ALU.mult), reads=[dt_all, A_b], writes=[a_all])
        cum_all, tot_all = gate_cumsums(G, es0, 'd_', a_all, 16)
        P.act(lambda e: e.activation(out=acum_all[:], in_=cum_all[:], func=AF.Exp), reads=[cum_all], writes=[acum_all])
        P.act(lambda e: e.activation(out=etot_all[:], in_=tot_all[:], func=AF.Exp), reads=[tot_all], writes=[etot_all])
        P.dve(lambda e: e.tensor_tensor(out=wgt_all[:], in0=tot_all[:], in1=cum_all[:], op=ALU.subtract), reads=[tot_all, cum_all], writes=[wgt_all])
        P.act(lambda e: e.activation(out=wgt_all[:], in_=wgt_all[:], func=AF.Exp), reads=[wgt_all], writes=[wgt_all])
        P.dve(lambda e: e.tensor_tensor(out=wgt_all[:], in0=wgt_all[:], in1=dt_all[:], op=ALU.mult), reads=[wgt_all, dt_all], writes=[wgt_all])
        P.flush()
    NB = 2
    xt = [sb(es, 'd_xt%d' % i, [128, 512]) for i in range(NB)]
    Bt = [sb(es, 'd_Bt%d' % i, [128, 2, 128]) for i in range(NB)]
    Ct = [sb(es, 'd_Ct%d' % i, [128, 2, 128]) for i in range(NB)]
    Btm = [sb(es, 'd_Btm%d' % i, [128, 256]) for i in range(NB)]
    dz = [sb(es, 'd_dz%d' % i, [128, 512]) for i in range(NB)]
    Hs = [sb(es, 'd_Hs%d' % d, [128, 8, 64]) for d in range(2)]
    for d in range(2):
        P.pool(lambda e, d=d: e.memset(Hs[d][:], 0.0), writes=[Hs[d]])
    rbig = sb(es, 'd_rbig', [128, 8, 128])
    ex = sb(es, 'd_ex', [128, 8, 128])
    pmb = [sb(es, 'd_pm%d' % i, [128, 8, 128]) for i in range(2)]
    wx2 = [sb(es, 'd_wx%d' % i, [128, 8, 64]) for i in range(2)]
    tmp = sb(es, 'd_tmp', [128, 8, 64]); htmp = sb(es, 'd_htmp', [128, 8, 64])
    yz = sb(es, 'd_yz', [128, 512]); sz = sb(es, 'd_sz', [128, 512]); ssq = sb(es, 'd_ssq', [128, 1]); junk = sb(es, 'd_junk', [128, 512])
    ytb = [sb(es, 'd_ytb%d' % i, [128, 4, 128], BF16) for i in range(2)]

    def chunk_pass(q, d, it, first_pass):
        tok = q * 128
        b = it % NB
        wx = wx2[it % 2]; pm_ = pmb[it % 2]
        P.dma(xt[b][:], sc['xTM'][tok:tok + 128, :], reads=[sc['xTM']], writes=[xt[b]])
        P.dma(Bt[b][:], sc['BT'][:, :, tok:tok + 128].rearrange("g p t -> p g t"), reads=[sc['BT']], writes=[Bt[b]])
        P.dma(Ct[b][:], sc['CT'][:, :, tok:tok + 128].rearrange("g p t -> p g t"), reads=[sc['CT']], writes=[Ct[b]])
        P.dma(Btm[b][:], sc['BTM'][tok:tok + 128, :], reads=[sc['BTM']], writes=[Btm[b]])
        if not first_pass:
            P.dma(dz[b][:], sc['dzTM'][tok:tok + 128, :], reads=[sc['dzTM']], writes=[dz[b]])
        mask = G.triU if d == 0 else G.triL
        P.dve(lambda e: e.tensor_tensor(out=rbig[:], in0=bc_ap(a_all[:, q, 8 * d:8 * d + 8], [[1, 8], [0, 128]]), in1=bc_ap(mask[:], [[0, 8], [1, 128]]), op=ALU.mult),
              reads=[a_all, mask], writes=[rbig])
        cb = [G.nextps(), G.nextps()]
        for hf in range(2):
            P.pe(lambda e, hf=hf: e.matmul(cb[hf][:, :], lhsT=G.ones[:], rhs=rbig[:, 4 * hf:4 * hf + 4, :], start=True, stop=False), reads=[G.ones, rbig], writes=[cb[hf]])
            for hh in range(4):
                P.pe(lambda e, hf=hf, hh=hh: e.matmul(cb[hf][:, 128 * hh:128 * hh + 128], lhsT=rbig[:, 4 * hf + hh, :], rhs=negones[:], start=False, stop=False),
                     reads=[rbig, negones], writes=[cb[hf]])
            P.pe(lambda e, hf=hf: e.matmul(cb[hf][:, :], lhsT=G.identb[:], rhs=nm[d][:], start=False, stop=True), reads=[G.identb, nm[d]], writes=[cb[hf]])
            P.act(lambda e, hf=hf: e.activation(out=ex[:, 4 * hf:4 * hf + 4, :], in_=cb[hf][:, :].rearrange("p (h t) -> p h t", h=4), func=AF.Exp), reads=[cb[hf]], writes=[ex])
        pss = G.nextps()
        for g in range(2):
            P.pe(lambda e, g=g: e.matmul(pss[:, 128 * g:128 * g + 128], lhsT=Bt[b][:, g, :], rhs=Ct[b][:, g, :], start=True, stop=True), reads=[Bt[b], Ct[b]], writes=[pss])
        P.dve(lambda e: e.tensor_tensor(out=ex[:], in0=ex[:], in1=bc_ap(dt_all[:, q, 8 * d:8 * d + 8], [[1, 8], [0, 128]]), op=ALU.mult), reads=[ex, dt_all], writes=[ex])
        P.dve(lambda e: e.tensor_tensor(out=pm_[:].rearrange("p (g h) t -> p g h t", g=2), in0=ex[:].rearrange("p (g h) t -> p g h t", g=2),
                                        in1=bc_ap(pss[:, 0:256], [[128, 2], [0, 4], [1, 128]]), op=ALU.mult), reads=[ex, pss], writes=[pm_])
        P.pool(lambda e: e.tensor_tensor(out=wx[:], in0=xt[b][:].rearrange("p (h e) -> p h e", h=8), in1=bc_ap(wgt_all[:, q, 8 * d:8 * d + 8], [[1, 8], [0, 64]]), op=ALU.mult),
               reads=[xt[b], wgt_all], writes=[wx])
        yield
        psd = G.nextps()
        pso = G.nextps()
        for g in range(2):
            P.pe(lambda e, g=g: e.matmul(pso[:, 256 * g:256 * g + 256], lhsT=Ct[b][:, g, :], rhs=Hs[d][:, 4 * g:4 * g + 4, :], start=True, stop=True), reads=[Ct[b], Hs[d]], writes=[pso])
        for h in range(8):
            P.pe(lambda e, h=h: e.matmul(psd[:, 64 * h:64 * h + 64], lhsT=pm_[:, h, :], rhs=xt[b][:, 64 * h:64 * h + 64], start=True, stop=True), reads=[pm_, xt[b]], writes=[psd])
        P.dve(lambda e: e.tensor_tensor(out=tmp[:], in0=pso[:, :].rearrange("p (h e) -> p h e", h=8), in1=bc_ap(acum_all[:, q, 8 * d:8 * d + 8], [[1, 8], [0, 64]]), op=ALU.mult),
              reads=[pso, acum_all], writes=[tmp])
        if first_pass:
            P.dve(lambda e: e.tensor_tensor(out=yacc[:, q, :], in0=psd[:, :], in1=tmp[:].rearrange("p h e -> p (h e)"), op=ALU.add), reads=[psd, tmp], writes=[yacc])
        else:
            P.dve(lambda e: e.tensor_tensor(out=tmp[:].rearrange("p h e -> p (h e)"), in0=psd[:, :], in1=tmp[:].rearrange("p h e -> p (h e)"), op=ALU.add), reads=[psd, tmp], writes=[tmp])
            P.pool(lambda e: e.tensor_tensor(out=yacc[:, q, :], in0=yacc[:, q, :], in1=tmp[:].rearrange("p h e -> p (h e)"), op=ALU.add), reads=[tmp, yacc], writes=[yacc])
        pst = G.nextps()
        for g in range(2):
            P.pe(lambda e, g=g: e.matmul(pst[:, 256 * g:256 * g + 256], lhsT=Btm[b][:, 128 * g:128 * g + 128], rhs=wx[:, 4 * g:4 * g + 4, :], start=True, stop=True), reads=[Btm[b], wx], writes=[pst])
        P.dve(lambda e: e.tensor_tensor(out=htmp[:], in0=Hs[d][:], in1=bc_ap(etot_all[:, q, 8 * d:8 * d + 8], [[1, 8], [0, 64]]), op=ALU.mult), reads=[Hs[d], etot_all], writes=[htmp])
        P.dve(lambda e: e.tensor_tensor(out=Hs[d][:].rearrange("p h e -> p (h e)"), in0=pst[:, :], in1=htmp[:].rearrange("p h e -> p (h e)"), op=ALU.add), reads=[pst, htmp], writes=[Hs[d]])
        if not first_pass and (with_ctx or q >= 2):
            bb = it % 2
            P.pool(lambda e: e.tensor_tensor(out=tmp[:], in0=xt[b][:].rearrange("p (h e) -> p h e", h=8), in1=bc_ap(D_b[:], [[1, 8], [0, 64]]), op=ALU.mult), reads=[xt[b], D_b], writes=[tmp])
            P.pool(lambda e: e.tensor_tensor(out=yz[:], in0=yacc[:, q, :], in1=tmp[:].rearrange("p h e -> p (h e)"), op=ALU.add), reads=[yacc, tmp], writes=[yz])
            P.act(lambda e: e.activation(out=sz[:], in_=dz[b][:], func=AF.Silu), reads=[dz[b]], writes=[sz])
            P.dve(lambda e: e.tensor_tensor(out=yz[:], in0=yz[:], in1=sz[:], op=ALU.mult), reads=[yz, sz], writes=[yz])
            P.act(lambda e: e.activation(out=junk[:], in_=yz[:], func=AF.Square, accum_out=ssq[:, 0:1]), reads=[yz], writes=[junk, ssq])
            P.act(lambda e: e.activation(out=ssq[:], in_=ssq[:], func=AF.Sqrt, scale=1.0 / 512, bias=G.epsb[:, 0:1]), reads=[ssq, G.epsb], writes=[ssq])
            P.dve(lambda e: e.reciprocal(out=ssq[:], in_=ssq[:]), reads=[ssq], writes=[ssq])
            P.dve(lambda e: e.scalar_tensor_tensor(out=yz[:], in0=yz[:], scalar=ssq[:, 0:1], in1=nw_b[:], op0=ALU.mult, op1=ALU.mult), reads=[yz, ssq, nw_b], writes=[yz])
            ps = G.nextps()
            for c in range(4):
                P.pe(lambda e, ps=ps, c=c: e.transpose(out=ps[:, 128 * c:128 * c + 128], in_=yz[:, 128 * c:128 * c + 128], identity=G.ident[:]), reads=[yz, G.ident], writes=[ps])
            P.act(lambda e, ps=ps: e.activation(out=ytb[bb][:], in_=ps[:, :].rearrange("p (c t) -> p c t", c=4), func=AF.Copy), reads=[ps], writes=[ytb[bb]])
            P.dma(yT[6:10, :, tok:tok + 128].rearrange("c p t -> p c t"), ytb[bb][:], reads=[ytb[bb]], writes=[yT])

    steps = []
    it = 0
    for q in range(NT128):
        steps.append(chunk_pass(q, 0, it, True)); it += 1
    for q in [1, 0] + list(range(NT128 - 1, 1, -1)):
        steps.append(chunk_pass(q, 1, it, False)); it += 1
    return steps
```

```python
import contextlib
import math
import numpy as np
import concourse.bass as bass
import concourse.mybir as mybir
from concourse.bass_utils import run_bass_kernel_spmd

F32 = mybir.dt.float32
BF16 = mybir.dt.bfloat16
ALU = mybir.AluOpType
AF = mybir.ActivationFunctionType
AX = mybir.AxisListType

ENGS = ('pe', 'act', 'dve', 'pool', 'sp')
NDMASEM = 12
SAME_ENGINE_SYNC = True
BF_M = True
BF_D = True

D = 1024
SEQ = 4096
CTX = 256
S = SEQ + CTX
DEPTH = 2
GRID_W = 64
EPS = 1e-6
IN_TOTAL = 7712
FFN_H = 2816
O_MQ, O_MK, O_MV, O_MO, O_MI, O_MF, O_SU = 0, 256, 512, 768, 1024, 1032, 1040
O_NQ, O_NK, O_NV = 1296, 1552, 1808
O_DZ, O_DX, O_DB, O_DC, O_DDT, O_GATE = 2064, 2576, 3088, 3344, 3600, 3616
NT128 = S // 128
TT = [(0, 256)] + [(256 + 512 * i, 512) for i in range(8)]


class Res:
    __slots__ = ('name', 'lw', 'rd')

    def __init__(self, name=''):
        self.name = name
        self.lw = None
        self.rd = []


class Inst:
    __slots__ = ('eng', 'fn', 'dma', 'seq', 'deps', 'signal', 'idx', 'clock', 'dsem', 'dval', 'dmaid', 'emitted')

    def __init__(self, eng, fn, dma):
        self.eng = eng
        self.fn = fn
        self.dma = dma
        self.deps = []
        self.signal = False
        self.idx = None
        self.clock = None
        self.emitted = False


class Prog:
    def __init__(self, nc, es):
        self.nc = nc
        self.ins = {e: [] for e in ENGS}
        self.known = {e: {x: -1 for x in ENGS} for e in ENGS}
        self.known_dma = {e: set() for e in ENGS}
        self.ndma = {e: 0 for e in ENGS}
        self.dma_list = {e: [] for e in ENGS}
        self.all_dma = []
        self.sigcount = {e: 0 for e in ENGS}
        self.sem = {e: es.enter_context(nc.semaphore('s_' + e)) for e in ENGS}
        self.dsem = {}
        for e in ('sp', 'pool', 'act'):
            for k in range(NDMASEM):
                self.dsem[(e, k)] = es.enter_context(nc.semaphore('d_%s_%d' % (e, k)))
        self.pos = {e: 0 for e in ENGS}
        self.ninst = 0

    def op(self, eng, fn, reads=(), writes=(), dma=False):
        ins = Inst(eng, fn, dma)
        lst = self.ins[eng]
        ins.seq = len(lst)
        self.ninst += 1
        need = []
        for r in reads:
            r = getattr(r, 'r', r)
            if r.lw is not None:
                need.append(r.lw)
        for w in writes:
            w = getattr(w, 'r', w)
            if w.lw is not None:
                need.append(w.lw)
            need.extend(w.rd)
        if dma:
            j = self.ndma[eng]
            self.ndma[eng] += 1
            ins.dmaid = len(self.all_dma)
            self.all_dma.append(ins)
            ins.dsem = (eng, j % NDMASEM)
            ins.dval = 16 * (j // NDMASEM + 1)
            if j >= NDMASEM:
                need.append(self.dma_list[eng][j - NDMASEM])
            self.dma_list[eng].append(ins)
            ins.signal = True
        kn = self.known[eng]
        kd = self.known_dma[eng]
        deps = []
        for d in need:
            if d.dma:
                if d.dmaid in kd:
                    continue
                kd.add(d.dmaid)
                deps.append(d)
                for x, s in d.clock.items():
                    if s > kn[x]:
                        kn[x] = s
            else:
                if d.eng == eng and (eng == 'pe' or not SAME_ENGINE_SYNC):
                    continue
                if d.seq <= kn[d.eng]:
                    continue
                assert not d.emitted or d.signal, "dependency on already-emitted unsignalled inst"
                deps.append(d)
                d.signal = True
                kn[d.eng] = d.seq
                for x, s in d.clock.items():
                    if s > kn[x]:
                        kn[x] = s
        best = {}
        out = []
        for d in deps:
            if d.dma:
                out.append(d)
            elif d.eng not in best or best[d.eng].seq < d.seq:
                best[d.eng] = d
        out.extend(best.values())
        ins.deps = out
        ins.clock = dict(kn)
        lst.append(ins)
        for r in reads:
            r = getattr(r, 'r', r)
            r.rd.append(ins)
        for w in writes:
            w = getattr(w, 'r', w)
            w.lw = ins
            w.rd = []
        return ins

    def pe(self, fn, reads=(), writes=()):
        return self.op('pe', fn, reads, writes)

    def act(self, fn, reads=(), writes=()):
        return self.op('act', fn, reads, writes)

    def dve(self, fn, reads=(), writes=()):
        return self.op('dve', fn, reads, writes)

    def pool(self, fn, reads=(), writes=()):
        return self.op('pool', fn, reads, writes)

    def dmaq(self, q, out, in_, reads=(), writes=(), **kw):
        return self.op(q, lambda e: e.dma_start(out=out, in_=in_, **kw), reads, writes, dma=True)

    def dma(self, out, in_, reads=(), writes=(), **kw):
        return self.dmaq('sp', out, in_, reads, writes, **kw)

    def barrier(self):
        lasts = []
        for e in ENGS:
            for i in reversed(self.ins[e]):
                if not i.dma and i.fn is not None:
                    lasts.append(i)
                    break
        pend = []
        for e in ENGS:
            pend.extend(self.dma_list[e][-NDMASEM:])
        for e in ENGS:
            ins = Inst(e, None, False)
            ins.seq = len(self.ins[e])
            kn = self.known[e]
            kd = self.known_dma[e]
            for d in lasts:
                if d.seq <= kn[d.eng]:
                    continue
                assert not d.emitted or d.signal
                ins.deps.append(d)
                d.signal = True
                kn[d.eng] = d.seq
            for d in pend:
                if d.dmaid in kd:
                    continue
                kd.add(d.dmaid)
                ins.deps.append(d)
            ins.clock = dict(kn)
            self.ins[e].append(ins)

    def flush(self, final_wait=()):
        self.barrier()
        nc = self.nc
        for e in ENGS:
            for i in self.ins[e][self.pos[e]:]:
                if i.signal and not i.dma:
                    self.sigcount[e] += 1
                    i.idx = self.sigcount[e]
        sem, dsem = self.sem, self.dsem

        def run(e, eng):
            for i in self.ins[e][self.pos[e]:]:
                for d in i.deps:
                    if d.dma:
                        eng.wait_ge(dsem[d.dsem], d.dval)
                    else:
                        eng.wait_ge(sem[d.eng], d.idx)
                i.emitted = True
                if i.fn is None:
                    continue
                bi = i.fn(eng)
                if i.dma:
                    bi.then_inc(dsem[i.dsem], 16)
                elif i.signal:
                    bi.then_inc(sem[e], 1)
            if e == 'sp':
                for d in final_wait:
                    eng.wait_ge(dsem[d.dsem], d.dval)
            self.pos[e] = len(self.ins[e])

        with nc.Block() as block:
            @block.tensor
            def _(eng):
                run('pe', eng)

            @block.scalar
            def _(eng):
                run('act', eng)

            @block.vector
            def _(eng):
                run('dve', eng)

            @block.gpsimd
            def _(eng):
                run('pool', eng)

            @block.sync
            def _(eng):
                run('sp', eng)


class Buf:
    __slots__ = ('t', 'r')

    def __init__(self, t, name=''):
        self.t = t
        self.r = Res(name)

    def __getitem__(self, k):
        return self.t[k]


class Ctx:
    pass


def build(debug_outs=(), stop_after=None):
    nc = bass.Bass("TRN2", target_bir_lowering=False)
    top = contextlib.ExitStack()
    G = Ctx()
    G.nc = nc
    G.dbg = set(debug_outs)
    with top:
        P = Prog(nc, top)
        G.P = P

        def din(name, shape):
            return nc.dram_tensor(name, list(shape), F32, kind="ExternalInput").ap()

        I = {}
        I['x'] = din('x', [SEQ, D]); I['ctx'] = din('ctx', [CTX, D])
        I['c'] = din('c', [1, D]); I['c_ctx'] = din('c_ctx', [1, D])
        for nm, shp in WEIGHT_SHAPES:
            I[nm] = din(nm, shp)
        G.I = I
        G.out = nc.dram_tensor('out', [SEQ, D], F32, kind="ExternalOutput").ap()
        G.scr = {}

        def scratch(name, shape, dt=F32):
            kind = "ExternalOutput" if name in G.dbg else "Internal"
            b = Buf(nc.dram_tensor(name, list(shape), dt, kind=kind).ap(), name)
            G.scr[name] = b
            return b
        G.scratch = scratch

        G.uid = 0

        def sb(es, name, shape, dt=F32):
            G.uid += 1
            return Buf(es.enter_context(nc.sbuf_tensor('%s_%d' % (name, G.uid), list(shape), dt)), name)
        G.sb = sb
        G.ps = [Buf(top.enter_context(nc.psum_tensor('ps%d' % i, [128, 512], F32)), 'ps%d' % i) for i in range(8)]
        G.psi = 0

        def nextps():
            b = G.ps[G.psi % 8]
            G.psi += 1
            return b
        G.nextps = nextps

        stages = [('consts', stage_consts), ('adaln', stage_adaln), ('load', stage_load)]
        for l in range(DEPTH):
            stages += [('proj%d' % l, lambda G, l=l: stage_norm_proj(G, l))]
            stages += STAGES_AFTER_PROJ(l)
        for nm, st in stages:
            st(G)
            P.flush()
            if stop_after is not None and nm == stop_after:
                break
        P.flush(final_wait=P.all_dma[-3 * NDMASEM:])
        G.keep.close()
    return nc


WEIGHT_SHAPES = [
    ('ada_w', [2, 1024, 6144]), ('ada_b', [2, 6144]), ('norm1_w', [2, 1024]), ('norm2_w', [2, 1024]),
    ('w_in', [2, 1024, 7712]), ('mlstm_conv_w', [2, 7, 512]), ('mlstm_conv_b', [2, 512]),
    ('mlstm_ib', [2, 2, 4]), ('mlstm_fb', [2, 2, 4]), ('mlstm_norm_w', [2, 256]),
    ('s5_lam_re', [2, 2, 16, 64]), ('s5_lam_im', [2, 2, 16, 64]), ('s5_log_dt', [2, 2, 16]),
    ('s5_b_re', [2, 16, 64, 16]), ('s5_b_im', [2, 16, 64, 16]), ('s5_c_re', [2, 16, 16, 64]),
    ('s5_c_im', [2, 16, 16, 64]), ('s5_d', [2, 256]), ('s5_glu_w', [2, 256, 512]), ('na_rpb', [2, 4, 15, 31]),
    ('ssd_conv_w', [2, 7, 1024]), ('ssd_conv_b', [2, 1024]), ('ssd_a_log', [2, 2, 8]), ('ssd_dt_bias', [2, 2, 8]),
    ('ssd_d', [2, 8]), ('ssd_norm_w', [2, 512]), ('w_branch_a', [2, 256, 1024]), ('w_branch_b', [2, 256, 1024]),
    ('w_branch_c', [2, 256, 1024]), ('w_branch_d', [2, 512, 1024]), ('w_out', [2, 1024, 1024]),
    ('ffn_w_in', [2, 1024, 5632]), ('ffn_w_out', [2, 2816, 1024]), ('final_norm_w', [1024]),
]


def stage_consts(G):
    nc, P = G.nc, G.P
    top = contextlib.ExitStack()
    G.keep = top
    sb = G.sb
    G.ones = sb(top, 'ones', [128, 128]); G.ident = sb(top, 'ident', [128, 128]); G.identb = sb(top, 'identb', [128, 128], BF16)
    G.triU = sb(top, 'triU', [128, 128]); G.triL = sb(top, 'triL', [128, 128])
    G.onesb = sb(top, 'onesb', [128, 128], BF16)
    P.pool(lambda e: e.memset(G.ones[:], 1.0), writes=[G.ones])
    P.pool(lambda e: e.affine_select(out=G.ident[:], in_=G.ones[:], pattern=[[-1, 128]], compare_op=ALU.is_equal,
                                     fill=0.0, base=0, channel_multiplier=1), reads=[G.ones], writes=[G.ident])
    P.pool(lambda e: e.affine_select(out=G.triU[:], in_=G.ones[:], pattern=[[1, 128]], compare_op=ALU.is_ge,
                                     fill=0.0, base=0, channel_multiplier=-1), reads=[G.ones], writes=[G.triU])
    P.pool(lambda e: e.affine_select(out=G.triL[:], in_=G.ones[:], pattern=[[-1, 128]], compare_op=ALU.is_ge,
                                     fill=0.0, base=0, channel_multiplier=1), reads=[G.ones], writes=[G.triL])
    P.dve(lambda e: e.tensor_copy(out=G.identb[:], in_=G.ident[:]), reads=[G.ident], writes=[G.identb])
    P.dve(lambda e: e.tensor_copy(out=G.onesb[:], in_=G.ones[:]), reads=[G.ones], writes=[G.onesb])
    G.cm = sb(top, 'cm', [128, DEPTH, 6, 8, 2])
    G.epsb = sb(top, 'epsb', [128, 1])
    P.pool(lambda e: e.memset(G.epsb[:], EPS), writes=[G.epsb])
    G.scratch('xsT', [8, 128, S])
    G.scratch('hxT', [8, 128, S], BF16)


def stage_adaln(G):
    nc, P, I = G.nc, G.P, G.I
    with contextlib.ExitStack() as es:
        sb = G.sb
        c2 = sb(es, 'c2', [128, 8, 2]); sc2 = sb(es, 'sc2', [128, 8, 2])
        P.dma(c2[:, :, 0], I['c'][0, :].rearrange("(k p) -> p k", p=128), writes=[c2], allow_slow_non_contiguous=True)
        P.dma(c2[:, :, 1], I['c_ctx'][0, :].rearrange("(k p) -> p k", p=128), writes=[c2], allow_slow_non_contiguous=True)
        P.act(lambda e: e.activation(out=sc2[:], in_=c2[:], func=AF.Silu), reads=[c2], writes=[sc2])
        wt = [sb(es, 'adaw%d' % i, [128, 8, 512]) for i in range(3)]
        bias = sb(es, 'adab', [128, DEPTH, 48]); nw = sb(es, 'nw', [128, DEPTH, 2, 8])
        modtm = sb(es, 'modtm', [2, 6144]); cmraw = sb(es, 'cmraw', [128, 48, 2])
        modT = G.scratch('modT', [DEPTH, 2, 6144])
        for l in range(DEPTH):
            P.dma(bias[:, l, :], I['ada_b'][l, :].rearrange("(j p) -> p j", p=128), writes=[bias], allow_slow_non_contiguous=True)
            P.dma(nw[:, l, 0, :], I['norm1_w'][l, :].rearrange("(k p) -> p k", p=128), writes=[nw], allow_slow_non_contiguous=True)
            P.dma(nw[:, l, 1, :], I['norm2_w'][l, :].rearrange("(k p) -> p k", p=128), writes=[nw], allow_slow_non_contiguous=True)
        n = 0
        for l in range(DEPTH):
            for c in range(12):
                w = wt[n % 3]; n += 1
                P.dma(w[:], I['ada_w'][l, :, c * 512:(c + 1) * 512].rearrange("(k p) n -> p k n", p=128), writes=[w])
                ps = G.nextps()
                for k in range(8):
                    P.pe(lambda e, w=w, k=k, ps=ps: e.matmul(ps[0:2, :], lhsT=sc2[:, k, :], rhs=w[:, k, :], start=(k == 0), stop=(k == 7)), reads=[w, sc2], writes=[ps])
                P.act(lambda e, ps=ps, c=c: e.activation(out=modtm[:, c * 512:(c + 1) * 512], in_=ps[0:2, :], func=AF.Copy), reads=[ps], writes=[modtm])
            P.dma(modT[l], modtm[:], reads=[modtm], writes=[modT])
            for s in range(2):
                P.dma(cmraw[:, :, s], modT[l, s, :].rearrange("(j p) -> p j", p=128), reads=[modT], writes=[cmraw], allow_slow_non_contiguous=True)
            for s in range(2):
                P.dve(lambda e, l=l, s=s: e.tensor_tensor(
                    out=G.cm[:, l, :, :, s], in0=cmraw[:, :, s].rearrange("p (w k) -> p w k", w=6),
                    in1=bias[:, l, :].rearrange("p (w k) -> p w k", w=6), op=ALU.add), reads=[cmraw, bias], writes=[G.cm])
            for (wi, ni) in ((1, 0), (4, 1)):
                for s in range(2):
                    P.dve(lambda e, l=l, s=s, wi=wi, ni=ni: e.scalar_tensor_tensor(
                        out=G.cm[:, l, wi, :, s], in0=G.cm[:, l, wi, :, s], scalar=1.0, in1=nw[:, l, ni, :],
                        op0=ALU.add, op1=ALU.mult), reads=[G.cm, nw], writes=[G.cm])
        if 'cm_dbg' in G.dbg:
            d = G.scratch('cm_dbg', [128, DEPTH * 6 * 8 * 2])
            P.dma(d[:], G.cm[:].rearrange("p l w k s -> p (l w k s)"), reads=[G.cm], writes=[d])
        P.flush()


def stage_load(G):
    nc, P, I = G.nc, G.P, G.I
    xsT = G.scr['xsT']
    with contextlib.ExitStack() as es:
        sb = G.sb
        xin = [sb(es, 'xin%d' % i, [128, D]) for i in range(3)]
        xo = [sb(es, 'xo%d' % i, [128, 8, 512]) for i in range(2)]
        ti = 0
        for gi, (t0, n) in enumerate(TT):
            o = xo[gi % 2]
            for q in range(n // 128):
                xi = xin[ti % 3]; ti += 1
                tok = t0 + q * 128
                src = I['ctx'][tok:tok + 128, :] if tok < CTX else I['x'][tok - CTX:tok - CTX + 128, :]
                P.dma(xi[:], src, writes=[xi])
                for half in range(2):
                    ps = G.nextps()
                    for kk in range(4):
                        k = half * 4 + kk
                        P.pe(lambda e, ps=ps, kk=kk, k=k, xi=xi: e.transpose(out=ps[:, kk * 128:(kk + 1) * 128], in_=xi[:, k * 128:(k + 1) * 128],
                                                                          identity=G.ident[:]), reads=[xi, G.ident], writes=[ps])
                    eng = P.act if half == 0 else P.dve
                    if half == 0:
                        P.act(lambda e, ps=ps, o=o, q=q: e.activation(out=o[:, 0:4, q * 128:(q + 1) * 128], in_=ps[:].rearrange("p (k t) -> p k t", k=4), func=AF.Copy),
                              reads=[ps], writes=[o])
                    else:
                        P.dve(lambda e, ps=ps, o=o, q=q: e.tensor_copy(out=o[:, 4:8, q * 128:(q + 1) * 128], in_=ps[:].rearrange("p (k t) -> p k t", k=4)),
                              reads=[ps], writes=[o])
            P.dma(xsT[:, :, t0:t0 + n].rearrange("k p t -> p k t"), o[:, :, 0:n], reads=[o], writes=[xsT])
        P.flush()


def kernel(**inputs):
    n = 8
    nc = build()
    in_maps = []
    w = {nm: np.ascontiguousarray(inputs[nm], dtype=np.float32) for nm, _ in WEIGHT_SHAPES}
    for b in range(n):
        m = dict(w)
        m['x'] = np.ascontiguousarray(inputs['x'][b], dtype=np.float32)
        m['ctx'] = np.ascontiguousarray(inputs['ctx'][b], dtype=np.float32)
        m['c'] = np.ascontiguousarray(inputs['c'][b:b + 1], dtype=np.float32)
        m['c_ctx'] = np.ascontiguousarray(inputs['c_ctx'][None, :], dtype=np.float32)
        in_maps.append(m)
    res = run_bass_kernel_spmd(nc, in_maps, core_ids=list(range(n)))
    return np.stack([r['out'] for r in res.results], axis=0)


def STAGES_AFTER_PROJ(l):
    return [('na%d' % l, lambda G, l=l: stage_mixers(G, l)), ('s5%d' % l, lambda G, l=l: stage_s5(G, l)), ('tail%d' % l, lambda G, l=l: stage_tail(G, l))]


def bc_ap(ap, pattern):
    return bass.AP(tensor=ap.tensor, offset=ap.offset, ap=[list(ap.ap[0])] + [list(p) for p in pattern])


def emit_norm(G, xt, n, l, which_gam, s, out_fn, out_res, tmp_bufs):
    P = G.P
    sq, rstd, tmp = tmp_bufs
    ps = G.nextps()
    P.act(lambda e: e.activation(out=sq[:, :, 0:n], in_=xt[:, :, 0:n], func=AF.Square), reads=[xt], writes=[sq])
    for k in range(8):
        P.pe(lambda e, k=k: e.matmul(ps[:, 0:n], lhsT=G.onesb[:], rhs=sq[:, k, 0:n], start=(k == 0), stop=(k == 7)),
             reads=[sq, G.onesb], writes=[ps])
    P.act(lambda e: e.activation(out=rstd[:, 0:n], in_=ps[:, 0:n], func=AF.Sqrt, scale=1.0 / D, bias=G.epsb[:, 0:1]),
          reads=[ps, G.epsb], writes=[rstd])
    P.dve(lambda e: e.reciprocal(out=rstd[:, 0:n], in_=rstd[:, 0:n]), reads=[rstd], writes=[rstd])
    for k in range(8):
        t = tmp[k % len(tmp)]
        P.dve(lambda e, k=k, t=t: e.tensor_tensor(out=t[:, 0:n], in0=xt[:, k, 0:n], in1=rstd[:, 0:n], op=ALU.mult),
              reads=[xt, rstd], writes=[t])
        P.act(lambda e, k=k, t=t: e.activation(out=out_fn(k), in_=t[:, 0:n], func=AF.Identity,
                                               scale=G.cm[:, l, which_gam, k, s:s + 1], bias=G.cm[:, l, which_gam - 1, k, s:s + 1]),
              reads=[t, G.cm], writes=[out_res])


def stage_norm_proj(G, l):
    nc, P, I = G.nc, G.P, G.I
    sb = G.sb
    xsT = G.scr['xsT']
    hxT_d = G.scr['hxT']
    sc = G.scr
    if l == 0:
        dM = BF16 if BF_M else F32; dD = BF16 if BF_D else F32
        G.scratch('mqT', [2, 128, S], dM); G.scratch('mkT', [2, 128, S], dM); G.scratch('mkTM', [S, 256], dM); G.scratch('mvTM', [S, 256], dM)
        G.scratch('TM1', [S, 784]); G.scratch('nvTM', [S, 256], BF16); G.scratch('dzTM', [S, 512]); G.scratch('ddtTM', [S, 16])
        G.scratch('nqT', [2, 128, S], BF16); G.scratch('nkT', [2, 128, S], BF16)
        G.scratch('xTM', [S, 512], dD); G.scratch('BT', [2, 128, S], dD); G.scratch('BTM', [S, 256], dD); G.scratch('CT', [2, 128, S], dD)
    with contextlib.ExitStack() as es:
        hxT = sb(es, 'hxT_sb', [128, 8, S], BF16)
        hres = [Res('hx%d' % i) for i in range(len(TT))]
        with contextlib.ExitStack() as es2:
            xt = [sb(es2, 'nxt%d' % i, [128, 8, 512]) for i in range(3)]
            sq = [sb(es2, 'nsq%d' % i, [128, 8, 512], BF16) for i in range(2)]; rstd = [sb(es2, 'nrstd%d' % i, [128, 512]) for i in range(2)]
            tmp = [sb(es2, 'ntmp%d' % i, [128, 512]) for i in range(4)]
            for ti, (t0, n) in enumerate(TT):
                x = xt[ti % 3]
                P.dma(x[:, :, 0:n], xsT[:, :, t0:t0 + n].rearrange("k p t -> p k t"), reads=[xsT], writes=[x])
                emit_norm(G, x, n, l, 1, 1 if ti == 0 else 0, lambda k, t0=t0, n=n: hxT[:, k, t0:t0 + n], hres[ti], (sq[ti % 2], rstd[ti % 2], tmp))
                P.dma(hxT_d[:, :, t0:t0 + n].rearrange("k p t -> p k t"), hxT[:, :, t0:t0 + n], reads=[hres[ti]], writes=[hxT_d])
            P.flush()
        tmgroups = [(O_MV, 512), (O_MI, 272), (O_NV, 256), (O_DZ, 512), (O_DDT, 16)]
        with contextlib.ExitStack() as es2:
            wtm = sb(es2, 'wtm', [128, 8, 1568], BF16)
            off = 0
            offs = []
            for (c0, n) in tmgroups:
                P.dmaq('pool', wtm[:, :, off:off + n], I['w_in'][l, :, c0:c0 + n].rearrange("(k p) n -> p k n", p=128), writes=[wtm])
                offs.append(off)
                off += n
            st = [sb(es2, 'tmst%d' % i, [128, 1312]) for i in range(2)]
            stb = [sb(es2, 'tmstb%d' % i, [128, 256], BF16) for i in range(2)]
            stm = [sb(es2, 'tmstm%d' % i, [128, 256], BF16 if BF_M else F32) for i in range(2)]
            for q in range(NT128):
                tok = q * 128
                ti = 0 if tok < CTX else 1 + (tok - CTX) // 512
                s_, sb_ = st[q % 2], stb[q % 2]
                for gi, (c0, n) in enumerate(tmgroups):
                    ps = G.nextps()
                    for k in range(8):
                        P.pe(lambda e, ps=ps, k=k, n=n, o=offs[gi], tok=tok: e.matmul(ps[:, 0:n], lhsT=hxT[:, k, tok:tok + 128], rhs=wtm[:, k, o:o + n],
                                                                                     start=(k == 0), stop=(k == 7)), reads=[hres[ti], wtm], writes=[ps])
                    if gi == 2:
                        P.act(lambda e, ps=ps, sb_=sb_: e.activation(out=sb_[:], in_=ps[:, 0:256], func=AF.Copy), reads=[ps], writes=[sb_])
                    else:
                        so = {0: 0, 1: 512, 3: 784, 4: 1296}[gi]
                        if gi % 2 == 0:
                            P.dve(lambda e, ps=ps, s_=s_, so=so, n=n: e.tensor_copy(out=s_[:, so:so + n], in_=ps[:, 0:n]), reads=[ps], writes=[s_])
                        else:
                            P.act(lambda e, ps=ps, s_=s_, so=so, n=n: e.activation(out=s_[:, so:so + n], in_=ps[:, 0:n], func=AF.Copy), reads=[ps], writes=[s_])
                sm_ = stm[q % 2]
                P.act(lambda e, s_=s_, sm_=sm_: e.activation(out=sm_[:], in_=s_[:, 0:256], func=AF.Copy), reads=[s_], writes=[sm_])
                P.dma(sc['mvTM'][tok:tok + 128, :], sm_[:], reads=[sm_], writes=[sc['mvTM']])
                P.dma(sc['TM1'][tok:tok + 128, :], s_[:, 0:784], reads=[s_], writes=[sc['TM1']])
                P.dma(sc['dzTM'][tok:tok + 128, :], s_[:, 784:1296], reads=[s_], writes=[sc['dzTM']])
                P.dma(sc['ddtTM'][tok:tok + 128, :], s_[:, 1296:1312], reads=[s_], writes=[sc['ddtTM']])
                P.dma(sc['nvTM'][tok:tok + 128, :], sb_[:], reads=[sb_], writes=[sc['nvTM']])
            P.flush()
        with contextlib.ExitStack() as es2:
            PADL = S + 12
            XOFF = 265
            rowbuf = [sb(es2, 'rowbuf%d' % i, [128, PADL], BF16) for i in range(2)]
            dgt = [sb(es2, 'dgt%d' % i, [128, 7, 128], BF16) for i in range(2)]
            cacc = sb(es2, 'cacc', [128, S])
            cout = [sb(es2, 'cout%d' % i, [128, S]) for i in range(1)]
            wfm = [sb(es2, 'wfm%d' % i, [128, 8, 128], BF16) for i in range(2)]
            cw_m = sb(es2, 'cw_m', [128, 4, 7]); cb_m = sb(es2, 'cb_m', [128, 4])
            cw_d = sb(es2, 'cw_d', [128, 8, 7]); cb_d = sb(es2, 'cb_d', [128, 8])
            for c in range(4):
                P.dma(cw_m[:, c, :], I['mlstm_conv_w'][l, :, c * 128:(c + 1) * 128].rearrange("j p -> p j"), writes=[cw_m], allow_slow_non_contiguous=True)
            P.dma(cb_m[:], I['mlstm_conv_b'][l].rearrange("(c p) -> p c", p=128), writes=[cb_m], allow_slow_non_contiguous=True)
            for c in range(8):
                P.dma(cw_d[:, c, :], I['ssd_conv_w'][l, :, c * 128:(c + 1) * 128].rearrange("j p -> p j"), writes=[cw_d], allow_slow_non_contiguous=True)
            P.dma(cb_d[:], I['ssd_conv_b'][l].rearrange("(c p) -> p c", p=128), writes=[cb_d], allow_slow_non_contiguous=True)
            for rb in rowbuf:
                P.pool(lambda e, rb=rb: e.memset(rb[:], 0.0), writes=[rb])
            cosT = sb(es2, 'cosT', [128, SEQ]); sinT = sb(es2, 'sinT', [128, SEQ]); perm = sb(es2, 'perm', [128, 128])
            build_rope_tables(G, es2, cosT, sinT, perm)
            trst = [sb(es2, 'trst%d' % i, [128, 4, 128], BF16) for i in range(2)]
            trst32 = [sb(es2, 'trst32%d' % i, [128, 4, 128]) for i in range(2)]
            cbf = [sb(es2, 'cbf%d' % i, [128, S], BF16) for i in range(1)]
            chunks = [(O_MQ + 128 * i, 'mq', i) for i in range(2)] + [(O_MK + 128 * i, 'mk', i) for i in range(2)]
            chunks += [(O_NQ + 128 * i, 'nq', i) for i in range(2)] + [(O_NK + 128 * i, 'nk', i) for i in range(2)]
            chunks += [(O_DX + 128 * i, 'dx', i) for i in range(4)] + [(O_DB + 128 * i, 'dB', i) for i in range(2)]
            chunks += [(O_DC + 128 * i, 'dC', i) for i in range(2)]
            trn = 0
            pc_it = precast_iter(G, l)
            for ci, (c0, kind, idx) in enumerate(chunks):
                w = wfm[ci % 2]; rb = rowbuf[ci % 2]; co = cout[0]; dg = dgt[ci % 2]
                P.dmaq('pool', w[:], I['w_in'][l, :, c0:c0 + 128].rearrange("(k p) n -> p k n", p=128), writes=[w])
                if ci >= 1:
                    for _ in range(12):
                        next(pc_it, None)
                for ti, (t0, n) in enumerate(TT):
                    ps = G.nextps()
                    for k in range(8):
                        P.pe(lambda e, ps=ps, k=k, n=n, t0=t0, w=w: e.matmul(ps[:, 0:n], lhsT=w[:, k, :], rhs=hxT[:, k, t0:t0 + n], start=(k == 0), stop=(k == 7)),
                             reads=[hres[ti], w], writes=[ps])
                    if kind in ('nq', 'nk'):
                        dst = cbf[0]
                        scale = 0.125 if kind == 'nq' else 1.0
                        P.act(lambda e, ps=ps, n=n, t0=t0, dst=dst, scale=scale: e.activation(out=dst[:, t0:t0 + n], in_=ps[:, 0:n], func=AF.Copy, scale=scale),
                              reads=[ps], writes=[dst])
                    else:
                        o0 = 3 if ti == 0 else XOFF + (t0 - CTX)
                        if ti % 2 == 0:
                            P.act(lambda e, ps=ps, n=n, o0=o0, rb=rb: e.activation(out=rb[:, o0:o0 + n], in_=ps[:, 0:n], func=AF.Copy), reads=[ps], writes=[rb])
                        else:
                            P.dve(lambda e, ps=ps, n=n, o0=o0, rb=rb: e.tensor_copy(out=rb[:, o0:o0 + n], in_=ps[:, 0:n]), reads=[ps], writes=[rb])
                if kind in ('nq', 'nk'):
                    dst = cbf[0]
                    P.dma(sc['nqT' if kind == 'nq' else 'nkT'][idx], dst[:], reads=[dst], writes=[sc['nqT' if kind == 'nq' else 'nkT']])
                    continue
                if kind in ('mq', 'mk'):
                    cw, cb, cidx = cw_m, cb_m, (idx if kind == 'mq' else 2 + idx)
                else:
                    cw, cb, cidx = cw_d, cb_d, {'dx': idx, 'dB': 4 + idx, 'dC': 6 + idx}[kind]
                for j in range(7):
                    P.dve(lambda e, j=j, dg=dg, cw=cw, cidx=cidx: e.tensor_scalar(out=dg[:, j, :], in0=G.ident[:], scalar1=cw[:, cidx, j:j + 1], scalar2=None, op0=ALU.mult),
                          reads=[G.ident, cw], writes=[dg])
                segs = [(3, CTX, 0)] + [(XOFF + 512 * t, 512, CTX + 512 * t) for t in range(8)]
                for si, (o0, n, d0) in enumerate(segs):
                    ps = G.nextps()
                    for j in range(7):
                        P.pe(lambda e, ps=ps, j=j, o0=o0, n=n, dg=dg, rb=rb: e.matmul(ps[:, 0:n], lhsT=dg[:, j, :], rhs=rb[:, o0 - 3 + j:o0 - 3 + j + n], start=(j == 0), stop=(j == 6)),
                             reads=[dg, rb], writes=[ps])
                    P.act(lambda e, ps=ps, co=co, cb=cb, cidx=cidx, n=n, d0=d0: e.activation(out=co[:, d0:d0 + n], in_=ps[:, 0:n], func=AF.Silu, bias=cb[:, cidx:cidx + 1]),
                          reads=[ps, cb], writes=[co])
                if kind in ('mq', 'mk'):
                    for t in range(8):
                        t0 = CTX + 512 * t
                        ps = G.nextps()
                        P.pe(lambda e, ps=ps, t0=t0, co=co: e.matmul(ps[:, :], lhsT=perm[:], rhs=co[:, t0:t0 + 512], start=True, stop=True),
                             reads=[perm, co], writes=[ps])
                        P.dve(lambda e, ps=ps, t0=t0: e.tensor_tensor(out=cacc[:, t0:t0 + 512], in0=ps[:, :], in1=sinT[:, t0 - CTX:t0 - CTX + 512], op=ALU.mult),
                              reads=[ps, sinT], writes=[cacc])
                    P.pool(lambda e, co=co: e.tensor_tensor(out=co[:, CTX:S], in0=co[:, CTX:S], in1=cosT[:], op=ALU.mult), reads=[co, cosT], writes=[co])
                    P.dve(lambda e, co=co: e.tensor_tensor(out=co[:, CTX:S], in0=co[:, CTX:S], in1=cacc[:, CTX:S], op=ALU.add), reads=[co, cacc], writes=[co])
                    if kind == 'mq':
                        P.act(lambda e, co=co: e.activation(out=co[:], in_=co[:], func=AF.Copy, scale=0.125), reads=[co], writes=[co])
                name = {'mq': 'mqT', 'mk': 'mkT', 'dB': 'BT', 'dC': 'CT'}.get(kind)
                use16 = BF_M if kind in ('mq', 'mk') else BF_D
                if name is not None and not use16:
                    P.dma(sc[name][idx], co[:], reads=[co], writes=[sc[name]])
                elif name is not None:
                    c16 = cbf[0]
                    P.act(lambda e, co=co, c16=c16: e.activation(out=c16[:], in_=co[:], func=AF.Copy), reads=[co], writes=[c16])
                    P.dma(sc[name][idx], c16[:], reads=[c16], writes=[sc[name]])
                tmname = {'mk': 'mkTM', 'dx': 'xTM', 'dB': 'BTM'}.get(kind)
                if tmname is not None:
                    for g4 in range(0, NT128, 4):
                        nq = min(4, NT128 - g4)
                        ps = G.nextps()
                        tb = (trst if use16 else trst32)[trn % 2]; trn += 1
                        for qq in range(nq):
                            tok = (g4 + qq) * 128
                            P.pe(lambda e, ps=ps, qq=qq, tok=tok, co=co: e.transpose(out=ps[:, qq * 128:(qq + 1) * 128], in_=co[:, tok:tok + 128], identity=G.ident[:]),
                                 reads=[co, G.ident], writes=[ps])
                        P.act(lambda e, ps=ps, tb=tb, nq=nq: e.activation(out=tb[:, 0:nq, :], in_=ps[:, 0:nq * 128].rearrange("p (q c) -> p q c", q=nq), func=AF.Copy),
                              reads=[ps], writes=[tb])
                        P.dma(sc[tmname][g4 * 128:(g4 + nq) * 128, idx * 128:(idx + 1) * 128].rearrange("(q p) c -> p q c", p=128), tb[:, 0:nq, :],
                              reads=[tb], writes=[sc[tmname]])
            for _ in pc_it:
                pass
            P.flush()


def build_rope_tables(G, es, cosT, sinT, perm):
    nc, P = G.nc, G.P
    sb = G.sb
    I32 = mybir.dt.int32
    pi_i = sb(es, 'rp_pi', [128, 1], I32); pf = sb(es, 'rp_pf', [128, 1]); inv = sb(es, 'rp_inv', [128, 1])
    mcol = sb(es, 'rp_mcol', [128, 1]); mrow = sb(es, 'rp_mrow', [128, 1]); t1 = sb(es, 'rp_t1', [128, 1])
    posf = sb(es, 'rp_pos', [128, 64]); ang = sb(es, 'rp_ang', [128, 64]); nn = sb(es, 'rp_n', [128, 64]); ni = sb(es, 'rp_ni', [128, 64], I32)
    cs = sb(es, 'rp_cs', [128, 64]); sn = sb(es, 'rp_sn', [128, 64]); fix = sb(es, 'rp_fix', [128, 64])
    csr = sb(es, 'rp_csr', [128, 64]); csc = sb(es, 'rp_csc', [128, 64]); snr = sb(es, 'rp_snr', [128, 64]); snc = sb(es, 'rp_snc', [128, 64])
    P.pool(lambda e: e.iota(pi_i[:], pattern=[[0, 1]], base=0, channel_multiplier=1), writes=[pi_i])
    P.dve(lambda e: e.tensor_copy(out=pf[:], in_=pi_i[:]), reads=[pi_i], writes=[pf])
    P.dve(lambda e: e.tensor_single_scalar(out=pi_i[:], in_=pi_i[:], scalar=15, op=ALU.bitwise_and), reads=[pi_i], writes=[pi_i])
    P.dve(lambda e: e.tensor_copy(out=inv[:], in_=pi_i[:]), reads=[pi_i], writes=[inv])
    P.act(lambda e: e.activation(out=inv[:], in_=inv[:], func=AF.Exp, scale=-math.log(10000.0) / 16.0), reads=[inv], writes=[inv])
    P.dve(lambda e: e.tensor_single_scalar(out=mcol[:], in_=pf[:], scalar=32.0, op=ALU.is_ge), reads=[pf], writes=[mcol])
    P.dve(lambda e: e.tensor_single_scalar(out=t1[:], in_=pf[:], scalar=64.0, op=ALU.is_ge), reads=[pf], writes=[t1])
    P.dve(lambda e: e.tensor_tensor(out=mcol[:], in0=mcol[:], in1=t1[:], op=ALU.subtract), reads=[mcol, t1], writes=[mcol])
    P.dve(lambda e: e.tensor_single_scalar(out=t1[:], in_=pf[:], scalar=96.0, op=ALU.is_ge), reads=[pf], writes=[t1])
    P.dve(lambda e: e.tensor_tensor(out=mcol[:], in0=mcol[:], in1=t1[:], op=ALU.add), reads=[mcol, t1], writes=[mcol])
    P.dve(lambda e: e.tensor_scalar(out=mrow[:], in0=mcol[:], scalar1=-1.0, scalar2=1.0, op0=ALU.mult, op1=ALU.add), reads=[mcol], writes=[mrow])
    P.pool(lambda e: e.iota(posf[:], pattern=[[1, 64]], base=0, channel_multiplier=0, allow_small_or_imprecise_dtypes=True), writes=[posf])
    P.dve(lambda e: e.tensor_scalar(out=ang[:], in0=posf[:], scalar1=inv[:, 0:1], scalar2=None, op0=ALU.mult), reads=[posf, inv], writes=[ang])

    def sincos(dst, shift):
        P.dve(lambda e: e.tensor_scalar(out=nn[:], in0=ang[:], scalar1=shift, scalar2=1.0 / (2 * math.pi), op0=ALU.add, op1=ALU.mult), reads=[ang], writes=[nn])
        P.dve(lambda e: e.tensor_copy(out=ni[:], in_=nn[:]), reads=[nn], writes=[ni])
        P.dve(lambda e: e.tensor_copy(out=nn[:], in_=ni[:]), reads=[ni], writes=[nn])
        P.dve(lambda e: e.scalar_tensor_tensor(out=nn[:], in0=nn[:], scalar=-2 * math.pi, in1=ang[:], op0=ALU.mult, op1=ALU.add), reads=[nn, ang], writes=[nn])
        P.dve(lambda e: e.tensor_scalar(out=nn[:], in0=nn[:], scalar1=shift, scalar2=None, op0=ALU.add), reads=[nn], writes=[nn])
        P.dve(lambda e: e.tensor_scalar(out=fix[:], in0=nn[:], scalar1=math.pi, scalar2=-2 * math.pi, op0=ALU.is_gt, op1=ALU.mult), reads=[nn], writes=[fix])
        P.dve(lambda e: e.tensor_tensor(out=nn[:], in0=nn[:], in1=fix[:], op=ALU.add), reads=[nn, fix], writes=[nn])
        P.dve(lambda e: e.tensor_scalar(out=fix[:], in0=nn[:], scalar1=-math.pi, scalar2=2 * math.pi, op0=ALU.is_lt, op1=ALU.mult), reads=[nn], writes=[fix])
        P.dve(lambda e: e.tensor_tensor(out=nn[:], in0=nn[:], in1=fix[:], op=ALU.add), reads=[nn, fix], writes=[nn])
        P.act(lambda e: e.activation(out=dst[:], in_=nn[:], func=AF.Sin), reads=[nn], writes=[dst])
    sincos(sn, 0.0)
    sincos(cs, math.pi / 2)
    for (src, a, b) in ((cs, csr, csc), (sn, snr, snc)):
        P.dve(lambda e, src=src, a=a: e.tensor_scalar(out=a[:], in0=src[:], scalar1=mrow[:, 0:1], scalar2=None, op0=ALU.mult), reads=[src, mrow], writes=[a])
        P.dve(lambda e, src=src, b=b: e.tensor_scalar(out=b[:], in0=src[:], scalar1=mcol[:, 0:1], scalar2=None, op0=ALU.mult), reads=[src, mcol], writes=[b])
    for (dst, a, b) in ((cosT, csr, csc), (sinT, snr, snc)):
        d3 = dst[:].rearrange("p (r c) -> p r c", r=64)
        P.dve(lambda e, d3=d3, a=a: e.tensor_copy(out=d3, in_=bc_ap(a[:], [[1, 64], [0, 64]])), reads=[a], writes=[dst])
        P.dve(lambda e, d3=d3, b=b: e.tensor_tensor(out=d3, in0=d3, in1=bc_ap(b[:], [[0, 64], [1, 64]]), op=ALU.add), reads=[b, dst], writes=[dst])
    for b in range(4):
        P.act(lambda e, b=b: e.activation(out=perm[:, 32 * b:32 * b + 16], in_=G.ident[:, 32 * b + 16:32 * b + 32], func=AF.Copy, scale=-1.0),
              reads=[G.ident], writes=[perm])
        P.act(lambda e, b=b: e.activation(out=perm[:, 32 * b + 16:32 * b + 32], in_=G.ident[:, 32 * b:32 * b + 16], func=AF.Copy),
              reads=[G.ident], writes=[perm])


def load_bcast(G, es, name, src_row_ap, n):
    t = G.sb(es, name, [128, n])
    G.P.dma(t[:], src_row_ap.partition_broadcast(128), writes=[t])
    return t


def mlstm_steps(G, l, es):
    nc, P, I = G.nc, G.P, G.I
    sb = G.sb
    sc = G.scr
    with_ctx = l < DEPTH - 1
    if 'yT' not in sc:
        G.scratch('yT', [10, 128, S], BF16)
    yT = sc['yT']
    if True:
        hacc = sb(es, 'm_hacc', [128, NT128, 256])
        gates = sb(es, 'm_gates', [128, NT128, 24])
        ib_b = load_bcast(G, es, 'm_ib', I['mlstm_ib'][l:l + 1].rearrange("o d h -> o (d h)"), 8)
        fb_b = load_bcast(G, es, 'm_fb', I['mlstm_fb'][l:l + 1].rearrange("o d h -> o (d h)"), 8)
        nw_b = load_bcast(G, es, 'm_nw', I['mlstm_norm_w'][l:l + 1, :], 256)
        NB = 2
        qT = [sb(es, 'm_qT%d' % i, [128, 2, 128]) for i in range(NB)]
        kT = [sb(es, 'm_kT%d' % i, [128, 2, 128]) for i in range(NB)]
        kTM = [sb(es, 'm_kTM%d' % i, [128, 256]) for i in range(NB)]
        Vp = [sb(es, 'm_Vp%d' % i, [128, 4, 65]) for i in range(NB)]
        gi_ = [sb(es, 'm_gi%d' % i, [128, 16]) for i in range(NB)]
        mo = [sb(es, 'm_mo%d' % i, [128, 256]) for i in range(NB)]
        for v in Vp:
            P.pool(lambda e, v=v: e.memset(v[:], 1.0), writes=[v])
        Cbd = [[sb(es, 'm_C%d%d' % (pr, d), [128, 130]) for d in range(2)] for pr in range(2)]
        for pr in range(2):
            for d in range(2):
                P.pool(lambda e, c=Cbd[pr][d]: e.memset(c[:], 0.0), writes=[Cbd[pr][d]])
        g1 = sb(es, 'm_g1', [128, 8]); g2 = sb(es, 'm_g2', [128, 8]); cum = sb(es, 'm_cum', [128, 16])
        pmt = [sb(es, 'm_pmt%d' % i, [128, 128]) for i in range(2)]
        pm = [sb(es, 'm_pm%d' % i, [128, 128]) for i in range(8)]
        uV = [sb(es, 'm_uV%d' % i, [128, 130]) for i in range(2)]
        ep = [sb(es, 'm_ep%d' % i, [128, 8]) for i in range(2)]
        stt = [sb(es, 'm_stt%d' % i, [128, 65]) for i in range(2)]
        ho = [sb(es, 'm_ho%d' % i, [128, 256]) for i in range(2)]
        sg = [sb(es, 'm_sg%d' % i, [128, 256]) for i in range(2)]
        ss = [sb(es, 'm_ss%d' % i, [128, 4]) for i in range(2)]
        junk = sb(es, 'm_junk', [128, 64])
        ytb = [sb(es, 'm_ytb%d' % i, [128, 2, 128], BF16) for i in range(2)]
        cnt = {'pm': 0, 'n': 0}

        def chunk_pass(q, d, it, first_pass):
            tok = q * 128
            b = it % NB
            P.dma(qT[b][:], sc['mqT'][:, :, tok:tok + 128].rearrange("c p t -> p c t"), reads=[sc['mqT']], writes=[qT[b]])
            P.dma(kT[b][:], sc['mkT'][:, :, tok:tok + 128].rearrange("c p t -> p c t"), reads=[sc['mkT']], writes=[kT[b]])
            P.dma(kTM[b][:], sc['mkTM'][tok:tok + 128, :], reads=[sc['mkTM']], writes=[kTM[b]])
            P.dma(Vp[b][:, :, 0:64], sc['TM1'][tok:tok + 128, 0:256].rearrange("p (h e) -> p h e", h=4), reads=[sc['TM1']], writes=[Vp[b]])
            if first_pass:
                g = gi_[b]
                P.dma(g[:], sc['TM1'][tok:tok + 128, 512:528], reads=[sc['TM1']], writes=[g])
                P.dve(lambda e: e.tensor_tensor(out=g1[:], in0=g[:, 0:8], in1=ib_b[:], op=ALU.add), reads=[g, ib_b], writes=[g1])
                P.dve(lambda e: e.tensor_tensor(out=g2[:], in0=g[:, 8:16], in1=fb_b[:], op=ALU.add), reads=[g, fb_b], writes=[g2])
                P.act(lambda e: e.activation(out=g2[:], in_=g2[:], func=AF.Exp, scale=-1.0), reads=[g2], writes=[g2])
                P.act(lambda e: e.activation(out=g2[:], in_=g2[:], func=AF.Ln, bias=1.0), reads=[g2], writes=[g2])
                P.dve(lambda e: e.tensor_scalar(out=g2[:], in0=g2[:], scalar1=-1.0, scalar2=None, op0=ALU.mult), reads=[g2], writes=[g2])
                ps = G.nextps()
                P.pe(lambda e, ps=ps: e.matmul(ps[:, 0:4], lhsT=G.triU[:], rhs=g2[:, 0:4], start=True, stop=True), reads=[G.triU, g2], writes=[ps])
                P.pe(lambda e, ps=ps: e.matmul(ps[:, 4:8], lhsT=G.triL[:], rhs=g2[:, 4:8], start=True, stop=True), reads=[G.triL, g2], writes=[ps])
                P.pe(lambda e, ps=ps: e.matmul(ps[:, 8:16], lhsT=G.ones[:], rhs=g2[:, 0:8], start=True, stop=True), reads=[G.ones, g2], writes=[ps])
                P.dve(lambda e, ps=ps: e.tensor_copy(out=cum[:], in_=ps[:, 0:16]), reads=[ps], writes=[cum])
                P.act(lambda e: e.activation(out=gates[:, q, 0:8], in_=cum[:, 0:8], func=AF.Exp), reads=[cum], writes=[gates])
                P.dve(lambda e: e.tensor_tensor(out=g1[:], in0=g1[:], in1=cum[:, 0:8], op=ALU.subtract), reads=[g1, cum], writes=[g1])
                P.act(lambda e: e.activation(out=gates[:, q, 8:16], in_=g1[:], func=AF.Exp), reads=[g1], writes=[gates])
                P.act(lambda e: e.activation(out=gates[:, q, 16:24], in_=cum[:, 8:16], func=AF.Exp), reads=[cum], writes=[gates])
            else:
                P.dma(mo[b][:], sc['TM1'][tok:tok + 128, 256:512], reads=[sc['TM1']], writes=[mo[b]])
            mask = G.triU if d == 0 else G.triL
            pms_all = []
            for pr in range(2):
                pms = []
                pms_all.append(pms)
                for hh in range(2):
                    h = 2 * pr + hh
                    j = 4 * d + h
                    ps = G.nextps()
                    P.pe(lambda e, ps=ps, hh=hh, pr=pr: e.matmul(ps[:, 0:128], lhsT=kT[b][64 * hh:64 * hh + 64, pr, :], rhs=qT[b][64 * hh:64 * hh + 64, pr, :],
                                                                  start=True, stop=True), reads=[kT[b], qT[b]], writes=[ps])
                    t_ = pmt[cnt['pm'] % 2]; p_ = pm[cnt['pm'] % 8]; cnt['pm'] += 1
                    P.act(lambda e, ps=ps, t_=t_, j=j: e.activation(out=t_[:], in_=ps[:, 0:128], func=AF.Copy, scale=gates[:, q, 8 + j:9 + j]),
                          reads=[ps, gates], writes=[t_])
                    P.pool(lambda e, t_=t_, p_=p_: e.tensor_tensor(out=p_[:], in0=t_[:], in1=mask[:], op=ALU.mult), reads=[t_, mask], writes=[p_])
                    pms.append(p_)
            yield
            for pr in range(2):
                pms = pms_all[pr]
                j0 = 4 * d + 2 * pr
                C = Cbd[pr][d]
                ps2 = G.nextps()
                P.pe(lambda e, ps2=ps2, pr=pr, C=C: e.matmul(ps2[:, 0:130], lhsT=qT[b][:, pr, :], rhs=C[:], start=True, stop=False), reads=[qT[b], C], writes=[ps2])
                for hh in range(2):
                    P.pe(lambda e, ps2=ps2, hh=hh, pr=pr, p_=pms[hh]: e.matmul(ps2[:, 65 * hh:65 * hh + 65], lhsT=p_[:], rhs=Vp[b][:, 2 * pr + hh, :],
                                                                              start=False, stop=(hh == 1)), reads=[pms[hh], Vp[b]], writes=[ps2])
                e_ = ep[cnt['n'] % 2]; cnt['n'] += 1
                p3 = ps2[:, 0:130].rearrange("p (h e) -> p h e", e=65)
                P.dve(lambda e, e_=e_, p3=p3, j0=j0: e.tensor_tensor(out=e_[:, 0:2], in0=p3[:, :, 64], in1=gates[:, q, j0:j0 + 2], op=ALU.mult), reads=[ps2, gates], writes=[e_])
                P.dve(lambda e, e_=e_: e.scalar_tensor_tensor(out=e_[:, 6:8], in0=e_[:, 0:2], scalar=-1.0, in1=e_[:, 0:2], op0=ALU.mult, op1=ALU.max), reads=[e_], writes=[e_])
                P.dve(lambda e, e_=e_: e.tensor_scalar(out=e_[:, 0:2], in0=e_[:, 6:8], scalar1=1.0, scalar2=None, op0=ALU.max), reads=[e_], writes=[e_])
                P.dve(lambda e, e_=e_: e.reciprocal(out=e_[:, 2:4], in_=e_[:, 0:2]), reads=[e_], writes=[e_])
                P.dve(lambda e, e_=e_, j0=j0: e.tensor_tensor(out=e_[:, 4:6], in0=e_[:, 2:4], in1=gates[:, q, j0:j0 + 2], op=ALU.mult), reads=[e_, gates], writes=[e_])
                for hh in range(2):
                    h = 2 * pr + hh
                    if first_pass:
                        P.act(lambda e, hh=hh, h=h, e_=e_, p3=p3: e.activation(out=hacc[:, q, 64 * h:64 * h + 64], in_=p3[:, hh, 0:64], func=AF.Copy, scale=e_[:, 4 + hh:5 + hh]),
                              reads=[ps2, e_], writes=[hacc])
                    else:
                        P.dve(lambda e, hh=hh, h=h, e_=e_, p3=p3: e.scalar_tensor_tensor(out=hacc[:, q, 64 * h:64 * h + 64], in0=p3[:, hh, 0:64], scalar=e_[:, 4 + hh:5 + hh],
                                                                                      in1=hacc[:, q, 64 * h:64 * h + 64], op0=ALU.mult, op1=ALU.add),
                              reads=[ps2, e_, hacc], writes=[hacc])
                uv = uV[cnt['n'] % 2]
                for hh in range(2):
                    h = 2 * pr + hh
                    j = 4 * d + h
                    P.dve(lambda e, uv=uv, hh=hh, h=h, j=j: e.tensor_scalar(out=uv[:, 65 * hh:65 * hh + 65], in0=Vp[b][:, h, :], scalar1=gates[:, q, 8 + j:9 + j], scalar2=None, op0=ALU.mult),
                          reads=[Vp[b], gates], writes=[uv])
                ps3 = G.nextps()
                P.pe(lambda e, ps3=ps3, uv=uv, pr=pr: e.matmul(ps3[:, 0:130], lhsT=kTM[b][:, 128 * pr:128 * pr + 128], rhs=uv[:], start=True, stop=True), reads=[kTM[b], uv], writes=[ps3])
                for hh in range(2):
                    j = 4 * d + 2 * pr + hh
                    rows = slice(64 * hh, 64 * hh + 64)
                    cols = slice(65 * hh, 65 * hh + 65)
                    P.dve(lambda e, ps3=ps3, rows=rows, cols=cols, C=C: e.tensor_tensor(out=C[rows, cols], in0=ps3[rows, cols], in1=C[rows, cols], op=ALU.add), reads=[ps3, C], writes=[C])
                    P.dve(lambda e, rows=rows, cols=cols, C=C, j=j: e.tensor_scalar(out=C[rows, cols], in0=C[rows, cols], scalar1=gates[rows, q, 16 + j:17 + j], scalar2=None, op0=ALU.mult),
                          reads=[C, gates], writes=[C])
            if not first_pass and (with_ctx or q >= 2):
                bb = it % 2
                P.act(lambda e: e.activation(out=sg[bb][:], in_=mo[b][:], func=AF.Sigmoid), reads=[mo[b]], writes=[sg[bb]])
                P.dve(lambda e: e.tensor_tensor(out=ho[bb][:], in0=hacc[:, q, :], in1=sg[bb][:], op=ALU.mult), reads=[hacc, sg[bb]], writes=[ho[bb]])
                for h in range(4):
                    P.act(lambda e, h=h: e.activation(out=junk[:], in_=ho[bb][:, 64 * h:64 * h + 64], func=AF.Square, accum_out=ss[bb][:, h:h + 1]), reads=[ho[bb]], writes=[junk, ss[bb]])
                P.act(lambda e: e.activation(out=ss[bb][:], in_=ss[bb][:], func=AF.Sqrt, scale=1.0 / 64, bias=G.epsb[:, 0:1]), reads=[ss[bb], G.epsb], writes=[ss[bb]])
                P.dve(lambda e: e.reciprocal(out=ss[bb][:], in_=ss[bb][:]), reads=[ss[bb]], writes=[ss[bb]])
                for h in range(4):
                    P.dve(lambda e, h=h: e.scalar_tensor_tensor(out=ho[bb][:, 64 * h:64 * h + 64], in0=ho[bb][:, 64 * h:64 * h + 64], scalar=ss[bb][:, h:h + 1],
                                                                 in1=nw_b[:, 64 * h:64 * h + 64], op0=ALU.mult, op1=ALU.mult), reads=[ho[bb], ss[bb], nw_b], writes=[ho[bb]])
                ps = G.nextps()
                for c in range(2):
                    P.pe(lambda e, ps=ps, c=c: e.transpose(out=ps[:, 128 * c:128 * c + 128], in_=ho[bb][:, 128 * c:128 * c + 128], identity=G.ident[:]), reads=[ho[bb], G.ident], writes=[ps])
                P.act(lambda e, ps=ps: e.activation(out=ytb[bb][:], in_=ps[:, 0:256].rearrange("p (c t) -> p c t", c=2), func=AF.Copy), reads=[ps], writes=[ytb[bb]])
                P.dma(yT[0:2, :, tok:tok + 128].rearrange("c p t -> p c t"), ytb[bb][:], reads=[ytb[bb]], writes=[yT])

        steps = []
        it = 0
        for q in range(NT128):
            steps.append(chunk_pass(q, 0, it, True)); it += 1
        order = [1, 0] + list(range(NT128 - 1, 1, -1))
        for q in order:
            steps.append(chunk_pass(q, 1, it, False)); it += 1
        return steps


def ssd_steps(G, l, es):
    nc, P, I = G.nc, G.P, G.I
    sb = G.sb
    sc = G.scr
    with_ctx = l < DEPTH - 1
    yT = sc['yT']
    if True:
        yacc = sb(es, 'd_yacc', [128, NT128, 512])
        alog_b = load_bcast(G, es, 'd_alog', I['ssd_a_log'][l:l + 1].rearrange("o d h -> o (d h)"), 16)
        dtb_b = load_bcast(G, es, 'd_dtb', I['ssd_dt_bias'][l:l + 1].rearrange("o d h -> o (d h)"), 16)
        D_b = load_bcast(G, es, 'd_D', I['ssd_d'][l:l + 1, :], 8)
        nw_b = load_bcast(G, es, 'd_nw', I['ssd_norm_w'][l:l + 1, :], 512)
        A_b = sb(es, 'd_A', [128, 16])
        P.act(lambda e: e.activation(out=A_b[:], in_=alog_b[:], func=AF.Exp), reads=[alog_b], writes=[A_b])
        P.dve(lambda e: e.tensor_scalar(out=A_b[:], in0=A_b[:], scalar1=-1.0, scalar2=None, op0=ALU.mult), reads=[A_b], writes=[A_b])
        NB = 2
        xt = [sb(es, 'd_xt%d' % i, [128, 512]) for i in range(NB)]
        Bt = [sb(es, 'd_Bt%d' % i, [128, 2, 128]) for i in range(NB)]
        Ct = [sb(es, 'd_Ct%d' % i, [128, 2, 128]) for i in range(NB)]
        Btm = [sb(es, 'd_Btm%d' % i, [128, 256]) for i in range(NB)]
        ddt = [sb(es, 'd_ddt%d' % i, [128, 16]) for i in range(NB)]
        dz = [sb(es, 'd_dz%d' % i, [128, 512]) for i in range(NB)]
        Hs = [sb(es, 'd_Hs%d' % d, [128, 8, 64]) for d in range(2)]
        for d in range(2):
            P.pool(lambda e, d=d: e.memset(Hs[d][:], 0.0), writes=[Hs[d]])
        dt = sb(es, 'd_dt', [128, 16]); a_ = sb(es, 'd_a', [128, 16]); cum = sb(es, 'd_cum', [128, 32])
        negcum = sb(es, 'd_negcum', [128, 16]); wgt = sb(es, 'd_w', [128, 16])
        rbig = sb(es, 'd_rbig', [128, 8, 128])
        scm = [sb(es, 'd_scm%d' % g, [128, 128]) for g in range(2)]
        arg = [sb(es, 'd_arg%d' % i, [128, 128]) for i in range(2)]
        ex = [sb(es, 'd_ex%d' % i, [128, 128]) for i in range(2)]
        pmb = [sb(es, 'd_pm%d' % i, [128, 128]) for i in range(16)]
        tmp = sb(es, 'd_tmp', [128, 8, 64]); wx2 = [sb(es, 'd_wx%d' % i, [128, 8, 64]) for i in range(2)]; htmp = sb(es, 'd_htmp', [128, 8, 64])
        acum2 = [sb(es, 'd_acum%d' % i, [128, 16]) for i in range(2)]; etot2 = [sb(es, 'd_etot%d' % i, [128, 16]) for i in range(2)]
        yz = sb(es, 'd_yz', [128, 512]); sz = sb(es, 'd_sz', [128, 512]); ssq = sb(es, 'd_ssq', [128, 1]); junk = sb(es, 'd_junk', [128, 512])
        ytb = [sb(es, 'd_ytb%d' % i, [128, 4, 128], BF16) for i in range(2)]
        cnt = {'n': 0}

        def chunk_pass(q, d, it, first_pass):
            tok = q * 128
            b = it % NB
            acum = acum2[it % 2]; etot = etot2[it % 2]; wx = wx2[it % 2]
            P.dma(xt[b][:], sc['xTM'][tok:tok + 128, :], reads=[sc['xTM']], writes=[xt[b]])
            P.dma(Bt[b][:], sc['BT'][:, :, tok:tok + 128].rearrange("g p t -> p g t"), reads=[sc['BT']], writes=[Bt[b]])
            P.dma(Ct[b][:], sc['CT'][:, :, tok:tok + 128].rearrange("g p t -> p g t"), reads=[sc['CT']], writes=[Ct[b]])
            P.dma(Btm[b][:], sc['BTM'][tok:tok + 128, :], reads=[sc['BTM']], writes=[Btm[b]])
            P.dma(ddt[b][:], sc['ddtTM'][tok:tok + 128, :], reads=[sc['ddtTM']], writes=[ddt[b]])
            if not first_pass:
                P.dma(dz[b][:], sc['dzTM'][tok:tok + 128, :], reads=[sc['dzTM']], writes=[dz[b]])
            P.dve(lambda e: e.tensor_tensor(out=dt[:], in0=ddt[b][:], in1=dtb_b[:], op=ALU.add), reads=[ddt[b], dtb_b], writes=[dt])
            P.act(lambda e: e.activation(out=dt[:], in_=dt[:], func=AF.Exp), reads=[dt], writes=[dt])
            P.act(lambda e: e.activation(out=dt[:], in_=dt[:], func=AF.Ln, bias=1.0), reads=[dt], writes=[dt])
            P.dve(lambda e: e.tensor_tensor(out=a_[:], in0=dt[:], in1=A_b[:], op=ALU.mult), reads=[dt, A_b], writes=[a_])
            ps = G.nextps()
            P.pe(lambda e, ps=ps: e.matmul(ps[:, 0:8], lhsT=G.triU[:], rhs=a_[:, 0:8], start=True, stop=True), reads=[G.triU, a_], writes=[ps])
            P.pe(lambda e, ps=ps: e.matmul(ps[:, 8:16], lhsT=G.triL[:], rhs=a_[:, 8:16], start=True, stop=True), reads=[G.triL, a_], writes=[ps])
            P.pe(lambda e, ps=ps: e.matmul(ps[:, 16:32], lhsT=G.ones[:], rhs=a_[:, 0:16], start=True, stop=True), reads=[G.ones, a_], writes=[ps])
            P.dve(lambda e, ps=ps: e.tensor_copy(out=cum[:], in_=ps[:, 0:32]), reads=[ps], writes=[cum])
            P.dve(lambda e: e.tensor_scalar(out=negcum[:], in0=cum[:, 0:16], scalar1=-1.0, scalar2=None, op0=ALU.mult), reads=[cum], writes=[negcum])
            P.act(lambda e: e.activation(out=acum[:], in_=cum[:, 0:16], func=AF.Exp), reads=[cum], writes=[acum])
            P.act(lambda e: e.activation(out=etot[:], in_=cum[:, 16:32], func=AF.Exp), reads=[cum], writes=[etot])
            P.dve(lambda e: e.tensor_tensor(out=wgt[:], in0=cum[:, 16:32], in1=cum[:, 0:16], op=ALU.subtract), reads=[cum], writes=[wgt])
            P.act(lambda e: e.activation(out=wgt[:], in_=wgt[:], func=AF.Exp), reads=[wgt], writes=[wgt])
            P.dve(lambda e: e.tensor_tensor(out=wgt[:], in0=wgt[:], in1=dt[:], op=ALU.mult), reads=[wgt, dt], writes=[wgt])
            mask = G.triU if d == 0 else G.triL
            P.dve(lambda e: e.tensor_tensor(out=rbig[:], in0=bc_ap(a_[:, 8 * d:8 * d + 8], [[1, 8], [0, 128]]), in1=bc_ap(mask[:], [[0, 8], [1, 128]]), op=ALU.mult),
                  reads=[a_, mask], writes=[rbig])
            cb = [G.nextps(), G.nextps()]
            for hf in range(2):
                P.pe(lambda e, hf=hf: e.matmul(cb[hf][:, :], lhsT=G.ones[:], rhs=rbig[:, 4 * hf:4 * hf + 4, :], start=True, stop=True), reads=[G.ones, rbig], writes=[cb[hf]])
            for g in range(2):
                ps = G.nextps()
                P.pe(lambda e, ps=ps, g=g: e.matmul(ps[:, 0:128], lhsT=Bt[b][:, g, :], rhs=Ct[b][:, g, :], start=True, stop=True), reads=[Bt[b], Ct[b]], writes=[ps])
                P.dve(lambda e, ps=ps, g=g: e.tensor_tensor(out=scm[g][:], in0=ps[:, 0:128], in1=mask[:], op=ALU.mult), reads=[ps, mask], writes=[scm[g]])
            pm_list = []
            for h in range(8):
                j = 8 * d + h
                g = h // 4
                n_ = cnt['n']; cnt['n'] += 1
                ar = arg[n_ % 2]; ex_ = ex[n_ % 2]; pm_ = pmb[n_ % 16]
                cbv = cb[h // 4][:, 128 * (h % 4):128 * (h % 4) + 128]
                P.dve(lambda e, ar=ar, cbv=cbv, j=j: e.tensor_scalar(out=ar[:], in0=cbv, scalar1=negcum[:, j:j + 1], scalar2=0.0, op0=ALU.add, op1=ALU.min),
                      reads=[cb[h // 4], negcum], writes=[ar])
                P.act(lambda e, ar=ar, ex_=ex_: e.activation(out=ex_[:], in_=ar[:], func=AF.Exp), reads=[ar], writes=[ex_])
                P.dve(lambda e, ex_=ex_, pm_=pm_, j=j, g=g: e.scalar_tensor_tensor(out=pm_[:], in0=ex_[:], scalar=dt[:, j:j + 1], in1=scm[g][:], op0=ALU.mult, op1=ALU.mult),
                      reads=[ex_, dt, scm[g]], writes=[pm_])
                pm_list.append(pm_)
            P.dve(lambda e: e.tensor_tensor(out=wx[:], in0=xt[b][:].rearrange("p (h e) -> p h e", h=8), in1=bc_ap(wgt[:, 8 * d:8 * d + 8], [[1, 8], [0, 64]]), op=ALU.mult),
                  reads=[xt[b], wgt], writes=[wx])
            yield
            psd = G.nextps()
            pso = G.nextps()
            for g in range(2):
                P.pe(lambda e, g=g: e.matmul(pso[:, 256 * g:256 * g + 256], lhsT=Ct[b][:, g, :], rhs=Hs[d][:, 4 * g:4 * g + 4, :], start=True, stop=True),
                     reads=[Ct[b], Hs[d]], writes=[pso])
            for h in range(8):
                pm_ = pm_list[h]
                P.pe(lambda e, pm_=pm_, h=h: e.matmul(psd[:, 64 * h:64 * h + 64], lhsT=pm_[:], rhs=xt[b][:, 64 * h:64 * h + 64], start=True, stop=True),
                     reads=[pm_, xt[b]], writes=[psd])
            P.dve(lambda e: e.tensor_tensor(out=tmp[:], in0=pso[:, :].rearrange("p (h e) -> p h e", h=8), in1=bc_ap(acum[:, 8 * d:8 * d + 8], [[1, 8], [0, 64]]), op=ALU.mult),
                  reads=[pso, acum], writes=[tmp])
            if first_pass:
                P.dve(lambda e: e.tensor_tensor(out=yacc[:, q, :], in0=psd[:, :], in1=tmp[:].rearrange("p h e -> p (h e)"), op=ALU.add), reads=[psd, tmp], writes=[yacc])
            else:
                P.dve(lambda e: e.tensor_tensor(out=tmp[:].rearrange("p h e -> p (h e)"), in0=psd[:, :], in1=tmp[:].rearrange("p h e -> p (h e)"), op=ALU.add), reads=[psd, tmp], writes=[tmp])
                P.pool(lambda e: e.tensor_tensor(out=yacc[:, q, :], in0=yacc[:, q, :], in1=tmp[:].rearrange("p h e -> p (h e)"), op=ALU.add), reads=[tmp, yacc], writes=[yacc])
            pst = G.nextps()
            for g in range(2):
                P.pe(lambda e, g=g: e.matmul(pst[:, 256 * g:256 * g + 256], lhsT=Btm[b][:, 128 * g:128 * g + 128], rhs=wx[:, 4 * g:4 * g + 4, :], start=True, stop=True),
                     reads=[Btm[b], wx], writes=[pst])
            P.dve(lambda e: e.tensor_tensor(out=htmp[:], in0=Hs[d][:], in1=bc_ap(etot[:, 8 * d:8 * d + 8], [[1, 8], [0, 64]]), op=ALU.mult), reads=[Hs[d], etot], writes=[htmp])
            P.dve(lambda e: e.tensor_tensor(out=Hs[d][:].rearrange("p h e -> p (h e)"), in0=pst[:, :], in1=htmp[:].rearrange("p h e -> p (h e)"), op=ALU.add),
                  reads=[pst, htmp], writes=[Hs[d]])
            if not first_pass and (with_ctx or q >= 2):
                bb = it % 2
                P.dve(lambda e: e.tensor_tensor(out=tmp[:], in0=xt[b][:].rearrange("p (h e) -> p h e", h=8), in1=bc_ap(D_b[:], [[1, 8], [0, 64]]), op=ALU.mult), reads=[xt[b], D_b], writes=[tmp])
                P.dve(lambda e: e.tensor_tensor(out=yz[:], in0=yacc[:, q, :], in1=tmp[:].rearrange("p h e -> p (h e)"), op=ALU.add), reads=[yacc, tmp], writes=[yz])
                P.act(lambda e: e.activation(out=sz[:], in_=dz[b][:], func=AF.Silu), reads=[dz[b]], writes=[sz])
                P.dve(lambda e: e.tensor_tensor(out=yz[:], in0=yz[:], in1=sz[:], op=ALU.mult), reads=[yz, sz], writes=[yz])
                P.act(lambda e: e.activation(out=junk[:], in_=yz[:], func=AF.Square, accum_out=ssq[:, 0:1]), reads=[yz], writes=[junk, ssq])
                P.act(lambda e: e.activation(out=ssq[:], in_=ssq[:], func=AF.Sqrt, scale=1.0 / 512, bias=G.epsb[:, 0:1]), reads=[ssq, G.epsb], writes=[ssq])
                P.dve(lambda e: e.reciprocal(out=ssq[:], in_=ssq[:]), reads=[ssq], writes=[ssq])
                P.dve(lambda e: e.scalar_tensor_tensor(out=yz[:], in0=yz[:], scalar=ssq[:, 0:1], in1=nw_b[:], op0=ALU.mult, op1=ALU.mult), reads=[yz, ssq, nw_b], writes=[yz])
                ps = G.nextps()
                for c in range(4):
                    P.pe(lambda e, ps=ps, c=c: e.transpose(out=ps[:, 128 * c:128 * c + 128], in_=yz[:, 128 * c:128 * c + 128], identity=G.ident[:]), reads=[yz, G.ident], writes=[ps])
                P.act(lambda e, ps=ps: e.activation(out=ytb[bb][:], in_=ps[:, :].rearrange("p (c t) -> p c t", c=4), func=AF.Copy), reads=[ps], writes=[ytb[bb]])
                P.dma(yT[6:10, :, tok:tok + 128].rearrange("c p t -> p c t"), ytb[bb][:], reads=[ytb[bb]], writes=[yT])

        steps = []
        it = 0
        for q in range(NT128):
            steps.append(chunk_pass(q, 0, it, True)); it += 1
        order = [1, 0] + list(range(NT128 - 1, 1, -1))
        for q in order:
            steps.append(chunk_pass(q, 1, it, False)); it += 1
        return steps


def stage_mlstm_ssd(G, l):
    with contextlib.ExitStack() as es:
        a = mlstm_steps2(G, l, es)
        b = ssd_steps2(G, l, es)
        n = len(a)
        assert len(b) == n
        next(a[0]); next(b[0])
        for i in range(n):
            if i + 1 < n:
                next(a[i + 1]); next(b[i + 1])
            for g in (a[i], b[i]):
                try:
                    next(g)
                except StopIteration:
                    pass
        G.P.flush()


def na_units(G, l, es):
    nc, P, I = G.nc, G.P, G.I
    sb = G.sb
    sc = G.scr
    with_ctx = l < DEPTH - 1
    yT = sc['yT']
    if 'rpbpad' not in sc:
        G.scratch('rpbpad', [60, 192])
    rpbpad = sc['rpbpad']
    NEG = -30000.0
    if True:
        qT = sb(es, 'n_qT', [128, 2, S], BF16); kT = sb(es, 'n_kT', [128, 2, S], BF16)
        Vp = sb(es, 'n_Vp', [128, NT128, 4, 66], BF16)
        Vp2 = sb(es, 'n_Vp2', [128, NT128 - 1, 4, 66], BF16)
        BT = sb(es, 'n_BT', [128, 4, 14, 64])
        P.dma(qT[:], sc['nqT'][:].rearrange("c p t -> p c t"), reads=[sc['nqT']], writes=[qT])
        P.dma(kT[:], sc['nkT'][:].rearrange("c p t -> p c t"), reads=[sc['nkT']], writes=[kT])
        if 'na_q' in G.dbg:
            dq = G.scratch('na_q', [128, 2 * S], BF16)
            P.dma(dq[:], qT[:].rearrange('p c t -> p (c t)'), reads=[qT], writes=[dq])
            dk = G.scratch('na_k', [128, 2 * S], BF16)
            P.dma(dk[:], kT[:].rearrange('p c t -> p (c t)'), reads=[kT], writes=[dk])
        P.pool(lambda e: e.memset(Vp[:], 1.0), writes=[Vp])
        P.pool(lambda e: e.memset(Vp2[:], 1.0), writes=[Vp2])
        for q in range(NT128):
            P.dma(Vp[:, q, :, 0:64], sc['nvTM'][q * 128:(q + 1) * 128, :].rearrange("p (h e) -> p h e", h=4), reads=[sc['nvTM']], writes=[Vp])
        for q in range(NT128 - 1):
            P.dma(Vp2[:, q, :, 0:64], sc['nvTM'][q * 128 + 64:(q + 1) * 128 + 64, :].rearrange("p (h e) -> p h e", h=4), reads=[sc['nvTM']], writes=[Vp2])
        with contextlib.ExitStack() as es2:
            pad = sb(es2, 'n_pad', [60, 192])
            P.pool(lambda e: e.memset(pad[:], 0.0), writes=[pad])
            P.dma(pad[:, 80:111], I['na_rpb'][l].rearrange("h a b -> (h a) b"), writes=[pad])
            P.dma(rpbpad[:], pad[:], reads=[pad], writes=[rpbpad])
            L = sb(es2, 'n_L', [64, 4, 15, 64])
            for h in range(4):
                src = bass.AP(tensor=rpbpad.t.tensor, offset=rpbpad.t.offset + h * 15 * 192 + 32, ap=[[1, 64], [192, 15], [1, 64]])
                P.dma(L[:, h, :, :], src, reads=[rpbpad], writes=[L])
            antiI = sb(es2, 'n_antiI', [64, 64])
            P.pool(lambda e: e.affine_select(out=antiI[:], in_=G.ones[0:64, 0:64], pattern=[[1, 64]], compare_op=ALU.is_equal, fill=0.0, base=-63, channel_multiplier=1), reads=[G.ones], writes=[antiI])
            ckf = sb(es2, 'n_ckf', [128, 64]); c0 = sb(es2, 'n_c0', [128, 64]); m1 = sb(es2, 'n_m1', [128, 64]); m01 = sb(es2, 'n_m01', [128, 64]); negb = sb(es2, 'n_negb', [128, 64])
            P.pool(lambda e: e.iota(ckf[:], pattern=[[0, 64]], base=0, channel_multiplier=1, allow_small_or_imprecise_dtypes=True), writes=[ckf])
            P.dve(lambda e: e.tensor_scalar(out=ckf[64:128, :], in0=ckf[64:128, :], scalar1=-64.0, scalar2=None, op0=ALU.add), reads=[ckf], writes=[ckf])
            P.pool(lambda e: e.iota(c0[:], pattern=[[1, 64]], base=0, channel_multiplier=0, allow_small_or_imprecise_dtypes=True), writes=[c0])
            P.dve(lambda e: e.tensor_scalar(out=c0[:], in0=c0[:], scalar1=-8.0, scalar2=0.0, op0=ALU.add, op1=ALU.max), reads=[c0], writes=[c0])
            P.dve(lambda e: e.tensor_scalar(out=c0[:], in0=c0[:], scalar1=48.0, scalar2=None, op0=ALU.min), reads=[c0], writes=[c0])
            P.dve(lambda e: e.tensor_tensor(out=ckf[:], in0=ckf[:], in1=c0[:], op=ALU.subtract), reads=[ckf, c0], writes=[ckf])
            P.dve(lambda e: e.tensor_single_scalar(out=m1[:], in_=ckf[:], scalar=0.0, op=ALU.is_ge), reads=[ckf], writes=[m1])
            P.dve(lambda e: e.tensor_single_scalar(out=m01[:], in_=ckf[:], scalar=15.0, op=ALU.is_le), reads=[ckf], writes=[m01])
            P.dve(lambda e: e.tensor_tensor(out=m01[:], in0=m01[:], in1=m1[:], op=ALU.mult), reads=[m01, m1], writes=[m01])
            P.dve(lambda e: e.tensor_scalar(out=negb[:], in0=m01[:], scalar1=-1.0, scalar2=-NEG, op0=ALU.add, op1=ALU.mult), reads=[m01], writes=[negb])
            for h in range(4):
                for d0 in range(14):
                    ps = G.nextps()
                    P.pe(lambda e, ps=ps, h=h, d0=d0: e.matmul(ps[:, 0:64], lhsT=L[:, h, d0:d0 + 2, :].rearrange("p a b -> p (a b)"), rhs=antiI[:], start=True, stop=True),
                         reads=[L, antiI], writes=[ps])
                    P.dve(lambda e, ps=ps, h=h, d0=d0: e.tensor_tensor(out=BT[:, h, d0, :], in0=ps[:, 0:64], in1=m01[:], op=ALU.mult), reads=[ps, m01], writes=[BT])
                    P.pool(lambda e, h=h, d0=d0: e.tensor_tensor(out=BT[:, h, d0, :], in0=BT[:, h, d0, :], in1=negb[:], op=ALU.add), reads=[BT, negb], writes=[BT])
            P.flush()
        yield 'setup'
        ssb = [sb(es, 'n_ssb%d' % i, [128, 256]) for i in range(2)]
        pex = [sb(es, 'n_pex%d' % i, [128, 6, 64], BF16) for i in range(3)]
        rec = [sb(es, 'n_rec%d' % i, [128, 4]) for i in range(2)]
        O = [sb(es, 'n_O%d' % i, [128, 256]) for i in range(2)]
        ytb = [sb(es, 'n_ytb%d' % i, [128, 2, 128], BF16) for i in range(2)]
        pexc = [sb(es, 'n_pexc%d' % i, [128, 2, 128], BF16) for i in range(2)]
        n = 0
        if with_ctx:
            def ctx_tile(qt):
                nonlocal n
                o_ = O[qt % 2]; r_ = rec[qt % 2]
                for h in range(4):
                    pr, hh = h // 2, h % 2
                    ps = G.nextps()
                    for j in range(2):
                        P.pe(lambda e, ps=ps, j=j, pr=pr, hh=hh: e.matmul(ps[:, 128 * j:128 * j + 128], lhsT=kT[64 * hh:64 * hh + 64, pr, 128 * j:128 * j + 128],
                                                                         rhs=qT[64 * hh:64 * hh + 64, pr, 128 * qt:128 * qt + 128], start=True, stop=True), reads=[kT, qT], writes=[ps])
                    pc = pexc[n % 2]; n += 1
                    P.act(lambda e, ps=ps, pc=pc: e.activation(out=pc[:], in_=ps[:, 0:256].rearrange("p (j q) -> p j q", j=2), func=AF.Exp), reads=[ps], writes=[pc])
                    if 'na_dbg' in G.dbg and qt == 0 and h == 0:
                        dd = G.scratch('na_dbg', [128, 256], BF16)
                        P.dma(dd[:], pc[:].rearrange('p j q -> p (j q)'), reads=[pc], writes=[dd])
                    po = G.nextps()
                    for j in range(2):
                        P.pe(lambda e, po=po, j=j, pc=pc, h=h: e.matmul(po[:, 0:65], lhsT=pc[:, j, :], rhs=Vp[:, j, h, 0:65], start=(j == 0), stop=(j == 1)), reads=[pc, Vp], writes=[po])
                    P.dve(lambda e, po=po, r_=r_, h=h: e.reciprocal(out=r_[:, h:h + 1], in_=po[:, 64:65]), reads=[po], writes=[r_])
                    P.dve(lambda e, po=po, r_=r_, h=h, o_=o_: e.tensor_scalar(out=o_[:, 64 * h:64 * h + 64], in0=po[:, 0:64], scalar1=r_[:, h:h + 1], scalar2=None, op0=ALU.mult),
                          reads=[po, r_], writes=[o_])
                ps = G.nextps()
                yb_ = ytb[qt % 2]
                for c in range(2):
                    P.pe(lambda e, ps=ps, c=c, o_=o_: e.transpose(out=ps[:, 128 * c:128 * c + 128], in_=o_[:, 128 * c:128 * c + 128], identity=G.ident[:]), reads=[o_, G.ident], writes=[ps])
                P.act(lambda e, ps=ps, yb_=yb_: e.activation(out=yb_[:], in_=ps[:, 0:256].rearrange("p (c t) -> p c t", c=2), func=AF.Copy), reads=[ps], writes=[yb_])
                P.dma(yT[4:6, :, 128 * qt:128 * qt + 128].rearrange("c p t -> p c t"), yb_[:], reads=[yb_], writes=[yT])
            for qt in range(2):
                ctx_tile(qt)
                yield

        def lat_unit(rp, sub, h, o_, r_):
            nonlocal n
            if True:
                r = 2 * rp + sub
                r0 = min(max(r - 4, 0), 56)
                rows = slice(64 * sub, 64 * sub + 64)
                qtok = CTX + 64 * r
                if True:
                    pr, hh = h // 2, h % 2
                    ps = G.nextps()
                    ktoks = [CTX + 64 * (r0 + 2 * j) for j in range(4)] + [0, 128]
                    for j in range(6):
                        P.pe(lambda e, ps=ps, j=j, pr=pr, hh=hh, kt=ktoks[j]: e.matmul(ps[:, 64 * j:64 * j + 64], lhsT=kT[64 * hh:64 * hh + 64, pr, kt:kt + 128],
                                                                                      rhs=qT[64 * hh:64 * hh + 64, pr, qtok:qtok + 64], start=True, stop=True), reads=[kT, qT], writes=[ps])
                    s_ = ssb[n % 2]; pe_ = pex[n % 3]; n += 1
                    d0 = r0 - r + 7
                    P.dve(lambda e, ps=ps, s_=s_, h=h, d0=d0: e.tensor_tensor(out=s_[:].rearrange("p (j q) -> p j q", j=4), in0=ps[:, 0:256].rearrange("p (j q) -> p j q", j=4),
                                                                              in1=BT[:, h, d0:d0 + 7:2, :], op=ALU.add), reads=[ps, BT], writes=[s_])
                    P.act(lambda e, s_=s_, pe_=pe_: e.activation(out=pe_[:, 0:4, :], in_=s_[:].rearrange("p (j q) -> p j q", j=4), func=AF.Exp), reads=[s_], writes=[pe_])
                    P.act(lambda e, ps=ps, pe_=pe_: e.activation(out=pe_[:, 4:6, :], in_=ps[:, 256:384].rearrange("p (j q) -> p j q", j=2), func=AF.Exp), reads=[ps], writes=[pe_])
                    yield
                    po = G.nextps()
                    ktile = [(kt // 128) for kt in ktoks]
                    koff = [(kt % 128) for kt in ktoks]
                    for j in range(6):
                        vsrc = Vp if koff[j] == 0 else Vp2
                        P.pe(lambda e, po=po, j=j, pe_=pe_, h=h, kt=ktile[j], vsrc=vsrc: e.matmul(po[rows, 0:65], lhsT=pe_[:, j, :], rhs=vsrc[:, kt, h, 0:65], start=(j == 0), stop=(j == 5)),
                             reads=[pe_, vsrc], writes=[po])
                    P.dve(lambda e, po=po, r_=r_, h=h: e.reciprocal(out=r_[rows, h:h + 1], in_=po[rows, 64:65]), reads=[po], writes=[r_])
                    P.dve(lambda e, po=po, r_=r_, h=h, o_=o_: e.tensor_scalar(out=o_[rows, 64 * h:64 * h + 64], in0=po[rows, 0:64], scalar1=r_[rows, h:h + 1], scalar2=None, op0=ALU.mult),
                          reads=[po, r_], writes=[o_])
        def lat_finish(rp, o_):
            ps = G.nextps()
            yb_ = ytb[rp % 2]
            tok = CTX + 128 * rp
            for c in range(2):
                P.pe(lambda e, ps=ps, c=c, o_=o_: e.transpose(out=ps[:, 128 * c:128 * c + 128], in_=o_[:, 128 * c:128 * c + 128], identity=G.ident[:]), reads=[o_, G.ident], writes=[ps])
            P.act(lambda e, ps=ps, yb_=yb_: e.activation(out=yb_[:], in_=ps[:, 0:256].rearrange("p (c t) -> p c t", c=2), func=AF.Copy), reads=[ps], writes=[yb_])
            P.dma(yT[4:6, :, tok:tok + 128].rearrange("c p t -> p c t"), yb_[:], reads=[yb_], writes=[yT])

        def units():
            for rp in range(32):
                yield from lat_pair(rp)
        pending = []
        import itertools

        def unit_iter():
            for rp in range(32):
                o_ = O[rp % 2]; r_ = rec[rp % 2]
                for sub in range(2):
                    for h in range(4):
                        yield (rp, sub, h, o_, r_)
        prev = None
        for (rp, sub, h, o_, r_) in unit_iter():
            g = lat_unit(rp, sub, h, o_, r_)
            next(g)
            if prev is not None:
                pg, prp, psub, ph, po_ = prev
                for _ in pg:
                    pass
                if psub == 1 and ph == 3:
                    lat_finish(prp, po_)
            prev = (g, rp, sub, h, o_)
            yield
        pg, prp, psub, ph, po_ = prev
        for _ in pg:
            pass
        lat_finish(prp, po_)


class LazyPS:
    __slots__ = ('t', 'r')

    def __init__(self):
        self.t = None
        self.r = None

    def __getitem__(self, k):
        return self.t[k]


class Rec:
    def __init__(self, G):
        self.G = G
        self.P = G.P
        self.ops = []

    def op(self, eng, fn, reads=(), writes=(), dma=False):
        self.ops.append((eng, fn, tuple(reads), tuple(writes), dma))

    def pe(self, fn, reads=(), writes=()):
        self.op('pe', fn, reads, writes)

    def act(self, fn, reads=(), writes=()):
        self.op('act', fn, reads, writes)

    def dve(self, fn, reads=(), writes=()):
        self.op('dve', fn, reads, writes)

    def pool(self, fn, reads=(), writes=()):
        self.op('pool', fn, reads, writes)

    def dmaq(self, q, out, in_, reads=(), writes=(), **kw):
        self.op(q, lambda e: e.dma_start(out=out, in_=in_, **kw), reads, writes, dma=True)

    def dma(self, out, in_, reads=(), writes=(), **kw):
        self.dmaq('sp', out, in_, reads, writes, **kw)

    def issue(self, o):
        G = self.G
        for x in o[2] + o[3]:
            if isinstance(x, LazyPS) and x.t is None:
                b = G.ps[G.psi % 8]
                G.psi += 1
                x.t, x.r = b.t, b.r
        self.P.op(*o)

    def replay(self):
        for o in self.ops:
            self.issue(o)
        self.ops = []

    def flush(self):
        self.replay()
        self.P.flush()


class View:
    __slots__ = ('t', 'r')

    def __init__(self, t, r):
        self.t = t
        self.r = r

    def __getitem__(self, k):
        return self.t[k]


class GProxy:
    def __init__(self, G, name, banks):
        self.__dict__['_G'] = G
        self.__dict__['P'] = Rec(G)
        self.__dict__['_banks'] = banks
        self.__dict__['_bi'] = 0
        scr = dict(G.scr)
        scr['yT'] = View(G.scr['yT'].t, Res('yT_' + name))
        self.__dict__['scr'] = scr

    def nextps(self):
        b = self._G.ps[self._banks[self._bi % len(self._banks)]]
        self.__dict__['_bi'] = self._bi + 1
        return b

    def scratch(self, name, shape, dt=F32):
        b = self._G.scratch(name, shape, dt)
        self.scr[name] = b
        return b

    def __getattr__(self, k):
        return getattr(self._G, k)


def merge_streams(recs):
    pos = [0] * len(recs)
    tot = [max(1, len(r.ops)) for r in recs]
    while True:
        best, bi = None, -1
        for i, r in enumerate(recs):
            if pos[i] < len(r.ops):
                f = pos[i] / tot[i]
                if best is None or f < best:
                    best, bi = f, i
        if bi < 0:
            break
        recs[bi].issue(recs[bi].ops[pos[bi]])
        pos[bi] += 1
    for r in recs:
        r.ops = []


def stage_mixers(G, l):
    if 'yT' not in G.scr:
        G.scratch('yT', [10, 128, S], BF16)
    if 'hacc_d' not in G.scr:
        G.scratch('hacc_d', [S, 256]); G.scratch('yacc_d', [S, 512])
    with contextlib.ExitStack() as es:
        GA, GB, GC = GProxy(G, 'a', [0, 1, 2]), GProxy(G, 'd', [3, 4, 5]), GProxy(G, 'c', [6, 7])
        na = na_units(GC, l, es)
        next(na)
        a = mlstm_steps2(GA, l, es)
        b = ssd_steps2(GB, l, es)
        for steps in (a, b):
            n = len(steps)
            next(steps[0])
            for i in range(n):
                if i + 1 < n:
                    next(steps[i + 1])
                for _ in steps[i]:
                    pass
        for _ in na:
            pass
        merge_streams([GA.P, GB.P, GC.P])
        G.P.flush()


def emit_sin(G, dst, ang, shift, nn, ni, fix):
    P = G.P
    P.dve(lambda e: e.tensor_scalar(out=nn[:], in0=ang[:], scalar1=shift, scalar2=1.0 / (2 * math.pi), op0=ALU.add, op1=ALU.mult), reads=[ang], writes=[nn])
    P.dve(lambda e: e.tensor_copy(out=ni[:], in_=nn[:]), reads=[nn], writes=[ni])
    P.dve(lambda e: e.tensor_copy(out=nn[:], in_=ni[:]), reads=[ni], writes=[nn])
    P.dve(lambda e: e.scalar_tensor_tensor(out=nn[:], in0=nn[:], scalar=-2 * math.pi, in1=ang[:], op0=ALU.mult, op1=ALU.add), reads=[nn, ang], writes=[nn])
    P.dve(lambda e: e.tensor_scalar(out=nn[:], in0=nn[:], scalar1=shift, scalar2=None, op0=ALU.add), reads=[nn], writes=[nn])
    P.dve(lambda e: e.tensor_scalar(out=fix[:], in0=nn[:], scalar1=math.pi, scalar2=-2 * math.pi, op0=ALU.is_gt, op1=ALU.mult), reads=[nn], writes=[fix])
    P.dve(lambda e: e.tensor_tensor(out=nn[:], in0=nn[:], in1=fix[:], op=ALU.add), reads=[nn, fix], writes=[nn])
    P.dve(lambda e: e.tensor_scalar(out=fix[:], in0=nn[:], scalar1=-math.pi, scalar2=2 * math.pi, op0=ALU.is_lt, op1=ALU.mult), reads=[nn], writes=[fix])
    P.dve(lambda e: e.tensor_tensor(out=nn[:], in0=nn[:], in1=fix[:], op=ALU.add), reads=[nn, fix], writes=[nn])
    P.dve(lambda e: e.tensor_scalar(out=nn[:], in0=nn[:], scalar1=3.1415925, scalar2=-3.1415925, op0=ALU.min, op1=ALU.max), reads=[nn], writes=[nn])
    P.act(lambda e: e.activation(out=dst[:], in_=nn[:], func=AF.Sin), reads=[nn], writes=[dst])


S5_BT = [(0, 32, 1)] + [(32 + 128 * i, 128, 35 + 128 * i) for i in range(4)]
ZW = 548


def s5_colF(k):
    return 3 + k


def s5_colB(k):
    return (k - 31) if k >= 32 else (513 + k)


def stage_s5(G, l):
    nc, P, I = G.nc, G.P, G.I
    sb = G.sb
    sc = G.scr
    with_ctx = l < DEPTH - 1
    yT = sc['yT']
    I32 = mybir.dt.int32
    with contextlib.ExitStack() as es:
        Toe = sb(es, 's_Toe', [128, 16, 128])
        PCrD = [sb(es, 's_PCrD%d' % d, [128, 16, 128]) for d in range(2)]; PCiND = [sb(es, 's_PCiND%d' % d, [128, 16, 128]) for d in range(2)]
        for t_ in PCrD + PCiND:
            P.pool(lambda e, t_=t_: e.memset(t_[:], 0.0), writes=[t_])
        AA = sb(es, 's_AA', [128, 16, 2]); AXm = sb(es, 's_AX', [128, 16, 2])
        A32A = sb(es, 's_A32A', [128, 16, 2]); A32X = sb(es, 's_A32X', [128, 16, 2])
        Pwr = sb(es, 's_Pwr', [128, 16, 32]); Pwi = sb(es, 's_Pwi', [128, 16, 32])
        PBT = sb(es, 's_PBT', [128, 2, 16, 128])
        with contextlib.ExitStack() as es2:
            with contextlib.ExitStack() as es3:
                lamr = sb(es3, 's_lamr', [128, 16]); lami = sb(es3, 's_lami', [128, 16]); dtl = sb(es3, 's_dt', [128, 16])
                Bre = sb(es3, 's_Bre', [128, 16, 16]); Bim = sb(es3, 's_Bim', [128, 16, 16])
                Cre = sb(es3, 's_Cre', [128, 16, 16]); Cim = sb(es3, 's_Cim', [128, 16, 16])
                for d in range(2):
                    hs = slice(64 * d, 64 * d + 64)
                    P.dma(lamr[hs, :], I['s5_lam_re'][l, d].rearrange("g p -> p g"), writes=[lamr], allow_slow_non_contiguous=True)
                    P.dma(lami[hs, :], I['s5_lam_im'][l, d].rearrange("g p -> p g"), writes=[lami], allow_slow_non_contiguous=True)
                    P.dma(dtl[hs, :], I['s5_log_dt'][l, d:d + 1, :].partition_broadcast(64), writes=[dtl])
                    P.dma(Bre[hs], I['s5_b_re'][l].rearrange("g p c -> p g c"), writes=[Bre])
                    P.dma(Bim[hs], I['s5_b_im'][l].rearrange("g p c -> p g c"), writes=[Bim])
                    for g in range(16):
                        P.dma(Cre[hs, g, :], I['s5_c_re'][l, g].rearrange("c p -> p c"), writes=[Cre], allow_slow_non_contiguous=True)
                        P.dma(Cim[hs, g, :], I['s5_c_im'][l, g].rearrange("c p -> p c"), writes=[Cim], allow_slow_non_contiguous=True)
                P.act(lambda e: e.activation(out=dtl[:], in_=dtl[:], func=AF.Exp), reads=[dtl], writes=[dtl])
                lrd = sb(es3, 's_lrd', [128, 16]); lid = sb(es3, 's_lid', [128, 16])
                P.dve(lambda e: e.tensor_tensor(out=lrd[:], in0=lamr[:], in1=dtl[:], op=ALU.mult), reads=[lamr, dtl], writes=[lrd])
                P.dve(lambda e: e.tensor_tensor(out=lid[:], in0=lami[:], in1=dtl[:], op=ALU.mult), reads=[lami, dtl], writes=[lid])
                NJ = 24
                jv = sb(es3, 's_jv', [128, NJ])
                P.pool(lambda e: e.iota(jv[:, 0:16], pattern=[[1, 16]], base=0, channel_multiplier=0, allow_small_or_imprecise_dtypes=True), writes=[jv])
                P.pool(lambda e: e.iota(jv[:, 16:24], pattern=[[-1, 8]], base=0, channel_multiplier=0, allow_small_or_imprecise_dtypes=True), writes=[jv])
                mag = sb(es3, 's_mag', [128, 16, NJ]); ang = sb(es3, 's_ang', [128, 16, NJ])
                P.dve(lambda e: e.tensor_tensor(out=mag[:], in0=bc_ap(lrd[:], [[1, 16], [0, NJ]]), in1=bc_ap(jv[:], [[0, 16], [1, NJ]]), op=ALU.mult), reads=[lrd, jv], writes=[mag])
                P.dve(lambda e: e.tensor_tensor(out=ang[:], in0=bc_ap(lid[:], [[1, 16], [0, NJ]]), in1=bc_ap(jv[:], [[0, 16], [1, NJ]]), op=ALU.mult), reads=[lid, jv], writes=[ang])
                P.act(lambda e: e.activation(out=mag[:], in_=mag[:], func=AF.Exp), reads=[mag], writes=[mag])
                nn = sb(es3, 's_nn', [128, 16, NJ]); ni = sb(es3, 's_ni', [128, 16, NJ], I32); fix = sb(es3, 's_fix', [128, 16, NJ])
                Pr = sb(es3, 's_Pr', [128, 16, NJ]); Pi = sb(es3, 's_Pi', [128, 16, NJ])
                emit_sin(G, Pi, ang, 0.0, nn, ni, fix)
                emit_sin(G, Pr, ang, math.pi / 2, nn, ni, fix)
                P.dve(lambda e: e.tensor_tensor(out=Pr[:], in0=Pr[:], in1=mag[:], op=ALU.mult), reads=[Pr, mag], writes=[Pr])
                P.dve(lambda e: e.tensor_tensor(out=Pi[:], in0=Pi[:], in1=mag[:], op=ALU.mult), reads=[Pi, mag], writes=[Pi])
                t1 = sb(es3, 's_t1', [128, 16]); t2 = sb(es3, 's_t2', [128, 16]); nr = sb(es3, 's_nr', [128, 16]); rden = sb(es3, 's_rden', [128, 16])
                cr = sb(es3, 's_cr', [128, 16]); ci = sb(es3, 's_ci', [128, 16])
                P.dve(lambda e: e.tensor_tensor(out=t1[:], in0=lamr[:], in1=lamr[:], op=ALU.mult), reads=[lamr], writes=[t1])
                P.dve(lambda e: e.tensor_tensor(out=t2[:], in0=lami[:], in1=lami[:], op=ALU.mult), reads=[lami], writes=[t2])
                P.dve(lambda e: e.tensor_tensor(out=t1[:], in0=t1[:], in1=t2[:], op=ALU.add), reads=[t1, t2], writes=[t1])
                P.dve(lambda e: e.reciprocal(out=rden[:], in_=t1[:]), reads=[t1], writes=[rden])
                P.dve(lambda e: e.tensor_scalar(out=nr[:], in0=Pr[:, :, 1], scalar1=-1.0, scalar2=None, op0=ALU.add), reads=[Pr], writes=[nr])
                P.dve(lambda e: e.tensor_tensor(out=t1[:], in0=nr[:], in1=lamr[:], op=ALU.mult), reads=[nr, lamr], writes=[t1])
                P.dve(lambda e: e.tensor_tensor(out=t2[:], in0=Pi[:, :, 1], in1=lami[:], op=ALU.mult), reads=[Pi, lami], writes=[t2])
                P.dve(lambda e: e.tensor_tensor(out=t1[:], in0=t1[:], in1=t2[:], op=ALU.add), reads=[t1, t2], writes=[t1])
                P.dve(lambda e: e.tensor_tensor(out=cr[:], in0=t1[:], in1=rden[:], op=ALU.mult), reads=[t1, rden], writes=[cr])
                P.dve(lambda e: e.tensor_tensor(out=t1[:], in0=Pi[:, :, 1], in1=lamr[:], op=ALU.mult), reads=[Pi, lamr], writes=[t1])
                P.dve(lambda e: e.tensor_tensor(out=t2[:], in0=nr[:], in1=lami[:], op=ALU.mult), reads=[nr, lami], writes=[t2])
                P.dve(lambda e: e.tensor_tensor(out=t1[:], in0=t1[:], in1=t2[:], op=ALU.subtract), reads=[t1, t2], writes=[t1])
                P.dve(lambda e: e.tensor_tensor(out=ci[:], in0=t1[:], in1=rden[:], op=ALU.mult), reads=[t1, rden], writes=[ci])
                Bbr = sb(es3, 's_Bbr', [128, 16, 16]); Bbi = sb(es3, 's_Bbi', [128, 16, 16]); tb = sb(es3, 's_tb', [128, 16, 16])
                crb = bc_ap(cr[:], [[1, 16], [0, 16]]); cib = bc_ap(ci[:], [[1, 16], [0, 16]])
                P.dve(lambda e: e.tensor_tensor(out=Bbr[:], in0=Bre[:], in1=crb, op=ALU.mult), reads=[Bre, cr], writes=[Bbr])
                P.dve(lambda e: e.tensor_tensor(out=tb[:], in0=Bim[:], in1=cib, op=ALU.mult), reads=[Bim, ci], writes=[tb])
                P.dve(lambda e: e.tensor_tensor(out=Bbr[:], in0=Bbr[:], in1=tb[:], op=ALU.subtract), reads=[Bbr, tb], writes=[Bbr])
                P.dve(lambda e: e.tensor_tensor(out=Bbi[:], in0=Bim[:], in1=crb, op=ALU.mult), reads=[Bim, cr], writes=[Bbi])
                P.dve(lambda e: e.tensor_tensor(out=tb[:], in0=Bre[:], in1=cib, op=ALU.mult), reads=[Bre, ci], writes=[tb])
                P.dve(lambda e: e.tensor_tensor(out=Bbi[:], in0=Bbi[:], in1=tb[:], op=ALU.add), reads=[Bbi, tb], writes=[Bbi])
                P.dve(lambda e: e.tensor_copy(out=AA[:, :, 0], in_=Pr[:, :, 8]), reads=[Pr], writes=[AA])
                P.dve(lambda e: e.tensor_copy(out=AA[:, :, 1], in_=Pr[:, :, 8]), reads=[Pr], writes=[AA])
                P.dve(lambda e: e.tensor_scalar(out=AXm[:, :, 0], in0=Pi[:, :, 8], scalar1=-1.0, scalar2=None, op0=ALU.mult), reads=[Pi], writes=[AXm])
                P.dve(lambda e: e.tensor_copy(out=AXm[:, :, 1], in_=Pi[:, :, 8]), reads=[Pi], writes=[AXm])
                jv2 = sb(es3, 's_jv2', [128, 32]); mag2 = sb(es3, 's_mag2', [128, 16, 32]); ang2 = sb(es3, 's_ang2', [128, 16, 32])
                nn2 = sb(es3, 's_nn2', [128, 16, 32]); ni2 = sb(es3, 's_ni2', [128, 16, 32], I32); fix2 = sb(es3, 's_fix2', [128, 16, 32])
                P.pool(lambda e: e.iota(jv2[:], pattern=[[8, 32]], base=8, channel_multiplier=0, allow_small_or_imprecise_dtypes=True), writes=[jv2])
                P.dve(lambda e: e.tensor_tensor(out=mag2[:], in0=bc_ap(lrd[:], [[1, 16], [0, 32]]), in1=bc_ap(jv2[:], [[0, 16], [1, 32]]), op=ALU.mult), reads=[lrd, jv2], writes=[mag2])
                P.dve(lambda e: e.tensor_tensor(out=ang2[:], in0=bc_ap(lid[:], [[1, 16], [0, 32]]), in1=bc_ap(jv2[:], [[0, 16], [1, 32]]), op=ALU.mult), reads=[lid, jv2], writes=[ang2])
                P.act(lambda e: e.activation(out=mag2[:], in_=mag2[:], func=AF.Exp), reads=[mag2], writes=[mag2])
                emit_sin(G, Pwi, ang2, 0.0, nn2, ni2, fix2)
                emit_sin(G, Pwr, ang2, math.pi / 2, nn2, ni2, fix2)
                P.dve(lambda e: e.tensor_tensor(out=Pwr[:], in0=Pwr[:], in1=mag2[:], op=ALU.mult), reads=[Pwr, mag2], writes=[Pwr])
                P.dve(lambda e: e.tensor_tensor(out=Pwi[:], in0=Pwi[:], in1=mag2[:], op=ALU.mult), reads=[Pwi, mag2], writes=[Pwi])
                P.dve(lambda e: e.tensor_copy(out=A32A[:, :, 0], in_=Pwr[:, :, 31]), reads=[Pwr], writes=[A32A])
                P.dve(lambda e: e.tensor_copy(out=A32A[:, :, 1], in_=Pwr[:, :, 31]), reads=[Pwr], writes=[A32A])
                P.dve(lambda e: e.tensor_scalar(out=A32X[:, :, 0], in0=Pwi[:, :, 31], scalar1=-1.0, scalar2=None, op0=ALU.mult), reads=[Pwi], writes=[A32X])
                P.dve(lambda e: e.tensor_copy(out=A32X[:, :, 1], in_=Pwi[:, :, 31]), reads=[Pwi], writes=[A32X])
                PBr = sb(es3, 's_PBr', [128, 16, 8, 16]); PBi = sb(es3, 's_PBi', [128, 16, 8, 16])
                PCr0 = sb(es3, 's_PCr0', [128, 16, 8, 16]); PCi0N = sb(es3, 's_PCi0N', [128, 16, 8, 16])
                tm1 = sb(es3, 's_tm1', [128, 16, 8, 16])

                def powslice(T_, d, start, step):
                    a_ = T_[64 * d:64 * d + 64, :, :]
                    return bass.AP(tensor=a_.tensor, offset=a_.offset + start, ap=[list(a_.ap[0]), [NJ, 16], [step, 8], [0, 16]])

                def vec16(T_, d):
                    a_ = T_[64 * d:64 * d + 64, :, :]
                    return bass.AP(tensor=a_.tensor, offset=a_.offset, ap=[list(a_.ap[0]), [16, 16], [0, 8], [1, 16]])

                def cmul(outr, outi, d, pstart, pstep, Vr, Vi, neg_im):
                    hs = slice(64 * d, 64 * d + 64)
                    pr_, pi_ = powslice(Pr, d, pstart, pstep), powslice(Pi, d, pstart, pstep)
                    vr_, vi_ = vec16(Vr, d), vec16(Vi, d)
                    P.dve(lambda e: e.tensor_tensor(out=outr[hs], in0=pr_, in1=vr_, op=ALU.mult), reads=[Pr, Vr], writes=[outr])
                    P.dve(lambda e: e.tensor_tensor(out=tm1[hs], in0=pi_, in1=vi_, op=ALU.mult), reads=[Pi, Vi], writes=[tm1])
                    P.dve(lambda e: e.tensor_tensor(out=outr[hs], in0=outr[hs], in1=tm1[hs], op=ALU.subtract), reads=[outr, tm1], writes=[outr])
                    P.dve(lambda e: e.tensor_tensor(out=outi[hs], in0=pr_, in1=vi_, op=ALU.mult), reads=[Pr, Vi], writes=[outi])
                    P.dve(lambda e: e.tensor_tensor(out=tm1[hs], in0=pi_, in1=vr_, op=ALU.mult), reads=[Pi, Vr], writes=[tm1])
                    if neg_im:
                        P.dve(lambda e: e.scalar_tensor_tensor(out=outi[hs], in0=outi[hs], scalar=-1.0, in1=tm1[hs], op0=ALU.mult, op1=ALU.subtract), reads=[outi, tm1], writes=[outi])
                    else:
                        P.dve(lambda e: e.tensor_tensor(out=outi[hs], in0=outi[hs], in1=tm1[hs], op=ALU.add), reads=[outi, tm1], writes=[outi])

                class V4:
                    def __init__(s_, ap, r):
                        s_.ap_ = ap; s_.r = r

                    def __getitem__(s_, k):
                        return s_.ap_[k]
                PCr_v = [V4(PCrD[d][:].rearrange("p g (t c) -> p g t c", t=8), PCrD[d].r) for d in range(2)]
                PCiN_v = [V4(PCiND[d][:].rearrange("p g (t c) -> p g t c", t=8), PCiND[d].r) for d in range(2)]
                cmul(PBr, PBi, 0, 16, 1, Bbr, Bbi, False)
                cmul(PBr, PBi, 1, 0, 1, Bbr, Bbi, False)
                cmul(PCr0, PCi0N, 0, 0, 1, Cre, Cim, True)
                cmul(PCr0, PCi0N, 1, 16, 1, Cre, Cim, True)
                cmul(PCr_v[0], PCiN_v[0], 0, 8, 1, Cre, Cim, True)
                cmul(PCr_v[1], PCiN_v[1], 1, 8, -1, Cre, Cim, True)
                ia = sb(es3, 's_ia', [128, 128], I32); ib = sb(es3, 's_ib', [128, 128], I32); fa = sb(es3, 's_fa', [128, 128]); fb = sb(es3, 's_fb', [128, 128])
                mF = sb(es3, 's_mF', [128, 128]); mB = sb(es3, 's_mB', [128, 128])
                P.pool(lambda e: e.iota(ia[:], pattern=[[0, 128]], base=0, channel_multiplier=1), writes=[ia])
                P.pool(lambda e: e.iota(ib[:], pattern=[[1, 128]], base=0, channel_multiplier=0), writes=[ib])
                P.dve(lambda e: e.tensor_single_scalar(out=ia[:], in_=ia[:], scalar=4, op=ALU.arith_shift_right), reads=[ia], writes=[ia])
                P.dve(lambda e: e.tensor_single_scalar(out=ib[:], in_=ib[:], scalar=4, op=ALU.arith_shift_right), reads=[ib], writes=[ib])
                P.dve(lambda e: e.tensor_copy(out=fa[:], in_=ia[:]), reads=[ia], writes=[fa])
                P.dve(lambda e: e.tensor_copy(out=fb[:], in_=ib[:]), reads=[ib], writes=[fb])
                P.dve(lambda e: e.tensor_tensor(out=mF[:], in0=fa[:], in1=fb[:], op=ALU.is_le), reads=[fa, fb], writes=[mF])
                P.dve(lambda e: e.tensor_tensor(out=mB[:], in0=fa[:], in1=fb[:], op=ALU.is_ge), reads=[fa, fb], writes=[mB])
                tt_ = [sb(es3, 's_tt%d' % i, [128, 128]) for i in range(2)]
                for g in range(16):
                    pss = []
                    for d in range(2):
                        hs = slice(64 * d, 64 * d + 64)
                        ps = G.nextps()
                        P.pe(lambda e, ps=ps, hs=hs, g=g: e.matmul(ps[:, 0:128], lhsT=PBr[hs, g].rearrange("p s c -> p (s c)"), rhs=PCr0[hs, g].rearrange("p s c -> p (s c)"), start=True, stop=False),
                             reads=[PBr, PCr0], writes=[ps])
                        P.pe(lambda e, ps=ps, hs=hs, g=g: e.matmul(ps[:, 0:128], lhsT=PBi[hs, g].rearrange("p s c -> p (s c)"), rhs=PCi0N[hs, g].rearrange("p s c -> p (s c)"), start=False, stop=True),
                             reads=[PBi, PCi0N], writes=[ps])
                        pss.append(ps)
                    P.dve(lambda e, g=g, ps=pss[0]: e.tensor_tensor(out=tt_[0][:], in0=ps[:, 0:128], in1=mF[:], op=ALU.mult), reads=[pss[0], mF], writes=[tt_[0]])
                    P.dve(lambda e, g=g, ps=pss[1]: e.tensor_tensor(out=tt_[1][:], in0=ps[:, 0:128], in1=mB[:], op=ALU.mult), reads=[pss[1], mB], writes=[tt_[1]])
                    P.pool(lambda e, g=g: e.tensor_tensor(out=Toe[:, g, :], in0=tt_[0][:], in1=tt_[1][:], op=ALU.add), reads=[tt_[0], tt_[1]], writes=[Toe])
                for ri, src in enumerate((PBr, PBi)):
                    for g4 in range(4):
                        ps = G.nextps()
                        for gg in range(4):
                            g = 4 * g4 + gg
                            P.pe(lambda e, ps=ps, gg=gg, g=g, src=src: e.transpose(out=ps[:, 128 * gg:128 * gg + 128], in_=src[:, g].rearrange("p s c -> p (s c)"), identity=G.ident[:]),
                                 reads=[src, G.ident], writes=[ps])
                        P.act(lambda e, ps=ps, ri=ri, g4=g4: e.activation(out=PBT[:, ri, 4 * g4:4 * g4 + 4, :], in_=ps[:, :].rearrange("p (g q) -> p g q", g=4), func=AF.Copy),
                              reads=[ps], writes=[PBT])
                P.flush()
            if 's5_p0' in G.dbg:
                return
            Z = sb(es, 's_Z', [128, 16, 2, ZW])
            X = sb(es, 's_X', [128, 16, 544])
            P.pool(lambda e: e.memset(Z[:], 0.0), writes=[Z])
            with contextlib.ExitStack() as es3:
                U = [sb(es3, 's_U%d' % i, [128, 8, 256]) for i in range(2)]
                Uc = [sb(es3, 's_Uc%d' % i, [128, 16, 128]) for i in range(2)]
                for ti, (k0, nb, zc) in enumerate(S5_BT):
                    u = U[ti % 2]; uc = Uc[ti % 2]
                    src = bass.AP(tensor=sc['TM1'].t.tensor, offset=sc['TM1'].t.offset + k0 * 8 * 784 + 528, ap=[[8 * 784, nb], [784, 8], [1, 256]])
                    P.dma(u[0:nb], src, reads=[sc['TM1']], writes=[u])
                    P.pool(lambda e, u=u, uc=uc, nb=nb: e.tensor_copy(out=uc[0:nb].rearrange("p g (s c) -> p g s c", s=8), in_=u[0:nb].rearrange("p s (g c) -> p g s c", g=16)),
                           reads=[u], writes=[uc])
                    for g4 in range(4):
                        ps = G.nextps()
                        for gg in range(4):
                            P.pe(lambda e, ps=ps, gg=gg, g=4 * g4 + gg, uc=uc, nb=nb: e.transpose(out=ps[:, 128 * gg:128 * gg + nb], in_=uc[0:nb, g, :], identity=G.ident[0:nb, 0:nb]),
                                 reads=[uc, G.ident], writes=[ps])
                        xdst = X[:, 4 * g4:4 * g4 + 4, k0:k0 + nb]
                        P.act(lambda e, ps=ps, xdst=xdst, nb=nb: e.activation(out=xdst, in_=ps[:, :].rearrange("p (g q) -> p g q", g=4)[:, :, 0:nb], func=AF.Copy), reads=[ps], writes=[X])
                for g in range(16):
                    for ri in range(2):
                        for (k0, nb) in ((0, 32), (32, 256), (288, 256)):
                            ps = G.nextps()
                            P.pe(lambda e, ps=ps, g=g, ri=ri, k0=k0, nb=nb: e.matmul(ps[:, 0:nb], lhsT=PBT[:, ri, g, :], rhs=X[:, g, k0:k0 + nb], start=True, stop=True), reads=[PBT, X], writes=[ps])
                            zf, zb = s5_colF(k0), s5_colB(k0)
                            P.act(lambda e, ps=ps, g=g, ri=ri, zf=zf, nb=nb: e.activation(out=Z[0:64, g, ri, zf:zf + nb], in_=ps[0:64, 0:nb], func=AF.Copy), reads=[ps], writes=[Z])
                            P.dve(lambda e, ps=ps, g=g, ri=ri, zb=zb, nb=nb: e.tensor_copy(out=Z[64:128, g, ri, zb:zb + nb], in_=ps[64:128, 0:nb]), reads=[ps], writes=[Z])
                P.flush()
        if 's5_p2' in G.dbg:
            return
        with contextlib.ExitStack() as es2:
            m1 = [sb(es2, 's_m1%d' % d, [128, 16, 2, 17]) for d in range(2)]
            m2 = [sb(es2, 's_m2%d' % d, [128, 16, 2, 17]) for d in range(2)]
            Sb = sb(es2, 's_Sb', [128, 16, 2, 17])
            t1 = [sb(es2, 's_c1%d' % d, [128, 2, 16, 32]) for d in range(2)]
            t2 = [sb(es2, 's_c2%d' % d, [128, 2, 16, 32]) for d in range(2)]
            Zres = [Res('Zf'), Res('Zb')]

            def zview(d, col, swap=False, nseg=17):
                zp = Z[64 * d:64 * d + 64, :, :, col]
                if swap:
                    return bass.AP(tensor=zp.tensor, offset=zp.offset + ZW, ap=[list(zp.ap[0]), [2 * ZW, 16], [-ZW, 2], [32, nseg]])
                return bass.AP(tensor=zp.tensor, offset=zp.offset, ap=[list(zp.ap[0]), [2 * ZW, 16], [ZW, 2], [32, nseg]])

            def tbc(T_, d, n):
                a_ = T_[64 * d:64 * d + 64]
                return bass.AP(tensor=a_.tensor, offset=a_.offset, ap=[list(a_.ap[0]), [2, 16], [1, 2], [0, n]])

            def sbv(d, idx, swap=False):
                a_ = Sb[64 * d:64 * d + 64, :, :, idx]
                if swap:
                    return bass.AP(tensor=a_.tensor, offset=a_.offset + 17, ap=[list(a_.ap[0]), [34, 16], [-17, 2]])
                return a_
            def direction(d):
                hs = slice(64 * d, 64 * d + 64)
                emit = P.dve if d == 0 else P.pool
                base = 3 if d == 0 else 1
                js = range(1, 32) if d == 0 else range(30, -1, -1)
                for j in js:
                    cur = base + j
                    prev = cur - 1 if d == 0 else cur + 1
                    emit(lambda e, d=d, prev=prev: e.tensor_tensor(out=m1[d][hs], in0=zview(d, prev), in1=tbc(AA, d, 17), op=ALU.mult), reads=[Zres[d], AA], writes=[m1[d]])
                    emit(lambda e, d=d, prev=prev: e.tensor_tensor(out=m2[d][hs], in0=zview(d, prev, True), in1=tbc(AXm, d, 17), op=ALU.mult), reads=[Zres[d], AXm], writes=[m2[d]])
                    emit(lambda e, d=d: e.tensor_tensor(out=m1[d][hs], in0=m1[d][hs], in1=m2[d][hs], op=ALU.add), reads=[m1[d], m2[d]], writes=[m1[d]])
                    emit(lambda e, d=d, cur=cur: e.tensor_tensor(out=zview(d, cur), in0=zview(d, cur), in1=m1[d][hs], op=ALU.add), reads=[m1[d], Zres[d]], writes=[Zres[d]])
                sres = Res('Sb%d' % d)
                if d == 0:
                    order = list(range(0, 16)); endcol = lambda sg: 3 + 32 * sg + 31
                else:
                    order = list(range(16, 0, -1)); endcol = lambda sg: 1 + 32 * sg
                for n_, sg in enumerate(order):
                    if n_ == 0:
                        emit(lambda e, d=d, sg=sg: e.tensor_copy(out=Sb[hs, :, :, sg], in_=Z[hs, :, :, endcol(sg)]), reads=[Zres[d]], writes=[sres])
                    else:
                        pv = order[n_ - 1]
                        emit(lambda e, d=d, pv=pv: e.tensor_tensor(out=m1[d][hs, :, :, 0], in0=sbv(d, pv), in1=A32A[hs], op=ALU.mult), reads=[sres, A32A], writes=[m1[d]])
                        emit(lambda e, d=d, pv=pv: e.tensor_tensor(out=m2[d][hs, :, :, 0], in0=sbv(d, pv, True), in1=A32X[hs], op=ALU.mult), reads=[sres, A32X], writes=[m2[d]])
                        emit(lambda e, d=d: e.tensor_tensor(out=m1[d][hs, :, :, 0], in0=m1[d][hs, :, :, 0], in1=m2[d][hs, :, :, 0], op=ALU.add), reads=[m1[d], m2[d]], writes=[m1[d]])
                        emit(lambda e, d=d, sg=sg: e.tensor_tensor(out=Sb[hs, :, :, sg], in0=Z[hs, :, :, endcol(sg)], in1=m1[d][hs, :, :, 0], op=ALU.add), reads=[m1[d], Zres[d]], writes=[sres])
                def corr(gq):
                    gs = slice(2 * gq, 2 * gq + 2)

                    def pwv(T_, rev):
                        a_ = T_[hs, gs, :]
                        if rev:
                            return bass.AP(tensor=a_.tensor, offset=a_.offset + 31, ap=[list(a_.ap[0]), [32, 2], [0, 16], [-1, 32]])
                        return bass.AP(tensor=a_.tensor, offset=a_.offset, ap=[list(a_.ap[0]), [32, 2], [0, 16], [1, 32]])

                    def sbb(ri, start):
                        a_ = Sb[hs, gs, ri, start:start + 16]
                        return bass.AP(tensor=a_.tensor, offset=a_.offset, ap=[list(a_.ap[0]), [34, 2], [1, 16], [0, 32]])

                    def zt(ri, col0):
                        a_ = Z[hs, gs, ri, col0]
                        return bass.AP(tensor=a_.tensor, offset=a_.offset, ap=[list(a_.ap[0]), [2 * ZW, 2], [32, 16], [1, 32]])
                    rev = (d == 1)
                    s0 = 0 if d == 0 else 1
                    col0 = 35 if d == 0 else 1
                    a1, a2 = t1[d][hs], t2[d][hs]
                    for (ri, x_, y_, op_) in ((0, 0, 1, ALU.subtract), (1, 1, 0, ALU.add)):
                        emit(lambda e, x_=x_: e.tensor_tensor(out=a1, in0=pwv(Pwr, rev), in1=sbb(x_, s0), op=ALU.mult), reads=[Pwr, sres], writes=[t1[d]])
                        emit(lambda e, y_=y_: e.tensor_tensor(out=a2, in0=pwv(Pwi, rev), in1=sbb(y_, s0), op=ALU.mult), reads=[Pwi, sres], writes=[t2[d]])
                        emit(lambda e, op_=op_: e.tensor_tensor(out=a1, in0=a1, in1=a2, op=op_), reads=[t1[d], t2[d]], writes=[t1[d]])
                        emit(lambda e, ri=ri: e.tensor_tensor(out=zt(ri, col0), in0=zt(ri, col0), in1=a1, op=ALU.add), reads=[t1[d], Zres[d]], writes=[Zres[d]])
                for gq in range(8):
                    corr(gq)
            for d in range(2):
                direction(d)
            P.flush()
        if 's5_p3' in G.dbg:
            return
        with contextlib.ExitStack() as es2:
            U = [sb(es2, 's_U2%d' % i, [128, 8, 256]) for i in range(1)]
            Y = [sb(es2, 's_Y%d' % i, [128, 8, 256]) for i in range(1)]
            t3 = sb(es2, 's_t3', [128, 8, 256])
            gT = [sb(es2, 's_gT%d' % i, [128, 2, 1024], BF16) for i in range(1)]
            Dsk = load_bcast(G, es2, 's_D', I['s5_d'][l:l + 1, :], 256)
            wg = sb(es2, 's_wg', [128, 2, 512], BF16)
            P.dmaq('pool', wg[:], I['s5_glu_w'][l].rearrange("(k p) n -> p k n", p=128), writes=[wg])
            sg = [sb(es2, 's_sg%d' % i, [128, 512]) for i in range(2)]
            yb = [sb(es2, 's_yb%d' % i, [128, 512], BF16) for i in range(2)]
            cnt = {'n': 0}

            def out_tile(ti, k0, nb, zc):
                u = U[0]; y = Y[0]; g_ = gT[0]
                src = bass.AP(tensor=sc['TM1'].t.tensor, offset=sc['TM1'].t.offset + k0 * 8 * 784 + 528, ap=[[8 * 784, nb], [784, 8], [1, 256]])
                P.dma(u[0:nb], src, reads=[sc['TM1']], writes=[u])
                for g4 in range(4):
                    ps = G.nextps()
                    for gg in range(4):
                        g = 4 * g4 + gg
                        o = ps[0:nb, 128 * gg:128 * gg + 128]
                        P.pe(lambda e, o=o, g=g: e.matmul(o, lhsT=X[:, g, k0:k0 + nb], rhs=Toe[:, g, :], start=True, stop=False), reads=[X, Toe], writes=[ps])
                        zf = s5_colF(k0) - 1
                        zb = s5_colB(k0) + 1
                        P.pe(lambda e, o=o, g=g, zf=zf: e.matmul(o, lhsT=Z[:, g, 0, zf:zf + nb], rhs=PCrD[0][:, g, :], start=False, stop=False), reads=[Z, PCrD[0]], writes=[ps])
                        P.pe(lambda e, o=o, g=g, zf=zf: e.matmul(o, lhsT=Z[:, g, 1, zf:zf + nb], rhs=PCiND[0][:, g, :], start=False, stop=False), reads=[Z, PCiND[0]], writes=[ps])
                        P.pe(lambda e, o=o, g=g, zb=zb: e.matmul(o, lhsT=Z[:, g, 0, zb:zb + nb], rhs=PCrD[1][:, g, :], start=False, stop=False), reads=[Z, PCrD[1]], writes=[ps])
                        P.pe(lambda e, o=o, g=g, zb=zb: e.matmul(o, lhsT=Z[:, g, 1, zb:zb + nb], rhs=PCiND[1][:, g, :], start=False, stop=True), reads=[Z, PCiND[1]], writes=[ps])
                    ydst = y[0:nb].rearrange("p t (g c) -> p g t c", g=16)[:, 4 * g4:4 * g4 + 4]
                    if g4 % 2 == 0:
                        P.act(lambda e, ps=ps, ydst=ydst: e.activation(out=ydst, in_=ps[0:nb, :].rearrange("p (g t c) -> p g t c", g=4, t=8), func=AF.Copy), reads=[ps], writes=[y])
                    else:
                        P.dve(lambda e, ps=ps, ydst=ydst: e.tensor_copy(out=ydst, in_=ps[0:nb, :].rearrange("p (g t c) -> p g t c", g=4, t=8)), reads=[ps], writes=[y])
                if 's5_p4a' in G.dbg:
                    return
                P.pool(lambda e: e.tensor_tensor(out=t3[0:nb], in0=u[0:nb], in1=bc_ap(Dsk[0:nb, :], [[0, 8], [1, 256]]), op=ALU.mult), reads=[u, Dsk], writes=[t3])
                P.dve(lambda e: e.tensor_tensor(out=y[0:nb], in0=y[0:nb], in1=t3[0:nb], op=ALU.add), reads=[y, t3], writes=[y])
                P.pool(lambda e: e.tensor_tensor(out=t3[0:nb], in0=y[0:nb], in1=y[0:nb], op=ALU.mult), reads=[y], writes=[t3])
                P.dve(lambda e: e.tensor_scalar(out=t3[0:nb], in0=t3[0:nb], scalar1=0.044715, scalar2=1.0, op0=ALU.mult, op1=ALU.add), reads=[t3], writes=[t3])
                P.pool(lambda e: e.tensor_tensor(out=t3[0:nb], in0=t3[0:nb], in1=y[0:nb], op=ALU.mult), reads=[t3, y], writes=[t3])
                P.act(lambda e: e.activation(out=t3[0:nb], in_=t3[0:nb], func=AF.Sigmoid, scale=2.0 * math.sqrt(2.0 / math.pi)), reads=[t3], writes=[t3])
                P.dve(lambda e: e.tensor_tensor(out=y[0:nb], in0=y[0:nb], in1=t3[0:nb], op=ALU.mult), reads=[y, t3], writes=[y])
                if 's5_p4b' in G.dbg:
                    return
                for c2 in range(2):
                    for t4 in range(2):
                        ps = G.nextps()
                        for tt in range(4):
                            t = 4 * t4 + tt
                            P.pe(lambda e, ps=ps, tt=tt, t=t, c2=c2: e.transpose(out=ps[:, 128 * tt:128 * tt + nb], in_=y[0:nb, t, 128 * c2:128 * c2 + 128], identity=G.ident[0:nb, 0:nb]),
                                 reads=[y, G.ident], writes=[ps])
                        gdst = g_[:, c2, 0:8 * nb].rearrange("p (k t) -> p t k", t=8)[:, 4 * t4:4 * t4 + 4, :]
                        P.act(lambda e, ps=ps, gdst=gdst: e.activation(out=gdst, in_=ps[:, :].rearrange("p (t q) -> p t q", t=4)[:, :, 0:nb], func=AF.Copy), reads=[ps], writes=[g_])
                if 's5_p4c' in G.dbg:
                    return
                ntok = 8 * nb
                tok0 = 8 * k0
                for n0 in range(0, ntok, 512):
                    nn_ = min(512, ntok - n0)
                    for c in range(2):
                        psa = G.nextps(); psb = G.nextps()
                        for kc in range(2):
                            P.pe(lambda e, psa=psa, kc=kc, c=c, n0=n0, nn_=nn_: e.matmul(psa[:, 0:nn_], lhsT=wg[:, kc, 128 * c:128 * c + 128], rhs=g_[:, kc, n0:n0 + nn_], start=(kc == 0), stop=(kc == 1)),
                                 reads=[wg, g_], writes=[psa])
                        for kc in range(2):
                            P.pe(lambda e, psb=psb, kc=kc, c=c, n0=n0, nn_=nn_: e.matmul(psb[:, 0:nn_], lhsT=wg[:, kc, 256 + 128 * c:256 + 128 * c + 128], rhs=g_[:, kc, n0:n0 + nn_], start=(kc == 0), stop=(kc == 1)),
                                 reads=[wg, g_], writes=[psb])
                        s_ = sg[cnt['n'] % 2]; o_ = yb[cnt['n'] % 2]; cnt['n'] += 1
                        P.act(lambda e, psb=psb, s_=s_, nn_=nn_: e.activation(out=s_[:, 0:nn_], in_=psb[:, 0:nn_], func=AF.Sigmoid), reads=[psb], writes=[s_])
                        P.dve(lambda e, psa=psa, s_=s_, o_=o_, nn_=nn_: e.tensor_tensor(out=o_[:, 0:nn_], in0=psa[:, 0:nn_], in1=s_[:, 0:nn_], op=ALU.mult), reads=[psa, s_], writes=[o_])
                        P.dma(yT[2 + c, :, tok0 + n0:tok0 + n0 + nn_], o_[:, 0:nn_], reads=[o_], writes=[yT])

            for ti, (k0, nb, zc) in enumerate(S5_BT):
                if ti == 0 and not with_ctx:
                    continue
                out_tile(ti, k0, nb, zc)
            P.flush()


def emit_norm2(G, xt, n, gam_fn, sh_fn, out_fn, out_res, tmp_bufs):
    P = G.P
    sq, rstd, tmp = tmp_bufs
    ps = G.nextps()
    P.act(lambda e: e.activation(out=sq[:, :, 0:n], in_=xt[:, :, 0:n], func=AF.Square), reads=[xt], writes=[sq])
    for k in range(8):
        P.pe(lambda e, k=k: e.matmul(ps[:, 0:n], lhsT=G.onesb[:], rhs=sq[:, k, 0:n], start=(k == 0), stop=(k == 7)), reads=[sq, G.onesb], writes=[ps])
    P.act(lambda e: e.activation(out=rstd[:, 0:n], in_=ps[:, 0:n], func=AF.Sqrt, scale=1.0 / D, bias=G.epsb[:, 0:1]), reads=[ps, G.epsb], writes=[rstd])
    P.dve(lambda e: e.reciprocal(out=rstd[:, 0:n], in_=rstd[:, 0:n]), reads=[rstd], writes=[rstd])
    for k in range(8):
        t = tmp[k % len(tmp)]
        P.dve(lambda e, k=k, t=t: e.tensor_tensor(out=t[:, 0:n], in0=xt[:, k, 0:n], in1=rstd[:, 0:n], op=ALU.mult), reads=[xt, rstd], writes=[t])
        if sh_fn is None:
            P.act(lambda e, k=k, t=t: e.activation(out=out_fn(k), in_=t[:, 0:n], func=AF.Copy, scale=gam_fn(k)), reads=[t, G.cm], writes=[out_res])
        else:
            P.act(lambda e, k=k, t=t: e.activation(out=out_fn(k), in_=t[:, 0:n], func=AF.Identity, scale=gam_fn(k), bias=sh_fn(k)), reads=[t, G.cm], writes=[out_res])


def precast_tail_weights(G, l):
    for _ in precast_iter(G, l):
        pass


def precast_iter(G, l):
    P, I, sc = G.P, G.I, G.scr
    if 'wt_gate' not in sc:
        G.scratch('wt_gate', [32, 128, 8 * 128], BF16); G.scratch('wt_br', [32, 128, 4 * 128], BF16)
        G.scratch('wt_out', [8, 128, 8 * 128], BF16); G.scratch('wt_ffa', [22, 128, 8 * 128], BF16)
        G.scratch('wt_ffb', [22, 128, 8 * 128], BF16); G.scratch('wt_ff2', [8, 128, 22 * 128], BF16)
    BRW = [('w_branch_a', 2), ('w_branch_b', 2), ('w_branch_c', 2), ('w_branch_d', 4)]
    for oc in range(8):
        for i, (wname, nk) in enumerate(BRW):
            c = oc * 4 + i
            c0 = O_GATE + i * 1024 + oc * 128
            P.dmaq('pool', sc['wt_gate'][c].rearrange("p (k n) -> p k n", k=8), I['w_in'][l, :, c0:c0 + 128].rearrange("(k p) n -> p k n", p=128), writes=[sc['wt_gate']])
            yield
            P.dmaq('pool', sc['wt_br'][c, :, 0:nk * 128].rearrange("p (k n) -> p k n", k=nk), I[wname][l, :, oc * 128:(oc + 1) * 128].rearrange("(k p) n -> p k n", p=128), writes=[sc['wt_br']])
        yield
        P.dmaq('pool', sc['wt_out'][oc].rearrange("p (k n) -> p k n", k=8), I['w_out'][l, :, oc * 128:(oc + 1) * 128].rearrange("(k p) n -> p k n", p=128), writes=[sc['wt_out']])
        yield
        P.dmaq('pool', sc['wt_ff2'][oc].rearrange("p (k n) -> p k n", k=22), I['ffn_w_out'][l, :, oc * 128:(oc + 1) * 128].rearrange("(k p) n -> p k n", p=128), writes=[sc['wt_ff2']])
    for hc in range(22):
        yield
        P.dmaq('pool', sc['wt_ffa'][hc].rearrange("p (k n) -> p k n", k=8), I['ffn_w_in'][l, :, hc * 128:(hc + 1) * 128].rearrange("(k p) n -> p k n", p=128), writes=[sc['wt_ffa']])
        yield
        P.dmaq('pool', sc['wt_ffb'][hc].rearrange("p (k n) -> p k n", k=8), I['ffn_w_in'][l, :, FFN_H + hc * 128:FFN_H + (hc + 1) * 128].rearrange("(k p) n -> p k n", p=128), writes=[sc['wt_ffb']])


def stage_tail(G, l):
    nc, P, I = G.nc, G.P, G.I
    sb = G.sb
    sc = G.scr
    with_ctx = l < DEPTH - 1
    last = l == DEPTH - 1
    xsT, hxT_d, yT = sc['xsT'], sc['hxT'], sc['yT']
    BR = [('w_branch_a', 0, 2), ('w_branch_b', 2, 2), ('w_branch_c', 4, 2), ('w_branch_d', 6, 4)]
    with contextlib.ExitStack() as es:
        hx = sb(es, 't_hx', [128, 8, 512], BF16); y = sb(es, 't_y', [128, 10, 512], BF16); x = sb(es, 't_x', [128, 8, 512])
        m = sb(es, 't_m', [128, 8, 512], BF16); h2 = sb(es, 't_h2', [128, 8, 512], BF16); u = sb(es, 't_u', [128, 22, 512], BF16)
        sq = sb(es, 't_sq', [128, 8, 512], BF16); rstd = sb(es, 't_rstd', [128, 512]); tmp = [sb(es, 't_tmp%d' % i, [128, 512]) for i in range(2)]
        wg = [sb(es, 't_wg%d' % i, [128, 8, 128], BF16) for i in range(6)]
        wbr = [sb(es, 't_wbr%d' % i, [128, 4, 128], BF16) for i in range(6)]
        wo = [sb(es, 't_wo%d' % i, [128, 8, 128], BF16) for i in range(3)]
        wab = [sb(es, 't_wab%d' % i, [128, 8, 128], BF16) for i in range(8)]
        w2 = [sb(es, 't_w2%d' % i, [128, 22, 128], BF16) for i in range(3)]
        gs = [sb(es, 't_gs%d' % i, [128, 512]) for i in range(2)]
        acc = sb(es, 't_acc', [128, 512]); tm = [sb(es, 't_tm%d' % i, [128, 512]) for i in range(2)]
        if last:
            fnw = sb(es, 't_fnw', [128, 8])
            P.dma(fnw[:], I['final_norm_w'].rearrange("(k p) -> p k", p=128), writes=[fnw], allow_slow_non_contiguous=True)
            xn = sb(es, 't_xn', [128, 8, 512]); ot = [sb(es, 't_ot%d' % i, [128, D]) for i in range(2)]
        cnt = {'wg': 0, 'wbr': 0, 'wo': 0, 'wab': 0, 'w2': 0, 'g': 0, 'ot': 0}

        def tile(ti, t0, n):
            s = 1 if ti == 0 else 0
            P.dma(hx[:, :, 0:n], hxT_d[:, :, t0:t0 + n].rearrange("k p t -> p k t"), reads=[hxT_d], writes=[hx])
            P.dma(y[:, :, 0:n], yT[:, :, t0:t0 + n].rearrange("k p t -> p k t"), reads=[yT], writes=[y])
            P.dma(x[:, :, 0:n], xsT[:, :, t0:t0 + n].rearrange("k p t -> p k t"), reads=[xsT], writes=[x])
            for oc in range(8):
                for i, (wname, yb0, nk) in enumerate(BR):
                    w = wg[cnt['wg'] % 6]; cnt['wg'] += 1
                    c0 = O_GATE + i * 1024 + oc * 128
                    P.dma(w[:].rearrange("p k n -> p (k n)"), sc['wt_gate'][oc * 4 + i], reads=[sc['wt_gate']], writes=[w])
                    wb = wbr[cnt['wbr'] % 6]; cnt['wbr'] += 1
                    P.dma(wb[:, 0:nk, :].rearrange("p k n -> p (k n)"), sc['wt_br'][oc * 4 + i, :, 0:nk * 128], reads=[sc['wt_br']], writes=[wb])
                    psg = G.nextps(); psb = G.nextps()
                    for k in range(8):
                        P.pe(lambda e, psg=psg, k=k, w=w: e.matmul(psg[:, 0:n], lhsT=w[:, k, :], rhs=hx[:, k, 0:n], start=(k == 0), stop=(k == 7)), reads=[w, hx], writes=[psg])
                    for k in range(nk):
                        P.pe(lambda e, psb=psb, k=k, wb=wb, yb0=yb0, nk=nk: e.matmul(psb[:, 0:n], lhsT=wb[:, k, :], rhs=y[:, yb0 + k, 0:n], start=(k == 0), stop=(k == nk - 1)),
                             reads=[wb, y], writes=[psb])
                    g_ = gs[cnt['g'] % 2]; cnt['g'] += 1
                    P.act(lambda e, psg=psg, g_=g_: e.activation(out=g_[:, 0:n], in_=psg[:, 0:n], func=AF.Sigmoid), reads=[psg], writes=[g_])
                    if i == 0:
                        P.dve(lambda e, psb=psb, g_=g_: e.tensor_tensor(out=acc[:, 0:n], in0=psb[:, 0:n], in1=g_[:, 0:n], op=ALU.mult), reads=[psb, g_], writes=[acc])
                    else:
                        t_ = tm[i % 2]
                        P.dve(lambda e, psb=psb, g_=g_, t_=t_: e.tensor_tensor(out=t_[:, 0:n], in0=psb[:, 0:n], in1=g_[:, 0:n], op=ALU.mult), reads=[psb, g_], writes=[t_])
                        if i < 3:
                            P.pool(lambda e, t_=t_: e.tensor_tensor(out=acc[:, 0:n], in0=acc[:, 0:n], in1=t_[:, 0:n], op=ALU.add), reads=[acc, t_], writes=[acc])
                        else:
                            P.pool(lambda e, t_=t_, oc=oc: e.tensor_tensor(out=m[:, oc, 0:n], in0=acc[:, 0:n], in1=t_[:, 0:n], op=ALU.add), reads=[acc, t_], writes=[m])
            for oc in range(8):
                w = wo[cnt['wo'] % 3]; cnt['wo'] += 1
                P.dma(w[:].rearrange("p k n -> p (k n)"), sc['wt_out'][oc], reads=[sc['wt_out']], writes=[w])
                ps = G.nextps()
                for k in range(8):
                    P.pe(lambda e, ps=ps, k=k, w=w: e.matmul(ps[:, 0:n], lhsT=w[:, k, :], rhs=m[:, k, 0:n], start=(k == 0), stop=(k == 7)), reads=[w, m], writes=[ps])
                P.dve(lambda e, ps=ps, oc=oc: e.scalar_tensor_tensor(out=x[:, oc, 0:n], in0=ps[:, 0:n], scalar=G.cm[:, l, 2, oc, s:s + 1], in1=x[:, oc, 0:n], op0=ALU.mult, op1=ALU.add),
                      reads=[ps, G.cm, x], writes=[x])
            emit_norm2(G, x, n, lambda k: G.cm[:, l, 4, k, s:s + 1], lambda k: G.cm[:, l, 3, k, s:s + 1], lambda k: h2[:, k, 0:n], h2, (sq, rstd, tmp))
            for hc in range(22):
                wa = wab[cnt['wab'] % 8]; cnt['wab'] += 1
                wb = wab[cnt['wab'] % 8]; cnt['wab'] += 1
                P.dma(wa[:].rearrange("p k n -> p (k n)"), sc['wt_ffa'][hc], reads=[sc['wt_ffa']], writes=[wa])
                P.dma(wb[:].rearrange("p k n -> p (k n)"), sc['wt_ffb'][hc], reads=[sc['wt_ffb']], writes=[wb])
                psa = G.nextps(); psb = G.nextps()
                for k in range(8):
                    P.pe(lambda e, psa=psa, k=k, wa=wa: e.matmul(psa[:, 0:n], lhsT=wa[:, k, :], rhs=h2[:, k, 0:n], start=(k == 0), stop=(k == 7)), reads=[wa, h2], writes=[psa])
                for k in range(8):
                    P.pe(lambda e, psb=psb, k=k, wb=wb: e.matmul(psb[:, 0:n], lhsT=wb[:, k, :], rhs=h2[:, k, 0:n], start=(k == 0), stop=(k == 7)), reads=[wb, h2], writes=[psb])
                g_ = gs[cnt['g'] % 2]; cnt['g'] += 1
                P.act(lambda e, psa=psa, g_=g_: e.activation(out=g_[:, 0:n], in_=psa[:, 0:n], func=AF.Silu), reads=[psa], writes=[g_])
                P.dve(lambda e, psb=psb, g_=g_, hc=hc: e.tensor_tensor(out=u[:, hc, 0:n], in0=psb[:, 0:n], in1=g_[:, 0:n], op=ALU.mult), reads=[psb, g_], writes=[u])
            for oc in range(8):
                w = w2[cnt['w2'] % 3]; cnt['w2'] += 1
                P.dma(w[:].rearrange("p k n -> p (k n)"), sc['wt_ff2'][oc], reads=[sc['wt_ff2']], writes=[w])
                ps = G.nextps()
                for k in range(22):
                    P.pe(lambda e, ps=ps, k=k, w=w: e.matmul(ps[:, 0:n], lhsT=w[:, k, :], rhs=u[:, k, 0:n], start=(k == 0), stop=(k == 21)), reads=[w, u], writes=[ps])
                P.dve(lambda e, ps=ps, oc=oc: e.scalar_tensor_tensor(out=x[:, oc, 0:n], in0=ps[:, 0:n], scalar=G.cm[:, l, 5, oc, s:s + 1], in1=x[:, oc, 0:n], op0=ALU.mult, op1=ALU.add),
                      reads=[ps, G.cm, x], writes=[x])
            if not last:
                P.dma(xsT[:, :, t0:t0 + n].rearrange("k p t -> p k t"), x[:, :, 0:n], reads=[x], writes=[xsT])
            else:
                emit_norm2(G, x, n, lambda k: fnw[:, k:k + 1], None, lambda k: xn[:, k, 0:n], xn, (sq, rstd, tmp))
                for q in range(n // 128):
                    o = ot[cnt['ot'] % 2]; cnt['ot'] += 1
                    for half in range(2):
                        ps = G.nextps()
                        for kk in range(4):
                            k = 4 * half + kk
                            P.pe(lambda e, ps=ps, kk=kk, k=k, q=q: e.transpose(out=ps[:, 128 * kk:128 * kk + 128], in_=xn[:, k, 128 * q:128 * q + 128], identity=G.ident[:]),
                                 reads=[xn, G.ident], writes=[ps])
                        if half == 0:
                            P.act(lambda e, ps=ps, o=o: e.activation(out=o[:, 0:512], in_=ps[:, :], func=AF.Copy), reads=[ps], writes=[o])
                        else:
                            P.dve(lambda e, ps=ps, o=o: e.tensor_copy(out=o[:, 512:1024], in_=ps[:, :]), reads=[ps], writes=[o])
                    tok = t0 - CTX + 128 * q
                    P.dma(G.out[tok:tok + 128, :], o[:], reads=[o])

        for ti, (t0, n) in enumerate(TT):
            if ti == 0 and not with_ctx:
                continue
            tile(ti, t0, n)
        P.flush()


def gate_cumsums(G, es, pfx, lf_all, ncol):
    P, sb = G.P, G.sb
    h = ncol // 2
    lfF = sb(es, pfx + 'lfF', [128, NT128, h]); lfB = sb(es, pfx + 'lfB', [128, NT128, h])
    cum_all = sb(es, pfx + 'cum', [128, NT128, ncol]); tot_all = sb(es, pfx + 'tot', [128, NT128, ncol])
    P.dve(lambda e: e.tensor_copy(out=lfF[:], in_=lf_all[:, :, 0:h]), reads=[lf_all], writes=[lfF])
    P.dve(lambda e: e.tensor_copy(out=lfB[:], in_=lf_all[:, :, h:ncol]), reads=[lf_all], writes=[lfB])
    n = NT128 * h
    psF = G.nextps(); psB = G.nextps()
    P.pe(lambda e: e.matmul(psF[:, 0:n], lhsT=G.triU[:], rhs=lfF[:].rearrange("p q c -> p (q c)"), start=True, stop=True), reads=[G.triU, lfF], writes=[psF])
    P.pe(lambda e: e.matmul(psB[:, 0:n], lhsT=G.triL[:], rhs=lfB[:].rearrange("p q c -> p (q c)"), start=True, stop=True), reads=[G.triL, lfB], writes=[psB])
    P.dve(lambda e: e.tensor_copy(out=cum_all[:, :, 0:h], in_=psF[:, 0:n].rearrange("p (q c) -> p q c", c=h)), reads=[psF], writes=[cum_all])
    P.dve(lambda e: e.tensor_copy(out=cum_all[:, :, h:ncol], in_=psB[:, 0:n].rearrange("p (q c) -> p q c", c=h)), reads=[psB], writes=[cum_all])
    nt = NT128 * ncol
    half = nt // 2
    for i in range(2):
        ps = G.nextps()
        P.pe(lambda e, ps=ps, i=i: e.matmul(ps[:, 0:half], lhsT=G.ones[:], rhs=lf_all[:].rearrange("p q c -> p (q c)")[:, i * half:(i + 1) * half], start=True, stop=True),
             reads=[G.ones, lf_all], writes=[ps])
        P.act(lambda e, ps=ps, i=i: e.activation(out=tot_all[:].rearrange("p q c -> p (q c)")[:, i * half:(i + 1) * half], in_=ps[:, 0:half], func=AF.Copy), reads=[ps], writes=[tot_all])
    return cum_all, tot_all


def mlstm_steps2(G, l, es):
    nc, P, I = G.nc, G.P, G.I
    sb = G.sb
    sc = G.scr
    with_ctx = l < DEPTH - 1
    if 'yT' not in sc:
        G.scratch('yT', [10, 128, S], BF16)
    yT = sc['yT']
    if 'hacc_d' not in sc:
        G.scratch('hacc_d', [S, 256]); G.scratch('yacc_d', [S, 512])
    hacc_d = sc['hacc_d']
    BF16 = mybir.dt.bfloat16 if BF_M else F32
    hst = [sb(es, 'm_hst%d' % i, [128, 256]) for i in range(2)]; hprev = [sb(es, 'm_hprev%d' % i, [128, 256]) for i in range(2)]
    gates = sb(es, 'm_gates', [128, NT128, 24])
    nw_b = load_bcast(G, es, 'm_nw', I['mlstm_norm_w'][l:l + 1, :], 256)
    with contextlib.ExitStack() as es0:
        ib_b = load_bcast(G, es0, 'm_ib', I['mlstm_ib'][l:l + 1].rearrange("o d h -> o (d h)"), 8)
        fb_b = load_bcast(G, es0, 'm_fb', I['mlstm_fb'][l:l + 1].rearrange("o d h -> o (d h)"), 8)
        gi = sb(es0, 'm_gi', [128, NT128, 16]); li = sb(es0, 'm_li', [128, NT128, 8]); lf = sb(es0, 'm_lf', [128, NT128, 8])
        P.dma(gi[:], sc['TM1'][:, 512:528].rearrange("(q p) c -> p q c", p=128), reads=[sc['TM1']], writes=[gi])
        P.dve(lambda e: e.tensor_tensor(out=li[:], in0=gi[:, :, 0:8], in1=bc_ap(ib_b[:], [[0, NT128], [1, 8]]), op=ALU.add), reads=[gi, ib_b], writes=[li])
        P.dve(lambda e: e.tensor_tensor(out=lf[:], in0=gi[:, :, 8:16], in1=bc_ap(fb_b[:], [[0, NT128], [1, 8]]), op=ALU.add), reads=[gi, fb_b], writes=[lf])
        P.act(lambda e: e.activation(out=lf[:], in_=lf[:], func=AF.Exp, scale=-1.0), reads=[lf], writes=[lf])
        P.act(lambda e: e.activation(out=lf[:], in_=lf[:], func=AF.Ln, bias=1.0), reads=[lf], writes=[lf])
        P.dve(lambda e: e.tensor_scalar(out=lf[:], in0=lf[:], scalar1=-1.0, scalar2=None, op0=ALU.mult), reads=[lf], writes=[lf])
        cum_all, tot_all = gate_cumsums(G, es0, 'm_', lf, 8)
        P.act(lambda e: e.activation(out=gates[:, :, 0:8], in_=cum_all[:], func=AF.Exp), reads=[cum_all], writes=[gates])
        P.dve(lambda e: e.tensor_tensor(out=li[:], in0=li[:], in1=cum_all[:], op=ALU.subtract), reads=[li, cum_all], writes=[li])
        P.act(lambda e: e.activation(out=gates[:, :, 8:16], in_=li[:], func=AF.Exp), reads=[li], writes=[gates])
        P.act(lambda e: e.activation(out=gates[:, :, 16:24], in_=tot_all[:], func=AF.Exp), reads=[tot_all], writes=[gates])
        P.flush()
    NB = 2
    qT = [sb(es, 'm_qT%d' % i, [128, 2, 128], BF16) for i in range(NB)]
    kT = [sb(es, 'm_kT%d' % i, [128, 2, 128], BF16) for i in range(NB)]
    kTM = [sb(es, 'm_kTM%d' % i, [128, 256], BF16) for i in range(NB)]
    Vp = [sb(es, 'm_Vp%d' % i, [128, 4, 65], BF16) for i in range(NB)]
    mo = [sb(es, 'm_mo%d' % i, [128, 256]) for i in range(NB)]
    for v in Vp:
        P.pool(lambda e, v=v: e.memset(v[:], 1.0), writes=[v])
    Cd = [sb(es, 'm_C%d' % d, [128, 2, 130]) for d in range(2)]
    bmask = sb(es, 'm_bmask', [128, 2, 130])
    Cb = [sb(es, 'm_Cb%d' % d, [128, 2, 130], BF16) for d in range(2)]
    for d in range(2):
        P.pool(lambda e, d=d: e.memset(Cb[d][:], 0.0), writes=[Cb[d]])
    for d in range(2):
        P.pool(lambda e, d=d: e.memset(Cd[d][:], 0.0), writes=[Cd[d]])
    P.pool(lambda e: e.memset(bmask[:], 0.0), writes=[bmask])
    P.pool(lambda e: e.memset(bmask[0:64, :, 0:65], 1.0), writes=[bmask])
    P.pool(lambda e: e.memset(bmask[64:128, :, 65:130], 1.0), writes=[bmask])
    pmt = [sb(es, 'm_pmt%d' % i, [128, 128]) for i in range(2)]
    pm = [sb(es, 'm_pm%d' % i, [128, 128], BF16) for i in range(8)]
    uV = [sb(es, 'm_uV%d' % i, [128, 4, 65], BF16) for i in range(2)]
    ep = [sb(es, 'm_ep%d' % i, [128, 20]) for i in range(2)]
    ct = [sb(es, 'm_ct%d' % i, [128, 260]) for i in range(2)]
    htmp = sb(es, 'm_htmp', [128, 4, 64])
    ho = [sb(es, 'm_ho%d' % i, [128, 256]) for i in range(2)]
    sg = [sb(es, 'm_sg%d' % i, [128, 256]) for i in range(2)]
    ss = [sb(es, 'm_ss%d' % i, [128, 4]) for i in range(2)]
    junk = sb(es, 'm_junk', [128, 64])
    ytb = [sb(es, 'm_ytb%d' % i, [128, 2, 128], mybir.dt.bfloat16) for i in range(2)]
    cnt = {'pm': 0}

    def chunk_pass(q, d, it, first_pass):
        tok = q * 128
        b = it % NB
        P.dma(qT[b][:], sc['mqT'][:, :, tok:tok + 128].rearrange("c p t -> p c t"), reads=[sc['mqT']], writes=[qT[b]])
        P.dma(kT[b][:], sc['mkT'][:, :, tok:tok + 128].rearrange("c p t -> p c t"), reads=[sc['mkT']], writes=[kT[b]])
        P.dma(kTM[b][:], sc['mkTM'][tok:tok + 128, :], reads=[sc['mkTM']], writes=[kTM[b]])
        P.dma(Vp[b][:, :, 0:64], sc['mvTM'][tok:tok + 128, :].rearrange("p (h e) -> p h e", h=4), reads=[sc['mvTM']], writes=[Vp[b]])
        if not first_pass:
            P.dma(mo[b][:], sc['TM1'][tok:tok + 128, 256:512], reads=[sc['TM1']], writes=[mo[b]])
            P.dma(hprev[b][:], hacc_d[tok:tok + 128, :], reads=[hacc_d], writes=[hprev[b]])
        mask = G.triU if d == 0 else G.triL
        pms = []
        for h in range(4):
            pr, hh = h // 2, h % 2
            j = 4 * d + h
            ps = G.nextps()
            P.pe(lambda e, ps=ps, hh=hh, pr=pr: e.matmul(ps[:, 0:128], lhsT=kT[b][64 * hh:64 * hh + 64, pr, :], rhs=qT[b][64 * hh:64 * hh + 64, pr, :], start=True, stop=True),
                 reads=[kT[b], qT[b]], writes=[ps])
            t_ = pmt[cnt['pm'] % 2]; p_ = pm[cnt['pm'] % 8]; cnt['pm'] += 1
            P.act(lambda e, ps=ps, t_=t_, j=j: e.activation(out=t_[:], in_=ps[:, 0:128], func=AF.Copy, scale=gates[:, q, 8 + j:9 + j]), reads=[ps, gates], writes=[t_])
            P.pool(lambda e, t_=t_, p_=p_: e.tensor_tensor(out=p_[:], in0=t_[:], in1=mask[:], op=ALU.mult), reads=[t_, mask], writes=[p_])
            pms.append(p_)
        uv = uV[it % 2]
        P.dve(lambda e: e.tensor_tensor(out=uv[:], in0=Vp[b][:], in1=bc_ap(gates[:, q, 8 + 4 * d:12 + 4 * d], [[1, 4], [0, 65]]), op=ALU.mult), reads=[Vp[b], gates], writes=[uv])
        yield
        C = Cd[d]
        ps2 = G.nextps()
        for pr in range(2):
            P.pe(lambda e, pr=pr: e.matmul(ps2[:, 130 * pr:130 * pr + 130], lhsT=qT[b][:, pr, :], rhs=Cb[d][:, pr, :], start=(pr == 0), stop=False), reads=[qT[b], Cb[d]], writes=[ps2])
        for h in range(4):
            P.pe(lambda e, h=h, p_=pms[h]: e.matmul(ps2[:, 65 * h:65 * h + 65], lhsT=p_[:], rhs=Vp[b][:, h, :], start=False, stop=(h == 3)), reads=[pms[h], Vp[b]], writes=[ps2])
        e_ = ep[it % 2]
        p3 = ps2[:, 0:260].rearrange("p (h e) -> p h e", e=65)
        aq = gates[:, q, 4 * d:4 * d + 4]
        P.dve(lambda e: e.tensor_tensor(out=e_[:, 0:4], in0=p3[:, :, 64], in1=aq, op=ALU.mult), reads=[ps2, gates], writes=[e_])
        P.dve(lambda e: e.scalar_tensor_tensor(out=e_[:, 4:8], in0=e_[:, 0:4], scalar=-1.0, in1=e_[:, 0:4], op0=ALU.mult, op1=ALU.max), reads=[e_], writes=[e_])
        P.dve(lambda e: e.tensor_scalar(out=e_[:, 8:12], in0=e_[:, 4:8], scalar1=1.0, scalar2=None, op0=ALU.max), reads=[e_], writes=[e_])
        P.dve(lambda e: e.reciprocal(out=e_[:, 12:16], in_=e_[:, 8:12]), reads=[e_], writes=[e_])
        P.dve(lambda e: e.tensor_tensor(out=e_[:, 16:20], in0=e_[:, 12:16], in1=aq, op=ALU.mult), reads=[e_, gates], writes=[e_])
        sclb = bc_ap(e_[:, 16:20], [[1, 4], [0, 64]])
        hs_ = hst[it % 2]
        if first_pass:
            P.dve(lambda e: e.tensor_tensor(out=hs_[:].rearrange("p (h e) -> p h e", h=4), in0=p3[:, :, 0:64], in1=sclb, op=ALU.mult), reads=[ps2, e_], writes=[hs_])
            P.dma(hacc_d[tok:tok + 128, :], hs_[:], reads=[hs_], writes=[hacc_d])
        else:
            P.dve(lambda e: e.tensor_tensor(out=htmp[:], in0=p3[:, :, 0:64], in1=sclb, op=ALU.mult), reads=[ps2, e_], writes=[htmp])
            P.pool(lambda e: e.tensor_tensor(out=hs_[:], in0=hprev[b][:], in1=htmp[:].rearrange("p h e -> p (h e)"), op=ALU.add), reads=[hprev[b], htmp], writes=[hs_])
        ps3 = G.nextps()
        for pr in range(2):
            P.pe(lambda e, pr=pr: e.matmul(ps3[:, 130 * pr:130 * pr + 130], lhsT=kTM[b][:, 128 * pr:128 * pr + 128], rhs=uv[:, 2 * pr:2 * pr + 2, :], start=True, stop=True),
                 reads=[kTM[b], uv], writes=[ps3])
        c_ = ct[it % 2]
        Cf = C[:].rearrange("p a b -> p (a b)")
        P.dve(lambda e: e.tensor_tensor(out=c_[:], in0=ps3[:, 0:260], in1=Cf, op=ALU.add), reads=[ps3, C], writes=[c_])
        P.dve(lambda e: e.tensor_tensor(out=c_[:].rearrange("p (h e) -> p h e", h=4), in0=c_[:].rearrange("p (h e) -> p h e", h=4),
                                        in1=bc_ap(gates[:, q, 16 + 4 * d:20 + 4 * d], [[1, 4], [0, 65]]), op=ALU.mult), reads=[c_, gates], writes=[c_])
        P.pool(lambda e: e.tensor_tensor(out=Cf, in0=c_[:], in1=bmask[:].rearrange("p a b -> p (a b)"), op=ALU.mult), reads=[c_, bmask], writes=[C])
        P.pool(lambda e: e.tensor_tensor(out=Cb[d][:].rearrange("p a b -> p (a b)"), in0=c_[:], in1=bmask[:].rearrange("p a b -> p (a b)"), op=ALU.mult), reads=[c_, bmask], writes=[Cb[d]])
        if not first_pass and (with_ctx or q >= 2):
            bb = it % 2
            P.act(lambda e: e.activation(out=sg[bb][:], in_=mo[b][:], func=AF.Sigmoid), reads=[mo[b]], writes=[sg[bb]])
            P.dve(lambda e: e.tensor_tensor(out=ho[bb][:], in0=hs_[:], in1=sg[bb][:], op=ALU.mult), reads=[hs_, sg[bb]], writes=[ho[bb]])
            for h in range(4):
                P.act(lambda e, h=h: e.activation(out=junk[:], in_=ho[bb][:, 64 * h:64 * h + 64], func=AF.Square, accum_out=ss[bb][:, h:h + 1]), reads=[ho[bb]], writes=[junk, ss[bb]])
            P.act(lambda e: e.activation(out=ss[bb][:], in_=ss[bb][:], func=AF.Sqrt, scale=1.0 / 64, bias=G.epsb[:, 0:1]), reads=[ss[bb], G.epsb], writes=[ss[bb]])
            P.dve(lambda e: e.reciprocal(out=ss[bb][:], in_=ss[bb][:]), reads=[ss[bb]], writes=[ss[bb]])
            P.dve(lambda e: e.tensor_tensor(out=ho[bb][:].rearrange("p (h e) -> p h e", h=4), in0=ho[bb][:].rearrange("p (h e) -> p h e", h=4),
                                            in1=bc_ap(ss[bb][:], [[1, 4], [0, 64]]), op=ALU.mult), reads=[ho[bb], ss[bb]], writes=[ho[bb]])
            P.pool(lambda e: e.tensor_tensor(out=ho[bb][:], in0=ho[bb][:], in1=nw_b[:], op=ALU.mult), reads=[ho[bb], nw_b], writes=[ho[bb]])
            ps = G.nextps()
            for c in range(2):
                P.pe(lambda e, ps=ps, c=c: e.transpose(out=ps[:, 128 * c:128 * c + 128], in_=ho[bb][:, 128 * c:128 * c + 128], identity=G.ident[:]), reads=[ho[bb], G.ident], writes=[ps])
            P.act(lambda e, ps=ps: e.activation(out=ytb[bb][:], in_=ps[:, 0:256].rearrange("p (c t) -> p c t", c=2), func=AF.Copy), reads=[ps], writes=[ytb[bb]])
            P.dma(yT[0:2, :, tok:tok + 128].rearrange("c p t -> p c t"), ytb[bb][:], reads=[ytb[bb]], writes=[yT])

    steps = []
    it = 0
    for q in range(NT128):
        steps.append(chunk_pass(q, 0, it, True)); it += 1
    for q in [1, 0] + list(range(NT128 - 1, 1, -1)):
        steps.append(chunk_pass(q, 1, it, False)); it += 1
    return steps


def ssd_steps2(G, l, es):
    nc, P, I = G.nc, G.P, G.I
    sb = G.sb
    sc = G.scr
    with_ctx = l < DEPTH - 1
    yT = sc['yT']
    NEG = -30000.0
    yacc_d = sc['yacc_d']
    BF16 = mybir.dt.bfloat16 if BF_D else F32
    yst = [sb(es, 'd_yst%d' % i, [128, 512]) for i in range(2)]; yprev = [sb(es, 'd_yprev%d' % i, [128, 512]) for i in range(2)]
    D_b = load_bcast(G, es, 'd_D', I['ssd_d'][l:l + 1, :], 8)
    nw_b = load_bcast(G, es, 'd_nw', I['ssd_norm_w'][l:l + 1, :], 512)
    dt_all = sb(es, 'd_dtall', [128, NT128, 16]); a_all = sb(es, 'd_aall', [128, NT128, 16])
    acum_all = sb(es, 'd_acall', [128, NT128, 16]); etot_all = sb(es, 'd_etall', [128, NT128, 16]); wgt_all = sb(es, 'd_wgall', [128, NT128, 16])
    negones = sb(es, 'd_negones', [128, 128])
    nm = [sb(es, 'd_nm%d' % d, [128, 4, 128], mybir.dt.bfloat16) for d in range(2)]
    P.pool(lambda e: e.memset(negones[:], -1.0), writes=[negones])
    with contextlib.ExitStack() as es0:
        alog_b = load_bcast(G, es0, 'd_alog', I['ssd_a_log'][l:l + 1].rearrange("o d h -> o (d h)"), 16)
        dtb_b = load_bcast(G, es0, 'd_dtb', I['ssd_dt_bias'][l:l + 1].rearrange("o d h -> o (d h)"), 16)
        A_b = sb(es0, 'd_A', [128, 16]); nmf = sb(es0, 'd_nmf', [128, 128])
        P.act(lambda e: e.activation(out=A_b[:], in_=alog_b[:], func=AF.Exp), reads=[alog_b], writes=[A_b])
        P.dve(lambda e: e.tensor_scalar(out=A_b[:], in0=A_b[:], scalar1=-1.0, scalar2=None, op0=ALU.mult), reads=[A_b], writes=[A_b])
        for d, tri in enumerate((G.triU, G.triL)):
            P.dve(lambda e, tri=tri: e.tensor_scalar(out=nmf[:], in0=tri[:], scalar1=-1.0, scalar2=-NEG, op0=ALU.add, op1=ALU.mult), reads=[tri], writes=[nmf])
            P.dve(lambda e, d=d: e.tensor_copy(out=nm[d][:], in_=bc_ap(nmf[:], [[0, 4], [1, 128]])), reads=[nmf], writes=[nm[d]])
        P.dma(dt_all[:], sc['ddtTM'][:, :].rearrange("(q p) c -> p q c", p=128), reads=[sc['ddtTM']], writes=[dt_all])
        P.dve(lambda e: e.tensor_tensor(out=dt_all[:], in0=dt_all[:], in1=bc_ap(dtb_b[:], [[0, NT128], [1, 16]]), op=ALU.add), reads=[dt_all, dtb_b], writes=[dt_all])
        P.act(lambda e: e.activation(out=dt_all[:], in_=dt_all[:], func=AF.Exp), reads=[dt_all], writes=[dt_all])
        P.act(lambda e: e.activation(out=dt_all[:], in_=dt_all[:], func=AF.Ln, bias=1.0), reads=[dt_all], writes=[dt_all])
        P.dve(lambda e: e.tensor_tensor(out=a_all[:], in0=dt_all[:], in1=bc_ap(A_b[:], [[0, NT128], [1, 16]]), op=ALU.mult), reads=[dt_all, A_b], writes=[a_all])
        cum_all, tot_all = gate_cumsums(G, es0, 'd_', a_all, 16)
        P.act(lambda e: e.activation(out=acum_all[:], in_=cum_all[:], func=AF.Exp), reads=[cum_all], writes=[acum_all])
        P.act(lambda e: e.activation(out=etot_all[:], in_=tot_all[:], func=AF.Exp), reads=[tot_all], writes=[etot_all])
        P.dve(lambda e: e.tensor_tensor(out=wgt_all[:], in0=tot_all[:], in1=cum_all[:], op=ALU.subtract), reads=[tot_all, cum_all], writes=[wgt_all])
        P.act(lambda e: e.activation(out=wgt_all[:], in_=wgt_all[:], func=AF.Exp), reads=[wgt_all], writes=[wgt_all])
        P.dve(lambda e: e.tensor_tensor(out=wgt_all[:], in0=wgt_all[:], in1=dt_all[:], op=ALU.mult), reads=[wgt_all, dt_all], writes=[wgt_all])
        P.flush()
    NB = 2
    xt = [sb(es, 'd_xt%d' % i, [128, 512], BF16) for i in range(NB)]
    Bt = [sb(es, 'd_Bt%d' % i, [128, 2, 128], BF16) for i in range(NB)]
    Ct = [sb(es, 'd_Ct%d' % i, [128, 2, 128], BF16) for i in range(NB)]
    Btm = [sb(es, 'd_Btm%d' % i, [128, 256], BF16) for i in range(NB)]
    dz = [sb(es, 'd_dz%d' % i, [128, 512]) for i in range(NB)]
    Hs = [sb(es, 'd_Hs%d' % d, [128, 8, 64]) for d in range(2)]
    Hsb = [sb(es, 'd_Hsb%d' % d, [128, 8, 64], BF16) for d in range(2)]
    for d in range(2):
        P.pool(lambda e, d=d: e.memset(Hs[d][:], 0.0), writes=[Hs[d]])
        P.pool(lambda e, d=d: e.memset(Hsb[d][:], 0.0), writes=[Hsb[d]])
    rbig = sb(es, 'd_rbig', [128, 8, 128])
    ex = sb(es, 'd_ex', [128, 8, 128])
    pmb = [sb(es, 'd_pm%d' % i, [128, 8, 128], BF16) for i in range(2)]
    wx2 = [sb(es, 'd_wx%d' % i, [128, 8, 64], BF16) for i in range(2)]
    tmp = sb(es, 'd_tmp', [128, 8, 64]); htmp = sb(es, 'd_htmp', [128, 8, 64])
    yz = sb(es, 'd_yz', [128, 512]); sz = sb(es, 'd_sz', [128, 512]); ssq = sb(es, 'd_ssq', [128, 1]); junk = sb(es, 'd_junk', [128, 512])
    ytb = [sb(es, 'd_ytb%d' % i, [128, 4, 128], mybir.dt.bfloat16) for i in range(2)]

    def chunk_pass(q, d, it, first_pass):
        tok = q * 128
        b = it % NB
        wx = wx2[it % 2]; pm_ = pmb[it % 2]
        P.dma(xt[b][:], sc['xTM'][tok:tok + 128, :], reads=[sc['xTM']], writes=[xt[b]])
        P.dma(Bt[b][:], sc['BT'][:, :, tok:tok + 128].rearrange("g p t -> p g t"), reads=[sc['BT']], writes=[Bt[b]])
        P.dma(Ct[b][:], sc['CT'][:, :, tok:tok + 128].rearrange("g p t -> p g t"), reads=[sc['CT']], writes=[Ct[b]])
        P.dma(Btm[b][:], sc['BTM'][tok:tok + 128, :], reads=[sc['BTM']], writes=[Btm[b]])
        if not first_pass:
            P.dma(dz[b][:], sc['dzTM'][tok:tok + 128, :], reads=[sc['dzTM']], writes=[dz[b]])
            P.dma(yprev[b][:], yacc_d[tok:tok + 128, :], reads=[yacc_d], writes=[yprev[b]])
        mask = G.triU if d == 0 else G.triL
        P.dve(lambda e: e.tensor_tensor(out=rbig[:], in0=bc_ap(a_all[:, q, 8 * d:8 * d + 8], [[1, 8], [0, 128]]), in1=bc_ap(mask[:], [[0, 8], [1, 128]]), op=ALU.mult),
              reads=[a_all, mask], writes=[rbig])
        cb = [G.nextps(), G.nextps()]
        for hf in range(2):
            P.pe(lambda e, hf=hf: e.matmul(cb[hf][:, :], lhsT=G.ones[:], rhs=rbig[:, 4 * hf:4 * hf + 4, :], start=True, stop=False), reads=[G.ones, rbig], writes=[cb[hf]])
            for hh in range(4):
                P.pe(lambda e, hf=hf, hh=hh: e.matmul(cb[hf][:, 128 * hh:128 * hh + 128], lhsT=rbig[:, 4 * hf + hh, :], rhs=negones[:], start=False, stop=False),
                     reads=[rbig, negones], writes=[cb[hf]])
            P.pe(lambda e, hf=hf: e.matmul(cb[hf][:, :], lhsT=G.identb[:], rhs=nm[d][:], start=False, stop=True), reads=[G.identb, nm[d]], writes=[cb[hf]])
            P.act(lambda e, hf=hf: e.activation(out=ex[:, 4 * hf:4 * hf + 4, :], in_=cb[hf][:, :].rearrange("p (h t) -> p h t", h=4), func=AF.Exp), reads=[cb[hf]], writes=[ex])
        pss = G.nextps()
        for g in range(2):
            P.pe(lambda e, g=g: e.matmul(pss[:, 128 * g:128 * g + 128], lhsT=Bt[b][:, g, :], rhs=Ct[b][:, g, :], start=True, stop=True), reads=[Bt[b], Ct[b]], writes=[pss])
        P.dve(lambda e: e.tensor_tensor(out=ex[:], in0=ex[:], in1=bc_ap(dt_all[:, q, 8 * d:8 * d + 8], [[1, 8], [0, 128]]), op=ALU.mult), reads=[ex, dt_all], writes=[ex])
        P.dve(lambda e: e.tensor_tensor(out=pm_[:].rearrange("p (g h) t -> p g h t", g=2), in0=ex[:].rearrange("p (g h) t -> p g h t", g=2),
                                        in1=bc_ap(pss[:, 0:256], [[128, 2], [0, 4], [1, 128]]), op=ALU.mult), reads=[ex, pss], writes=[pm_])
        P.pool(lambda e: e.tensor_tensor(out=wx[:], in0=xt[b][:].rearrange("p (h e) -> p h e", h=8), in1=bc_ap(wgt_all[:, q, 8 * d:8 * d + 8], [[1, 8], [0, 64]]), op=ALU.mult),
               reads=[xt[b], wgt_all], writes=[wx])
        yield
        psd = G.nextps()
        pso = G.nextps()
        for g in range(2):
            P.pe(lambda e, g=g: e.matmul(pso[:, 256 * g:256 * g + 256], lhsT=Ct[b][:, g, :], rhs=Hsb[d][:, 4 * g:4 * g + 4, :], start=True, stop=True), reads=[Ct[b], Hsb[d]], writes=[pso])
        for h in range(8):
            P.pe(lambda e, h=h: e.matmul(psd[:, 64 * h:64 * h + 64], lhsT=pm_[:, h, :], rhs=xt[b][:, 64 * h:64 * h + 64], start=True, stop=True), reads=[pm_, xt[b]], writes=[psd])
        P.dve(lambda e: e.tensor_tensor(out=tmp[:], in0=pso[:, :].rearrange("p (h e) -> p h e", h=8), in1=bc_ap(acum_all[:, q, 8 * d:8 * d + 8], [[1, 8], [0, 64]]), op=ALU.mult),
              reads=[pso, acum_all], writes=[tmp])
        ys_ = yst[it % 2]
        if first_pass:
            P.dve(lambda e: e.tensor_tensor(out=ys_[:], in0=psd[:, :], in1=tmp[:].rearrange("p h e -> p (h e)"), op=ALU.add), reads=[psd, tmp], writes=[ys_])
            P.dma(yacc_d[tok:tok + 128, :], ys_[:], reads=[ys_], writes=[yacc_d])
        else:
            P.dve(lambda e: e.tensor_tensor(out=tmp[:].rearrange("p h e -> p (h e)"), in0=psd[:, :], in1=tmp[:].rearrange("p h e -> p (h e)"), op=ALU.add), reads=[psd, tmp], writes=[tmp])
            P.pool(lambda e: e.tensor_tensor(out=ys_[:], in0=yprev[b][:], in1=tmp[:].rearrange("p h e -> p (h e)"), op=ALU.add), reads=[tmp, yprev[b]], writes=[ys_])
        pst = G.nextps()
        for g in range(2):
            P.pe(lambda e, g=g: e.matmul(pst[:, 256 * g:256 * g + 256], lhsT=Btm[b][:, 128 * g:128 * g + 128], rhs=wx[:, 4 * g:4 * g + 4, :], start=True, stop=True), reads=[Btm[b], wx], writes=[pst])
        P.dve(lambda e: e.tensor_tensor(out=htmp[:], in0=Hs[d][:], in1=bc_ap(etot_all[:, q, 8 * d:8 * d + 8], [[1, 8], [0, 64]]), op=ALU.mult), reads=[Hs[d], etot_all], writes=[htmp])
        P.dve(lambda e: e.tensor_tensor(out=Hs[d][:].rearrange("p h e -> p (h e)"), in0=pst[:, :], in1=htmp[:].rearrange("p h e -> p (h e)"), op=ALU.add), reads=[pst, htmp], writes=[Hs[d]])
        P.act(lambda e: e.activation(out=Hsb[d][:], in_=Hs[d][:], func=AF.Copy), reads=[Hs[d]], writes=[Hsb[d]])
        if not first_pass and (with_ctx or q >= 2):
            bb = it % 2
            P.pool(lambda e: e.tensor_tensor(out=tmp[:], in0=xt[b][:].rearrange("p (h e) -> p h e", h=8), in1=bc_ap(D_b[:], [[1, 8], [0, 64]]), op=ALU.mult), reads=[xt[b], D_b], writes=[tmp])
            P.pool(lambda e: e.tensor_tensor(out=yz[:], in0=ys_[:], in1=tmp[:].rearrange("p h e -> p (h e)"), op=ALU.add), reads=[ys_, tmp], writes=[yz])
            P.act(lambda e: e.activation(out=sz[:], in_=dz[b][:], func=AF.Silu), reads=[dz[b]], writes=[sz])
            P.dve(lambda e: e.tensor_tensor(out=yz[:], in0=yz[:], in1=sz[:], op=ALU.mult), reads=[yz, sz], writes=[yz])
            P.act(lambda e: e.activation(out=junk[:], in_=yz[:], func=AF.Square, accum_out=ssq[:, 0:1]), reads=[yz], writes=[junk, ssq])
            P.act(lambda e: e.activation(out=ssq[:], in_=ssq[:], func=AF.Sqrt, scale=1.0 / 512, bias=G.epsb[:, 0:1]), reads=[ssq, G.epsb], writes=[ssq])
            P.dve(lambda e: e.reciprocal(out=ssq[:], in_=ssq[:]), reads=[ssq], writes=[ssq])
            P.dve(lambda e: e.scalar_tensor_tensor(out=yz[:], in0=yz[:], scalar=ssq[:, 0:1], in1=nw_b[:], op0=ALU.mult, op1=ALU.mult), reads=[yz, ssq, nw_b], writes=[yz])
            ps = G.nextps()
            for c in range(4):
                P.pe(lambda e, ps=ps, c=c: e.transpose(out=ps[:, 128 * c:128 * c + 128], in_=yz[:, 128 * c:128 * c + 128], identity=G.ident[:]), reads=[yz, G.ident], writes=[ps])
            P.act(lambda e, ps=ps: e.activation(out=ytb[bb][:], in_=ps[:, :].rearrange("p (c t) -> p c t", c=4), func=AF.Copy), reads=[ps], writes=[ytb[bb]])
            P.dma(yT[6:10, :, tok:tok + 128].rearrange("c p t -> p c t"), ytb[bb][:], reads=[ytb[bb]], writes=[yT])

    steps = []
    it = 0
    for q in range(NT128):
        steps.append(chunk_pass(q, 0, it, True)); it += 1
    for q in [1, 0] + list(range(NT128 - 1, 1, -1)):
        steps.append(chunk_pass(q, 1, it, False)); it += 1
    return steps
```

```python
import contextlib
import math
import numpy as np
import concourse.bass as bass
import concourse.mybir as mybir
from concourse.bass_utils import run_bass_kernel_spmd

F32 = mybir.dt.float32
BF16 = mybir.dt.bfloat16
ALU = mybir.AluOpType
AF = mybir.ActivationFunctionType
AX = mybir.AxisListType

ENGS = ('pe', 'act', 'dve', 'pool', 'sp')
NDMASEM = 12
SAME_ENGINE_SYNC = True
BF_M = True
BF_D = True

D = 1024
SEQ = 4096
CTX = 256
S = SEQ + CTX
DEPTH = 2
GRID_W = 64
EPS = 1e-6
IN_TOTAL = 7712
FFN_H = 2816
O_MQ, O_MK, O_MV, O_MO, O_MI, O_MF, O_SU = 0, 256, 512, 768, 1024, 1032, 1040
O_NQ, O_NK, O_NV = 1296, 1552, 1808
O_DZ, O_DX, O_DB, O_DC, O_DDT, O_GATE = 2064, 2576, 3088, 3344, 3600, 3616
NT128 = S // 128
TT = [(0, 256)] + [(256 + 512 * i, 512) for i in range(8)]


class Res:
    __slots__ = ('name', 'lw', 'rd')

    def __init__(self, name=''):
        self.name = name
        self.lw = None
        self.rd = []


class Inst:
    __slots__ = ('eng', 'fn', 'dma', 'seq', 'deps', 'signal', 'idx', 'clock', 'dsem', 'dval', 'dmaid', 'emitted')

    def __init__(self, eng, fn, dma):
        self.eng = eng
        self.fn = fn
        self.dma = dma
        self.deps = []
        self.signal = False
        self.idx = None
        self.clock = None
        self.emitted = False


class Prog:
    def __init__(self, nc, es):
        self.nc = nc
        self.ins = {e: [] for e in ENGS}
        self.known = {e: {x: -1 for x in ENGS} for e in ENGS}
        self.known_dma = {e: set() for e in ENGS}
        self.ndma = {e: 0 for e in ENGS}
        self.dma_list = {e: [] for e in ENGS}
        self.all_dma = []
        self.sigcount = {e: 0 for e in ENGS}
        self.sem = {e: es.enter_context(nc.semaphore('s_' + e)) for e in ENGS}
        self.dsem = {}
        for e in ('sp', 'pool', 'act'):
            for k in range(NDMASEM):
                self.dsem[(e, k)] = es.enter_context(nc.semaphore('d_%s_%d' % (e, k)))
        self.pos = {e: 0 for e in ENGS}
        self.ninst = 0

    def op(self, eng, fn, reads=(), writes=(), dma=False):
        ins = Inst(eng, fn, dma)
        lst = self.ins[eng]
        ins.seq = len(lst)
        self.ninst += 1
        need = []
        for r in reads:
            r = getattr(r, 'r', r)
            if r.lw is not None:
                need.append(r.lw)
        for w in writes:
            w = getattr(w, 'r', w)
            if w.lw is not None:
                need.append(w.lw)
            need.extend(w.rd)
        if dma:
            j = self.ndma[eng]
            self.ndma[eng] += 1
            ins.dmaid = len(self.all_dma)
            self.all_dma.append(ins)
            ins.dsem = (eng, j % NDMASEM)
            ins.dval = 16 * (j // NDMASEM + 1)
            if j >= NDMASEM:
                need.append(self.dma_list[eng][j - NDMASEM])
            self.dma_list[eng].append(ins)
            ins.signal = True
        kn = self.known[eng]
        kd = self.known_dma[eng]
        deps = []
        for d in need:
            if d.dma:
                if d.dmaid in kd:
                    continue
                kd.add(d.dmaid)
                deps.append(d)
                for x, s in d.clock.items():
                    if s > kn[x]:
                        kn[x] = s
            else:
                if d.eng == eng and (eng == 'pe' or not SAME_ENGINE_SYNC):
                    continue
                if d.seq <= kn[d.eng]:
                    continue
                assert not d.emitted or d.signal, "dependency on already-emitted unsignalled inst"
                deps.append(d)
                d.signal = True
                kn[d.eng] = d.seq
                for x, s in d.clock.items():
                    if s > kn[x]:
                        kn[x] = s
        best = {}
        out = []
        for d in deps:
            if d.dma:
                out.append(d)
            elif d.eng not in best or best[d.eng].seq < d.seq:
                best[d.eng] = d
        out.extend(best.values())
        ins.deps = out
        ins.clock = dict(kn)
        lst.append(ins)
        for r in reads:
            r = getattr(r, 'r', r)
            r.rd.append(ins)
        for w in writes:
            w = getattr(w, 'r', w)
            w.lw = ins
            w.rd = []
        return ins

    def pe(self, fn, reads=(), writes=()):
        return self.op('pe', fn, reads, writes)

    def act(self, fn, reads=(), writes=()):
        return self.op('act', fn, reads, writes)

    def dve(self, fn, reads=(), writes=()):
        return self.op('dve', fn, reads, writes)

    def pool(self, fn, reads=(), writes=()):
        return self.op('pool', fn, reads, writes)

    def dmaq(self, q, out, in_, reads=(), writes=(), **kw):
        return self.op(q, lambda e: e.dma_start(out=out, in_=in_, **kw), reads, writes, dma=True)

    def dma(self, out, in_, reads=(), writes=(), **kw):
        return self.dmaq('sp', out, in_, reads, writes, **kw)

    def barrier(self):
        lasts = []
        for e in ENGS:
            for i in reversed(self.ins[e]):
                if not i.dma and i.fn is not None:
                    lasts.append(i)
                    break
        pend = []
        for e in ENGS:
            pend.extend(self.dma_list[e][-NDMASEM:])
        for e in ENGS:
            ins = Inst(e, None, False)
            ins.seq = len(self.ins[e])
            kn = self.known[e]
            kd = self.known_dma[e]
            for d in lasts:
                if d.seq <= kn[d.eng]:
                    continue
                assert not d.emitted or d.signal
                ins.deps.append(d)
                d.signal = True
                kn[d.eng] = d.seq
            for d in pend:
                if d.dmaid in kd:
                    continue
                kd.add(d.dmaid)
                ins.deps.append(d)
            ins.clock = dict(kn)
            self.ins[e].append(ins)

    def flush(self, final_wait=()):
        self.barrier()
        nc = self.nc
        for e in ENGS:
            for i in self.ins[e][self.pos[e]:]:
                if i.signal and not i.dma:
                    self.sigcount[e] += 1
                    i.idx = self.sigcount[e]
        sem, dsem = self.sem, self.dsem

        def run(e, eng):
            for i in self.ins[e][self.pos[e]:]:
                for d in i.deps:
                    if d.dma:
                        eng.wait_ge(dsem[d.dsem], d.dval)
                    else:
                        eng.wait_ge(sem[d.eng], d.idx)
                i.emitted = True
                if i.fn is None:
                    continue
                bi = i.fn(eng)
                if i.dma:
                    bi.then_inc(dsem[i.dsem], 16)
                elif i.signal:
                    bi.then_inc(sem[e], 1)
            if e == 'sp':
                for d in final_wait:
                    eng.wait_ge(dsem[d.dsem], d.dval)
            self.pos[e] = len(self.ins[e])

        with nc.Block() as block:
            @block.tensor
            def _(eng):
                run('pe', eng)

            @block.scalar
            def _(eng):
                run('act', eng)

            @block.vector
            def _(eng):
                run('dve', eng)

            @block.gpsimd
            def _(eng):
                run('pool', eng)

            @block.sync
            def _(eng):
                run('sp', eng)


class Buf:
    __slots__ = ('t', 'r')

    def __init__(self, t, name=''):
        self.t = t
        self.r = Res(name)

    def __getitem__(self, k):
        return self.t[k]


class Ctx:
    pass


def build(debug_outs=(), stop_after=None):
    nc = bass.Bass("TRN2", target_bir_lowering=False)
    top = contextlib.ExitStack()
    G = Ctx()
    G.nc = nc
    G.dbg = set(debug_outs)
    with top:
        P = Prog(nc, top)
        G.P = P

        def din(name, shape):
            return nc.dram_tensor(name, list(shape), F32, kind="ExternalInput").ap()

        I = {}
        I['x'] = din('x', [SEQ, D]); I['ctx'] = din('ctx', [CTX, D])
        I['c'] = din('c', [1, D]); I['c_ctx'] = din('c_ctx', [1, D])
        for nm, shp in WEIGHT_SHAPES:
            I[nm] = din(nm, shp)
        G.I = I
        G.out = nc.dram_tensor('out', [SEQ, D], F32, kind="ExternalOutput").ap()
        G.scr = {}

        def scratch(name, shape, dt=F32):
            kind = "ExternalOutput" if name in G.dbg else "Internal"
            b = Buf(nc.dram_tensor(name, list(shape), dt, kind=kind).ap(), name)
            G.scr[name] = b
            return b
        G.scratch = scratch

        G.uid = 0

        def sb(es, name, shape, dt=F32):
            G.uid += 1
            return Buf(es.enter_context(nc.sbuf_tensor('%s_%d' % (name, G.uid), list(shape), dt)), name)
        G.sb = sb
        G.ps = [Buf(top.enter_context(nc.psum_tensor('ps%d' % i, [128, 512], F32)), 'ps%d' % i) for i in range(8)]
        G.psi = 0

        def nextps():
            b = G.ps[G.psi % 8]
            G.psi += 1
            return b
        G.nextps = nextps

        stages = [('consts', stage_consts), ('adaln', stage_adaln), ('load', stage_load)]
        for l in range(DEPTH):
            stages += [('proj%d' % l, lambda G, l=l: stage_norm_proj(G, l))]
            stages += STAGES_AFTER_PROJ(l)
        for nm, st in stages:
            st(G)
            P.flush()
            if stop_after is not None and nm == stop_after:
                break
        P.flush(final_wait=P.all_dma[-3 * NDMASEM:])
        G.keep.close()
    return nc


WEIGHT_SHAPES = [
    ('ada_w', [2, 1024, 6144]), ('ada_b', [2, 6144]), ('norm1_w', [2, 1024]), ('norm2_w', [2, 1024]),
    ('w_in', [2, 1024, 7712]), ('mlstm_conv_w', [2, 7, 512]), ('mlstm_conv_b', [2, 512]),
    ('mlstm_ib', [2, 2, 4]), ('mlstm_fb', [2, 2, 4]), ('mlstm_norm_w', [2, 256]),
    ('s5_lam_re', [2, 2, 16, 64]), ('s5_lam_im', [2, 2, 16, 64]), ('s5_log_dt', [2, 2, 16]),
    ('s5_b_re', [2, 16, 64, 16]), ('s5_b_im', [2, 16, 64, 16]), ('s5_c_re', [2, 16, 16, 64]),
    ('s5_c_im', [2, 16, 16, 64]), ('s5_d', [2, 256]), ('s5_glu_w', [2, 256, 512]), ('na_rpb', [2, 4, 15, 31]),
    ('ssd_conv_w', [2, 7, 1024]), ('ssd_conv_b', [2, 1024]), ('ssd_a_log', [2, 2, 8]), ('ssd_dt_bias', [2, 2, 8]),
    ('ssd_d', [2, 8]), ('ssd_norm_w', [2, 512]), ('w_branch_a', [2, 256, 1024]), ('w_branch_b', [2, 256, 1024]),
    ('w_branch_c', [2, 256, 1024]), ('w_branch_d', [2, 512, 1024]), ('w_out', [2, 1024, 1024]),
    ('ffn_w_in', [2, 1024, 5632]), ('ffn_w_out', [2, 2816, 1024]), ('final_norm_w', [1024]),
]


def stage_consts(G):
    nc, P = G.nc, G.P
    top = contextlib.ExitStack()
    G.keep = top
    sb = G.sb
    G.ones = sb(top, 'ones', [128, 128]); G.ident = sb(top, 'ident', [128, 128]); G.identb = sb(top, 'identb', [128, 128], BF16)
    G.triU = sb(top, 'triU', [128, 128]); G.triL = sb(top, 'triL', [128, 128])
    G.onesb = sb(top, 'onesb', [128, 128], BF16)
    P.pool(lambda e: e.memset(G.ones[:], 1.0), writes=[G.ones])
    P.pool(lambda e: e.affine_select(out=G.ident[:], in_=G.ones[:], pattern=[[-1, 128]], compare_op=ALU.is_equal,
                                     fill=0.0, base=0, channel_multiplier=1), reads=[G.ones], writes=[G.ident])
    P.pool(lambda e: e.affine_select(out=G.triU[:], in_=G.ones[:], pattern=[[1, 128]], compare_op=ALU.is_ge,
                                     fill=0.0, base=0, channel_multiplier=-1), reads=[G.ones], writes=[G.triU])
    P.pool(lambda e: e.affine_select(out=G.triL[:], in_=G.ones[:], pattern=[[-1, 128]], compare_op=ALU.is_ge,
                                     fill=0.0, base=0, channel_multiplier=1), reads=[G.ones], writes=[G.triL])
    P.dve(lambda e: e.tensor_copy(out=G.identb[:], in_=G.ident[:]), reads=[G.ident], writes=[G.identb])
    P.dve(lambda e: e.tensor_copy(out=G.onesb[:], in_=G.ones[:]), reads=[G.ones], writes=[G.onesb])
    G.cm = sb(top, 'cm', [128, DEPTH, 6, 8, 2])
    G.epsb = sb(top, 'epsb', [128, 1])
    P.pool(lambda e: e.memset(G.epsb[:], EPS), writes=[G.epsb])
    G.scratch('xsT', [8, 128, S])
    G.scratch('hxT', [8, 128, S], BF16)


def stage_adaln(G):
    nc, P, I = G.nc, G.P, G.I
    with contextlib.ExitStack() as es:
        sb = G.sb
        c2 = sb(es, 'c2', [128, 8, 2]); sc2 = sb(es, 'sc2', [128, 8, 2])
        P.dma(c2[:, :, 0], I['c'][0, :].rearrange("(k p) -> p k", p=128), writes=[c2], allow_slow_non_contiguous=True)
        P.dma(c2[:, :, 1], I['c_ctx'][0, :].rearrange("(k p) -> p k", p=128), writes=[c2], allow_slow_non_contiguous=True)
        P.act(lambda e: e.activation(out=sc2[:], in_=c2[:], func=AF.Silu), reads=[c2], writes=[sc2])
        wt = [sb(es, 'adaw%d' % i, [128, 8, 512]) for i in range(3)]
        bias = sb(es, 'adab', [128, DEPTH, 48]); nw = sb(es, 'nw', [128, DEPTH, 2, 8])
        modtm = sb(es, 'modtm', [2, 6144]); cmraw = sb(es, 'cmraw', [128, 48, 2])
        modT = G.scratch('modT', [DEPTH, 2, 6144])
        for l in range(DEPTH):
            P.dma(bias[:, l, :], I['ada_b'][l, :].rearrange("(j p) -> p j", p=128), writes=[bias], allow_slow_non_contiguous=True)
            P.dma(nw[:, l, 0, :], I['norm1_w'][l, :].rearrange("(k p) -> p k", p=128), writes=[nw], allow_slow_non_contiguous=True)
            P.dma(nw[:, l, 1, :], I['norm2_w'][l, :].rearrange("(k p) -> p k", p=128), writes=[nw], allow_slow_non_contiguous=True)
        n = 0
        for l in range(DEPTH):
            for c in range(12):
                w = wt[n % 3]; n += 1
                P.dma(w[:], I['ada_w'][l, :, c * 512:(c + 1) * 512].rearrange("(k p) n -> p k n", p=128), writes=[w])
                ps = G.nextps()
                for k in range(8):
                    P.pe(lambda e, w=w, k=k, ps=ps: e.matmul(ps[0:2, :], lhsT=sc2[:, k, :], rhs=w[:, k, :], start=(k == 0), stop=(k == 7)), reads=[w, sc2], writes=[ps])
                P.act(lambda e, ps=ps, c=c: e.activation(out=modtm[:, c * 512:(c + 1) * 512], in_=ps[0:2, :], func=AF.Copy), reads=[ps], writes=[modtm])
            P.dma(modT[l], modtm[:], reads=[modtm], writes=[modT])
            for s in range(2):
                P.dma(cmraw[:, :, s], modT[l, s, :].rearrange("(j p) -> p j", p=128), reads=[modT], writes=[cmraw], allow_slow_non_contiguous=True)
            for s in range(2):
                P.dve(lambda e, l=l, s=s: e.tensor_tensor(
                    out=G.cm[:, l, :, :, s], in0=cmraw[:, :, s].rearrange("p (w k) -> p w k", w=6),
                    in1=bias[:, l, :].rearrange("p (w k) -> p w k", w=6), op=ALU.add), reads=[cmraw, bias], writes=[G.cm])
            for (wi, ni) in ((1, 0), (4, 1)):
                for s in range(2):
                    P.dve(lambda e, l=l, s=s, wi=wi, ni=ni: e.scalar_tensor_tensor(
                        out=G.cm[:, l, wi, :, s], in0=G.cm[:, l, wi, :, s], scalar=1.0, in1=nw[:, l, ni, :],
                        op0=ALU.add, op1=ALU.mult), reads=[G.cm, nw], writes=[G.cm])
        if 'cm_dbg' in G.dbg:
            d = G.scratch('cm_dbg', [128, DEPTH * 6 * 8 * 2])
            P.dma(d[:], G.cm[:].rearrange("p l w k s -> p (l w k s)"), reads=[G.cm], writes=[d])
        P.flush()


def stage_load(G):
    nc, P, I = G.nc, G.P, G.I
    xsT = G.scr['xsT']
    with contextlib.ExitStack() as es:
        sb = G.sb
        xin = [sb(es, 'xin%d' % i, [128, D]) for i in range(3)]
        xo = [sb(es, 'xo%d' % i, [128, 8, 512]) for i in range(2)]
        ti = 0
        for gi, (t0, n) in enumerate(TT):
            o = xo[gi % 2]
            for q in range(n // 128):
                xi = xin[ti % 3]; ti += 1
                tok = t0 + q * 128
                src = I['ctx'][tok:tok + 128, :] if tok < CTX else I['x'][tok - CTX:tok - CTX + 128, :]
                P.dma(xi[:], src, writes=[xi])
                for half in range(2):
                    ps = G.nextps()
                    for kk in range(4):
                        k = half * 4 + kk
                        P.pe(lambda e, ps=ps, kk=kk, k=k, xi=xi: e.transpose(out=ps[:, kk * 128:(kk + 1) * 128], in_=xi[:, k * 128:(k + 1) * 128],
                                                                          identity=G.ident[:]), reads=[xi, G.ident], writes=[ps])
                    eng = P.act if half == 0 else P.dve
                    if half == 0:
                        P.act(lambda e, ps=ps, o=o, q=q: e.activation(out=o[:, 0:4, q * 128:(q + 1) * 128], in_=ps[:].rearrange("p (k t) -> p k t", k=4), func=AF.Copy),
                              reads=[ps], writes=[o])
                    else:
                        P.dve(lambda e, ps=ps, o=o, q=q: e.tensor_copy(out=o[:, 4:8, q * 128:(q + 1) * 128], in_=ps[:].rearrange("p (k t) -> p k t", k=4)),
                              reads=[ps], writes=[o])
            P.dma(xsT[:, :, t0:t0 + n].rearrange("k p t -> p k t"), o[:, :, 0:n], reads=[o], writes=[xsT])
        P.flush()


def kernel(**inputs):
    n = 8
    nc = build()
    in_maps = []
    w = {nm: np.ascontiguousarray(inputs[nm], dtype=np.float32) for nm, _ in WEIGHT_SHAPES}
    for b in range(n):
        m = dict(w)
        m['x'] = np.ascontiguousarray(inputs['x'][b], dtype=np.float32)
        m['ctx'] = np.ascontiguousarray(inputs['ctx'][b], dtype=np.float32)
        m['c'] = np.ascontiguousarray(inputs['c'][b:b + 1], dtype=np.float32)
        m['c_ctx'] = np.ascontiguousarray(inputs['c_ctx'][None, :], dtype=np.float32)
        in_maps.append(m)
    res = run_bass_kernel_spmd(nc, in_maps, core_ids=list(range(n)))
    return np.stack([r['out'] for r in res.results], axis=0)


def STAGES_AFTER_PROJ(l):
    return [('na%d' % l, lambda G, l=l: stage_mixers(G, l)), ('s5%d' % l, lambda G, l=l: stage_s5(G, l)), ('tail%d' % l, lambda G, l=l: stage_tail(G, l))]


def bc_ap(ap, pattern):
    return bass.AP(tensor=ap.tensor, offset=ap.offset, ap=[list(ap.ap[0])] + [list(p) for p in pattern])


def emit_norm(G, xt, n, l, which_gam, s, out_fn, out_res, tmp_bufs):
    P = G.P
    sq, rstd, tmp = tmp_bufs
    ps = G.nextps()
    P.act(lambda e: e.activation(out=sq[:, :, 0:n], in_=xt[:, :, 0:n], func=AF.Square), reads=[xt], writes=[sq])
    for k in range(8):
        P.pe(lambda e, k=k: e.matmul(ps[:, 0:n], lhsT=G.onesb[:], rhs=sq[:, k, 0:n], start=(k == 0), stop=(k == 7)),
             reads=[sq, G.onesb], writes=[ps])
    P.act(lambda e: e.activation(out=rstd[:, 0:n], in_=ps[:, 0:n], func=AF.Sqrt, scale=1.0 / D, bias=G.epsb[:, 0:1]),
          reads=[ps, G.epsb], writes=[rstd])
    P.dve(lambda e: e.reciprocal(out=rstd[:, 0:n], in_=rstd[:, 0:n]), reads=[rstd], writes=[rstd])
    for k in range(8):
        t = tmp[k % len(tmp)]
        P.dve(lambda e, k=k, t=t: e.tensor_tensor(out=t[:, 0:n], in0=xt[:, k, 0:n], in1=rstd[:, 0:n], op=ALU.mult),
              reads=[xt, rstd], writes=[t])
        P.act(lambda e, k=k, t=t: e.activation(out=out_fn(k), in_=t[:, 0:n], func=AF.Identity,
                                               scale=G.cm[:, l, which_gam, k, s:s + 1], bias=G.cm[:, l, which_gam - 1, k, s:s + 1]),
              reads=[t, G.cm], writes=[out_res])


def stage_norm_proj(G, l):
    nc, P, I = G.nc, G.P, G.I
    sb = G.sb
    xsT = G.scr['xsT']
    hxT_d = G.scr['hxT']
    sc = G.scr
    if l == 0:
        dM = BF16 if BF_M else F32; dD = BF16 if BF_D else F32
        G.scratch('mqT', [2, 128, S], dM); G.scratch('mkT', [2, 128, S], dM); G.scratch('mkTM', [S, 256], dM); G.scratch('mvTM', [S, 256], dM)
        G.scratch('TM1', [S, 784]); G.scratch('nvTM', [S, 256], BF16); G.scratch('dzTM', [S, 512]); G.scratch('ddtTM', [S, 16])
        G.scratch('nqT', [2, 128, S], BF16); G.scratch('nkT', [2, 128, S], BF16)
        G.scratch('xTM', [S, 512], dD); G.scratch('BT', [2, 128, S], dD); G.scratch('BTM', [S, 256], dD); G.scratch('CT', [2, 128, S], dD)
    with contextlib.ExitStack() as es:
        hxT = sb(es, 'hxT_sb', [128, 8, S], BF16)
        hres = [Res('hx%d' % i) for i in range(len(TT))]
        with contextlib.ExitStack() as es2:
            xt = [sb(es2, 'nxt%d' % i, [128, 8, 512]) for i in range(3)]
            sq = [sb(es2, 'nsq%d' % i, [128, 8, 512], BF16) for i in range(2)]; rstd = [sb(es2, 'nrstd%d' % i, [128, 512]) for i in range(2)]
            tmp = [sb(es2, 'ntmp%d' % i, [128, 512]) for i in range(4)]
            for ti, (t0, n) in enumerate(TT):
                x = xt[ti % 3]
                P.dma(x[:, :, 0:n], xsT[:, :, t0:t0 + n].rearrange("k p t -> p k t"), reads=[xsT], writes=[x])
                emit_norm(G, x, n, l, 1, 1 if ti == 0 else 0, lambda k, t0=t0, n=n: hxT[:, k, t0:t0 + n], hres[ti], (sq[ti % 2], rstd[ti % 2], tmp))
                P.dma(hxT_d[:, :, t0:t0 + n].rearrange("k p t -> p k t"), hxT[:, :, t0:t0 + n], reads=[hres[ti]], writes=[hxT_d])
            P.flush()
        tmgroups = [(O_MV, 512), (O_MI, 272), (O_NV, 256), (O_DZ, 512), (O_DDT, 16)]
        with contextlib.ExitStack() as es2:
            wtm = sb(es2, 'wtm', [128, 8, 1568], BF16)
            off = 0
            offs = []
            for (c0, n) in tmgroups:
                P.dmaq('pool', wtm[:, :, off:off + n], I['w_in'][l, :, c0:c0 + n].rearrange("(k p) n -> p k n", p=128), writes=[wtm])
                offs.append(off)
                off += n
            st = [sb(es2, 'tmst%d' % i, [128, 1312]) for i in range(2)]
            stb = [sb(es2, 'tmstb%d' % i, [128, 256], BF16) for i in range(2)]
            stm = [sb(es2, 'tmstm%d' % i, [128, 256], BF16 if BF_M else F32) for i in range(2)]
            for q in range(NT128):
                tok = q * 128
                ti = 0 if tok < CTX else 1 + (tok - CTX) // 512
                s_, sb_ = st[q % 2], stb[q % 2]
                for gi, (c0, n) in enumerate(tmgroups):
                    ps = G.nextps()
                    for k in range(8):
                        P.pe(lambda e, ps=ps, k=k, n=n, o=offs[gi], tok=tok: e.matmul(ps[:, 0:n], lhsT=hxT[:, k, tok:tok + 128], rhs=wtm[:, k, o:o + n],
                                                                                     start=(k == 0), stop=(k == 7)), reads=[hres[ti], wtm], writes=[ps])
                    if gi == 2:
                        P.act(lambda e, ps=ps, sb_=sb_: e.activation(out=sb_[:], in_=ps[:, 0:256], func=AF.Copy), reads=[ps], writes=[sb_])
                    else:
                        so = {0: 0, 1: 512, 3: 784, 4: 1296}[gi]
                        if gi % 2 == 0:
                            P.dve(lambda e, ps=ps, s_=s_, so=so, n=n: e.tensor_copy(out=s_[:, so:so + n], in_=ps[:, 0:n]), reads=[ps], writes=[s_])
                        else:
                            P.act(lambda e, ps=ps, s_=s_, so=so, n=n: e.activation(out=s_[:, so:so + n], in_=ps[:, 0:n], func=AF.Copy), reads=[ps], writes=[s_])
                sm_ = stm[q % 2]
                P.act(lambda e, s_=s_, sm_=sm_: e.activation(out=sm_[:], in_=s_[:, 0:256], func=AF.Copy), reads=[s_], writes=[sm_])
                P.dma(sc['mvTM'][tok:tok + 128, :], sm_[:], reads=[sm_], writes=[sc['mvTM']])
                P.dma(sc['TM1'][tok:tok + 128, :], s_[:, 0:784], reads=[s_], writes=[sc['TM1']])
                P.dma(sc['dzTM'][tok:tok + 128, :], s_[:, 784:1296], reads=[s_], writes=[sc['dzTM']])
                P.dma(sc['ddtTM'][tok:tok + 128, :], s_[:, 1296:1312], reads=[s_], writes=[sc['ddtTM']])
                P.dma(sc['nvTM'][tok:tok + 128, :], sb_[:], reads=[sb_], writes=[sc['nvTM']])
            P.flush()
        with contextlib.ExitStack() as es2:
            PADL = S + 12
            XOFF = 265
            rowbuf = [sb(es2, 'rowbuf%d' % i, [128, PADL], BF16) for i in range(2)]
            dgt = [sb(es2, 'dgt%d' % i, [128, 7, 128], BF16) for i in range(2)]
            cacc = sb(es2, 'cacc', [128, S])
            cout = [sb(es2, 'cout%d' % i, [128, S]) for i in range(1)]
            wfm = [sb(es2, 'wfm%d' % i, [128, 8, 128], BF16) for i in range(2)]
            cw_m = sb(es2, 'cw_m', [128, 4, 7]); cb_m = sb(es2, 'cb_m', [128, 4])
            cw_d = sb(es2, 'cw_d', [128, 8, 7]); cb_d = sb(es2, 'cb_d', [128, 8])
            for c in range(4):
                P.dma(cw_m[:, c, :], I['mlstm_conv_w'][l, :, c * 128:(c + 1) * 128].rearrange("j p -> p j"), writes=[cw_m], allow_slow_non_contiguous=True)
            P.dma(cb_m[:], I['mlstm_conv_b'][l].rearrange("(c p) -> p c", p=128), writes=[cb_m], allow_slow_non_contiguous=True)
            for c in range(8):
                P.dma(cw_d[:, c, :], I['ssd_conv_w'][l, :, c * 128:(c + 1) * 128].rearrange("j p -> p j"), writes=[cw_d], allow_slow_non_contiguous=True)
            P.dma(cb_d[:], I['ssd_conv_b'][l].rearrange("(c p) -> p c", p=128), writes=[cb_d], allow_slow_non_contiguous=True)
            for rb in rowbuf:
                P.pool(lambda e, rb=rb: e.memset(rb[:], 0.0), writes=[rb])
            cosT = sb(es2, 'cosT', [128, SEQ]); sinT = sb(es2, 'sinT', [128, SEQ]); perm = sb(es2, 'perm', [128, 128])
            build_rope_tables(G, es2, cosT, sinT, perm)
            trst = [sb(es2, 'trst%d' % i, [128, 4, 128], BF16) for i in range(2)]
            trst32 = [sb(es2, 'trst32%d' % i, [128, 4, 128]) for i in range(2)]
            cbf = [sb(es2, 'cbf%d' % i, [128, S], BF16) for i in range(1)]
            chunks = [(O_MQ + 128 * i, 'mq', i) for i in range(2)] + [(O_MK + 128 * i, 'mk', i) for i in range(2)]
            chunks += [(O_NQ + 128 * i, 'nq', i) for i in range(2)] + [(O_NK + 128 * i, 'nk', i) for i in range(2)]
            chunks += [(O_DX + 128 * i, 'dx', i) for i in range(4)] + [(O_DB + 128 * i, 'dB', i) for i in range(2)]
            chunks += [(O_DC + 128 * i, 'dC', i) for i in range(2)]
            trn = 0
            pc_it = precast_iter(G, l)
            for ci, (c0, kind, idx) in enumerate(chunks):
                w = wfm[ci % 2]; rb = rowbuf[ci % 2]; co = cout[0]; dg = dgt[ci % 2]
                P.dmaq('pool', w[:], I['w_in'][l, :, c0:c0 + 128].rearrange("(k p) n -> p k n", p=128), writes=[w])
                if ci >= 1:
                    for _ in range(12):
                        next(pc_it, None)
                for ti, (t0, n) in enumerate(TT):
                    ps = G.nextps()
                    for k in range(8):
                        P.pe(lambda e, ps=ps, k=k, n=n, t0=t0, w=w: e.matmul(ps[:, 0:n], lhsT=w[:, k, :], rhs=hxT[:, k, t0:t0 + n], start=(k == 0), stop=(k == 7)),
                             reads=[hres[ti], w], writes=[ps])
                    if kind in ('nq', 'nk'):
                        dst = cbf[0]
                        scale = 0.125 if kind == 'nq' else 1.0
                        P.act(lambda e, ps=ps, n=n, t0=t0, dst=dst, scale=scale: e.activation(out=dst[:, t0:t0 + n], in_=ps[:, 0:n], func=AF.Copy, scale=scale),
                              reads=[ps], writes=[dst])
                    else:
                        o0 = 3 if ti == 0 else XOFF + (t0 - CTX)
                        if ti % 2 == 0:
                            P.act(lambda e, ps=ps, n=n, o0=o0, rb=rb: e.activation(out=rb[:, o0:o0 + n], in_=ps[:, 0:n], func=AF.Copy), reads=[ps], writes=[rb])
                        else:
                            P.dve(lambda e, ps=ps, n=n, o0=o0, rb=rb: e.tensor_copy(out=rb[:, o0:o0 + n], in_=ps[:, 0:n]), reads=[ps], writes=[rb])
                if kind in ('nq', 'nk'):
                    dst = cbf[0]
                    P.dma(sc['nqT' if kind == 'nq' else 'nkT'][idx], dst[:], reads=[dst], writes=[sc['nqT' if kind == 'nq' else 'nkT']])
                    continue
                if kind in ('mq', 'mk'):
                    cw, cb, cidx = cw_m, cb_m, (idx if kind == 'mq' else 2 + idx)
                else:
                    cw, cb, cidx = cw_d, cb_d, {'dx': idx, 'dB': 4 + idx, 'dC': 6 + idx}[kind]
                for j in range(7):
                    P.dve(lambda e, j=j, dg=dg, cw=cw, cidx=cidx: e.tensor_scalar(out=dg[:, j, :], in0=G.ident[:], scalar1=cw[:, cidx, j:j + 1], scalar2=None, op0=ALU.mult),
                          reads=[G.ident, cw], writes=[dg])
                segs = [(3, CTX, 0)] + [(XOFF + 512 * t, 512, CTX + 512 * t) for t in range(8)]
                for si, (o0, n, d0) in enumerate(segs):
                    ps = G.nextps()
                    for j in range(7):
                        P.pe(lambda e, ps=ps, j=j, o0=o0, n=n, dg=dg, rb=rb: e.matmul(ps[:, 0:n], lhsT=dg[:, j, :], rhs=rb[:, o0 - 3 + j:o0 - 3 + j + n], start=(j == 0), stop=(j == 6)),
                             reads=[dg, rb], writes=[ps])
                    P.act(lambda e, ps=ps, co=co, cb=cb, cidx=cidx, n=n, d0=d0: e.activation(out=co[:, d0:d0 + n], in_=ps[:, 0:n], func=AF.Silu, bias=cb[:, cidx:cidx + 1]),
                          reads=[ps, cb], writes=[co])
                if kind in ('mq', 'mk'):
                    for t in range(8):
                        t0 = CTX + 512 * t
                        ps = G.nextps()
                        P.pe(lambda e, ps=ps, t0=t0, co=co: e.matmul(ps[:, :], lhsT=perm[:], rhs=co[:, t0:t0 + 512], start=True, stop=True),
                             reads=[perm, co], writes=[ps])
                        P.dve(lambda e, ps=ps, t0=t0: e.tensor_tensor(out=cacc[:, t0:t0 + 512], in0=ps[:, :], in1=sinT[:, t0 - CTX:t0 - CTX + 512], op=ALU.mult),
                              reads=[ps, sinT], writes=[cacc])
                    P.pool(lambda e, co=co: e.tensor_tensor(out=co[:, CTX:S], in0=co[:, CTX:S], in1=cosT[:], op=ALU.mult), reads=[co, cosT], writes=[co])
                    P.dve(lambda e, co=co: e.tensor_tensor(out=co[:, CTX:S], in0=co[:, CTX:S], in1=cacc[:, CTX:S], op=ALU.add), reads=[co, cacc], writes=[co])
                    if kind == 'mq':
                        P.act(lambda e, co=co: e.activation(out=co[:], in_=co[:], func=AF.Copy, scale=0.125), reads=[co], writes=[co])
                name = {'mq': 'mqT', 'mk': 'mkT', 'dB': 'BT', 'dC': 'CT'}.get(kind)
                use16 = BF_M if kind in ('mq', 'mk') else BF_D
                if name is not None and not use16:
                    P.dma(sc[name][idx], co[:], reads=[co], writes=[sc[name]])
                elif name is not None:
                    c16 = cbf[0]
                    P.act(lambda e, co=co, c16=c16: e.activation(out=c16[:], in_=co[:], func=AF.Copy), reads=[co], writes=[c16])
                    P.dma(sc[name][idx], c16[:], reads=[c16], writes=[sc[name]])
                tmname = {'mk': 'mkTM', 'dx': 'xTM', 'dB': 'BTM'}.get(kind)
                if tmname is not None:
                    for g4 in range(0, NT128, 4):
                        nq = min(4, NT128 - g4)
                        ps = G.nextps()
                        tb = (trst if use16 else trst32)[trn % 2]; trn += 1
                        for qq in range(nq):
                            tok = (g4 + qq) * 128
                            P.pe(lambda e, ps=ps, qq=qq, tok=tok, co=co: e.transpose(out=ps[:, qq * 128:(qq + 1) * 128], in_=co[:, tok:tok + 128], identity=G.ident[:]),
                                 reads=[co, G.ident], writes=[ps])
                        P.act(lambda e, ps=ps, tb=tb, nq=nq: e.activation(out=tb[:, 0:nq, :], in_=ps[:, 0:nq * 128].rearrange("p (q c) -> p q c", q=nq), func=AF.Copy),
                              reads=[ps], writes=[tb])
                        P.dma(sc[tmname][g4 * 128:(g4 + nq) * 128, idx * 128:(idx + 1) * 128].rearrange("(q p) c -> p q c", p=128), tb[:, 0:nq, :],
                              reads=[tb], writes=[sc[tmname]])
            for _ in pc_it:
                pass
            P.flush()


def build_rope_tables(G, es, cosT, sinT, perm):
    nc, P = G.nc, G.P
    sb = G.sb
    I32 = mybir.dt.int32
    pi_i = sb(es, 'rp_pi', [128, 1], I32); pf = sb(es, 'rp_pf', [128, 1]); inv = sb(es, 'rp_inv', [128, 1])
    mcol = sb(es, 'rp_mcol', [128, 1]); mrow = sb(es, 'rp_mrow', [128, 1]); t1 = sb(es, 'rp_t1', [128, 1])
    posf = sb(es, 'rp_pos', [128, 64]); ang = sb(es, 'rp_ang', [128, 64]); nn = sb(es, 'rp_n', [128, 64]); ni = sb(es, 'rp_ni', [128, 64], I32)
    cs = sb(es, 'rp_cs', [128, 64]); sn = sb(es, 'rp_sn', [128, 64]); fix = sb(es, 'rp_fix', [128, 64])
    csr = sb(es, 'rp_csr', [128, 64]); csc = sb(es, 'rp_csc', [128, 64]); snr = sb(es, 'rp_snr', [128, 64]); snc = sb(es, 'rp_snc', [128, 64])
    P.pool(lambda e: e.iota(pi_i[:], pattern=[[0, 1]], base=0, channel_multiplier=1), writes=[pi_i])
    P.dve(lambda e: e.tensor_copy(out=pf[:], in_=pi_i[:]), reads=[pi_i], writes=[pf])
    P.dve(lambda e: e.tensor_single_scalar(out=pi_i[:], in_=pi_i[:], scalar=15, op=ALU.bitwise_and), reads=[pi_i], writes=[pi_i])
    P.dve(lambda e: e.tensor_copy(out=inv[:], in_=pi_i[:]), reads=[pi_i], writes=[inv])
    P.act(lambda e: e.activation(out=inv[:], in_=inv[:], func=AF.Exp, scale=-math.log(10000.0) / 16.0), reads=[inv], writes=[inv])
    P.dve(lambda e: e.tensor_single_scalar(out=mcol[:], in_=pf[:], scalar=32.0, op=ALU.is_ge), reads=[pf], writes=[mcol])
    P.dve(lambda e: e.tensor_single_scalar(out=t1[:], in_=pf[:], scalar=64.0, op=ALU.is_ge), reads=[pf], writes=[t1])
    P.dve(lambda e: e.tensor_tensor(out=mcol[:], in0=mcol[:], in1=t1[:], op=ALU.subtract), reads=[mcol, t1], writes=[mcol])
    P.dve(lambda e: e.tensor_single_scalar(out=t1[:], in_=pf[:], scalar=96.0, op=ALU.is_ge), reads=[pf], writes=[t1])
    P.dve(lambda e: e.tensor_tensor(out=mcol[:], in0=mcol[:], in1=t1[:], op=ALU.add), reads=[mcol, t1], writes=[mcol])
    P.dve(lambda e: e.tensor_scalar(out=mrow[:], in0=mcol[:], scalar1=-1.0, scalar2=1.0, op0=ALU.mult, op1=ALU.add), reads=[mcol], writes=[mrow])
    P.pool(lambda e: e.iota(posf[:], pattern=[[1, 64]], base=0, channel_multiplier=0, allow_small_or_imprecise_dtypes=True), writes=[posf])
    P.dve(lambda e: e.tensor_scalar(out=ang[:], in0=posf[:], scalar1=inv[:, 0:1], scalar2=None, op0=ALU.mult), reads=[posf, inv], writes=[ang])

    def sincos(dst, shift):
        P.dve(lambda e: e.tensor_scalar(out=nn[:], in0=ang[:], scalar1=shift, scalar2=1.0 / (2 * math.pi), op0=ALU.add, op1=ALU.mult), reads=[ang], writes=[nn])
        P.dve(lambda e: e.tensor_copy(out=ni[:], in_=nn[:]), reads=[nn], writes=[ni])
        P.dve(lambda e: e.tensor_copy(out=nn[:], in_=ni[:]), reads=[ni], writes=[nn])
        P.dve(lambda e: e.scalar_tensor_tensor(out=nn[:], in0=nn[:], scalar=-2 * math.pi, in1=ang[:], op0=ALU.mult, op1=ALU.add), reads=[nn, ang], writes=[nn])
        P.dve(lambda e: e.tensor_scalar(out=nn[:], in0=nn[:], scalar1=shift, scalar2=None, op0=ALU.add), reads=[nn], writes=[nn])
        P.dve(lambda e: e.tensor_scalar(out=fix[:], in0=nn[:], scalar1=math.pi, scalar2=-2 * math.pi, op0=ALU.is_gt, op1=ALU.mult), reads=[nn], writes=[fix])
        P.dve(lambda e: e.tensor_tensor(out=nn[:], in0=nn[:], in1=fix[:], op=ALU.add), reads=[nn, fix], writes=[nn])
        P.dve(lambda e: e.tensor_scalar(out=fix[:], in0=nn[:], scalar1=-math.pi, scalar2=2 * math.pi, op0=ALU.is_lt, op1=ALU.mult), reads=[nn], writes=[fix])
        P.dve(lambda e: e.tensor_tensor(out=nn[:], in0=nn[:], in1=fix[:], op=ALU.add), reads=[nn, fix], writes=[nn])
        P.act(lambda e: e.activation(out=dst[:], in_=nn[:], func=AF.Sin), reads=[nn], writes=[dst])
    sincos(sn, 0.0)
    sincos(cs, math.pi / 2)
    for (src, a, b) in ((cs, csr, csc), (sn, snr, snc)):
        P.dve(lambda e, src=src, a=a: e.tensor_scalar(out=a[:], in0=src[:], scalar1=mrow[:, 0:1], scalar2=None, op0=ALU.mult), reads=[src, mrow], writes=[a])
        P.dve(lambda e, src=src, b=b: e.tensor_scalar(out=b[:], in0=src[:], scalar1=mcol[:, 0:1], scalar2=None, op0=ALU.mult), reads=[src, mcol], writes=[b])
    for (dst, a, b) in ((cosT, csr, csc), (sinT, snr, snc)):
        d3 = dst[:].rearrange("p (r c) -> p r c", r=64)
        P.dve(lambda e, d3=d3, a=a: e.tensor_copy(out=d3, in_=bc_ap(a[:], [[1, 64], [0, 64]])), reads=[a], writes=[dst])
        P.dve(lambda e, d3=d3, b=b: e.tensor_tensor(out=d3, in0=d3, in1=bc_ap(b[:], [[0, 64], [1, 64]]), op=ALU.add), reads=[b, dst], writes=[dst])
    for b in range(4):
        P.act(lambda e, b=b: e.activation(out=perm[:, 32 * b:32 * b + 16], in_=G.ident[:, 32 * b + 16:32 * b + 32], func=AF.Copy, scale=-1.0),
              reads=[G.ident], writes=[perm])
        P.act(lambda e, b=b: e.activation(out=perm[:, 32 * b + 16:32 * b + 32], in_=G.ident[:, 32 * b:32 * b + 16], func=AF.Copy),
              reads=[G.ident], writes=[perm])


def load_bcast(G, es, name, src_row_ap, n):
    t = G.sb(es, name, [128, n])
    G.P.dma(t[:], src_row_ap.partition_broadcast(128), writes=[t])
    return t


def mlstm_steps(G, l, es):
    nc, P, I = G.nc, G.P, G.I
    sb = G.sb
    sc = G.scr
    with_ctx = l < DEPTH - 1
    if 'yT' not in sc:
        G.scratch('yT', [10, 128, S], BF16)
    yT = sc['yT']
    if True:
        hacc = sb(es, 'm_hacc', [128, NT128, 256])
        gates = sb(es, 'm_gates', [128, NT128, 24])
        ib_b = load_bcast(G, es, 'm_ib', I['mlstm_ib'][l:l + 1].rearrange("o d h -> o (d h)"), 8)
        fb_b = load_bcast(G, es, 'm_fb', I['mlstm_fb'][l:l + 1].rearrange("o d h -> o (d h)"), 8)
        nw_b = load_bcast(G, es, 'm_nw', I['mlstm_norm_w'][l:l + 1, :], 256)
        NB = 2
        qT = [sb(es, 'm_qT%d' % i, [128, 2, 128]) for i in range(NB)]
        kT = [sb(es, 'm_kT%d' % i, [128, 2, 128]) for i in range(NB)]
        kTM = [sb(es, 'm_kTM%d' % i, [128, 256]) for i in range(NB)]
        Vp = [sb(es, 'm_Vp%d' % i, [128, 4, 65]) for i in range(NB)]
        gi_ = [sb(es, 'm_gi%d' % i, [128, 16]) for i in range(NB)]
        mo = [sb(es, 'm_mo%d' % i, [128, 256]) for i in range(NB)]
        for v in Vp:
            P.pool(lambda e, v=v: e.memset(v[:], 1.0), writes=[v])
        Cbd = [[sb(es, 'm_C%d%d' % (pr, d), [128, 130]) for d in range(2)] for pr in range(2)]
        for pr in range(2):
            for d in range(2):
                P.pool(lambda e, c=Cbd[pr][d]: e.memset(c[:], 0.0), writes=[Cbd[pr][d]])
        g1 = sb(es, 'm_g1', [128, 8]); g2 = sb(es, 'm_g2', [128, 8]); cum = sb(es, 'm_cum', [128, 16])
        pmt = [sb(es, 'm_pmt%d' % i, [128, 128]) for i in range(2)]
        pm = [sb(es, 'm_pm%d' % i, [128, 128]) for i in range(8)]
        uV = [sb(es, 'm_uV%d' % i, [128, 130]) for i in range(2)]
        ep = [sb(es, 'm_ep%d' % i, [128, 8]) for i in range(2)]
        stt = [sb(es, 'm_stt%d' % i, [128, 65]) for i in range(2)]
        ho = [sb(es, 'm_ho%d' % i, [128, 256]) for i in range(2)]
        sg = [sb(es, 'm_sg%d' % i, [128, 256]) for i in range(2)]
        ss = [sb(es, 'm_ss%d' % i, [128, 4]) for i in range(2)]
        junk = sb(es, 'm_junk', [128, 64])
        ytb = [sb(es, 'm_ytb%d' % i, [128, 2, 128], BF16) for i in range(2)]
        cnt = {'pm': 0, 'n': 0}

        def chunk_pass(q, d, it, first_pass):
            tok = q * 128
            b = it % NB
            P.dma(qT[b][:], sc['mqT'][:, :, tok:tok + 128].rearrange("c p t -> p c t"), reads=[sc['mqT']], writes=[qT[b]])
            P.dma(kT[b][:], sc['mkT'][:, :, tok:tok + 128].rearrange("c p t -> p c t"), reads=[sc['mkT']], writes=[kT[b]])
            P.dma(kTM[b][:], sc['mkTM'][tok:tok + 128, :], reads=[sc['mkTM']], writes=[kTM[b]])
            P.dma(Vp[b][:, :, 0:64], sc['TM1'][tok:tok + 128, 0:256].rearrange("p (h e) -> p h e", h=4), reads=[sc['TM1']], writes=[Vp[b]])
            if first_pass:
                g = gi_[b]
                P.dma(g[:], sc['TM1'][tok:tok + 128, 512:528], reads=[sc['TM1']], writes=[g])
                P.dve(lambda e: e.tensor_tensor(out=g1[:], in0=g[:, 0:8], in1=ib_b[:], op=ALU.add), reads=[g, ib_b], writes=[g1])
                P.dve(lambda e: e.tensor_tensor(out=g2[:], in0=g[:, 8:16], in1=fb_b[:], op=ALU.add), reads=[g, fb_b], writes=[g2])
                P.act(lambda e: e.activation(out=g2[:], in_=g2[:], func=AF.Exp, scale=-1.0), reads=[g2], writes=[g2])
                P.act(lambda e: e.activation(out=g2[:], in_=g2[:], func=AF.Ln, bias=1.0), reads=[g2], writes=[g2])
                P.dve(lambda e: e.tensor_scalar(out=g2[:], in0=g2[:], scalar1=-1.0, scalar2=None, op0=ALU.mult), reads=[g2], writes=[g2])
                ps = G.nextps()
                P.pe(lambda e, ps=ps: e.matmul(ps[:, 0:4], lhsT=G.triU[:], rhs=g2[:, 0:4], start=True, stop=True), reads=[G.triU, g2], writes=[ps])
                P.pe(lambda e, ps=ps: e.matmul(ps[:, 4:8], lhsT=G.triL[:], rhs=g2[:, 4:8], start=True, stop=True), reads=[G.triL, g2], writes=[ps])
                P.pe(lambda e, ps=ps: e.matmul(ps[:, 8:16], lhsT=G.ones[:], rhs=g2[:, 0:8], start=True, stop=True), reads=[G.ones, g2], writes=[ps])
                P.dve(lambda e, ps=ps: e.tensor_copy(out=cum[:], in_=ps[:, 0:16]), reads=[ps], writes=[cum])
                P.act(lambda e: e.activation(out=gates[:, q, 0:8], in_=cum[:, 0:8], func=AF.Exp), reads=[cum], writes=[gates])
                P.dve(lambda e: e.tensor_tensor(out=g1[:], in0=g1[:], in1=cum[:, 0:8], op=ALU.subtract), reads=[g1, cum], writes=[g1])
                P.act(lambda e: e.activation(out=gates[:, q, 8:16], in_=g1[:], func=AF.Exp), reads=[g1], writes=[gates])
                P.act(lambda e: e.activation(out=gates[:, q, 16:24], in_=cum[:, 8:16], func=AF.Exp), reads=[cum], writes=[gates])
            else:
                P.dma(mo[b][:], sc['TM1'][tok:tok + 128, 256:512], reads=[sc['TM1']], writes=[mo[b]])
            mask = G.triU if d == 0 else G.triL
            pms_all = []
            for pr in range(2):
                pms = []
                pms_all.append(pms)
                for hh in range(2):
                    h = 2 * pr + hh
                    j = 4 * d + h
                    ps = G.nextps()
                    P.pe(lambda e, ps=ps, hh=hh, pr=pr: e.matmul(ps[:, 0:128], lhsT=kT[b][64 * hh:64 * hh + 64, pr, :], rhs=qT[b][64 * hh:64 * hh + 64, pr, :],
                                                                  start=True, stop=True), reads=[kT[b], qT[b]], writes=[ps])
                    t_ = pmt[cnt['pm'] % 2]; p_ = pm[cnt['pm'] % 8]; cnt['pm'] += 1
                    P.act(lambda e, ps=ps, t_=t_, j=j: e.activation(out=t_[:], in_=ps[:, 0:128], func=AF.Copy, scale=gates[:, q, 8 + j:9 + j]),
                          reads=[ps, gates], writes=[t_])
                    P.pool(lambda e, t_=t_, p_=p_: e.tensor_tensor(out=p_[:], in0=t_[:], in1=mask[:], op=ALU.mult), reads=[t_, mask], writes=[p_])
                    pms.append(p_)
            yield
            for pr in range(2):
                pms = pms_all[pr]
                j0 = 4 * d + 2 * pr
                C = Cbd[pr][d]
                ps2 = G.nextps()
                P.pe(lambda e, ps2=ps2, pr=pr, C=C: e.matmul(ps2[:, 0:130], lhsT=qT[b][:, pr, :], rhs=C[:], start=True, stop=False), reads=[qT[b], C], writes=[ps2])
                for hh in range(2):
                    P.pe(lambda e, ps2=ps2, hh=hh, pr=pr, p_=pms[hh]: e.matmul(ps2[:, 65 * hh:65 * hh + 65], lhsT=p_[:], rhs=Vp[b][:, 2 * pr + hh, :],
                                                                              start=False, stop=(hh == 1)), reads=[pms[hh], Vp[b]], writes=[ps2])
                e_ = ep[cnt['n'] % 2]; cnt['n'] += 1
                p3 = ps2[:, 0:130].rearrange("p (h e) -> p h e", e=65)
                P.dve(lambda e, e_=e_, p3=p3, j0=j0: e.tensor_tensor(out=e_[:, 0:2], in0=p3[:, :, 64], in1=gates[:, q, j0:j0 + 2], op=ALU.mult), reads=[ps2, gates], writes=[e_])
                P.dve(lambda e, e_=e_: e.scalar_tensor_tensor(out=e_[:, 6:8], in0=e_[:, 0:2], scalar=-1.0, in1=e_[:, 0:2], op0=ALU.mult, op1=ALU.max), reads=[e_], writes=[e_])
                P.dve(lambda e, e_=e_: e.tensor_scalar(out=e_[:, 0:2], in0=e_[:, 6:8], scalar1=1.0, scalar2=None, op0=ALU.max), reads=[e_], writes=[e_])
                P.dve(lambda e, e_=e_: e.reciprocal(out=e_[:, 2:4], in_=e_[:, 0:2]), reads=[e_], writes=[e_])
                P.dve(lambda e, e_=e_, j0=j0: e.tensor_tensor(out=e_[:, 4:6], in0=e_[:, 2:4], in1=gates[:, q, j0:j0 + 2], op=ALU.mult), reads=[e_, gates], writes=[e_])
                for hh in range(2):
                    h = 2 * pr + hh
                    if first_pass:
                        P.act(lambda e, hh=hh, h=h, e_=e_, p3=p3: e.activation(out=hacc[:, q, 64 * h:64 * h + 64], in_=p3[:, hh, 0:64], func=AF.Copy, scale=e_[:, 4 + hh:5 + hh]),
                              reads=[ps2, e_], writes=[hacc])
                    else:
                        P.dve(lambda e, hh=hh, h=h, e_=e_, p3=p3: e.scalar_tensor_tensor(out=hacc[:, q, 64 * h:64 * h + 64], in0=p3[:, hh, 0:64], scalar=e_[:, 4 + hh:5 + hh],
                                                                                      in1=hacc[:, q, 64 * h:64 * h + 64], op0=ALU.mult, op1=ALU.add),
                              reads=[ps2, e_, hacc], writes=[hacc])
                uv = uV[cnt['n'] % 2]
                for hh in range(2):
                    h = 2 * pr + hh
                    j = 4 * d + h
                    P.dve(lambda e, uv=uv, hh=hh, h=h, j=j: e.tensor_scalar(out=uv[:, 65 * hh:65 * hh + 65], in0=Vp[b][:, h, :], scalar1=gates[:, q, 8 + j:9 + j], scalar2=None, op0=ALU.mult),
                          reads=[Vp[b], gates], writes=[uv])
                ps3 = G.nextps()
                P.pe(lambda e, ps3=ps3, uv=uv, pr=pr: e.matmul(ps3[:, 0:130], lhsT=kTM[b][:, 128 * pr:128 * pr + 128], rhs=uv[:], start=True, stop=True), reads=[kTM[b], uv], writes=[ps3])
                for hh in range(2):
                    j = 4 * d + 2 * pr + hh
                    rows = slice(64 * hh, 64 * hh + 64)
                    cols = slice(65 * hh, 65 * hh + 65)
                    P.dve(lambda e, ps3=ps3, rows=rows, cols=cols, C=C: e.tensor_tensor(out=C[rows, cols], in0=ps3[rows, cols], in1=C[rows, cols], op=ALU.add), reads=[ps3, C], writes=[C])
                    P.dve(lambda e, rows=rows, cols=cols, C=C, j=j: e.tensor_scalar(out=C[rows, cols], in0=C[rows, cols], scalar1=gates[rows, q, 16 + j:17 + j], scalar2=None, op0=ALU.mult),
                          reads=[C, gates], writes=[C])
            if not first_pass and (with_ctx or q >= 2):
                bb = it % 2
                P.act(lambda e: e.activation(out=sg[bb][:], in_=mo[b][:], func=AF.Sigmoid), reads=[mo[b]], writes=[sg[bb]])
                P.dve(lambda e: e.tensor_tensor(out=ho[bb][:], in0=hacc[:, q, :], in1=sg[bb][:], op=ALU.mult), reads=[hacc, sg[bb]], writes=[ho[bb]])
                for h in range(4):
                    P.act(lambda e, h=h: e.activation(out=junk[:], in_=ho[bb][:, 64 * h:64 * h + 64], func=AF.Square, accum_out=ss[bb][:, h:h + 1]), reads=[ho[bb]], writes=[junk, ss[bb]])
                P.act(lambda e: e.activation(out=ss[bb][:], in_=ss[bb][:], func=AF.Sqrt, scale=1.0 / 64, bias=G.epsb[:, 0:1]), reads=[ss[bb], G.epsb], writes=[ss[bb]])
                P.dve(lambda e: e.reciprocal(out=ss[bb][:], in_=ss[bb][:]), reads=[ss[bb]], writes=[ss[bb]])
                for h in range(4):
                    P.dve(lambda e, h=h: e.scalar_tensor_tensor(out=ho[bb][:, 64 * h:64 * h + 64], in0=ho[bb][:, 64 * h:64 * h + 64], scalar=ss[bb][:, h:h + 1],
                                                                 in1=nw_b[:, 64 * h:64 * h + 64], op0=ALU.mult, op1=ALU.mult), reads=[ho[bb], ss[bb], nw_b], writes=[ho[bb]])
                ps = G.nextps()
                for c in range(2):
                    P.pe(lambda e, ps=ps, c=c: e.transpose(out=ps[:, 128 * c:128 * c + 128], in_=ho[bb][:, 128 * c:128 * c + 128], identity=G.ident[:]), reads=[ho[bb], G.ident], writes=[ps])
                P.act(lambda e, ps=ps: e.activation(out=ytb[bb][:], in_=ps[:, 0:256].rearrange("p (c t) -> p c t", c=2), func=AF.Copy), reads=[ps], writes=[ytb[bb]])
                P.dma(yT[0:2, :, tok:tok + 128].rearrange("c p t -> p c t"), ytb[bb][:], reads=[ytb[bb]], writes=[yT])

        steps = []
        it = 0
        for q in range(NT128):
            steps.append(chunk_pass(q, 0, it, True)); it += 1
        order = [1, 0] + list(range(NT128 - 1, 1, -1))
        for q in order:
            steps.append(chunk_pass(q, 1, it, False)); it += 1
        return steps


def ssd_steps(G, l, es):
    nc, P, I = G.nc, G.P, G.I
    sb = G.sb
    sc = G.scr
    with_ctx = l < DEPTH - 1
    yT = sc['yT']
    if True:
        yacc = sb(es, 'd_yacc', [128, NT128, 512])
        alog_b = load_bcast(G, es, 'd_alog', I['ssd_a_log'][l:l + 1].rearrange("o d h -> o (d h)"), 16)
        dtb_b = load_bcast(G, es, 'd_dtb', I['ssd_dt_bias'][l:l + 1].rearrange("o d h -> o (d h)"), 16)
        D_b = load_bcast(G, es, 'd_D', I['ssd_d'][l:l + 1, :], 8)
        nw_b = load_bcast(G, es, 'd_nw', I['ssd_norm_w'][l:l + 1, :], 512)
        A_b = sb(es, 'd_A', [128, 16])
        P.act(lambda e: e.activation(out=A_b[:], in_=alog_b[:], func=AF.Exp), reads=[alog_b], writes=[A_b])
        P.dve(lambda e: e.tensor_scalar(out=A_b[:], in0=A_b[:], scalar1=-1.0, scalar2=None, op0=ALU.mult), reads=[A_b], writes=[A_b])
        NB = 2
        xt = [sb(es, 'd_xt%d' % i, [128, 512]) for i in range(NB)]
        Bt = [sb(es, 'd_Bt%d' % i, [128, 2, 128]) for i in range(NB)]
        Ct = [sb(es, 'd_Ct%d' % i, [128, 2, 128]) for i in range(NB)]
        Btm = [sb(es, 'd_Btm%d' % i, [128, 256]) for i in range(NB)]
        ddt = [sb(es, 'd_ddt%d' % i, [128, 16]) for i in range(NB)]
        dz = [sb(es, 'd_dz%d' % i, [128, 512]) for i in range(NB)]
        Hs = [sb(es, 'd_Hs%d' % d, [128, 8, 64]) for d in range(2)]
        for d in range(2):
            P.pool(lambda e, d=d: e.memset(Hs[d][:], 0.0), writes=[Hs[d]])
        dt = sb(es, 'd_dt', [128, 16]); a_ = sb(es, 'd_a', [128, 16]); cum = sb(es, 'd_cum', [128, 32])
        negcum = sb(es, 'd_negcum', [128, 16]); wgt = sb(es, 'd_w', [128, 16])
        rbig = sb(es, 'd_rbig', [128, 8, 128])
        scm = [sb(es, 'd_scm%d' % g, [128, 128]) for g in range(2)]
        arg = [sb(es, 'd_arg%d' % i, [128, 128]) for i in range(2)]
        ex = [sb(es, 'd_ex%d' % i, [128, 128]) for i in range(2)]
        pmb = [sb(es, 'd_pm%d' % i, [128, 128]) for i in range(16)]
        tmp = sb(es, 'd_tmp', [128, 8, 64]); wx2 = [sb(es, 'd_wx%d' % i, [128, 8, 64]) for i in range(2)]; htmp = sb(es, 'd_htmp', [128, 8, 64])
        acum2 = [sb(es, 'd_acum%d' % i, [128, 16]) for i in range(2)]; etot2 = [sb(es, 'd_etot%d' % i, [128, 16]) for i in range(2)]
        yz = sb(es, 'd_yz', [128, 512]); sz = sb(es, 'd_sz', [128, 512]); ssq = sb(es, 'd_ssq', [128, 1]); junk = sb(es, 'd_junk', [128, 512])
        ytb = [sb(es, 'd_ytb%d' % i, [128, 4, 128], BF16) for i in range(2)]
        cnt = {'n': 0}

        def chunk_pass(q, d, it, first_pass):
            tok = q * 128
            b = it % NB
            acum = acum2[it % 2]; etot = etot2[it % 2]; wx = wx2[it % 2]
            P.dma(xt[b][:], sc['xTM'][tok:tok + 128, :], reads=[sc['xTM']], writes=[xt[b]])
            P.dma(Bt[b][:], sc['BT'][:, :, tok:tok + 128].rearrange("g p t -> p g t"), reads=[sc['BT']], writes=[Bt[b]])
            P.dma(Ct[b][:], sc['CT'][:, :, tok:tok + 128].rearrange("g p t -> p g t"), reads=[sc['CT']], writes=[Ct[b]])
            P.dma(Btm[b][:], sc['BTM'][tok:tok + 128, :], reads=[sc['BTM']], writes=[Btm[b]])
            P.dma(ddt[b][:], sc['ddtTM'][tok:tok + 128, :], reads=[sc['ddtTM']], writes=[ddt[b]])
            if not first_pass:
                P.dma(dz[b][:], sc['dzTM'][tok:tok + 128, :], reads=[sc['dzTM']], writes=[dz[b]])
            P.dve(lambda e: e.tensor_tensor(out=dt[:], in0=ddt[b][:], in1=dtb_b[:], op=ALU.add), reads=[ddt[b], dtb_b], writes=[dt])
            P.act(lambda e: e.activation(out=dt[:], in_=dt[:], func=AF.Exp), reads=[dt], writes=[dt])
            P.act(lambda e: e.activation(out=dt[:], in_=dt[:], func=AF.Ln, bias=1.0), reads=[dt], writes=[dt])
            P.dve(lambda e: e.tensor_tensor(out=a_[:], in0=dt[:], in1=A_b[:], op=ALU.mult), reads=[dt, A_b], writes=[a_])
            ps = G.nextps()
            P.pe(lambda e, ps=ps: e.matmul(ps[:, 0:8], lhsT=G.triU[:], rhs=a_[:, 0:8], start=True, stop=True), reads=[G.triU, a_], writes=[ps])
            P.pe(lambda e, ps=ps: e.matmul(ps[:, 8:16], lhsT=G.triL[:], rhs=a_[:, 8:16], start=True, stop=True), reads=[G.triL, a_], writes=[ps])
            P.pe(lambda e, ps=ps: e.matmul(ps[:, 16:32], lhsT=G.ones[:], rhs=a_[:, 0:16], start=True, stop=True), reads=[G.ones, a_], writes=[ps])
            P.dve(lambda e, ps=ps: e.tensor_copy(out=cum[:], in_=ps[:, 0:32]), reads=[ps], writes=[cum])
            P.dve(lambda e: e.tensor_scalar(out=negcum[:], in0=cum[:, 0:16], scalar1=-1.0, scalar2=None, op0=ALU.mult), reads=[cum], writes=[negcum])
            P.act(lambda e: e.activation(out=acum[:], in_=cum[:, 0:16], func=AF.Exp), reads=[cum], writes=[acum])
            P.act(lambda e: e.activation(out=etot[:], in_=cum[:, 16:32], func=AF.Exp), reads=[cum], writes=[etot])
            P.dve(lambda e: e.tensor_tensor(out=wgt[:], in0=cum[:, 16:32], in1=cum[:, 0:16], op=ALU.subtract), reads=[cum], writes=[wgt])
            P.act(lambda e: e.activation(out=wgt[:], in_=wgt[:], func=AF.Exp), reads=[wgt], writes=[wgt])
            P.dve(lambda e: e.tensor_tensor(out=wgt[:], in0=wgt[:], in1=dt[:], op=ALU.mult), reads=[wgt, dt], writes=[wgt])
            mask = G.triU if d == 0 else G.triL
            P.dve(lambda e: e.tensor_tensor(out=rbig[:], in0=bc_ap(a_[:, 8 * d:8 * d + 8], [[1, 8], [0, 128]]), in1=bc_ap(mask[:], [[0, 8], [1, 128]]), op=ALU.mult),
                  reads=[a_, mask], writes=[rbig])
            cb = [G.nextps(), G.nextps()]
            for hf in range(2):
                P.pe(lambda e, hf=hf: e.matmul(cb[hf][:, :], lhsT=G.ones[:], rhs=rbig[:, 4 * hf:4 * hf + 4, :], start=True, stop=True), reads=[G.ones, rbig], writes=[cb[hf]])
            for g in range(2):
                ps = G.nextps()
                P.pe(lambda e, ps=ps, g=g: e.matmul(ps[:, 0:128], lhsT=Bt[b][:, g, :], rhs=Ct[b][:, g, :], start=True, stop=True), reads=[Bt[b], Ct[b]], writes=[ps])
                P.dve(lambda e, ps=ps, g=g: e.tensor_tensor(out=scm[g][:], in0=ps[:, 0:128], in1=mask[:], op=ALU.mult), reads=[ps, mask], writes=[scm[g]])
            pm_list = []
            for h in range(8):
                j = 8 * d + h
                g = h // 4
                n_ = cnt['n']; cnt['n'] += 1
                ar = arg[n_ % 2]; ex_ = ex[n_ % 2]; pm_ = pmb[n_ % 16]
                cbv = cb[h // 4][:, 128 * (h % 4):128 * (h % 4) + 128]
                P.dve(lambda e, ar=ar, cbv=cbv, j=j: e.tensor_scalar(out=ar[:], in0=cbv, scalar1=negcum[:, j:j + 1], scalar2=0.0, op0=ALU.add, op1=ALU.min),
                      reads=[cb[h // 4], negcum], writes=[ar])
                P.act(lambda e, ar=ar, ex_=ex_: e.activation(out=ex_[:], in_=ar[:], func=AF.Exp), reads=[ar], writes=[ex_])
                P.dve(lambda e, ex_=ex_, pm_=pm_, j=j, g=g: e.scalar_tensor_tensor(out=pm_[:], in0=ex_[:], scalar=dt[:, j:j + 1], in1=scm[g][:], op0=ALU.mult, op1=ALU.mult),
                      reads=[ex_, dt, scm[g]], writes=[pm_])
                pm_list.append(pm_)
            P.dve(lambda e: e.tensor_tensor(out=wx[:], in0=xt[b][:].rearrange("p (h e) -> p h e", h=8), in1=bc_ap(wgt[:, 8 * d:8 * d + 8], [[1, 8], [0, 64]]), op=ALU.mult),
                  reads=[xt[b], wgt], writes=[wx])
            yield
            psd = G.nextps()
            pso = G.nextps()
            for g in range(2):
                P.pe(lambda e, g=g: e.matmul(pso[:, 256 * g:256 * g + 256], lhsT=Ct[b][:, g, :], rhs=Hs[d][:, 4 * g:4 * g + 4, :], start=True, stop=True),
                     reads=[Ct[b], Hs[d]], writes=[pso])
            for h in range(8):
                pm_ = pm_list[h]
                P.pe(lambda e, pm_=pm_, h=h: e.matmul(psd[:, 64 * h:64 * h + 64], lhsT=pm_[:], rhs=xt[b][:, 64 * h:64 * h + 64], start=True, stop=True),
                     reads=[pm_, xt[b]], writes=[psd])
            P.dve(lambda e: e.tensor_tensor(out=tmp[:], in0=pso[:, :].rearrange("p (h e) -> p h e", h=8), in1=bc_ap(acum[:, 8 * d:8 * d + 8], [[1, 8], [0, 64]]), op=ALU.mult),
                  reads=[pso, acum], writes=[tmp])
            if first_pass:
                P.dve(lambda e: e.tensor_tensor(out=yacc[:, q, :], in0=psd[:, :], in1=tmp[:].rearrange("p h e -> p (h e)"), op=ALU.add), reads=[psd, tmp], writes=[yacc])
            else:
                P.dve(lambda e: e.tensor_tensor(out=tmp[:].rearrange("p h e -> p (h e)"), in0=psd[:, :], in1=tmp[:].rearrange("p h e -> p (h e)"), op=ALU.add), reads=[psd, tmp], writes=[tmp])
                P.pool(lambda e: e.tensor_tensor(out=yacc[:, q, :], in0=yacc[:, q, :], in1=tmp[:].rearrange("p h e -> p (h e)"), op=ALU.add), reads=[tmp, yacc], writes=[yacc])
            pst = G.nextps()
            for g in range(2):
                P.pe(lambda e, g=g: e.matmul(pst[:, 256 * g:256 * g + 256], lhsT=Btm[b][:, 128 * g:128 * g + 128], rhs=wx[:, 4 * g:4 * g + 4, :], start=True, stop=True),
                     reads=[Btm[b], wx], writes=[pst])
            P.dve(lambda e: e.tensor_tensor(out=htmp[:], in0=Hs[d][:], in1=bc_ap(etot[:, 8 * d:8 * d + 8], [[1, 8], [0, 64]]), op=ALU.mult), reads=[Hs[d], etot], writes=[htmp])
            P.dve(lambda e: e.tensor_tensor(out=Hs[d][:].rearrange("p h e -> p (h e)"), in0=pst[:, :], in1=htmp[:].rearrange("p h e -> p (h e)"), op=ALU.add),
                  reads=[pst, htmp], writes=[Hs[d]])
            if not first_pass and (with_ctx or q >= 2):
                bb = it % 2
                P.dve(lambda e: e.tensor_tensor(out=tmp[:], in0=xt[b][:].rearrange("p (h e) -> p h e", h=8), in1=bc_ap(D_b[:], [[1, 8], [0, 64]]), op=ALU.mult), reads=[xt[b], D_b], writes=[tmp])
                P.dve(lambda e: e.tensor_tensor(out=yz[:], in0=yacc[:, q, :], in1=tmp[:].rearrange("p h e -> p (h e)"), op=ALU.add), reads=[yacc, tmp], writes=[yz])
                P.act(lambda e: e.activation(out=sz[:], in_=dz[b][:], func=AF.Silu), reads=[dz[b]], writes=[sz])
                P.dve(lambda e: e.tensor_tensor(out=yz[:], in0=yz[:], in1=sz[:], op=ALU.mult), reads=[yz, sz], writes=[yz])
                P.act(lambda e: e.activation(out=junk[:], in_=yz[:], func=AF.Square, accum_out=ssq[:, 0:1]), reads=[yz], writes=[junk, ssq])
                P.act(lambda e: e.activation(out=ssq[:], in_=ssq[:], func=AF.Sqrt, scale=1.0 / 512, bias=G.epsb[:, 0:1]), reads=[ssq, G.epsb], writes=[ssq])
                P.dve(lambda e: e.reciprocal(out=ssq[:], in_=ssq[:]), reads=[ssq], writes=[ssq])
                P.dve(lambda e: e.scalar_tensor_tensor(out=yz[:], in0=yz[:], scalar=ssq[:, 0:1], in1=nw_b[:], op0=ALU.mult, op1=ALU.mult), reads=[yz, ssq, nw_b], writes=[yz])
                ps = G.nextps()
                for c in range(4):
                    P.pe(lambda e, ps=ps, c=c: e.transpose(out=ps[:, 128 * c:128 * c + 128], in_=yz[:, 128 * c:128 * c + 128], identity=G.ident[:]), reads=[yz, G.ident], writes=[ps])
                P.act(lambda e, ps=ps: e.activation(out=ytb[bb][:], in_=ps[:, :].rearrange("p (c t) -> p c t", c=4), func=AF.Copy), reads=[ps], writes=[ytb[bb]])
                P.dma(yT[6:10, :, tok:tok + 128].rearrange("c p t -> p c t"), ytb[bb][:], reads=[ytb[bb]], writes=[yT])

        steps = []
        it = 0
        for q in range(NT128):
            steps.append(chunk_pass(q, 0, it, True)); it += 1
        order = [1, 0] + list(range(NT128 - 1, 1, -1))
        for q in order:
            steps.append(chunk_pass(q, 1, it, False)); it += 1
        return steps


def stage_mlstm_ssd(G, l):
    with contextlib.ExitStack() as es:
        a = mlstm_steps2(G, l, es)
        b = ssd_steps2(G, l, es)
        n = len(a)
        assert len(b) == n
        next(a[0]); next(b[0])
        for i in range(n):
            if i + 1 < n:
                next(a[i + 1]); next(b[i + 1])
            for g in (a[i], b[i]):
                try:
                    next(g)
                except StopIteration:
                    pass
        G.P.flush()


def na_units(G, l, es):
    nc, P, I = G.nc, G.P, G.I
    sb = G.sb
    sc = G.scr
    with_ctx = l < DEPTH - 1
    yT = sc['yT']
    if 'rpbpad' not in sc:
        G.scratch('rpbpad', [60, 192])
    rpbpad = sc['rpbpad']
    NEG = -30000.0
    if True:
        qT = sb(es, 'n_qT', [128, 2, S], BF16); kT = sb(es, 'n_kT', [128, 2, S], BF16)
        Vp = sb(es, 'n_Vp', [128, NT128, 4, 66], BF16)
        Vp2 = sb(es, 'n_Vp2', [128, NT128 - 1, 4, 66], BF16)
        BT = sb(es, 'n_BT', [128, 4, 14, 64])
        P.dma(qT[:], sc['nqT'][:].rearrange("c p t -> p c t"), reads=[sc['nqT']], writes=[qT])
        P.dma(kT[:], sc['nkT'][:].rearrange("c p t -> p c t"), reads=[sc['nkT']], writes=[kT])
        if 'na_q' in G.dbg:
            dq = G.scratch('na_q', [128, 2 * S], BF16)
            P.dma(dq[:], qT[:].rearrange('p c t -> p (c t)'), reads=[qT], writes=[dq])
            dk = G.scratch('na_k', [128, 2 * S], BF16)
            P.dma(dk[:], kT[:].rearrange('p c t -> p (c t)'), reads=[kT], writes=[dk])
        P.pool(lambda e: e.memset(Vp[:], 1.0), writes=[Vp])
        P.pool(lambda e: e.memset(Vp2[:], 1.0), writes=[Vp2])
        for q in range(NT128):
            P.dma(Vp[:, q, :, 0:64], sc['nvTM'][q * 128:(q + 1) * 128, :].rearrange("p (h e) -> p h e", h=4), reads=[sc['nvTM']], writes=[Vp])
        for q in range(NT128 - 1):
            P.dma(Vp2[:, q, :, 0:64], sc['nvTM'][q * 128 + 64:(q + 1) * 128 + 64, :].rearrange("p (h e) -> p h e", h=4), reads=[sc['nvTM']], writes=[Vp2])
        with contextlib.ExitStack() as es2:
            pad = sb(es2, 'n_pad', [60, 192])
            P.pool(lambda e: e.memset(pad[:], 0.0), writes=[pad])
            P.dma(pad[:, 80:111], I['na_rpb'][l].rearrange("h a b -> (h a) b"), writes=[pad])
            P.dma(rpbpad[:], pad[:], reads=[pad], writes=[rpbpad])
            L = sb(es2, 'n_L', [64, 4, 15, 64])
            for h in range(4):
                src = bass.AP(tensor=rpbpad.t.tensor, offset=rpbpad.t.offset + h * 15 * 192 + 32, ap=[[1, 64], [192, 15], [1, 64]])
                P.dma(L[:, h, :, :], src, reads=[rpbpad], writes=[L])
            antiI = sb(es2, 'n_antiI', [64, 64])
            P.pool(lambda e: e.affine_select(out=antiI[:], in_=G.ones[0:64, 0:64], pattern=[[1, 64]], compare_op=ALU.is_equal, fill=0.0, base=-63, channel_multiplier=1), reads=[G.ones], writes=[antiI])
            ckf = sb(es2, 'n_ckf', [128, 64]); c0 = sb(es2, 'n_c0', [128, 64]); m1 = sb(es2, 'n_m1', [128, 64]); m01 = sb(es2, 'n_m01', [128, 64]); negb = sb(es2, 'n_negb', [128, 64])
            P.pool(lambda e: e.iota(ckf[:], pattern=[[0, 64]], base=0, channel_multiplier=1, allow_small_or_imprecise_dtypes=True), writes=[ckf])
            P.dve(lambda e: e.tensor_scalar(out=ckf[64:128, :], in0=ckf[64:128, :], scalar1=-64.0, scalar2=None, op0=ALU.add), reads=[ckf], writes=[ckf])
            P.pool(lambda e: e.iota(c0[:], pattern=[[1, 64]], base=0, channel_multiplier=0, allow_small_or_imprecise_dtypes=True), writes=[c0])
            P.dve(lambda e: e.tensor_scalar(out=c0[:], in0=c0[:], scalar1=-8.0, scalar2=0.0, op0=ALU.add, op1=ALU.max), reads=[c0], writes=[c0])
            P.dve(lambda e: e.tensor_scalar(out=c0[:], in0=c0[:], scalar1=48.0, scalar2=None, op0=ALU.min), reads=[c0], writes=[c0])
            P.dve(lambda e: e.tensor_tensor(out=ckf[:], in0=ckf[:], in1=c0[:], op=ALU.subtract), reads=[ckf, c0], writes=[ckf])
            P.dve(lambda e: e.tensor_single_scalar(out=m1[:], in_=ckf[:], scalar=0.0, op=ALU.is_ge), reads=[ckf], writes=[m1])
            P.dve(lambda e: e.tensor_single_scalar(out=m01[:], in_=ckf[:], scalar=15.0, op=ALU.is_le), reads=[ckf], writes=[m01])
            P.dve(lambda e: e.tensor_tensor(out=m01[:], in0=m01[:], in1=m1[:], op=ALU.mult), reads=[m01, m1], writes=[m01])
            P.dve(lambda e: e.tensor_scalar(out=negb[:], in0=m01[:], scalar1=-1.0, scalar2=-NEG, op0=ALU.add, op1=ALU.mult), reads=[m01], writes=[negb])
            for h in range(4):
                for d0 in range(14):
                    ps = G.nextps()
                    P.pe(lambda e, ps=ps, h=h, d0=d0: e.matmul(ps[:, 0:64], lhsT=L[:, h, d0:d0 + 2, :].rearrange("p a b -> p (a b)"), rhs=antiI[:], start=True, stop=True),
                         reads=[L, antiI], writes=[ps])
                    P.dve(lambda e, ps=ps, h=h, d0=d0: e.tensor_tensor(out=BT[:, h, d0, :], in0=ps[:, 0:64], in1=m01[:], op=ALU.mult), reads=[ps, m01], writes=[BT])
                    P.pool(lambda e, h=h, d0=d0: e.tensor_tensor(out=BT[:, h, d0, :], in0=BT[:, h, d0, :], in1=negb[:], op=ALU.add), reads=[BT, negb], writes=[BT])
            P.flush()
        yield 'setup'
        ssb = [sb(es, 'n_ssb%d' % i, [128, 256]) for i in range(2)]
        pex = [sb(es, 'n_pex%d' % i, [128, 6, 64], BF16) for i in range(3)]
        rec = [sb(es, 'n_rec%d' % i, [128, 4]) for i in range(2)]
        O = [sb(es, 'n_O%d' % i, [128, 256]) for i in range(2)]
        ytb = [sb(es, 'n_ytb%d' % i, [128, 2, 128], BF16) for i in range(2)]
        pexc = [sb(es, 'n_pexc%d' % i, [128, 2, 128], BF16) for i in range(2)]
        n = 0
        if with_ctx:
            def ctx_tile(qt):
                nonlocal n
                o_ = O[qt % 2]; r_ = rec[qt % 2]
                for h in range(4):
                    pr, hh = h // 2, h % 2
                    ps = G.nextps()
                    for j in range(2):
                        P.pe(lambda e, ps=ps, j=j, pr=pr, hh=hh: e.matmul(ps[:, 128 * j:128 * j + 128], lhsT=kT[64 * hh:64 * hh + 64, pr, 128 * j:128 * j + 128],
                                                                         rhs=qT[64 * hh:64 * hh + 64, pr, 128 * qt:128 * qt + 128], start=True, stop=True), reads=[kT, qT], writes=[ps])
                    pc = pexc[n % 2]; n += 1
                    P.act(lambda e, ps=ps, pc=pc: e.activation(out=pc[:], in_=ps[:, 0:256].rearrange("p (j q) -> p j q", j=2), func=AF.Exp), reads=[ps], writes=[pc])
                    if 'na_dbg' in G.dbg and qt == 0 and h == 0:
                        dd = G.scratch('na_dbg', [128, 256], BF16)
                        P.dma(dd[:], pc[:].rearrange('p j q -> p (j q)'), reads=[pc], writes=[dd])
                    po = G.nextps()
                    for j in range(2):
                        P.pe(lambda e, po=po, j=j, pc=pc, h=h: e.matmul(po[:, 0:65], lhsT=pc[:, j, :], rhs=Vp[:, j, h, 0:65], start=(j == 0), stop=(j == 1)), reads=[pc, Vp], writes=[po])
                    P.dve(lambda e, po=po, r_=r_, h=h: e.reciprocal(out=r_[:, h:h + 1], in_=po[:, 64:65]), reads=[po], writes=[r_])
                    P.dve(lambda e, po=po, r_=r_, h=h, o_=o_: e.tensor_scalar(out=o_[:, 64 * h:64 * h + 64], in0=po[:, 0:64], scalar1=r_[:, h:h + 1], scalar2=None, op0=ALU.mult),
                          reads=[po, r_], writes=[o_])
                ps = G.nextps()
                yb_ = ytb[qt % 2]
                for c in range(2):
                    P.pe(lambda e, ps=ps, c=c, o_=o_: e.transpose(out=ps[:, 128 * c:128 * c + 128], in_=o_[:, 128 * c:128 * c + 128], identity=G.ident[:]), reads=[o_, G.ident], writes=[ps])
                P.act(lambda e, ps=ps, yb_=yb_: e.activation(out=yb_[:], in_=ps[:, 0:256].rearrange("p (c t) -> p c t", c=2), func=AF.Copy), reads=[ps], writes=[yb_])
                P.dma(yT[4:6, :, 128 * qt:128 * qt + 128].rearrange("c p t -> p c t"), yb_[:], reads=[yb_], writes=[yT])
            for qt in range(2):
                ctx_tile(qt)
                yield

        def lat_unit(rp, sub, h, o_, r_):
            nonlocal n
            if True:
                r = 2 * rp + sub
                r0 = min(max(r - 4, 0), 56)
                rows = slice(64 * sub, 64 * sub + 64)
                qtok = CTX + 64 * r
                if True:
                    pr, hh = h // 2, h % 2
                    ps = G.nextps()
                    ktoks = [CTX + 64 * (r0 + 2 * j) for j in range(4)] + [0, 128]
                    for j in range(6):
                        P.pe(lambda e, ps=ps, j=j, pr=pr, hh=hh, kt=ktoks[j]: e.matmul(ps[:, 64 * j:64 * j + 64], lhsT=kT[64 * hh:64 * hh + 64, pr, kt:kt + 128],
                                                                                      rhs=qT[64 * hh:64 * hh + 64, pr, qtok:qtok + 64], start=True, stop=True), reads=[kT, qT], writes=[ps])
                    s_ = ssb[n % 2]; pe_ = pex[n % 3]; n += 1
                    d0 = r0 - r + 7
                    P.dve(lambda e, ps=ps, s_=s_, h=h, d0=d0: e.tensor_tensor(out=s_[:].rearrange("p (j q) -> p j q", j=4), in0=ps[:, 0:256].rearrange("p (j q) -> p j q", j=4),
                                                                              in1=BT[:, h, d0:d0 + 7:2, :], op=ALU.add), reads=[ps, BT], writes=[s_])
                    P.act(lambda e, s_=s_, pe_=pe_: e.activation(out=pe_[:, 0:4, :], in_=s_[:].rearrange("p (j q) -> p j q", j=4), func=AF.Exp), reads=[s_], writes=[pe_])
                    P.act(lambda e, ps=ps, pe_=pe_: e.activation(out=pe_[:, 4:6, :], in_=ps[:, 256:384].rearrange("p (j q) -> p j q", j=2), func=AF.Exp), reads=[ps], writes=[pe_])
                    yield
                    po = G.nextps()
                    ktile = [(kt // 128) for kt in ktoks]
                    koff = [(kt % 128) for kt in ktoks]
                    for j in range(6):
                        vsrc = Vp if koff[j] == 0 else Vp2
                        P.pe(lambda e, po=po, j=j, pe_=pe_, h=h, kt=ktile[j], vsrc=vsrc: e.matmul(po[rows, 0:65], lhsT=pe_[:, j, :], rhs=vsrc[:, kt, h, 0:65], start=(j == 0), stop=(j == 5)),
                             reads=[pe_, vsrc], writes=[po])
                    P.dve(lambda e, po=po, r_=r_, h=h: e.reciprocal(out=r_[rows, h:h + 1], in_=po[rows, 64:65]), reads=[po], writes=[r_])
                    P.dve(lambda e, po=po, r_=r_, h=h, o_=o_: e.tensor_scalar(out=o_[rows, 64 * h:64 * h + 64], in0=po[rows, 0:64], scalar1=r_[rows, h:h + 1], scalar2=None, op0=ALU.mult),
                          reads=[po, r_], writes=[o_])
        def lat_finish(rp, o_):
            ps = G.nextps()
            yb_ = ytb[rp % 2]
            tok = CTX + 128 * rp
            for c in range(2):
                P.pe(lambda e, ps=ps, c=c, o_=o_: e.transpose(out=ps[:, 128 * c:128 * c + 128], in_=o_[:, 128 * c:128 * c + 128], identity=G.ident[:]), reads=[o_, G.ident], writes=[ps])
            P.act(lambda e, ps=ps, yb_=yb_: e.activation(out=yb_[:], in_=ps[:, 0:256].rearrange("p (c t) -> p c t", c=2), func=AF.Copy), reads=[ps], writes=[yb_])
            P.dma(yT[4:6, :, tok:tok + 128].rearrange("c p t -> p c t"), yb_[:], reads=[yb_], writes=[yT])

        def units():
            for rp in range(32):
                yield from lat_pair(rp)
        pending = []
        import itertools

        def unit_iter():
            for rp in range(32):
                o_ = O[rp % 2]; r_ = rec[rp % 2]
                for sub in range(2):
                    for h in range(4):
                        yield (rp, sub, h, o_, r_)
        prev = None
        for (rp, sub, h, o_, r_) in unit_iter():
            g = lat_unit(rp, sub, h, o_, r_)
            next(g)
            if prev is not None:
                pg, prp, psub, ph, po_ = prev
                for _ in pg:
                    pass
                if psub == 1 and ph == 3:
                    lat_finish(prp, po_)
            prev = (g, rp, sub, h, o_)
            yield
        pg, prp, psub, ph, po_ = prev
        for _ in pg:
            pass
        lat_finish(prp, po_)


class LazyPS:
    __slots__ = ('t', 'r')

    def __init__(self):
        self.t = None
        self.r = None

    def __getitem__(self, k):
        return self.t[k]


class Rec:
    def __init__(self, G):
        self.G = G
        self.P = G.P
        self.ops = []

    def op(self, eng, fn, reads=(), writes=(), dma=False):
        self.ops.append((eng, fn, tuple(reads), tuple(writes), dma))

    def pe(self, fn, reads=(), writes=()):
        self.op('pe', fn, reads, writes)

    def act(self, fn, reads=(), writes=()):
        self.op('act', fn, reads, writes)

    def dve(self, fn, reads=(), writes=()):
        self.op('dve', fn, reads, writes)

    def pool(self, fn, reads=(), writes=()):
        self.op('pool', fn, reads, writes)

    def dmaq(self, q, out, in_, reads=(), writes=(), **kw):
        self.op(q, lambda e: e.dma_start(out=out, in_=in_, **kw), reads, writes, dma=True)

    def dma(self, out, in_, reads=(), writes=(), **kw):
        self.dmaq('sp', out, in_, reads, writes, **kw)

    def issue(self, o):
        G = self.G
        for x in o[2] + o[3]:
            if isinstance(x, LazyPS) and x.t is None:
                b = G.ps[G.psi % 8]
                G.psi += 1
                x.t, x.r = b.t, b.r
        self.P.op(*o)

    def replay(self):
        for o in self.ops:
            self.issue(o)
        self.ops = []

    def flush(self):
        self.replay()
        self.P.flush()


class View:
    __slots__ = ('t', 'r')

    def __init__(self, t, r):
        self.t = t
        self.r = r

    def __getitem__(self, k):
        return self.t[k]


class GProxy:
    def __init__(self, G, name, banks):
        self.__dict__['_G'] = G
        self.__dict__['P'] = Rec(G)
        self.__dict__['_banks'] = banks
        self.__dict__['_bi'] = 0
        scr = dict(G.scr)
        scr['yT'] = View(G.scr['yT'].t, Res('yT_' + name))
        self.__dict__['scr'] = scr

    def nextps(self):
        b = self._G.ps[self._banks[self._bi % len(self._banks)]]
        self.__dict__['_bi'] = self._bi + 1
        return b

    def scratch(self, name, shape, dt=F32):
        b = self._G.scratch(name, shape, dt)
        self.scr[name] = b
        return b

    def __getattr__(self, k):
        return getattr(self._G, k)


def merge_streams(recs):
    pos = [0] * len(recs)
    tot = [max(1, len(r.ops)) for r in recs]
    while True:
        best, bi = None, -1
        for i, r in enumerate(recs):
            if pos[i] < len(r.ops):
                f = pos[i] / tot[i]
                if best is None or f < best:
                    best, bi = f, i
        if bi < 0:
            break
        recs[bi].issue(recs[bi].ops[pos[bi]])
        pos[bi] += 1
    for r in recs:
        r.ops = []


def stage_mixers(G, l):
    if 'yT' not in G.scr:
        G.scratch('yT', [10, 128, S], BF16)
    if 'hacc_d' not in G.scr:
        G.scratch('hacc_d', [S, 256]); G.scratch('yacc_d', [S, 512])
    with contextlib.ExitStack() as es:
        GA, GB, GC = GProxy(G, 'a', [0, 1, 2]), GProxy(G, 'd', [3, 4, 5]), GProxy(G, 'c', [6, 7])
        na = na_units(GC, l, es)
        next(na)
        a = mlstm_steps2(GA, l, es)
        b = ssd_steps2(GB, l, es)
        for steps in (a, b):
            n = len(steps)
            next(steps[0])
            for i in range(n):
                if i + 1 < n:
                    next(steps[i + 1])
                for _ in steps[i]:
                    pass
        for _ in na:
            pass
        merge_streams([GA.P, GB.P, GC.P])
        G.P.flush()


def emit_sin(G, dst, ang, shift, nn, ni, fix):
    P = G.P
    P.dve(lambda e: e.tensor_scalar(out=nn[:], in0=ang[:], scalar1=shift, scalar2=1.0 / (2 * math.pi), op0=ALU.add, op1=ALU.mult), reads=[ang], writes=[nn])
    P.dve(lambda e: e.tensor_copy(out=ni[:], in_=nn[:]), reads=[nn], writes=[ni])
    P.dve(lambda e: e.tensor_copy(out=nn[:], in_=ni[:]), reads=[ni], writes=[nn])
    P.dve(lambda e: e.scalar_tensor_tensor(out=nn[:], in0=nn[:], scalar=-2 * math.pi, in1=ang[:], op0=ALU.mult, op1=ALU.add), reads=[nn, ang], writes=[nn])
    P.dve(lambda e: e.tensor_scalar(out=nn[:], in0=nn[:], scalar1=shift, scalar2=None, op0=ALU.add), reads=[nn], writes=[nn])
    P.dve(lambda e: e.tensor_scalar(out=fix[:], in0=nn[:], scalar1=math.pi, scalar2=-2 * math.pi, op0=ALU.is_gt, op1=ALU.mult), reads=[nn], writes=[fix])
    P.dve(lambda e: e.tensor_tensor(out=nn[:], in0=nn[:], in1=fix[:], op=ALU.add), reads=[nn, fix], writes=[nn])
    P.dve(lambda e: e.tensor_scalar(out=fix[:], in0=nn[:], scalar1=-math.pi, scalar2=2 * math.pi, op0=ALU.is_lt, op1=ALU.mult), reads=[nn], writes=[fix])
    P.dve(lambda e: e.tensor_tensor(out=nn[:], in0=nn[:], in1=fix[:], op=ALU.add), reads=[nn, fix], writes=[nn])
    P.dve(lambda e: e.tensor_scalar(out=nn[:], in0=nn[:], scalar1=3.1415925, scalar2=-3.1415925, op0=ALU.min, op1=ALU.max), reads=[nn], writes=[nn])
    P.act(lambda e: e.activation(out=dst[:], in_=nn[:], func=AF.Sin), reads=[nn], writes=[dst])


S5_BT = [(0, 32, 1)] + [(32 + 128 * i, 128, 35 + 128 * i) for i in range(4)]
ZW = 548


def s5_colF(k):
    return 3 + k


def s5_colB(k):
    return (k - 31) if k >= 32 else (513 + k)


def stage_s5(G, l):
    nc, P, I = G.nc, G.P, G.I
    sb = G.sb
    sc = G.scr
    with_ctx = l < DEPTH - 1
    yT = sc['yT']
    I32 = mybir.dt.int32
    with contextlib.ExitStack() as es:
        Toe = sb(es, 's_Toe', [128, 16, 128])
        PCrD = [sb(es, 's_PCrD%d' % d, [128, 16, 128]) for d in range(2)]; PCiND = [sb(es, 's_PCiND%d' % d, [128, 16, 128]) for d in range(2)]
        for t_ in PCrD + PCiND:
            P.pool(lambda e, t_=t_: e.memset(t_[:], 0.0), writes=[t_])
        AA = sb(es, 's_AA', [128, 16, 2]); AXm = sb(es, 's_AX', [128, 16, 2])
        A32A = sb(es, 's_A32A', [128, 16, 2]); A32X = sb(es, 's_A32X', [128, 16, 2])
        Pwr = sb(es, 's_Pwr', [128, 16, 32]); Pwi = sb(es, 's_Pwi', [128, 16, 32])
        PBT = sb(es, 's_PBT', [128, 2, 16, 128])
        with contextlib.ExitStack() as es2:
            with contextlib.ExitStack() as es3:
                lamr = sb(es3, 's_lamr', [128, 16]); lami = sb(es3, 's_lami', [128, 16]); dtl = sb(es3, 's_dt', [128, 16])
                Bre = sb(es3, 's_Bre', [128, 16, 16]); Bim = sb(es3, 's_Bim', [128, 16, 16])
                Cre = sb(es3, 's_Cre', [128, 16, 16]); Cim = sb(es3, 's_Cim', [128, 16, 16])
                for d in range(2):
                    hs = slice(64 * d, 64 * d + 64)
                    P.dma(lamr[hs, :], I['s5_lam_re'][l, d].rearrange("g p -> p g"), writes=[lamr], allow_slow_non_contiguous=True)
                    P.dma(lami[hs, :], I['s5_lam_im'][l, d].rearrange("g p -> p g"), writes=[lami], allow_slow_non_contiguous=True)
                    P.dma(dtl[hs, :], I['s5_log_dt'][l, d:d + 1, :].partition_broadcast(64), writes=[dtl])
                    P.dma(Bre[hs], I['s5_b_re'][l].rearrange("g p c -> p g c"), writes=[Bre])
                    P.dma(Bim[hs], I['s5_b_im'][l].rearrange("g p c -> p g c"), writes=[Bim])
                    for g in range(16):
                        P.dma(Cre[hs, g, :], I['s5_c_re'][l, g].rearrange("c p -> p c"), writes=[Cre], allow_slow_non_contiguous=True)
                        P.dma(Cim[hs, g, :], I['s5_c_im'][l, g].rearrange("c p -> p c"), writes=[Cim], allow_slow_non_contiguous=True)
                P.act(lambda e: e.activation(out=dtl[:], in_=dtl[:], func=AF.Exp), reads=[dtl], writes=[dtl])
                lrd = sb(es3, 's_lrd', [128, 16]); lid = sb(es3, 's_lid', [128, 16])
                P.dve(lambda e: e.tensor_tensor(out=lrd[:], in0=lamr[:], in1=dtl[:], op=ALU.mult), reads=[lamr, dtl], writes=[lrd])
                P.dve(lambda e: e.tensor_tensor(out=lid[:], in0=lami[:], in1=dtl[:], op=ALU.mult), reads=[lami, dtl], writes=[lid])
                NJ = 24
                jv = sb(es3, 's_jv', [128, NJ])
                P.pool(lambda e: e.iota(jv[:, 0:16], pattern=[[1, 16]], base=0, channel_multiplier=0, allow_small_or_imprecise_dtypes=True), writes=[jv])
                P.pool(lambda e: e.iota(jv[:, 16:24], pattern=[[-1, 8]], base=0, channel_multiplier=0, allow_small_or_imprecise_dtypes=True), writes=[jv])
                mag = sb(es3, 's_mag', [128, 16, NJ]); ang = sb(es3, 's_ang', [128, 16, NJ])
                P.dve(lambda e: e.tensor_tensor(out=mag[:], in0=bc_ap(lrd[:], [[1, 16], [0, NJ]]), in1=bc_ap(jv[:], [[0, 16], [1, NJ]]), op=ALU.mult), reads=[lrd, jv], writes=[mag])
                P.dve(lambda e: e.tensor_tensor(out=ang[:], in0=bc_ap(lid[:], [[1, 16], [0, NJ]]), in1=bc_ap(jv[:], [[0, 16], [1, NJ]]), op=ALU.mult), reads=[lid, jv], writes=[ang])
                P.act(lambda e: e.activation(out=mag[:], in_=mag[:], func=AF.Exp), reads=[mag], writes=[mag])
                nn = sb(es3, 's_nn', [128, 16, NJ]); ni = sb(es3, 's_ni', [128, 16, NJ], I32); fix = sb(es3, 's_fix', [128, 16, NJ])
                Pr = sb(es3, 's_Pr', [128, 16, NJ]); Pi = sb(es3, 's_Pi', [128, 16, NJ])
                emit_sin(G, Pi, ang, 0.0, nn, ni, fix)
                emit_sin(G, Pr, ang, math.pi / 2, nn, ni, fix)
                P.dve(lambda e: e.tensor_tensor(out=Pr[:], in0=Pr[:], in1=mag[:], op=ALU.mult), reads=[Pr, mag], writes=[Pr])
                P.dve(lambda e: e.tensor_tensor(out=Pi[:], in0=Pi[:], in1=mag[:], op=ALU.mult), reads=[Pi, mag], writes=[Pi])
                t1 = sb(es3, 's_t1', [128, 16]); t2 = sb(es3, 's_t2', [128, 16]); nr = sb(es3, 's_nr', [128, 16]); rden = sb(es3, 's_rden', [128, 16])
                cr = sb(es3, 's_cr', [128, 16]); ci = sb(es3, 's_ci', [128, 16])
                P.dve(lambda e: e.tensor_tensor(out=t1[:], in0=lamr[:], in1=lamr[:], op=ALU.mult), reads=[lamr], writes=[t1])
                P.dve(lambda e: e.tensor_tensor(out=t2[:], in0=lami[:], in1=lami[:], op=ALU.mult), reads=[lami], writes=[t2])
                P.dve(lambda e: e.tensor_tensor(out=t1[:], in0=t1[:], in1=t2[:], op=ALU.add), reads=[t1, t2], writes=[t1])
                P.dve(lambda e: e.reciprocal(out=rden[:], in_=t1[:]), reads=[t1], writes=[rden])
                P.dve(lambda e: e.tensor_scalar(out=nr[:], in0=Pr[:, :, 1], scalar1=-1.0, scalar2=None, op0=ALU.add), reads=[Pr], writes=[nr])
                P.dve(lambda e: e.tensor_tensor(out=t1[:], in0=nr[:], in1=lamr[:], op=ALU.mult), reads=[nr, lamr], writes=[t1])
                P.dve(lambda e: e.tensor_tensor(out=t2[:], in0=Pi[:, :, 1], in1=lami[:], op=ALU.mult), reads=[Pi, lami], writes=[t2])
                P.dve(lambda e: e.tensor_tensor(out=t1[:], in0=t1[:], in1=t2[:], op=ALU.add), reads=[t1, t2], writes=[t1])
                P.dve(lambda e: e.tensor_tensor(out=cr[:], in0=t1[:], in1=rden[:], op=ALU.mult), reads=[t1, rden], writes=[cr])
                P.dve(lambda e: e.tensor_tensor(out=t1[:], in0=Pi[:, :, 1], in1=lamr[:], op=ALU.mult), reads=[Pi, lamr], writes=[t1])
                P.dve(lambda e: e.tensor_tensor(out=t2[:], in0=nr[:], in1=lami[:], op=ALU.mult), reads=[nr, lami], writes=[t2])
                P.dve(lambda e: e.tensor_tensor(out=t1[:], in0=t1[:], in1=t2[:], op=ALU.subtract), reads=[t1, t2], writes=[t1])
                P.dve(lambda e: e.tensor_tensor(out=ci[:], in0=t1[:], in1=rden[:], op=ALU.mult), reads=[t1, rden], writes=[ci])
                Bbr = sb(es3, 's_Bbr', [128, 16, 16]); Bbi = sb(es3, 's_Bbi', [128, 16, 16]); tb = sb(es3, 's_tb', [128, 16, 16])
                crb = bc_ap(cr[:], [[1, 16], [0, 16]]); cib = bc_ap(ci[:], [[1, 16], [0, 16]])
                P.dve(lambda e: e.tensor_tensor(out=Bbr[:], in0=Bre[:], in1=crb, op=ALU.mult), reads=[Bre, cr], writes=[Bbr])
                P.dve(lambda e: e.tensor_tensor(out=tb[:], in0=Bim[:], in1=cib, op=ALU.mult), reads=[Bim, ci], writes=[tb])
                P.dve(lambda e: e.tensor_tensor(out=Bbr[:], in0=Bbr[:], in1=tb[:], op=ALU.subtract), reads=[Bbr, tb], writes=[Bbr])
                P.dve(lambda e: e.tensor_tensor(out=Bbi[:], in0=Bim[:], in1=crb, op=ALU.mult), reads=[Bim, cr], writes=[Bbi])
                P.dve(lambda e: e.tensor_tensor(out=tb[:], in0=Bre[:], in1=cib, op=ALU.mult), reads=[Bre, ci], writes=[tb])
                P.dve(lambda e: e.tensor_tensor(out=Bbi[:], in0=Bbi[:], in1=tb[:], op=ALU.add), reads=[Bbi, tb], writes=[Bbi])
                P.dve(lambda e: e.tensor_copy(out=AA[:, :, 0], in_=Pr[:, :, 8]), reads=[Pr], writes=[AA])
                P.dve(lambda e: e.tensor_copy(out=AA[:, :, 1], in_=Pr[:, :, 8]), reads=[Pr], writes=[AA])
                P.dve(lambda e: e.tensor_scalar(out=AXm[:, :, 0], in0=Pi[:, :, 8], scalar1=-1.0, scalar2=None, op0=ALU.mult), reads=[Pi], writes=[AXm])
                P.dve(lambda e: e.tensor_copy(out=AXm[:, :, 1], in_=Pi[:, :, 8]), reads=[Pi], writes=[AXm])
                jv2 = sb(es3, 's_jv2', [128, 32]); mag2 = sb(es3, 's_mag2', [128, 16, 32]); ang2 = sb(es3, 's_ang2', [128, 16, 32])
                nn2 = sb(es3, 's_nn2', [128, 16, 32]); ni2 = sb(es3, 's_ni2', [128, 16, 32], I32); fix2 = sb(es3, 's_fix2', [128, 16, 32])
                P.pool(lambda e: e.iota(jv2[:], pattern=[[8, 32]], base=8, channel_multiplier=0, allow_small_or_imprecise_dtypes=True), writes=[jv2])
                P.dve(lambda e: e.tensor_tensor(out=mag2[:], in0=bc_ap(lrd[:], [[1, 16], [0, 32]]), in1=bc_ap(jv2[:], [[0, 16], [1, 32]]), op=ALU.mult), reads=[lrd, jv2], writes=[mag2])
                P.dve(lambda e: e.tensor_tensor(out=ang2[:], in0=bc_ap(lid[:], [[1, 16], [0, 32]]), in1=bc_ap(jv2[:], [[0, 16], [1, 32]]), op=ALU.mult), reads=[lid, jv2], writes=[ang2])
                P.act(lambda e: e.activation(out=mag2[:], in_=mag2[:], func=AF.Exp), reads=[mag2], writes=[mag2])
                emit_sin(G, Pwi, ang2, 0.0, nn2, ni2, fix2)
                emit_sin(G, Pwr, ang2, math.pi / 2, nn2, ni2, fix2)
                P.dve(lambda e: e.tensor_tensor(out=Pwr[:], in0=Pwr[:], in1=mag2[:], op=ALU.mult), reads=[Pwr, mag2], writes=[Pwr])
                P.dve(lambda e: e.tensor_tensor(out=Pwi[:], in0=Pwi[:], in1=mag2[:], op=ALU.mult), reads=[Pwi, mag2], writes=[Pwi])
                P.dve(lambda e: e.tensor_copy(out=A32A[:, :, 0], in_=Pwr[:, :, 31]), reads=[Pwr], writes=[A32A])
                P.dve(lambda e: e.tensor_copy(out=A32A[:, :, 1], in_=Pwr[:, :, 31]), reads=[Pwr], writes=[A32A])
                P.dve(lambda e: e.tensor_scalar(out=A32X[:, :, 0], in0=Pwi[:, :, 31], scalar1=-1.0, scalar2=None, op0=ALU.mult), reads=[Pwi], writes=[A32X])
                P.dve(lambda e: e.tensor_copy(out=A32X[:, :, 1], in_=Pwi[:, :, 31]), reads=[Pwi], writes=[A32X])
                PBr = sb(es3, 's_PBr', [128, 16, 8, 16]); PBi = sb(es3, 's_PBi', [128, 16, 8, 16])
                PCr0 = sb(es3, 's_PCr0', [128, 16, 8, 16]); PCi0N = sb(es3, 's_PCi0N', [128, 16, 8, 16])
                tm1 = sb(es3, 's_tm1', [128, 16, 8, 16])

                def powslice(T_, d, start, step):
                    a_ = T_[64 * d:64 * d + 64, :, :]
                    return bass.AP(tensor=a_.tensor, offset=a_.offset + start, ap=[list(a_.ap[0]), [NJ, 16], [step, 8], [0, 16]])

                def vec16(T_, d):
                    a_ = T_[64 * d:64 * d + 64, :, :]
                    return bass.AP(tensor=a_.tensor, offset=a_.offset, ap=[list(a_.ap[0]), [16, 16], [0, 8], [1, 16]])

                def cmul(outr, outi, d, pstart, pstep, Vr, Vi, neg_im):
                    hs = slice(64 * d, 64 * d + 64)
                    pr_, pi_ = powslice(Pr, d, pstart, pstep), powslice(Pi, d, pstart, pstep)
                    vr_, vi_ = vec16(Vr, d), vec16(Vi, d)
                    P.dve(lambda e: e.tensor_tensor(out=outr[hs], in0=pr_, in1=vr_, op=ALU.mult), reads=[Pr, Vr], writes=[outr])
                    P.dve(lambda e: e.tensor_tensor(out=tm1[hs], in0=pi_, in1=vi_, op=ALU.mult), reads=[Pi, Vi], writes=[tm1])
                    P.dve(lambda e: e.tensor_tensor(out=outr[hs], in0=outr[hs], in1=tm1[hs], op=ALU.subtract), reads=[outr, tm1], writes=[outr])
                    P.dve(lambda e: e.tensor_tensor(out=outi[hs], in0=pr_, in1=vi_, op=ALU.mult), reads=[Pr, Vi], writes=[outi])
                    P.dve(lambda e: e.tensor_tensor(out=tm1[hs], in0=pi_, in1=vr_, op=ALU.mult), reads=[Pi, Vr], writes=[tm1])
                    if neg_im:
                        P.dve(lambda e: e.scalar_tensor_tensor(out=outi[hs], in0=outi[hs], scalar=-1.0, in1=tm1[hs], op0=ALU.mult, op1=ALU.subtract), reads=[outi, tm1], writes=[outi])
                    else:
                        P.dve(lambda e: e.tensor_tensor(out=outi[hs], in0=outi[hs], in1=tm1[hs], op=ALU.add), reads=[outi, tm1], writes=[outi])

                class V4:
                    def __init__(s_, ap, r):
                        s_.ap_ = ap; s_.r = r

                    def __getitem__(s_, k):
                        return s_.ap_[k]
                PCr_v = [V4(PCrD[d][:].rearrange("p g (t c) -> p g t c", t=8), PCrD[d].r) for d in range(2)]
                PCiN_v = [V4(PCiND[d][:].rearrange("p g (t c) -> p g t c", t=8), PCiND[d].r) for d in range(2)]
                cmul(PBr, PBi, 0, 16, 1, Bbr, Bbi, False)
                cmul(PBr, PBi, 1, 0, 1, Bbr, Bbi, False)
                cmul(PCr0, PCi0N, 0, 0, 1, Cre, Cim, True)
                cmul(PCr0, PCi0N, 1, 16, 1, Cre, Cim, True)
                cmul(PCr_v[0], PCiN_v[0], 0, 8, 1, Cre, Cim, True)
                cmul(PCr_v[1], PCiN_v[1], 1, 8, -1, Cre, Cim, True)
                ia = sb(es3, 's_ia', [128, 128], I32); ib = sb(es3, 's_ib', [128, 128], I32); fa = sb(es3, 's_fa', [128, 128]); fb = sb(es3, 's_fb', [128, 128])
                mF = sb(es3, 's_mF', [128, 128]); mB = sb(es3, 's_mB', [128, 128])
                P.pool(lambda e: e.iota(ia[:], pattern=[[0, 128]], base=0, channel_multiplier=1), writes=[ia])
                P.pool(lambda e: e.iota(ib[:], pattern=[[1, 128]], base=0, channel_multiplier=0), writes=[ib])
                P.dve(lambda e: e.tensor_single_scalar(out=ia[:], in_=ia[:], scalar=4, op=ALU.arith_shift_right), reads=[ia], writes=[ia])
                P.dve(lambda e: e.tensor_single_scalar(out=ib[:], in_=ib[:], scalar=4, op=ALU.arith_shift_right), reads=[ib], writes=[ib])
                P.dve(lambda e: e.tensor_copy(out=fa[:], in_=ia[:]), reads=[ia], writes=[fa])
                P.dve(lambda e: e.tensor_copy(out=fb[:], in_=ib[:]), reads=[ib], writes=[fb])
                P.dve(lambda e: e.tensor_tensor(out=mF[:], in0=fa[:], in1=fb[:], op=ALU.is_le), reads=[fa, fb], writes=[mF])
                P.dve(lambda e: e.tensor_tensor(out=mB[:], in0=fa[:], in1=fb[:], op=ALU.is_ge), reads=[fa, fb], writes=[mB])
                tt_ = [sb(es3, 's_tt%d' % i, [128, 128]) for i in range(2)]
                for g in range(16):
                    pss = []
                    for d in range(2):
                        hs = slice(64 * d, 64 * d + 64)
                        ps = G.nextps()
                        P.pe(lambda e, ps=ps, hs=hs, g=g: e.matmul(ps[:, 0:128], lhsT=PBr[hs, g].rearrange("p s c -> p (s c)"), rhs=PCr0[hs, g].rearrange("p s c -> p (s c)"), start=True, stop=False),
                             reads=[PBr, PCr0], writes=[ps])
                        P.pe(lambda e, ps=ps, hs=hs, g=g: e.matmul(ps[:, 0:128], lhsT=PBi[hs, g].rearrange("p s c -> p (s c)"), rhs=PCi0N[hs, g].rearrange("p s c -> p (s c)"), start=False, stop=True),
                             reads=[PBi, PCi0N], writes=[ps])
                        pss.append(ps)
                    P.dve(lambda e, g=g, ps=pss[0]: e.tensor_tensor(out=tt_[0][:], in0=ps[:, 0:128], in1=mF[:], op=ALU.mult), reads=[pss[0], mF], writes=[tt_[0]])
                    P.dve(lambda e, g=g, ps=pss[1]: e.tensor_tensor(out=tt_[1][:], in0=ps[:, 0:128], in1=mB[:], op=ALU.mult), reads=[pss[1], mB], writes=[tt_[1]])
                    P.pool(lambda e, g=g: e.tensor_tensor(out=Toe[:, g, :], in0=tt_[0][:], in1=tt_[1][:], op=ALU.add), reads=[tt_[0], tt_[1]], writes=[Toe])
                for ri, src in enumerate((PBr, PBi)):
                    for g4 in range(4):
                        ps = G.nextps()
                        for gg in range(4):
                            g = 4 * g4 + gg
                            P.pe(lambda e, ps=ps, gg=gg, g=g, src=src: e.transpose(out=ps[:, 128 * gg:128 * gg + 128], in_=src[:, g].rearrange("p s c -> p (s c)"), identity=G.ident[:]),
                                 reads=[src, G.ident], writes=[ps])
                        P.act(lambda e, ps=ps, ri=ri, g4=g4: e.activation(out=PBT[:, ri, 4 * g4:4 * g4 + 4, :], in_=ps[:, :].rearrange("p (g q) -> p g q", g=4), func=AF.Copy),
                              reads=[ps], writes=[PBT])
                P.flush()
            if 's5_p0' in G.dbg:
                return
            Z = sb(es, 's_Z', [128, 16, 2, ZW])
            X = sb(es, 's_X', [128, 16, 544])
            P.pool(lambda e: e.memset(Z[:], 0.0), writes=[Z])
            with contextlib.ExitStack() as es3:
                U = [sb(es3, 's_U%d' % i, [128, 8, 256]) for i in range(2)]
                Uc = [sb(es3, 's_Uc%d' % i, [128, 16, 128]) for i in range(2)]
                for ti, (k0, nb, zc) in enumerate(S5_BT):
                    u = U[ti % 2]; uc = Uc[ti % 2]
                    src = bass.AP(tensor=sc['TM1'].t.tensor, offset=sc['TM1'].t.offset + k0 * 8 * 784 + 528, ap=[[8 * 784, nb], [784, 8], [1, 256]])
                    P.dma(u[0:nb], src, reads=[sc['TM1']], writes=[u])
                    P.pool(lambda e, u=u, uc=uc, nb=nb: e.tensor_copy(out=uc[0:nb].rearrange("p g (s c) -> p g s c", s=8), in_=u[0:nb].rearrange("p s (g c) -> p g s c", g=16)),
                           reads=[u], writes=[uc])
                    for g4 in range(4):
                        ps = G.nextps()
                        for gg in range(4):
                            P.pe(lambda e, ps=ps, gg=gg, g=4 * g4 + gg, uc=uc, nb=nb: e.transpose(out=ps[:, 128 * gg:128 * gg + nb], in_=uc[0:nb, g, :], identity=G.ident[0:nb, 0:nb]),
                                 reads=[uc, G.ident], writes=[ps])
                        xdst = X[:, 4 * g4:4 * g4 + 4, k0:k0 + nb]
                        P.act(lambda e, ps=ps, xdst=xdst, nb=nb: e.activation(out=xdst, in_=ps[:, :].rearrange("p (g q) -> p g q", g=4)[:, :, 0:nb], func=AF.Copy), reads=[ps], writes=[X])
                for g in range(16):
                    for ri in range(2):
                        for (k0, nb) in ((0, 32), (32, 256), (288, 256)):
                            ps = G.nextps()
                            P.pe(lambda e, ps=ps, g=g, ri=ri, k0=k0, nb=nb: e.matmul(ps[:, 0:nb], lhsT=PBT[:, ri, g, :], rhs=X[:, g, k0:k0 + nb], start=True, stop=True), reads=[PBT, X], writes=[ps])
                            zf, zb = s5_colF(k0), s5_colB(k0)
                            P.act(lambda e, ps=ps, g=g, ri=ri, zf=zf, nb=nb: e.activation(out=Z[0:64, g, ri, zf:zf + nb], in_=ps[0:64, 0:nb], func=AF.Copy), reads=[ps], writes=[Z])
                            P.dve(lambda e, ps=ps, g=g, ri=ri, zb=zb, nb=nb: e.tensor_copy(out=Z[64:128, g, ri, zb:zb + nb], in_=ps[64:128, 0:nb]), reads=[ps], writes=[Z])
                P.flush()
        if 's5_p2' in G.dbg:
            return
        with contextlib.ExitStack() as es2:
            m1 = [sb(es2, 's_m1%d' % d, [128, 16, 2, 17]) for d in range(2)]
            m2 = [sb(es2, 's_m2%d' % d, [128, 16, 2, 17]) for d in range(2)]
            Sb = sb(es2, 's_Sb', [128, 16, 2, 17])
            t1 = [sb(es2, 's_c1%d' % d, [128, 2, 16, 32]) for d in range(2)]
            t2 = [sb(es2, 's_c2%d' % d, [128, 2, 16, 32]) for d in range(2)]
            Zres = [Res('Zf'), Res('Zb')]

            def zview(d, col, swap=False, nseg=17):
                zp = Z[64 * d:64 * d + 64, :, :, col]
                if swap:
                    return bass.AP(tensor=zp.tensor, offset=zp.offset + ZW, ap=[list(zp.ap[0]), [2 * ZW, 16], [-ZW, 2], [32, nseg]])
                return bass.AP(tensor=zp.tensor, offset=zp.offset, ap=[list(zp.ap[0]), [2 * ZW, 16], [ZW, 2], [32, nseg]])

            def tbc(T_, d, n):
                a_ = T_[64 * d:64 * d + 64]
                return bass.AP(tensor=a_.tensor, offset=a_.offset, ap=[list(a_.ap[0]), [2, 16], [1, 2], [0, n]])

            def sbv(d, idx, swap=False):
                a_ = Sb[64 * d:64 * d + 64, :, :, idx]
                if swap:
                    return bass.AP(tensor=a_.tensor, offset=a_.offset + 17, ap=[list(a_.ap[0]), [34, 16], [-17, 2]])
                return a_
            def direction(d):
                hs = slice(64 * d, 64 * d + 64)
                emit = P.dve if d == 0 else P.pool
                base = 3 if d == 0 else 1
                js = range(1, 32) if d == 0 else range(30, -1, -1)
                for j in js:
                    cur = base + j
                    prev = cur - 1 if d == 0 else cur + 1
                    emit(lambda e, d=d, prev=prev: e.tensor_tensor(out=m1[d][hs], in0=zview(d, prev), in1=tbc(AA, d, 17), op=ALU.mult), reads=[Zres[d], AA], writes=[m1[d]])
                    emit(lambda e, d=d, prev=prev: e.tensor_tensor(out=m2[d][hs], in0=zview(d, prev, True), in1=tbc(AXm, d, 17), op=ALU.mult), reads=[Zres[d], AXm], writes=[m2[d]])
                    emit(lambda e, d=d: e.tensor_tensor(out=m1[d][hs], in0=m1[d][hs], in1=m2[d][hs], op=ALU.add), reads=[m1[d], m2[d]], writes=[m1[d]])
                    emit(lambda e, d=d, cur=cur: e.tensor_tensor(out=zview(d, cur), in0=zview(d, cur), in1=m1[d][hs], op=ALU.add), reads=[m1[d], Zres[d]], writes=[Zres[d]])
                sres = Res('Sb%d' % d)
                if d == 0:
                    order = list(range(0, 16)); endcol = lambda sg: 3 + 32 * sg + 31
                else:
                    order = list(range(16, 0, -1)); endcol = lambda sg: 1 + 32 * sg
                for n_, sg in enumerate(order):
                    if n_ == 0:
                        emit(lambda e, d=d, sg=sg: e.tensor_copy(out=Sb[hs, :, :, sg], in_=Z[hs, :, :, endcol(sg)]), reads=[Zres[d]], writes=[sres])
                    else:
                        pv = order[n_ - 1]
                        emit(lambda e, d=d, pv=pv: e.tensor_tensor(out=m1[d][hs, :, :, 0], in0=sbv(d, pv), in1=A32A[hs], op=ALU.mult), reads=[sres, A32A], writes=[m1[d]])
                        emit(lambda e, d=d, pv=pv: e.tensor_tensor(out=m2[d][hs, :, :, 0], in0=sbv(d, pv, True), in1=A32X[hs], op=ALU.mult), reads=[sres, A32X], writes=[m2[d]])
                        emit(lambda e, d=d: e.tensor_tensor(out=m1[d][hs, :, :, 0], in0=m1[d][hs, :, :, 0], in1=m2[d][hs, :, :, 0], op=ALU.add), reads=[m1[d], m2[d]], writes=[m1[d]])
                        emit(lambda e, d=d, sg=sg: e.tensor_tensor(out=Sb[hs, :, :, sg], in0=Z[hs, :, :, endcol(sg)], in1=m1[d][hs, :, :, 0], op=ALU.add), reads=[m1[d], Zres[d]], writes=[sres])
                def corr(gq):
                    gs = slice(2 * gq, 2 * gq + 2)

                    def pwv(T_, rev):
                        a_ = T_[hs, gs, :]
                        if rev:
                            return bass.AP(tensor=a_.tensor, offset=a_.offset + 31, ap=[list(a_.ap[0]), [32, 2], [0, 16], [-1, 32]])
                        return bass.AP(tensor=a_.tensor, offset=a_.offset, ap=[list(a_.ap[0]), [32, 2], [0, 16], [1, 32]])

                    def sbb(ri, start):
                        a_ = Sb[hs, gs, ri, start:start + 16]
                        return bass.AP(tensor=a_.tensor, offset=a_.offset, ap=[list(a_.ap[0]), [34, 2], [1, 16], [0, 32]])

                    def zt(ri, col0):
                        a_ = Z[hs, gs, ri, col0]
                        return bass.AP(tensor=a_.tensor, offset=a_.offset, ap=[list(a_.ap[0]), [2 * ZW, 2], [32, 16], [1, 32]])
                    rev = (d == 1)
                    s0 = 0 if d == 0 else 1
                    col0 = 35 if d == 0 else 1
                    a1, a2 = t1[d][hs], t2[d][hs]
                    for (ri, x_, y_, op_) in ((0, 0, 1, ALU.subtract), (1, 1, 0, ALU.add)):
                        emit(lambda e, x_=x_: e.tensor_tensor(out=a1, in0=pwv(Pwr, rev), in1=sbb(x_, s0), op=ALU.mult), reads=[Pwr, sres], writes=[t1[d]])
                        emit(lambda e, y_=y_: e.tensor_tensor(out=a2, in0=pwv(Pwi, rev), in1=sbb(y_, s0), op=ALU.mult), reads=[Pwi, sres], writes=[t2[d]])
                        emit(lambda e, op_=op_: e.tensor_tensor(out=a1, in0=a1, in1=a2, op=op_), reads=[t1[d], t2[d]], writes=[t1[d]])
                        emit(lambda e, ri=ri: e.tensor_tensor(out=zt(ri, col0), in0=zt(ri, col0), in1=a1, op=ALU.add), reads=[t1[d], Zres[d]], writes=[Zres[d]])
                for gq in range(8):
                    corr(gq)
            for d in range(2):
                direction(d)
            P.flush()
        if 's5_p3' in G.dbg:
            return
        with contextlib.ExitStack() as es2:
            U = [sb(es2, 's_U2%d' % i, [128, 8, 256]) for i in range(1)]
            Y = [sb(es2, 's_Y%d' % i, [128, 8, 256]) for i in range(1)]
            t3 = sb(es2, 's_t3', [128, 8, 256])
            gT = [sb(es2, 's_gT%d' % i, [128, 2, 1024], BF16) for i in range(1)]
            Dsk = load_bcast(G, es2, 's_D', I['s5_d'][l:l + 1, :], 256)
            wg = sb(es2, 's_wg', [128, 2, 512], BF16)
            P.dmaq('pool', wg[:], I['s5_glu_w'][l].rearrange("(k p) n -> p k n", p=128), writes=[wg])
            sg = [sb(es2, 's_sg%d' % i, [128, 512]) for i in range(2)]
            yb = [sb(es2, 's_yb%d' % i, [128, 512], BF16) for i in range(2)]
            cnt = {'n': 0}

            def out_tile(ti, k0, nb, zc):
                u = U[0]; y = Y[0]; g_ = gT[0]
                src = bass.AP(tensor=sc['TM1'].t.tensor, offset=sc['TM1'].t.offset + k0 * 8 * 784 + 528, ap=[[8 * 784, nb], [784, 8], [1, 256]])
                P.dma(u[0:nb], src, reads=[sc['TM1']], writes=[u])
                for g4 in range(4):
                    ps = G.nextps()
                    for gg in range(4):
                        g = 4 * g4 + gg
                        o = ps[0:nb, 128 * gg:128 * gg + 128]
                        P.pe(lambda e, o=o, g=g: e.matmul(o, lhsT=X[:, g, k0:k0 + nb], rhs=Toe[:, g, :], start=True, stop=False), reads=[X, Toe], writes=[ps])
                        zf = s5_colF(k0) - 1
                        zb = s5_colB(k0) + 1
                        P.pe(lambda e, o=o, g=g, zf=zf: e.matmul(o, lhsT=Z[:, g, 0, zf:zf + nb], rhs=PCrD[0][:, g, :], start=False, stop=False), reads=[Z, PCrD[0]], writes=[ps])
                        P.pe(lambda e, o=o, g=g, zf=zf: e.matmul(o, lhsT=Z[:, g, 1, zf:zf + nb], rhs=PCiND[0][:, g, :], start=False, stop=False), reads=[Z, PCiND[0]], writes=[ps])
                        P.pe(lambda e, o=o, g=g, zb=zb: e.matmul(o, lhsT=Z[:, g, 0, zb:zb + nb], rhs=PCrD[1][:, g, :], start=False, stop=False), reads=[Z, PCrD[1]], writes=[ps])
                        P.pe(lambda e, o=o, g=g, zb=zb: e.matmul(o, lhsT=Z[:, g, 1, zb:zb + nb], rhs=PCiND[1][:, g, :], start=False, stop=True), reads=[Z, PCiND[1]], writes=[ps])
                    ydst = y[0:nb].rearrange("p t (g c) -> p g t c", g=16)[:, 4 * g4:4 * g4 + 4]
                    if g4 % 2 == 0:
                        P.act(lambda e, ps=ps, ydst=ydst: e.activation(out=ydst, in_=ps[0:nb, :].rearrange("p (g t c) -> p g t c", g=4, t=8), func=AF.Copy), reads=[ps], writes=[y])
                    else:
                        P.dve(lambda e, ps=ps, ydst=ydst: e.tensor_copy(out=ydst, in_=ps[0:nb, :].rearrange("p (g t c) -> p g t c", g=4, t=8)), reads=[ps], writes=[y])
                if 's5_p4a' in G.dbg:
                    return
                P.pool(lambda e: e.tensor_tensor(out=t3[0:nb], in0=u[0:nb], in1=bc_ap(Dsk[0:nb, :], [[0, 8], [1, 256]]), op=ALU.mult), reads=[u, Dsk], writes=[t3])
                P.dve(lambda e: e.tensor_tensor(out=y[0:nb], in0=y[0:nb], in1=t3[0:nb], op=ALU.add), reads=[y, t3], writes=[y])
                P.pool(lambda e: e.tensor_tensor(out=t3[0:nb], in0=y[0:nb], in1=y[0:nb], op=ALU.mult), reads=[y], writes=[t3])
                P.dve(lambda e: e.tensor_scalar(out=t3[0:nb], in0=t3[0:nb], scalar1=0.044715, scalar2=1.0, op0=ALU.mult, op1=ALU.add), reads=[t3], writes=[t3])
                P.pool(lambda e: e.tensor_tensor(out=t3[0:nb], in0=t3[0:nb], in1=y[0:nb], op=ALU.mult), reads=[t3, y], writes=[t3])
                P.act(lambda e: e.activation(out=t3[0:nb], in_=t3[0:nb], func=AF.Sigmoid, scale=2.0 * math.sqrt(2.0 / math.pi)), reads=[t3], writes=[t3])
                P.dve(lambda e: e.tensor_tensor(out=y[0:nb], in0=y[0:nb], in1=t3[0:nb], op=ALU.mult), reads=[y, t3], writes=[y])
                if 's5_p4b' in G.dbg:
                    return
                for c2 in range(2):
                    for t4 in range(2):
                        ps = G.nextps()
                        for tt in range(4):
                            t = 4 * t4 + tt
                            P.pe(lambda e, ps=ps, tt=tt, t=t, c2=c2: e.transpose(out=ps[:, 128 * tt:128 * tt + nb], in_=y[0:nb, t, 128 * c2:128 * c2 + 128], identity=G.ident[0:nb, 0:nb]),
                                 reads=[y, G.ident], writes=[ps])
                        gdst = g_[:, c2, 0:8 * nb].rearrange("p (k t) -> p t k", t=8)[:, 4 * t4:4 * t4 + 4, :]
                        P.act(lambda e, ps=ps, gdst=gdst: e.activation(out=gdst, in_=ps[:, :].rearrange("p (t q) -> p t q", t=4)[:, :, 0:nb], func=AF.Copy), reads=[ps], writes=[g_])
                if 's5_p4c' in G.dbg:
                    return
                ntok = 8 * nb
                tok0 = 8 * k0
                for n0 in range(0, ntok, 512):
                    nn_ = min(512, ntok - n0)
                    for c in range(2):
                        psa = G.nextps(); psb = G.nextps()
                        for kc in range(2):
                            P.pe(lambda e, psa=psa, kc=kc, c=c, n0=n0, nn_=nn_: e.matmul(psa[:, 0:nn_], lhsT=wg[:, kc, 128 * c:128 * c + 128], rhs=g_[:, kc, n0:n0 + nn_], start=(kc == 0), stop=(kc == 1)),
                                 reads=[wg, g_], writes=[psa])
                        for kc in range(2):
                            P.pe(lambda e, psb=psb, kc=kc, c=c, n0=n0, nn_=nn_: e.matmul(psb[:, 0:nn_], lhsT=wg[:, kc, 256 + 128 * c:256 + 128 * c + 128], rhs=g_[:, kc, n0:n0 + nn_], start=(kc == 0), stop=(kc == 1)),
                                 reads=[wg, g_], writes=[psb])
                        s_ = sg[cnt['n'] % 2]; o_ = yb[cnt['n'] % 2]; cnt['n'] += 1
                        P.act(lambda e, psb=psb, s_=s_, nn_=nn_: e.activation(out=s_[:, 0:nn_], in_=psb[:, 0:nn_], func=AF.Sigmoid), reads=[psb], writes=[s_])
                        P.dve(lambda e, psa=psa, s_=s_, o_=o_, nn_=nn_: e.tensor_tensor(out=o_[:, 0:nn_], in0=psa[:, 0:nn_], in1=s_[:, 0:nn_], op=ALU.mult), reads=[psa, s_], writes=[o_])
                        P.dma(yT[2 + c, :, tok0 + n0:tok0 + n0 + nn_], o_[:, 0:nn_], reads=[o_], writes=[yT])

            for ti, (k0, nb, zc) in enumerate(S5_BT):
                if ti == 0 and not with_ctx:
                    continue
                out_tile(ti, k0, nb, zc)
            P.flush()


def emit_norm2(G, xt, n, gam_fn, sh_fn, out_fn, out_res, tmp_bufs):
    P = G.P
    sq, rstd, tmp = tmp_bufs
    ps = G.nextps()
    P.act(lambda e: e.activation(out=sq[:, :, 0:n], in_=xt[:, :, 0:n], func=AF.Square), reads=[xt], writes=[sq])
    for k in range(8):
        P.pe(lambda e, k=k: e.matmul(ps[:, 0:n], lhsT=G.onesb[:], rhs=sq[:, k, 0:n], start=(k == 0), stop=(k == 7)), reads=[sq, G.onesb], writes=[ps])
    P.act(lambda e: e.activation(out=rstd[:, 0:n], in_=ps[:, 0:n], func=AF.Sqrt, scale=1.0 / D, bias=G.epsb[:, 0:1]), reads=[ps, G.epsb], writes=[rstd])
    P.dve(lambda e: e.reciprocal(out=rstd[:, 0:n], in_=rstd[:, 0:n]), reads=[rstd], writes=[rstd])
    for k in range(8):
        t = tmp[k % len(tmp)]
        P.dve(lambda e, k=k, t=t: e.tensor_tensor(out=t[:, 0:n], in0=xt[:, k, 0:n], in1=rstd[:, 0:n], op=ALU.mult), reads=[xt, rstd], writes=[t])
        if sh_fn is None:
            P.act(lambda e, k=k, t=t: e.activation(out=out_fn(k), in_=t[:, 0:n], func=AF.Copy, scale=gam_fn(k)), reads=[t, G.cm], writes=[out_res])
        else:
            P.act(lambda e, k=k, t=t: e.activation(out=out_fn(k), in_=t[:, 0:n], func=AF.Identity, scale=gam_fn(k), bias=sh_fn(k)), reads=[t, G.cm], writes=[out_res])


def precast_tail_weights(G, l):
    for _ in precast_iter(G, l):
        pass


def precast_iter(G, l):
    P, I, sc = G.P, G.I, G.scr
    if 'wt_gate' not in sc:
        G.scratch('wt_gate', [32, 128, 8 * 128], BF16); G.scratch('wt_br', [32, 128, 4 * 128], BF16)
        G.scratch('wt_out', [8, 128, 8 * 128], BF16); G.scratch('wt_ffa', [22, 128, 8 * 128], BF16)
        G.scratch('wt_ffb', [22, 128, 8 * 128], BF16); G.scratch('wt_ff2', [8, 128, 22 * 128], BF16)
    BRW = [('w_branch_a', 2), ('w_branch_b', 2), ('w_branch_c', 2), ('w_branch_d', 4)]
    for oc in range(8):
        for i, (wname, nk) in enumerate(BRW):
            c = oc * 4 + i
            c0 = O_GATE + i * 1024 + oc * 128
            P.dmaq('pool', sc['wt_gate'][c].rearrange("p (k n) -> p k n", k=8), I['w_in'][l, :, c0:c0 + 128].rearrange("(k p) n -> p k n", p=128), writes=[sc['wt_gate']])
            yield
            P.dmaq('pool', sc['wt_br'][c, :, 0:nk * 128].rearrange("p (k n) -> p k n", k=nk), I[wname][l, :, oc * 128:(oc + 1) * 128].rearrange("(k p) n -> p k n", p=128), writes=[sc['wt_br']])
        yield
        P.dmaq('pool', sc['wt_out'][oc].rearrange("p (k n) -> p k n", k=8), I['w_out'][l, :, oc * 128:(oc + 1) * 128].rearrange("(k p) n -> p k n", p=128), writes=[sc['wt_out']])
        yield
        P.dmaq('pool', sc['wt_ff2'][oc].rearrange("p (k n) -> p k n", k=22), I['ffn_w_out'][l, :, oc * 128:(oc + 1) * 128].rearrange("(k p) n -> p k n", p=128), writes=[sc['wt_ff2']])
    for hc in range(22):
        yield
        P.dmaq('pool', sc['wt_ffa'][hc].rearrange("p (k n) -> p k n", k=8), I['ffn_w_in'][l, :, hc * 128:(hc + 1) * 128].rearrange("(k p) n -> p k n", p=128), writes=[sc['wt_ffa']])
        yield
        P.dmaq('pool', sc['wt_ffb'][hc].rearrange("p (k n) -> p k n", k=8), I['ffn_w_in'][l, :, FFN_H + hc * 128:FFN_H + (hc + 1) * 128].rearrange("(k p) n -> p k n", p=128), writes=[sc['wt_ffb']])


def stage_tail(G, l):
    nc, P, I = G.nc, G.P, G.I
    sb = G.sb
    sc = G.scr
    with_ctx = l < DEPTH - 1
    last = l == DEPTH - 1
    xsT, hxT_d, yT = sc['xsT'], sc['hxT'], sc['yT']
    BR = [('w_branch_a', 0, 2), ('w_branch_b', 2, 2), ('w_branch_c', 4, 2), ('w_branch_d', 6, 4)]
    with contextlib.ExitStack() as es:
        hx = sb(es, 't_hx', [128, 8, 512], BF16); y = sb(es, 't_y', [128, 10, 512], BF16); x = sb(es, 't_x', [128, 8, 512])
        m = sb(es, 't_m', [128, 8, 512], BF16); h2 = sb(es, 't_h2', [128, 8, 512], BF16); u = sb(es, 't_u', [128, 22, 512], BF16)
        sq = sb(es, 't_sq', [128, 8, 512], BF16); rstd = sb(es, 't_rstd', [128, 512]); tmp = [sb(es, 't_tmp%d' % i, [128, 512]) for i in range(2)]
        wg = [sb(es, 't_wg%d' % i, [128, 8, 128], BF16) for i in range(8)]
        wbr = [sb(es, 't_wbr%d' % i, [128, 4, 128], BF16) for i in range(8)]
        wo = [sb(es, 't_wo%d' % i, [128, 8, 128], BF16) for i in range(3)]
        wab = [sb(es, 't_wab%d' % i, [128, 8, 128], BF16) for i in range(12)]
        w2 = [sb(es, 't_w2%d' % i, [128, 22, 128], BF16) for i in range(3)]
        gs = [sb(es, 't_gs%d' % i, [128, 512]) for i in range(2)]
        acc = sb(es, 't_acc', [128, 512]); tm = [sb(es, 't_tm%d' % i, [128, 512]) for i in range(2)]
        if last:
            fnw = sb(es, 't_fnw', [128, 8])
            P.dma(fnw[:], I['final_norm_w'].rearrange("(k p) -> p k", p=128), writes=[fnw], allow_slow_non_contiguous=True)
            xn = sb(es, 't_xn', [128, 8, 512]); ot = [sb(es, 't_ot%d' % i, [128, D]) for i in range(2)]
        cnt = {'wg': 0, 'wbr': 0, 'wo': 0, 'wab': 0, 'w2': 0, 'g': 0, 'ot': 0}

        def tile(ti, t0, n):
            s = 1 if ti == 0 else 0
            P.dma(hx[:, :, 0:n], hxT_d[:, :, t0:t0 + n].rearrange("k p t -> p k t"), reads=[hxT_d], writes=[hx])
            P.dma(y[:, :, 0:n], yT[:, :, t0:t0 + n].rearrange("k p t -> p k t"), reads=[yT], writes=[y])
            P.dma(x[:, :, 0:n], xsT[:, :, t0:t0 + n].rearrange("k p t -> p k t"), reads=[xsT], writes=[x])
            for oc in range(8):
                for i, (wname, yb0, nk) in enumerate(BR):
                    w = wg[cnt['wg'] % 8]; cnt['wg'] += 1
                    c0 = O_GATE + i * 1024 + oc * 128
                    P.dma(w[:].rearrange("p k n -> p (k n)"), sc['wt_gate'][oc * 4 + i], reads=[sc['wt_gate']], writes=[w])
                    wb = wbr[cnt['wbr'] % 8]; cnt['wbr'] += 1
                    P.dma(wb[:, 0:nk, :].rearrange("p k n -> p (k n)"), sc['wt_br'][oc * 4 + i, :, 0:nk * 128], reads=[sc['wt_br']], writes=[wb])
                    psg = G.nextps(); psb = G.nextps()
                    for k in range(8):
                        P.pe(lambda e, psg=psg, k=k, w=w: e.matmul(psg[:, 0:n], lhsT=w[:, k, :], rhs=hx[:, k, 0:n], start=(k == 0), stop=(k == 7)), reads=[w, hx], writes=[psg])
                    for k in range(nk):
                        P.pe(lambda e, psb=psb, k=k, wb=wb, yb0=yb0, nk=nk: e.matmul(psb[:, 0:n], lhsT=wb[:, k, :], rhs=y[:, yb0 + k, 0:n], start=(k == 0), stop=(k == nk - 1)),
                             reads=[wb, y], writes=[psb])
                    g_ = gs[cnt['g'] % 2]; cnt['g'] += 1
                    P.act(lambda e, psg=psg, g_=g_: e.activation(out=g_[:, 0:n], in_=psg[:, 0:n], func=AF.Sigmoid), reads=[psg], writes=[g_])
                    if i == 0:
                        P.dve(lambda e, psb=psb, g_=g_: e.tensor_tensor(out=acc[:, 0:n], in0=psb[:, 0:n], in1=g_[:, 0:n], op=ALU.mult), reads=[psb, g_], writes=[acc])
                    else:
                        t_ = tm[i % 2]
                        P.dve(lambda e, psb=psb, g_=g_, t_=t_: e.tensor_tensor(out=t_[:, 0:n], in0=psb[:, 0:n], in1=g_[:, 0:n], op=ALU.mult), reads=[psb, g_], writes=[t_])
                        if i < 3:
                            P.pool(lambda e, t_=t_: e.tensor_tensor(out=acc[:, 0:n], in0=acc[:, 0:n], in1=t_[:, 0:n], op=ALU.add), reads=[acc, t_], writes=[acc])
                        else:
                            P.pool(lambda e, t_=t_, oc=oc: e.tensor_tensor(out=m[:, oc, 0:n], in0=acc[:, 0:n], in1=t_[:, 0:n], op=ALU.add), reads=[acc, t_], writes=[m])
            for oc in range(8):
                w = wo[cnt['wo'] % 3]; cnt['wo'] += 1
                P.dma(w[:].rearrange("p k n -> p (k n)"), sc['wt_out'][oc], reads=[sc['wt_out']], writes=[w])
                ps = G.nextps()
                for k in range(8):
                    P.pe(lambda e, ps=ps, k=k, w=w: e.matmul(ps[:, 0:n], lhsT=w[:, k, :], rhs=m[:, k, 0:n], start=(k == 0), stop=(k == 7)), reads=[w, m], writes=[ps])
                P.dve(lambda e, ps=ps, oc=oc: e.scalar_tensor_tensor(out=x[:, oc, 0:n], in0=ps[:, 0:n], scalar=G.cm[:, l, 2, oc, s:s + 1], in1=x[:, oc, 0:n], op0=ALU.mult, op1=ALU.add),
                      reads=[ps, G.cm, x], writes=[x])
            emit_norm2(G, x, n, lambda k: G.cm[:, l, 4, k, s:s + 1], lambda k: G.cm[:, l, 3, k, s:s + 1], lambda k: h2[:, k, 0:n], h2, (sq, rstd, tmp))
            for hc in range(22):
                wa = wab[cnt['wab'] % 12]; cnt['wab'] += 1
                wb = wab[cnt['wab'] % 12]; cnt['wab'] += 1
                P.dma(wa[:].rearrange("p k n -> p (k n)"), sc['wt_ffa'][hc], reads=[sc['wt_ffa']], writes=[wa])
                P.dma(wb[:].rearrange("p k n -> p (k n)"), sc['wt_ffb'][hc], reads=[sc['wt_ffb']], writes=[wb])
                psa = G.nextps(); psb = G.nextps()
                for k in range(8):
                    P.pe(lambda e, psa=psa, k=k, wa=wa: e.matmul(psa[:, 0:n], lhsT=wa[:, k, :], rhs=h2[:, k, 0:n], start=(k == 0), stop=(k == 7)), reads=[wa, h2], writes=[psa])
                for k in range(8):
                    P.pe(lambda e, psb=psb, k=k, wb=wb: e.matmul(psb[:, 0:n], lhsT=wb[:, k, :], rhs=h2[:, k, 0:n], start=(k == 0), stop=(k == 7)), reads=[wb, h2], writes=[psb])
                g_ = gs[cnt['g'] % 2]; cnt['g'] += 1
                P.act(lambda e, psa=psa, g_=g_: e.activation(out=g_[:, 0:n], in_=psa[:, 0:n], func=AF.Silu), reads=[psa], writes=[g_])
                P.dve(lambda e, psb=psb, g_=g_, hc=hc: e.tensor_tensor(out=u[:, hc, 0:n], in0=psb[:, 0:n], in1=g_[:, 0:n], op=ALU.mult), reads=[psb, g_], writes=[u])
            for oc in range(8):
                w = w2[cnt['w2'] % 3]; cnt['w2'] += 1
                P.dma(w[:].rearrange("p k n -> p (k n)"), sc['wt_ff2'][oc], reads=[sc['wt_ff2']], writes=[w])
                ps = G.nextps()
                for k in range(22):
                    P.pe(lambda e, ps=ps, k=k, w=w: e.matmul(ps[:, 0:n], lhsT=w[:, k, :], rhs=u[:, k, 0:n], start=(k == 0), stop=(k == 21)), reads=[w, u], writes=[ps])
                P.dve(lambda e, ps=ps, oc=oc: e.scalar_tensor_tensor(out=x[:, oc, 0:n], in0=ps[:, 0:n], scalar=G.cm[:, l, 5, oc, s:s + 1], in1=x[:, oc, 0:n], op0=ALU.mult, op1=ALU.add),
                      reads=[ps, G.cm, x], writes=[x])
            if not last:
                P.dma(xsT[:, :, t0:t0 + n].rearrange("k p t -> p k t"), x[:, :, 0:n], reads=[x], writes=[xsT])
            else:
                emit_norm2(G, x, n, lambda k: fnw[:, k:k + 1], None, lambda k: xn[:, k, 0:n], xn, (sq, rstd, tmp))
                for q in range(n // 128):
                    o = ot[cnt['ot'] % 2]; cnt['ot'] += 1
                    for half in range(2):
                        ps = G.nextps()
                        for kk in range(4):
                            k = 4 * half + kk
                            P.pe(lambda e, ps=ps, kk=kk, k=k, q=q: e.transpose(out=ps[:, 128 * kk:128 * kk + 128], in_=xn[:, k, 128 * q:128 * q + 128], identity=G.ident[:]),
                                 reads=[xn, G.ident], writes=[ps])
                        if half == 0:
                            P.act(lambda e, ps=ps, o=o: e.activation(out=o[:, 0:512], in_=ps[:, :], func=AF.Copy), reads=[ps], writes=[o])
                        else:
                            P.dve(lambda e, ps=ps, o=o: e.tensor_copy(out=o[:, 512:1024], in_=ps[:, :]), reads=[ps], writes=[o])
                    tok = t0 - CTX + 128 * q
                    P.dma(G.out[tok:tok + 128, :], o[:], reads=[o])

        for ti, (t0, n) in enumerate(TT):
            if ti == 0 and not with_ctx:
                continue
            tile(ti, t0, n)
        P.flush()


def gate_cumsums(G, es, pfx, lf_all, ncol):
    P, sb = G.P, G.sb
    h = ncol // 2
    lfF = sb(es, pfx + 'lfF', [128, NT128, h]); lfB = sb(es, pfx + 'lfB', [128, NT128, h])
    cum_all = sb(es, pfx + 'cum', [128, NT128, ncol]); tot_all = sb(es, pfx + 'tot', [128, NT128, ncol])
    P.dve(lambda e: e.tensor_copy(out=lfF[:], in_=lf_all[:, :, 0:h]), reads=[lf_all], writes=[lfF])
    P.dve(lambda e: e.tensor_copy(out=lfB[:], in_=lf_all[:, :, h:ncol]), reads=[lf_all], writes=[lfB])
    n = NT128 * h
    psF = G.nextps(); psB = G.nextps()
    P.pe(lambda e: e.matmul(psF[:, 0:n], lhsT=G.triU[:], rhs=lfF[:].rearrange("p q c -> p (q c)"), start=True, stop=True), reads=[G.triU, lfF], writes=[psF])
    P.pe(lambda e: e.matmul(psB[:, 0:n], lhsT=G.triL[:], rhs=lfB[:].rearrange("p q c -> p (q c)"), start=True, stop=True), reads=[G.triL, lfB], writes=[psB])
    P.dve(lambda e: e.tensor_copy(out=cum_all[:, :, 0:h], in_=psF[:, 0:n].rearrange("p (q c) -> p q c", c=h)), reads=[psF], writes=[cum_all])
    P.dve(lambda e: e.tensor_copy(out=cum_all[:, :, h:ncol], in_=psB[:, 0:n].rearrange("p (q c) -> p q c", c=h)), reads=[psB], writes=[cum_all])
    nt = NT128 * ncol
    half = nt // 2
    for i in range(2):
        ps = G.nextps()
        P.pe(lambda e, ps=ps, i=i: e.matmul(ps[:, 0:half], lhsT=G.ones[:], rhs=lf_all[:].rearrange("p q c -> p (q c)")[:, i * half:(i + 1) * half], start=True, stop=True),
             reads=[G.ones, lf_all], writes=[ps])
        P.act(lambda e, ps=ps, i=i: e.activation(out=tot_all[:].rearrange("p q c -> p (q c)")[:, i * half:(i + 1) * half], in_=ps[:, 0:half], func=AF.Copy), reads=[ps], writes=[tot_all])
    return cum_all, tot_all


def mlstm_steps2(G, l, es):
    nc, P, I = G.nc, G.P, G.I
    sb = G.sb
    sc = G.scr
    with_ctx = l < DEPTH - 1
    if 'yT' not in sc:
        G.scratch('yT', [10, 128, S], BF16)
    yT = sc['yT']
    if 'hacc_d' not in sc:
        G.scratch('hacc_d', [S, 256]); G.scratch('yacc_d', [S, 512])
    hacc_d = sc['hacc_d']
    BF16 = mybir.dt.bfloat16 if BF_M else F32
    hst = [sb(es, 'm_hst%d' % i, [128, 256]) for i in range(2)]; hprev = [sb(es, 'm_hprev%d' % i, [128, 256]) for i in range(2)]
    gates = sb(es, 'm_gates', [128, NT128, 24])
    nw_b = load_bcast(G, es, 'm_nw', I['mlstm_norm_w'][l:l + 1, :], 256)
    with contextlib.ExitStack() as es0:
        ib_b = load_bcast(G, es0, 'm_ib', I['mlstm_ib'][l:l + 1].rearrange("o d h -> o (d h)"), 8)
        fb_b = load_bcast(G, es0, 'm_fb', I['mlstm_fb'][l:l + 1].rearrange("o d h -> o (d h)"), 8)
        gi = sb(es0, 'm_gi', [128, NT128, 16]); li = sb(es0, 'm_li', [128, NT128, 8]); lf = sb(es0, 'm_lf', [128, NT128, 8])
        P.dma(gi[:], sc['TM1'][:, 512:528].rearrange("(q p) c -> p q c", p=128), reads=[sc['TM1']], writes=[gi])
        P.dve(lambda e: e.tensor_tensor(out=li[:], in0=gi[:, :, 0:8], in1=bc_ap(ib_b[:], [[0, NT128], [1, 8]]), op=ALU.add), reads=[gi, ib_b], writes=[li])
        P.dve(lambda e: e.tensor_tensor(out=lf[:], in0=gi[:, :, 8:16], in1=bc_ap(fb_b[:], [[0, NT128], [1, 8]]), op=ALU.add), reads=[gi, fb_b], writes=[lf])
        P.act(lambda e: e.activation(out=lf[:], in_=lf[:], func=AF.Exp, scale=-1.0), reads=[lf], writes=[lf])
        P.act(lambda e: e.activation(out=lf[:], in_=lf[:], func=AF.Ln, bias=1.0), reads=[lf], writes=[lf])
        P.dve(lambda e: e.tensor_scalar(out=lf[:], in0=lf[:], scalar1=-1.0, scalar2=None, op0=ALU.mult), reads=[lf], writes=[lf])
        cum_all, tot_all = gate_cumsums(G, es0, 'm_', lf, 8)
        P.act(lambda e: e.activation(out=gates[:, :, 0:8], in_=cum_all[:], func=AF.Exp), reads=[cum_all], writes=[gates])
        P.dve(lambda e: e.tensor_tensor(out=li[:], in0=li[:], in1=cum_all[:], op=ALU.subtract), reads=[li, cum_all], writes=[li])
        P.act(lambda e: e.activation(out=gates[:, :, 8:16], in_=li[:], func=AF.Exp), reads=[li], writes=[gates])
        P.act(lambda e: e.activation(out=gates[:, :, 16:24], in_=tot_all[:], func=AF.Exp), reads=[tot_all], writes=[gates])
        P.flush()
    NB = 2
    qT = [sb(es, 'm_qT%d' % i, [128, 2, 128], BF16) for i in range(NB)]
    kT = [sb(es, 'm_kT%d' % i, [128, 2, 128], BF16) for i in range(NB)]
    kTM = [sb(es, 'm_kTM%d' % i, [128, 256], BF16) for i in range(NB)]
    Vp = [sb(es, 'm_Vp%d' % i, [128, 4, 65], BF16) for i in range(NB)]
    mo = [sb(es, 'm_mo%d' % i, [128, 256]) for i in range(NB)]
    for v in Vp:
        P.pool(lambda e, v=v: e.memset(v[:], 1.0), writes=[v])
    Cd = [sb(es, 'm_C%d' % d, [128, 2, 130]) for d in range(2)]
    bmask = sb(es, 'm_bmask', [128, 2, 130])
    Cb = [sb(es, 'm_Cb%d' % d, [128, 2, 130], BF16) for d in range(2)]
    for d in range(2):
        P.pool(lambda e, d=d: e.memset(Cb[d][:], 0.0), writes=[Cb[d]])
    for d in range(2):
        P.pool(lambda e, d=d: e.memset(Cd[d][:], 0.0), writes=[Cd[d]])
    P.pool(lambda e: e.memset(bmask[:], 0.0), writes=[bmask])
    P.pool(lambda e: e.memset(bmask[0:64, :, 0:65], 1.0), writes=[bmask])
    P.pool(lambda e: e.memset(bmask[64:128, :, 65:130], 1.0), writes=[bmask])
    pmt = [sb(es, 'm_pmt%d' % i, [128, 128]) for i in range(2)]
    pm = [sb(es, 'm_pm%d' % i, [128, 128], BF16) for i in range(8)]
    uV = [sb(es, 'm_uV%d' % i, [128, 4, 65], BF16) for i in range(2)]
    ep = [sb(es, 'm_ep%d' % i, [128, 20]) for i in range(2)]
    ct = [sb(es, 'm_ct%d' % i, [128, 260]) for i in range(2)]
    htmp = sb(es, 'm_htmp', [128, 4, 64])
    ho = [sb(es, 'm_ho%d' % i, [128, 256]) for i in range(2)]
    sg = [sb(es, 'm_sg%d' % i, [128, 256]) for i in range(2)]
    ss = [sb(es, 'm_ss%d' % i, [128, 4]) for i in range(2)]
    junk = sb(es, 'm_junk', [128, 64])
    ytb = [sb(es, 'm_ytb%d' % i, [128, 2, 128], mybir.dt.bfloat16) for i in range(2)]
    cnt = {'pm': 0}

    def chunk_pass(q, d, it, first_pass):
        tok = q * 128
        b = it % NB
        P.dma(qT[b][:], sc['mqT'][:, :, tok:tok + 128].rearrange("c p t -> p c t"), reads=[sc['mqT']], writes=[qT[b]])
        P.dma(kT[b][:], sc['mkT'][:, :, tok:tok + 128].rearrange("c p t -> p c t"), reads=[sc['mkT']], writes=[kT[b]])
        P.dma(kTM[b][:], sc['mkTM'][tok:tok + 128, :], reads=[sc['mkTM']], writes=[kTM[b]])
        P.dma(Vp[b][:, :, 0:64], sc['mvTM'][tok:tok + 128, :].rearrange("p (h e) -> p h e", h=4), reads=[sc['mvTM']], writes=[Vp[b]])
        if not first_pass:
            P.dma(mo[b][:], sc['TM1'][tok:tok + 128, 256:512], reads=[sc['TM1']], writes=[mo[b]])
            P.dma(hprev[b][:], hacc_d[tok:tok + 128, :], reads=[hacc_d], writes=[hprev[b]])
        mask = G.triU if d == 0 else G.triL
        pms = []
        for h in range(4):
            pr, hh = h // 2, h % 2
            j = 4 * d + h
            ps = G.nextps()
            P.pe(lambda e, ps=ps, hh=hh, pr=pr: e.matmul(ps[:, 0:128], lhsT=kT[b][64 * hh:64 * hh + 64, pr, :], rhs=qT[b][64 * hh:64 * hh + 64, pr, :], start=True, stop=True),
                 reads=[kT[b], qT[b]], writes=[ps])
            t_ = pmt[cnt['pm'] % 2]; p_ = pm[cnt['pm'] % 8]; cnt['pm'] += 1
            P.act(lambda e, ps=ps, t_=t_, j=j: e.activation(out=t_[:], in_=ps[:, 0:128], func=AF.Copy, scale=gates[:, q, 8 + j:9 + j]), reads=[ps, gates], writes=[t_])
            P.pool(lambda e, t_=t_, p_=p_: e.tensor_tensor(out=p_[:], in0=t_[:], in1=mask[:], op=ALU.mult), reads=[t_, mask], writes=[p_])
            pms.append(p_)
        uv = uV[it % 2]
        P.dve(lambda e: e.tensor_tensor(out=uv[:], in0=Vp[b][:], in1=bc_ap(gates[:, q, 8 + 4 * d:12 + 4 * d], [[1, 4], [0, 65]]), op=ALU.mult), reads=[Vp[b], gates], writes=[uv])
        yield
        C = Cd[d]
        ps2 = G.nextps()
        for pr in range(2):
            P.pe(lambda e, pr=pr: e.matmul(ps2[:, 130 * pr:130 * pr + 130], lhsT=qT[b][:, pr, :], rhs=Cb[d][:, pr, :], start=(pr == 0), stop=False), reads=[qT[b], Cb[d]], writes=[ps2])
        for h in range(4):
            P.pe(lambda e, h=h, p_=pms[h]: e.matmul(ps2[:, 65 * h:65 * h + 65], lhsT=p_[:], rhs=Vp[b][:, h, :], start=False, stop=(h == 3)), reads=[pms[h], Vp[b]], writes=[ps2])
        e_ = ep[it % 2]
        p3 = ps2[:, 0:260].rearrange("p (h e) -> p h e", e=65)
        aq = gates[:, q, 4 * d:4 * d + 4]
        P.dve(lambda e: e.tensor_tensor(out=e_[:, 0:4], in0=p3[:, :, 64], in1=aq, op=ALU.mult), reads=[ps2, gates], writes=[e_])
        P.dve(lambda e: e.scalar_tensor_tensor(out=e_[:, 4:8], in0=e_[:, 0:4], scalar=-1.0, in1=e_[:, 0:4], op0=ALU.mult, op1=ALU.max), reads=[e_], writes=[e_])
        P.dve(lambda e: e.tensor_scalar(out=e_[:, 8:12], in0=e_[:, 4:8], scalar1=1.0, scalar2=None, op0=ALU.max), reads=[e_], writes=[e_])
        P.dve(lambda e: e.reciprocal(out=e_[:, 12:16], in_=e_[:, 8:12]), reads=[e_], writes=[e_])
        P.dve(lambda e: e.tensor_tensor(out=e_[:, 16:20], in0=e_[:, 12:16], in1=aq, op=ALU.mult), reads=[e_, gates], writes=[e_])
        sclb = bc_ap(e_[:, 16:20], [[1, 4], [0, 64]])
        hs_ = hst[it % 2]
        if first_pass:
            P.dve(lambda e: e.tensor_tensor(out=hs_[:].rearrange("p (h e) -> p h e", h=4), in0=p3[:, :, 0:64], in1=sclb, op=ALU.mult), reads=[ps2, e_], writes=[hs_])
            P.dma(hacc_d[tok:tok + 128, :], hs_[:], reads=[hs_], writes=[hacc_d])
        else:
            P.dve(lambda e: e.tensor_tensor(out=htmp[:], in0=p3[:, :, 0:64], in1=sclb, op=ALU.mult), reads=[ps2, e_], writes=[htmp])
            P.pool(lambda e: e.tensor_tensor(out=hs_[:], in0=hprev[b][:], in1=htmp[:].rearrange("p h e -> p (h e)"), op=ALU.add), reads=[hprev[b], htmp], writes=[hs_])
        ps3 = G.nextps()
        for pr in range(2):
            P.pe(lambda e, pr=pr: e.matmul(ps3[:, 130 * pr:130 * pr + 130], lhsT=kTM[b][:, 128 * pr:128 * pr + 128], rhs=uv[:, 2 * pr:2 * pr + 2, :], start=True, stop=True),
                 reads=[kTM[b], uv], writes=[ps3])
        c_ = ct[it % 2]
        Cf = C[:].rearrange("p a b -> p (a b)")
        P.dve(lambda e: e.tensor_tensor(out=c_[:], in0=ps3[:, 0:260], in1=Cf, op=ALU.add), reads=[ps3, C], writes=[c_])
        P.dve(lambda e: e.tensor_tensor(out=c_[:].rearrange("p (h e) -> p h e", h=4), in0=c_[:].rearrange("p (h e) -> p h e", h=4),
                                        in1=bc_ap(gates[:, q, 16 + 4 * d:20 + 4 * d], [[1, 4], [0, 65]]), op=ALU.mult), reads=[c_, gates], writes=[c_])
        P.pool(lambda e: e.tensor_tensor(out=Cf, in0=c_[:], in1=bmask[:].rearrange("p a b -> p (a b)"), op=ALU.mult), reads=[c_, bmask], writes=[C])
        P.pool(lambda e: e.tensor_tensor(out=Cb[d][:].rearrange("p a b -> p (a b)"), in0=c_[:], in1=bmask[:].rearrange("p a b -> p (a b)"), op=ALU.mult), reads=[c_, bmask], writes=[Cb[d]])
        if not first_pass and (with_ctx or q >= 2):
            bb = it % 2
            P.act(lambda e: e.activation(out=sg[bb][:], in_=mo[b][:], func=AF.Sigmoid), reads=[mo[b]], writes=[sg[bb]])
            P.dve(lambda e: e.tensor_tensor(out=ho[bb][:], in0=hs_[:], in1=sg[bb][:], op=ALU.mult), reads=[hs_, sg[bb]], writes=[ho[bb]])
            for h in range(4):
                P.act(lambda e, h=h: e.activation(out=junk[:], in_=ho[bb][:, 64 * h:64 * h + 64], func=AF.Square, accum_out=ss[bb][:, h:h + 1]), reads=[ho[bb]], writes=[junk, ss[bb]])
            P.act(lambda e: e.activation(out=ss[bb][:], in_=ss[bb][:], func=AF.Sqrt, scale=1.0 / 64, bias=G.epsb[:, 0:1]), reads=[ss[bb], G.epsb], writes=[ss[bb]])
            P.dve(lambda e: e.reciprocal(out=ss[bb][:], in_=ss[bb][:]), reads=[ss[bb]], writes=[ss[bb]])
            P.dve(lambda e: e.tensor_tensor(out=ho[bb][:].rearrange("p (h e) -> p h e", h=4), in0=ho[bb][:].rearrange("p (h e) -> p h e", h=4),
                                            in1=bc_ap(ss[bb][:], [[1, 4], [0, 64]]), op=ALU.mult), reads=[ho[bb], ss[bb]], writes=[ho[bb]])
            P.pool(lambda e: e.tensor_tensor(out=ho[bb][:], in0=ho[bb][:], in1=nw_b[:], op=ALU.mult), reads=[ho[bb], nw_b], writes=[ho[bb]])
            ps = G.nextps()
            for c in range(2):
                P.pe(lambda e, ps=ps, c=c: e.transpose(out=ps[:, 128 * c:128 * c + 128], in_=ho[bb][:, 128 * c:128 * c + 128], identity=G.ident[:]), reads=[ho[bb], G.ident], writes=[ps])
            P.act(lambda e, ps=ps: e.activation(out=ytb[bb][:], in_=ps[:, 0:256].rearrange("p (c t) -> p c t", c=2), func=AF.Copy), reads=[ps], writes=[ytb[bb]])
            P.dma(yT[0:2, :, tok:tok + 128].rearrange("c p t -> p c t"), ytb[bb][:], reads=[ytb[bb]], writes=[yT])

    steps = []
    it = 0
    for q in range(NT128):
        steps.append(chunk_pass(q, 0, it, True)); it += 1
    for q in [1, 0] + list(range(NT128 - 1, 1, -1)):
        steps.append(chunk_pass(q, 1, it, False)); it += 1
    return steps


def ssd_steps2(G, l, es):
    nc, P, I = G.nc, G.P, G.I
    sb = G.sb
    sc = G.scr
    with_ctx = l < DEPTH - 1
    yT = sc['yT']
    NEG = -30000.0
    yacc_d = sc['yacc_d']
    BF16 = mybir.dt.bfloat16 if BF_D else F32
    yst = [sb(es, 'd_yst%d' % i, [128, 512]) for i in range(2)]; yprev = [sb(es, 'd_yprev%d' % i, [128, 512]) for i in range(2)]
    D_b = load_bcast(G, es, 'd_D', I['ssd_d'][l:l + 1, :], 8)
    nw_b = load_bcast(G, es, 'd_nw', I['ssd_norm_w'][l:l + 1, :], 512)
    dt_all = sb(es, 'd_dtall', [128, NT128, 16]); a_all = sb(es, 'd_aall', [128, NT128, 16])
    acum_all = sb(es, 'd_acall', [128, NT128, 16]); etot_all = sb(es, 'd_etall', [128, NT128, 16]); wgt_all = sb(es, 'd_wgall', [128, NT128, 16])
    negones = sb(es, 'd_negones', [128, 128])
    nm = [sb(es, 'd_nm%d' % d, [128, 4, 128], mybir.dt.bfloat16) for d in range(2)]
    P.pool(lambda e: e.memset(negones[:], -1.0), writes=[negones])
    with contextlib.ExitStack() as es0:
        alog_b = load_bcast(G, es0, 'd_alog', I['ssd_a_log'][l:l + 1].rearrange("o d h -> o (d h)"), 16)
        dtb_b = load_bcast(G, es0, 'd_dtb', I['ssd_dt_bias'][l:l + 1].rearrange("o d h -> o (d h)"), 16)
        A_b = sb(es0, 'd_A', [128, 16]); nmf = sb(es0, 'd_nmf', [128, 128])
        P.act(lambda e: e.activation(out=A_b[:], in_=alog_b[:], func=AF.Exp), reads=[alog_b], writes=[A_b])
        P.dve(lambda e: e.tensor_scalar(out=A_b[:], in0=A_b[:], scalar1=-1.0, scalar2=None, op0=ALU.mult), reads=[A_b], writes=[A_b])
        for d, tri in enumerate((G.triU, G.triL)):
            P.dve(lambda e, tri=tri: e.tensor_scalar(out=nmf[:], in0=tri[:], scalar1=-1.0, scalar2=-NEG, op0=ALU.add, op1=ALU.mult), reads=[tri], writes=[nmf])
            P.dve(lambda e, d=d: e.tensor_copy(out=nm[d][:], in_=bc_ap(nmf[:], [[0, 4], [1, 128]])), reads=[nmf], writes=[nm[d]])
        P.dma(dt_all[:], sc['ddtTM'][:, :].rearrange("(q p) c -> p q c", p=128), reads=[sc['ddtTM']], writes=[dt_all])
        P.dve(lambda e: e.tensor_tensor(out=dt_all[:], in0=dt_all[:], in1=bc_ap(dtb_b[:], [[0, NT128], [1, 16]]), op=ALU.add), reads=[dt_all, dtb_b], writes=[dt_all])
        P.act(lambda e: e.activation(out=dt_all[:], in_=dt_all[:], func=AF.Exp), reads=[dt_all], writes=[dt_all])
        P.act(lambda e: e.activation(out=dt_all[:], in_=dt_all[:], func=AF.Ln, bias=1.0), reads=[dt_all], writes=[dt_all])
        P.dve(lambda e: e.tensor_tensor(out=a_all[:], in0=dt_all[:], in1=bc_ap(A_b[:], [[0, NT128], [1, 16]]), op=ALU.mult), reads=[dt_all, A_b], writes=[a_all])
        cum_all, tot_all = gate_cumsums(G, es0, 'd_', a_all, 16)
        P.act(lambda e: e.activation(out=acum_all[:], in_=cum_all[:], func=AF.Exp), reads=[cum_all], writes=[acum_all])
        P.act(lambda e: e.activation(out=etot_all[:], in_=tot_all[:], func=AF.Exp), reads=[tot_all], writes=[etot_all])
        P.dve(lambda e: e.tensor_tensor(out=wgt_all[:], in0=tot_all[:], in1=cum_all[:], op=ALU.subtract), reads=[tot_all, cum_all], writes=[wgt_all])
        P.act(lambda e: e.activation(out=wgt_all[:], in_=wgt_all[:], func=AF.Exp), reads=[wgt_all], writes=[wgt_all])
        P.dve(lambda e: e.tensor_tensor(out=wgt_all[:], in0=wgt_all[:], in1=dt_all[:], op=ALU.mult), reads=[wgt_all, dt_all], writes=[wgt_all])
        P.flush()
    NB = 2
    xt = [sb(es, 'd_xt%d' % i, [128, 512], BF16) for i in range(NB)]
    Bt = [sb(es, 'd_Bt%d' % i, [128, 2, 128], BF16) for i in range(NB)]
    Ct = [sb(es, 'd_Ct%d' % i, [128, 2, 128], BF16) for i in range(NB)]
    Btm = [sb(es, 'd_Btm%d' % i, [128, 256], BF16) for i in range(NB)]
    dz = [sb(es, 'd_dz%d' % i, [128, 512]) for i in range(NB)]
    Hs = [sb(es, 'd_Hs%d' % d, [128, 8, 64]) for d in range(2)]
    Hsb = [sb(es, 'd_Hsb%d' % d, [128, 8, 64], BF16) for d in range(2)]
    for d in range(2):
        P.pool(lambda e, d=d: e.memset(Hs[d][:], 0.0), writes=[Hs[d]])
        P.pool(lambda e, d=d: e.memset(Hsb[d][:], 0.0), writes=[Hsb[d]])
    rbig = sb(es, 'd_rbig', [128, 8, 128])
    ex = sb(es, 'd_ex', [128, 8, 128])
    pmb = [sb(es, 'd_pm%d' % i, [128, 8, 128], BF16) for i in range(2)]
    wx2 = [sb(es, 'd_wx%d' % i, [128, 8, 64], BF16) for i in range(2)]
    tmp = sb(es, 'd_tmp', [128, 8, 64]); htmp = sb(es, 'd_htmp', [128, 8, 64])
    yz = sb(es, 'd_yz', [128, 512]); sz = sb(es, 'd_sz', [128, 512]); ssq = sb(es, 'd_ssq', [128, 1]); junk = sb(es, 'd_junk', [128, 512])
    ytb = [sb(es, 'd_ytb%d' % i, [128, 4, 128], mybir.dt.bfloat16) for i in range(2)]

    def chunk_pass(q, d, it, first_pass):
        tok = q * 128
        b = it % NB
        wx = wx2[it % 2]; pm_ = pmb[it % 2]
        P.dma(xt[b][:], sc['xTM'][tok:tok + 128, :], reads=[sc['xTM']], writes=[xt[b]])
        P.dma(Bt[b][:], sc['BT'][:, :, tok:tok + 128].rearrange("g p t -> p g t"), reads=[sc['BT']], writes=[Bt[b]])
        P.dma(Ct[b][:], sc['CT'][:, :, tok:tok + 128].rearrange("g p t -> p g t"), reads=[sc['CT']], writes=[Ct[b]])
        P.dma(Btm[b][:], sc['BTM'][tok:tok + 128, :], reads=[sc['BTM']], writes=[Btm[b]])
        if not first_pass:
            P.dma(dz[b][:], sc['dzTM'][tok:tok + 128, :], reads=[sc['dzTM']], writes=[dz[b]])
            P.dma(yprev[b][:], yacc_d[tok:tok + 128, :], reads=[yacc_d], writes=[yprev[b]])
        mask = G.triU if d == 0 else G.triL
        P.dve(lambda e: e.tensor_tensor(out=rbig[:], in0=bc_ap(a_all[:, q, 8 * d:8 * d + 8], [[1, 8], [0, 128]]), in1=bc_ap(mask[:], [[0, 8], [1, 128]]), op=ALU.mult),
              reads=[a_all, mask], writes=[rbig])
        cb = [G.nextps(), G.nextps()]
        for hf in range(2):
            P.pe(lambda e, hf=hf: e.matmul(cb[hf][:, :], lhsT=G.ones[:], rhs=rbig[:, 4 * hf:4 * hf + 4, :], start=True, stop=False), reads=[G.ones, rbig], writes=[cb[hf]])
            for hh in range(4):
                P.pe(lambda e, hf=hf, hh=hh: e.matmul(cb[hf][:, 128 * hh:128 * hh + 128], lhsT=rbig[:, 4 * hf + hh, :], rhs=negones[:], start=False, stop=False),
                     reads=[rbig, negones], writes=[cb[hf]])
            P.pe(lambda e, hf=hf: e.matmul(cb[hf][:, :], lhsT=G.identb[:], rhs=nm[d][:], start=False, stop=True), reads=[G.identb, nm[d]], writes=[cb[hf]])
            P.act(lambda e, hf=hf: e.activation(out=ex[:, 4 * hf:4 * hf + 4, :], in_=cb[hf][:, :].rearrange("p (h t) -> p h t", h=4), func=AF.Exp), reads=[cb[hf]], writes=[ex])
        pss = G.nextps()
        for g in range(2):
            P.pe(lambda e, g=g: e.matmul(pss[:, 128 * g:128 * g + 128], lhsT=Bt[b][:, g, :], rhs=Ct[b][:, g, :], start=True, stop=True), reads=[Bt[b], Ct[b]], writes=[pss])
        P.dve(lambda e: e.tensor_tensor(out=ex[:], in0=ex[:], in1=bc_ap(dt_all[:, q, 8 * d:8 * d + 8], [[1, 8], [0, 128]]), op=ALU.mult), reads=[ex, dt_all], writes=[ex])
        P.dve(lambda e: e.tensor_tensor(out=pm_[:].rearrange("p (g h) t -> p g h t", g=2), in0=ex[:].rearrange("p (g h) t -> p g h t", g=2),
                                        in1=bc_ap(pss[:, 0:256], [[128, 2], [0, 4], [1, 128]]), op=ALU.mult), reads=[ex, pss], writes=[pm_])
        P.pool(lambda e: e.tensor_tensor(out=wx[:], in0=xt[b][:].rearrange("p (h e) -> p h e", h=8), in1=bc_ap(wgt_all[:, q, 8 * d:8 * d + 8], [[1, 8], [0, 64]]), op=ALU.mult),
               reads=[xt[b], wgt_all], writes=[wx])
        yield
        psd = G.nextps()
        pso = G.nextps()
        for g in range(2):
            P.pe(lambda e, g=g: e.matmul(pso[:, 256 * g:256 * g + 256], lhsT=Ct[b][:, g, :], rhs=Hsb[d][:, 4 * g:4 * g + 4, :], start=True, stop=True), reads=[Ct[b], Hsb[d]], writes=[pso])
        for h in range(8):
            P.pe(lambda e, h=h: e.matmul(psd[:, 64 * h:64 * h + 64], lhsT=pm_[:, h, :], rhs=xt[b][:, 64 * h:64 * h + 64], start=True, stop=True), reads=[pm_, xt[b]], writes=[psd])
        P.dve(lambda e: e.tensor_tensor(out=tmp[:], in0=pso[:, :].rearrange("p (h e) -> p h e", h=8), in1=bc_ap(acum_all[:, q, 8 * d:8 * d + 8], [[1, 8], [0, 64]]), op=ALU.mult),
              reads=[pso, acum_all], writes=[tmp])
        ys_ = yst[it % 2]
        if first_pass:
            P.dve(lambda e: e.tensor_tensor(out=ys_[:], in0=psd[:, :], in1=tmp[:].rearrange("p h e -> p (h e)"), op=ALU.add), reads=[psd, tmp], writes=[ys_])
            P.dma(yacc_d[tok:tok + 128, :], ys_[:], reads=[ys_], writes=[yacc_d])
        else:
            P.dve(lambda e: e.tensor_tensor(out=tmp[:].rearrange("p h e -> p (h e)"), in0=psd[:, :], in1=tmp[:].rearrange("p h e -> p (h e)"), op=ALU.add), reads=[psd, tmp], writes=[tmp])
            P.pool(lambda e: e.tensor_tensor(out=ys_[:], in0=yprev[b][:], in1=tmp[:].rearrange("p h e -> p (h e)"), op=ALU.add), reads=[tmp, yprev[b]], writes=[ys_])
        pst = G.nextps()
        for g in range(2):
            P.pe(lambda e, g=g: e.matmul(pst[:, 256 * g:256 * g + 256], lhsT=Btm[b][:, 128 * g:128 * g + 128], rhs=wx[:, 4 * g:4 * g + 4, :], start=True, stop=True), reads=[Btm[b], wx], writes=[pst])
        P.dve(lambda e: e.tensor_tensor(out=htmp[:], in0=Hs[d][:], in1=bc_ap(etot_all[:, q, 8 * d:8 * d + 8], [[1, 8], [0, 64]]), op=ALU.mult), reads=[Hs[d], etot_all], writes=[htmp])
        P.dve(lambda e: e.tensor_tensor(out=Hs[d][:].rearrange("p h e -> p (h e)"), in0=pst[:, :], in1=htmp[:].rearrange("p h e -> p (h e)"), op=ALU.add), reads=[pst, htmp], writes=[Hs[d]])
        P.act(lambda e: e.activation(out=Hsb[d][:], in_=Hs[d][:], func=AF.Copy), reads=[Hs[d]], writes=[Hsb[d]])
        if not first_pass and (with_ctx or q >= 2):
            bb = it % 2
            P.pool(lambda e: e.tensor_tensor(out=tmp[:], in0=xt[b][:].rearrange("p (h e) -> p h e", h=8), in1=bc_ap(D_b[:], [[1, 8], [0, 64]]), op=ALU.mult), reads=[xt[b], D_b], writes=[tmp])
            P.pool(lambda e: e.tensor_tensor(out=yz[:], in0=ys_[:], in1=tmp[:].rearrange("p h e -> p (h e)"), op=ALU.add), reads=[ys_, tmp], writes=[yz])
            P.act(lambda e: e.activation(out=sz[:], in_=dz[b][:], func=AF.Silu), reads=[dz[b]], writes=[sz])
            P.dve(lambda e: e.tensor_tensor(out=yz[:], in0=yz[:], in1=sz[:], op=ALU.mult), reads=[yz, sz], writes=[yz])
            P.act(lambda e: e.activation(out=junk[:], in_=yz[:], func=AF.Square, accum_out=ssq[:, 0:1]), reads=[yz], writes=[junk, ssq])
            P.act(lambda e: e.activation(out=ssq[:], in_=ssq[:], func=AF.Sqrt, scale=1.0 / 512, bias=G.epsb[:, 0:1]), reads=[ssq, G.epsb], writes=[ssq])
            P.dve(lambda e: e.reciprocal(out=ssq[:], in_=ssq[:]), reads=[ssq], writes=[ssq])
            P.dve(lambda e: e.scalar_tensor_tensor(out=yz[:], in0=yz[:], scalar=ssq[:, 0:1], in1=nw_b[:], op0=ALU.mult, op1=ALU.mult), reads=[yz, ssq, nw_b], writes=[yz])
            ps = G.nextps()
            for c in range(4):
                P.pe(lambda e, ps=ps, c=c: e.transpose(out=ps[:, 128 * c:128 * c + 128], in_=yz[:, 128 * c:128 * c + 128], identity=G.ident[:]), reads=[yz, G.ident], writes=[ps])
            P.act(lambda e, ps=ps: e.activation(out=ytb[bb][:], in_=ps[:, :].rearrange("p (c t) -> p c t", c=4), func=AF.Copy), reads=[ps], writes=[ytb[bb]])
            P.dma(yT[6:10, :, tok:tok + 128].rearrange("c p t -> p c t"), ytb[bb][:], reads=[ytb[bb]], writes=[yT])

    steps = []
    it = 0
    for q in range(NT128):
        steps.append(chunk_pass(q, 0, it, True)); it += 1
    for q in [1, 0] + list(range(NT128 - 1, 1, -1)):
        steps.append(chunk_pass(q, 1, it, False)); it += 1
    return steps
```
